# Optimizing a Trainium2 kernel written in Bass

```python
import jax, jax.numpy as jnp
from jax import lax
import numpy as np

D_MODEL = 1024
BATCH = 8
SEQ = 4096
DEPTH = 2

CHUNK = 64
N_META = 16
DEEPNORM_ALPHA = (2 * DEPTH) ** 0.25
DEEPNORM_BETA = (8 * DEPTH) ** -0.25
N_EVEN = (DEPTH + 1) // 2
N_ODD = DEPTH // 2
LN_EPS = 1e-5
M_WIDTH = D_MODEL
M_HEADS = 4
M_HEAD_DIM = M_WIDTH // M_HEADS
M_QKV_BLOCK = 4
M_CONV = 4
R_WIDTH = D_MODEL
R_BLOCKS = 8
R_BLOCK_DIM = R_WIDTH // R_BLOCKS
R_CONV = 4
R_C = 8.0
A_IN_WIDTH = 2 * M_WIDTH + 2 * R_WIDTH
C_HEAD_DIM = 128
C_HEADS = D_MODEL // C_HEAD_DIM
C_Q_RANK = 3 * D_MODEL // 8
C_KV_RANK = D_MODEL // 4
IDX_HEADS = 8
IDX_DIM = 64
TOP_K_MAX = 256
Q_BLOCK = 128
C_IN_WIDTH = C_Q_RANK + C_KV_RANK + IDX_DIM + IDX_HEADS
D_FF = 11 * D_MODEL // 4
FFN_CONV = 3

kernel_name = "hybrid_mlstm_rglru_dsa_convffn_deepnorm"


def layer_norm(x, g, b):
    xf = x.astype(jnp.float32)
    mu = xf.mean(-1, keepdims=True)
    var = jnp.square(xf - mu).mean(-1, keepdims=True)
    return ((xf - mu) * lax.rsqrt(var + LN_EPS) * g + b).astype(x.dtype)


def rms_norm(x, g):
    xf = x.astype(jnp.float32)
    return (xf * lax.rsqrt(jnp.square(xf).mean(-1, keepdims=True) + LN_EPS) * g).astype(x.dtype)


def causal_dwconv(x, w, b):
    k = w.shape[0]
    y = lax.conv_general_dilated(x, w[:, None, :], window_strides=(1,), padding=[(k - 1, 0)],
                                 dimension_numbers=('NWC', 'WIO', 'NWC'),
                                 feature_group_count=x.shape[-1])
    return y + b


def block_diag(x, w):
    nb, bs, bo = w.shape
    xs = x.reshape(x.shape[:-1] + (nb, bs))
    return jnp.einsum('btnc,ncd->btnd', xs, w).reshape(x.shape[:-1] + (nb * bo,))


def chunk_ids(n):
    pos = jnp.arange(n)
    return jnp.where(pos < N_META, 0, (pos - N_META) // CHUNK + 1)


def mlstm_chunk_scan(q, k, v, ig, lf):
    nc, bsz, nh, L, dh = q.shape
    tril = jnp.tril(jnp.ones((L, L), dtype=bool))

    def step(carry, inp):
        C, n, m = carry
        qc, kc, vc, ic, fc = inp
        g = jnp.cumsum(fc, axis=-1)
        d = jnp.where(tril, g[..., :, None] - g[..., None, :] + ic[..., None, :], -jnp.inf)
        m_inter = g + m[..., None]
        m_t = jnp.maximum(m_inter, d.max(-1))
        w_inter = jnp.exp(m_inter - m_t)
        s = jnp.einsum('bhtd,bhsd->bhts', qc, kc) * jnp.exp(d - m_t[..., None])
        num = w_inter[..., None] * jnp.einsum('bhvd,bhtd->bhtv', C, qc) + jnp.einsum('bhts,bhsv->bhtv', s, vc)
        den = w_inter * jnp.einsum('bhd,bhtd->bht', n, qc) + s.sum(-1)
        h = num / jnp.maximum(jnp.abs(den), jnp.exp(-m_t))[..., None]
        g_end = g[..., -1]
        a_s = g_end[..., None] - g + ic
        m_new = jnp.maximum(g_end + m, a_s.max(-1))
        decay = jnp.exp(g_end + m - m_new)
        w_s = jnp.exp(a_s - m_new[..., None])
        C = decay[..., None, None] * C + jnp.einsum('bhs,bhsv,bhsd->bhvd', w_s, vc, kc)
        n = decay[..., None] * n + jnp.einsum('bhs,bhsd->bhd', w_s, kc)
        return (C, n, m_new), h

    init = (jnp.zeros((bsz, nh, dh, dh), jnp.float32), jnp.zeros((bsz, nh, dh), jnp.float32),
            jnp.zeros((bsz, nh), jnp.float32))
    _, h = lax.scan(step, init, (q, k, v, ig, lf))
    return h


def mlstm_group(xm, og, conv_w, conv_b, wq, wk, wv, w_if, b_if, norm_g):
    bsz, t, _ = xm.shape
    xc = jax.nn.silu(causal_dwconv(xm, conv_w, conv_b))
    q = block_diag(xc, wq)
    k = block_diag(xc, wk)
    v = block_diag(xm, wv)
    gates = (jnp.concatenate([q, k, v], axis=-1) @ w_if + b_if).astype(jnp.float32)
    ig = gates[..., :M_HEADS]
    lf = jax.nn.log_sigmoid(gates[..., M_HEADS:])
    k = k * (M_HEAD_DIM ** -0.5)
    pad = (-t) % CHUNK
    nc = (t + pad) // CHUNK

    def to_chunks(a):
        a = jnp.pad(a.astype(jnp.float32), [(0, 0), (0, pad)] + [(0, 0)] * (a.ndim - 2))
        a = a.reshape((bsz, nc, CHUNK) + a.shape[2:])
        return jnp.swapaxes(jnp.moveaxis(a, 1, 0), 2, 3)

    heads = lambda a: a.reshape(bsz, t, M_HEADS, M_HEAD_DIM)
    h = mlstm_chunk_scan(to_chunks(heads(q)), to_chunks(heads(k)), to_chunks(heads(v)),
                         to_chunks(ig), to_chunks(lf))
    h = jnp.moveaxis(jnp.swapaxes(h, 2, 3), 0, 1).reshape(bsz, nc * CHUNK, M_HEADS, M_HEAD_DIM)[:, :t]
    mu = h.mean(-1, keepdims=True)
    var = jnp.square(h - mu).mean(-1, keepdims=True)
    hn = ((h - mu) * lax.rsqrt(var + LN_EPS)).reshape(bsz, t, M_WIDTH) * norm_g
    return (jax.nn.sigmoid(og.astype(jnp.float32)) * hn).astype(xm.dtype)


def rglru_group(xr, gr, conv_w, conv_b, w_a, b_a, w_x, b_x, lam):
    xc = causal_dwconv(xr, conv_w, conv_b)
    r = jax.nn.sigmoid((block_diag(xc, w_a) + b_a).astype(jnp.float32))
    i = jax.nn.sigmoid((block_diag(xc, w_x) + b_x).astype(jnp.float32))
    log_a = -R_C * r * jax.nn.softplus(-lam.astype(jnp.float32))
    a = jnp.exp(log_a)
    u = jnp.sqrt(-jnp.expm1(2.0 * log_a)) * (i * xc.astype(jnp.float32))

    def combine(lhs, rhs):
        a1, b1 = lhs
        a2, b2 = rhs
        return a1 * a2, a2 * b1 + b2

    _, h = lax.associative_scan(combine, (a, u), axis=1)
    return (jax.nn.gelu(gr.astype(jnp.float32)) * h).astype(xr.dtype)


def even_mixer(x, w_in, conv_w, conv_b, wq, wk, wv, w_if, b_if, norm_g,
               r_conv_w, r_conv_b, w_a, b_a, w_x, b_x, lam, w_out):
    proj = x @ w_in
    xm = proj[..., :M_WIDTH]
    og = proj[..., M_WIDTH:2 * M_WIDTH]
    xr = proj[..., 2 * M_WIDTH:2 * M_WIDTH + R_WIDTH]
    gr = proj[..., 2 * M_WIDTH + R_WIDTH:]
    hm = mlstm_group(xm, og, conv_w, conv_b, wq, wk, wv, w_if, b_if, norm_g)
    hr = rglru_group(xr, gr, r_conv_w, r_conv_b, w_a, b_a, w_x, b_x, lam)
    return jnp.concatenate([hm, hr], axis=-1) @ w_out


def odd_mixer(x, w_in, q_norm_g, kv_norm_g, w_uq, w_uk, w_uv, w_qidx, kidx_g, kidx_b, w_out):
    bsz, t, _ = x.shape
    proj = x @ w_in
    o1 = C_Q_RANK
    o2 = o1 + C_KV_RANK
    o3 = o2 + IDX_DIM
    c_q = rms_norm(proj[..., :o1], q_norm_g)
    c_kv = rms_norm(proj[..., o1:o2], kv_norm_g)
    k_idx = layer_norm(proj[..., o2:o3], kidx_g, kidx_b)
    w_idx = proj[..., o3:] * (IDX_HEADS ** -0.5)
    q_idx = (c_q @ w_qidx).reshape(bsz, t, IDX_HEADS, IDX_DIM)
    q = (c_q @ w_uq).reshape(bsz, t, C_HEADS, C_HEAD_DIM)
    q_lat = jnp.einsum('bthd,hrd->bthr', q, w_uk)
    k_sel = min(TOP_K_MAX, t // 4)
    pad = (-t) % Q_BLOCK
    nb = (t + pad) // Q_BLOCK
    cid_all = chunk_ids(t + pad)
    cid_k = cid_all[:t]
    cid_q = cid_all.reshape(nb, Q_BLOCK)

    def blocks(a):
        a = jnp.pad(a, [(0, 0), (0, pad)] + [(0, 0)] * (a.ndim - 2))
        return jnp.moveaxis(a.reshape((bsz, nb, Q_BLOCK) + a.shape[2:]), 1, 0)

    def attend(blk):
        qi, wi, ql, cq = blk
        s = jnp.einsum('bqhd,bkd->bqhk', qi, k_idx).astype(jnp.float32)
        score = jnp.einsum('bqhk,bqh->bqk', jax.nn.relu(s), wi.astype(jnp.float32))
        allowed = cid_k[None, :] <= cq[:, None]
        score = jnp.where(allowed[None], score, -jnp.inf)
        vals, idx = lax.top_k(score, k_sel)
        valid = jnp.isfinite(vals)
        kv = jax.vmap(lambda c, i: c[i])(c_kv, idx)
        logits = jnp.einsum('bqhr,bqkr->bqhk', ql, kv).astype(jnp.float32) * (C_HEAD_DIM ** -0.5)
        logits = jnp.where(valid[:, :, None, :], logits, -jnp.inf)
        p = jax.nn.softmax(logits, axis=-1)
        return jnp.einsum('bqhk,bqkr->bqhr', p.astype(kv.dtype), kv)

    o_lat = lax.map(attend, (blocks(q_idx), blocks(w_idx), blocks(q_lat), cid_q))
    o_lat = jnp.moveaxis(o_lat, 0, 1).reshape(bsz, nb * Q_BLOCK, C_HEADS, C_KV_RANK)[:, :t]
    out = jnp.einsum('bthr,hrd->bthd', o_lat, w_uv).reshape(bsz, t, C_HEADS * C_HEAD_DIM)
    return out @ w_out


def conv_ffn(x, w_up, conv_w, conv_b, w_down):
    up = x @ w_up
    u = up[..., :D_FF]
    g = up[..., D_FF:]
    h = jax.nn.gelu(causal_dwconv(g, conv_w, conv_b)) * u
    return h @ w_down


def setup_inputs(seed: int = 0) -> dict:
    key = jax.random.key(seed)
    it = iter(jax.random.split(key, 48))

    def nrm(shape, scale):
        return jax.random.normal(next(it), shape, jnp.float32) * scale

    d = D_MODEL
    ne, no = N_EVEN, N_ODD
    a0 = jax.random.uniform(next(it), (ne, R_WIDTH), jnp.float32, minval=0.9, maxval=0.999)
    s0 = a0 ** (1.0 / R_C)
    b_if = jnp.concatenate([nrm((ne, M_HEADS), 0.1),
                            jnp.linspace(3.0, 6.0, M_HEADS)[None, :] + nrm((ne, M_HEADS), 0.01)], axis=-1)
    return {
        "x": nrm((BATCH, SEQ, d), 1.0),
        "meta_tokens": nrm((N_META, d), 1.0),
        "a_w_in": nrm((ne, d, A_IN_WIDTH), d ** -0.5),
        "a_conv_w": nrm((ne, M_CONV, M_WIDTH), M_CONV ** -0.5),
        "a_conv_b": nrm((ne, M_WIDTH), 0.01),
        "a_wq": nrm((ne, M_WIDTH // M_QKV_BLOCK, M_QKV_BLOCK, M_QKV_BLOCK), M_QKV_BLOCK ** -0.5),
        "a_wk": nrm((ne, M_WIDTH // M_QKV_BLOCK, M_QKV_BLOCK, M_QKV_BLOCK), M_QKV_BLOCK ** -0.5),
        "a_wv": nrm((ne, M_WIDTH // M_QKV_BLOCK, M_QKV_BLOCK, M_QKV_BLOCK), M_QKV_BLOCK ** -0.5),
        "a_w_if": nrm((ne, 3 * M_WIDTH, 2 * M_HEADS), 0.1 * (3 * M_WIDTH) ** -0.5),
        "a_b_if": b_if,
        "a_norm_g": 1.0 + nrm((ne, M_WIDTH), 0.02),
        "b_conv_w": nrm((ne, R_CONV, R_WIDTH), R_CONV ** -0.5),
        "b_conv_b": nrm((ne, R_WIDTH), 0.01),
        "b_w_a": nrm((ne, R_BLOCKS, R_BLOCK_DIM, R_BLOCK_DIM), R_BLOCK_DIM ** -0.5),
        "b_b_a": nrm((ne, R_WIDTH), 0.01),
        "b_w_x": nrm((ne, R_BLOCKS, R_BLOCK_DIM, R_BLOCK_DIM), R_BLOCK_DIM ** -0.5),
        "b_b_x": nrm((ne, R_WIDTH), 0.01),
        "b_lambda": jnp.log(s0) - jnp.log1p(-s0),
        "ab_w_out": nrm((ne, M_WIDTH + R_WIDTH, d), DEEPNORM_BETA * (M_WIDTH + R_WIDTH) ** -0.5),
        "c_w_in": nrm((no, d, C_IN_WIDTH), d ** -0.5),
        "c_q_norm_g": 1.0 + nrm((no, C_Q_RANK), 0.02),
        "c_kv_norm_g": 1.0 + nrm((no, C_KV_RANK), 0.02),
        "c_w_uq": nrm((no, C_Q_RANK, C_HEADS * C_HEAD_DIM), C_Q_RANK ** -0.5),
        "c_w_uk": nrm((no, C_HEADS, C_KV_RANK, C_HEAD_DIM), C_KV_RANK ** -0.5),
        "c_w_uv": nrm((no, C_HEADS, C_KV_RANK, C_HEAD_DIM), C_KV_RANK ** -0.5),
        "c_w_qidx": nrm((no, C_Q_RANK, IDX_HEADS * IDX_DIM), C_Q_RANK ** -0.5),
        "c_kidx_ln_g": 1.0 + nrm((no, IDX_DIM), 0.02),
        "c_kidx_ln_b": nrm((no, IDX_DIM), 0.01),
        "c_w_out": nrm((no, C_HEADS * C_HEAD_DIM, d), DEEPNORM_BETA * (C_HEADS * C_HEAD_DIM) ** -0.5),
        "mix_ln_g": 1.0 + nrm((DEPTH, d), 0.02),
        "mix_ln_b": nrm((DEPTH, d), 0.01),
        "ffn_w_up": nrm((DEPTH, d, 2 * D_FF), d ** -0.5),
        "ffn_conv_w": nrm((DEPTH, FFN_CONV, D_FF), FFN_CONV ** -0.5),
        "ffn_conv_b": nrm((DEPTH, D_FF), 0.01),
        "ffn_w_down": nrm((DEPTH, D_FF, d), DEEPNORM_BETA * D_FF ** -0.5),
        "ffn_ln_g": 1.0 + nrm((DEPTH, d), 0.02),
        "ffn_ln_b": nrm((DEPTH, d), 0.01),
    }


def reference(x, meta_tokens, a_w_in, a_conv_w, a_conv_b, a_wq, a_wk, a_wv, a_w_if, a_b_if, a_norm_g,
              b_conv_w, b_conv_b, b_w_a, b_b_a, b_w_x, b_b_x, b_lambda, ab_w_out,
              c_w_in, c_q_norm_g, c_kv_norm_g, c_w_uq, c_w_uk, c_w_uv, c_w_qidx, c_kidx_ln_g, c_kidx_ln_b,
              c_w_out, mix_ln_g, mix_ln_b, ffn_w_up, ffn_conv_w, ffn_conv_b, ffn_w_down, ffn_ln_g, ffn_ln_b):
    bsz = x.shape[0]
    meta = jnp.broadcast_to(meta_tokens[None].astype(x.dtype), (bsz, N_META, D_MODEL))
    h = jnp.concatenate([meta, x], axis=1)
    for layer in range(DEPTH):
        j = layer // 2
        if layer % 2 == 0:
            mix = even_mixer(h, a_w_in[j], a_conv_w[j], a_conv_b[j], a_wq[j], a_wk[j], a_wv[j],
                             a_w_if[j], a_b_if[j], a_norm_g[j], b_conv_w[j], b_conv_b[j],
                             b_w_a[j], b_b_a[j], b_w_x[j], b_b_x[j], b_lambda[j], ab_w_out[j])
        else:
            mix = odd_mixer(h, c_w_in[j], c_q_norm_g[j], c_kv_norm_g[j], c_w_uq[j], c_w_uk[j],
                            c_w_uv[j], c_w_qidx[j], c_kidx_ln_g[j], c_kidx_ln_b[j], c_w_out[j])
        h = layer_norm(DEEPNORM_ALPHA * h + mix, mix_ln_g[layer], mix_ln_b[layer])
        ffn = conv_ffn(h, ffn_w_up[layer], ffn_conv_w[layer], ffn_conv_b[layer], ffn_w_down[layer])
        h = layer_norm(DEEPNORM_ALPHA * h + ffn, ffn_ln_g[layer], ffn_ln_b[layer])
    return h[:, N_META:]
```

```python
import numpy as np
import concourse.bass as bass
import concourse.mybir as mybir
from concourse.bass_utils import run_bass_kernel_spmd

F32 = mybir.dt.float32
BF16 = mybir.dt.bfloat16
AF = mybir.ActivationFunctionType
ALU = mybir.AluOpType
AX = mybir.AxisListType

SEM_LIMIT = 30000


class Buf:
    __slots__ = ("name", "w", "r", "wf")

    def __init__(self, name):
        self.name = name
        self.w = {}
        self.r = {}
        self.wf = {}


class View:
    __slots__ = ("ap", "buf")

    def __init__(self, ap, buf):
        self.ap = ap
        self.buf = buf

    def __getitem__(self, idx):
        return View(self.ap[idx], self.buf)

    def re(self, pattern, **kw):
        return View(self.ap.rearrange(pattern, **kw), self.buf)


class Tile:
    def __init__(self, handle, buf):
        self.h = handle
        self.buf = buf

    def __getitem__(self, idx):
        return View(self.h[idx], self.buf)


class DramT:
    def __init__(self, handle, name):
        self.h = handle
        self.name = name
        self.bufs = {}

    def __getitem__(self, idx):
        return self.v(None)[idx]

    def v(self, key):
        if key not in self.bufs:
            self.bufs[key] = Buf(f"{self.name}:{key}")
        return View(self.h.ap(), self.bufs[key])


class _Op:
    __slots__ = ("fn", "deps", "sig", "val", "dma", "sem")

    def __init__(self, fn, deps, dma):
        self.fn = fn
        self.deps = deps
        self.sig = dma is not None
        self.val = None
        self.dma = dma
        self.sem = None


ENGS = ("pe", "act", "dve", "pool", "sp")
NSLOT = 8


class Prog:
    def __init__(self, nc):
        self.nc = nc
        self.ops = {e: [] for e in ENGS}
        self.ctx = []
        self.dma_cnt = {}
        self.dma_rr = {q: 0 for q in ENGS}
        self.n_sb = 0

    def dram(self, name, shape, dt, kind="Internal"):
        h = self.nc.dram_tensor(name, list(shape), dt, kind=kind)
        return DramT(h, name)

    def sb(self, name, shape, dt):
        self.n_sb += 1
        name = f"{name}_{self.n_sb}"
        g = self.nc.sbuf_tensor(name, list(shape), dt)
        h = g.__enter__()
        self.ctx.append(g)
        return Tile(h, Buf(name))

    def ps(self, name, shape, dt=F32):
        self.n_sb += 1
        name = f"{name}_{self.n_sb}"
        g = self.nc.psum_tensor(name, list(shape), dt)
        h = g.__enter__()
        self.ctx.append(g)
        return Tile(h, Buf(name))

    def op(self, eng, fn, r=(), w=(), wp=(), dmaq=False):
        deps = {}

        def add(d):
            for k, v in d.items():
                if deps.get(k, -1) < v:
                    deps[k] = v

        for v in r:
            add(v.buf.w)
        for v in w:
            add(v.buf.w)
            add(v.buf.r)
        for v in wp:
            add(v.buf.r)
            add(v.buf.wf)
        idx = len(self.ops[eng])
        dma = None
        if dmaq:
            slot = self.dma_rr[eng]
            self.dma_rr[eng] = (slot + 1) % NSLOT
            key = ("d", eng, slot)
            n = self.dma_cnt.get(key, 0)
            if n > 0:
                if deps.get(key, -1) < n:
                    deps[key] = n
            self.dma_cnt[key] = n + 1
            dma = (key, n + 1)
            tok = (key, n + 1)
        else:
            tok = (("c", eng), idx)
        if eng == "pe":
            deps.pop(("c", "pe"), None)
        self.ops[eng].append(_Op(fn, deps, dma))
        for v in r:
            if v.buf.r.get(tok[0], -1) < tok[1]:
                v.buf.r[tok[0]] = tok[1]
        for v in w:
            v.buf.w = {tok[0]: tok[1]}
            v.buf.wf = {tok[0]: tok[1]}
            v.buf.r = {}
        for v in wp:
            if v.buf.w.get(tok[0], -1) < tok[1]:
                v.buf.w[tok[0]] = tok[1]

    def _e(self, eng):
        return eng

    def mm(self, out, lhsT, rhs, start=True, stop=True, **kw):
        self.op("pe", lambda e: e.matmul(out.ap, lhsT.ap, rhs.ap, start=start, stop=stop, **kw),
                r=[lhsT, rhs], wp=[out] if not start else (), w=[out] if start else ())

    def transpose(self, out, in_, ident):
        self.op("pe", lambda e: e.transpose(out.ap, in_.ap, ident.ap), r=[in_, ident], w=[out])

    def act(self, out, in_, func, scale=1.0, bias=0.0, accum=None, part=False):
        r = [in_]
        sc = scale
        bi = bias
        if isinstance(scale, View):
            r.append(scale)
            sc = scale.ap
        if isinstance(bias, View):
            r.append(bias)
            bi = bias.ap
        kw = {}
        ws = [out]
        if accum is not None:
            kw["accum_out"] = accum.ap
            ws.append(accum)
        self.op("act", lambda e: e.activation(out.ap, in_.ap, func, scale=sc, bias=bi, **kw),
                r=r, w=() if part else ws, wp=ws if part else ())

    def tt(self, eng, out, a, b, op, part=False):
        self.op(eng, lambda e: e.tensor_tensor(out.ap, a.ap, b.ap, op), r=[a, b],
                w=() if part else [out], wp=[out] if part else ())

    def ts(self, eng, out, a, s1, op0, s2=None, op1=None, accum=None, part=False):
        r = [a]
        x1 = s1
        x2 = s2
        if isinstance(s1, View):
            r.append(s1)
            x1 = s1.ap
        if isinstance(s2, View):
            r.append(s2)
            x2 = s2.ap
        kw = {}
        ws = [out]
        if op1 is not None:
            kw["op1"] = op1
        if accum is not None:
            kw["accum_out"] = accum.ap
            ws.append(accum)
        self.op(eng, lambda e: e.tensor_scalar(out.ap, a.ap, x1, x2, op0, **kw), r=r,
                w=() if part else ws, wp=ws if part else ())

    def stt(self, out, in0, scalar, in1, op0, op1, part=False, eng="dve"):
        r = [in0, in1]
        sc = scalar
        if isinstance(scalar, View):
            r.append(scalar)
            sc = scalar.ap
        self.op(eng, lambda e: e.scalar_tensor_tensor(out.ap, in0.ap, sc, in1.ap, op0, op1), r=r,
                w=() if part else [out], wp=[out] if part else ())

    def copy(self, eng, out, in_, part=False):
        if eng == "act":
            self.op(eng, lambda e: e.copy(out.ap, in_.ap), r=[in_],
                    w=() if part else [out], wp=[out] if part else ())
        else:
            self.op(eng, lambda e: e.tensor_copy(out.ap, in_.ap), r=[in_],
                    w=() if part else [out], wp=[out] if part else ())

    def memset(self, eng, out, val):
        self.op(eng, lambda e: e.memset(out.ap, val), w=[out])

    def scan(self, out, d0, d1, init, op0, op1, part=False):
        r = [d0, d1]
        ini = init
        if isinstance(init, View):
            r.append(init)
            ini = init.ap
        self.op("dve", lambda e: e.tensor_tensor_scan(out.ap, d0.ap, d1.ap, ini, op0, op1), r=r,
                w=() if part else [out], wp=[out] if part else ())

    def dma(self, q, out, in_, part=False, **kw):
        self.op(q, lambda e: e.dma_start(out=out.ap, in_=in_.ap, **kw), r=[in_],
                w=() if part else [out], wp=[out] if part else (), dmaq=True)

    def _sem(self, name):
        if name not in self.sems:
            g = self.nc.semaphore(name)
            self.sems[name] = g.__enter__()
            self.sem_ctx.append(g)
        return self.sems[name]

    def _dma_tok(self, key, n):
        per = SEM_LIMIT // 16
        j = (n - 1) // per
        return f"d_{key[1]}_{key[2]}_{j}", 16 * ((n - 1) % per + 1)

    def _resolve(self, k, v):
        if k[0] == "c":
            lst = self.ops[k[1]]
            while lst[v].val is None:
                v += 1
            o = lst[v]
            return o.sem, o.val
        return self._dma_tok(k, v)

    def mark(self):
        return len(self.ctx)

    def flush(self, final=False):
        nc = self.nc
        ops = self.ops
        if not hasattr(self, "sems"):
            self.sems = {}
            self.sem_ctx = []
            self.emitted = {e: 0 for e in ENGS}
            self.sigcnt = {e: 0 for e in ENGS}
            self.seen = {e: {} for e in ENGS}
        for e in ENGS:
            for o in ops[e][self.emitted[e]:]:
                for k, v in o.deps.items():
                    if k[0] == "c" and v >= self.emitted[k[1]]:
                        ops[k[1]][v].sig = True
        fdeps = {}
        for e in ENGS:
            for j in range(len(ops[e]) - 1, self.emitted[e] - 1, -1):
                if ops[e][j].dma is None and ops[e][j].fn is not None:
                    ops[e][j].sig = True
                    fdeps[("c", e)] = j
                    break
        if final:
            for key, n in self.dma_cnt.items():
                fdeps[key] = n
            fdeps.pop(("c", "sp"), None)
            ops["sp"].append(_Op(None, fdeps, None))
        for e in ENGS:
            for o in ops[e][self.emitted[e]:]:
                if o.dma is None and o.sig and o.fn is not None:
                    c = self.sigcnt[e]
                    o.sem = f"c_{e}_{c // SEM_LIMIT}"
                    o.val = c % SEM_LIMIT + 1
                    self._sem(o.sem)
                    self.sigcnt[e] = c + 1
                elif o.dma is not None:
                    self._sem(self._dma_tok(*o.dma)[0])
        block = nc.Block()
        block.__enter__()
        names = {"pe": "tensor", "act": "scalar", "dve": "vector", "pool": "gpsimd", "sp": "sync"}
        for e in ENGS:
            def body(eng, e=e):
                seen = self.seen[e]
                for o in ops[e][self.emitted[e]:]:
                    for k, v in o.deps.items():
                        s, val = self._resolve(k, v)
                        if seen.get(s, 0) >= val:
                            continue
                        seen[s] = val
                        eng.wait_ge(self.sems[s], val)
                    if o.fn is None:
                        continue
                    ins = o.fn(eng)
                    if o.dma is not None:
                        s, val = self._dma_tok(*o.dma)
                        ins.then_inc(self.sems[s], 16)
                    elif o.sig:
                        ins.then_inc(self.sems[o.sem], 1)
                    o.fn = True
            getattr(block, names[e])(body)
        block.__exit__(None, None, None)
        for e in ENGS:
            self.emitted[e] = len(ops[e])

    def release(self, mark):
        while len(self.ctx) > mark:
            self.ctx.pop().__exit__(None, None, None)

    def finish(self):
        self.flush(final=True)
        self.release(0)
        for g in reversed(self.sem_ctx):
            g.__exit__(None, None, None)


D = 1024
SEQ = 4096
NMETA = 16
T = SEQ + NMETA
DFF = 2816
NFC = DFF // 128
ALPHA = 4.0 ** 0.25
EPS = 1e-5
TILES = [(i * 512, 512) for i in range(8)] + [(4096, 16)]
KSEL = 256
NBIS = 14


def bc(view, shape):
    return View(view.ap.broadcast_to(list(shape)), view.buf)


def bc3(view, n):
    a = view.ap.unsqueeze(1)
    shp = list(a.shape)
    shp[1] = n
    return View(a.broadcast_to(shp), view.buf)


class VecPack:
    def __init__(self):
        self.cols = []
        self.off = {}
        self.n = 0

    def add(self, name, vec):
        vec = np.asarray(vec, np.float32).reshape(-1)
        nchunk = (vec.size + 127) // 128
        pad = np.zeros(nchunk * 128, np.float32)
        pad[:vec.size] = vec
        self.cols.append(pad.reshape(nchunk, 128).T)
        self.off[name] = self.n
        self.n += nchunk

    def array(self):
        return np.ascontiguousarray(np.concatenate(self.cols, axis=1))


def vec_layout():
    vp = VecPack()
    pack_vecs(vp, None)
    return vp.off, vp.n


def pack_vecs(vp, inp):
    def g(name, shape):
        if inp is None:
            return np.zeros(shape, np.float32)
        return np.asarray(inp[name], np.float32)

    cw = g("a_conv_w", (1, 4, D))[0]
    for j in range(4):
        vp.add(f"a_cw{j}", cw[j])
    vp.add("a_cb", g("a_conv_b", (1, D))[0])
    vp.add("a_ng", g("a_norm_g", (1, D))[0])
    cw = g("b_conv_w", (1, 4, D))[0]
    for j in range(4):
        vp.add(f"b_cw{j}", cw[j])
    vp.add("b_cb", g("b_conv_b", (1, D))[0])
    vp.add("b_ba", g("b_b_a", (1, D))[0])
    vp.add("b_bx", g("b_b_x", (1, D))[0])
    vp.add("b_lam", g("b_lambda", (1, D))[0])
    mg = g("mix_ln_g", (2, D))
    mb = g("mix_ln_b", (2, D))
    fg = g("ffn_ln_g", (2, D))
    fb = g("ffn_ln_b", (2, D))
    fcw = g("ffn_conv_w", (2, 3, DFF))
    fcb = g("ffn_conv_b", (2, DFF))
    for l in range(2):
        vp.add(f"mix_g{l}", mg[l])
        vp.add(f"mix_b{l}", mb[l])
        vp.add(f"ffn_g{l}", fg[l])
        vp.add(f"ffn_b{l}", fb[l])
        for j in range(3):
            vp.add(f"f_cw{l}_{j}", fcw[l, j])
        vp.add(f"f_cb{l}", fcb[l])
    vp.add("c_qg", g("c_q_norm_g", (1, 384))[0])
    vp.add("c_kvg", g("c_kv_norm_g", (1, 256))[0])
    kg = g("c_kidx_ln_g", (1, 64))[0]
    kb = g("c_kidx_ln_b", (1, 64))[0]
    vp.add("c_kig", np.concatenate([kg, kg]))
    vp.add("c_kib", np.concatenate([kb, kb]))


def block_diag_host(w):
    w = np.asarray(w, np.float32)
    out = np.zeros((8, 128, 128), np.float32)
    for n in range(256):
        c, b = divmod(n, 32)
        out[c, 4 * b:4 * b + 4, 4 * b:4 * b + 4] = w[n]
    return out


def make_consts():
    c = np.zeros((128, 1536), np.float32)
    c[:, 0:128] = np.eye(128, dtype=np.float32)
    c[:, 128:256] = np.triu(np.ones((128, 128), np.float32))
    c[:, 256:384] = 1.0
    for h in range(4):
        c[h, 384 + 128 * h: 384 + 128 * (h + 1)] = 1.0
        c[4 + h, 896 + 128 * h: 896 + 128 * (h + 1)] = 1.0
        c[h, 1408 + h] = 1.0
        c[4 + h, 1412 + h] = 1.0
    return c


class Ctx:
    pass


def fm(dram, key=None):
    return dram.v(key).re("(c p) t -> p c t", p=128)


def load_w(P, dst, src_view, KC):
    for kc in range(KC):
        P.dma("pool", dst[:, kc, :], src_view[:, kc, :], part=True, max_dma_last_dim=4096)


def setup_consts(P, C, din):
    C.cf = P.sb("c_f32", [128, 384], F32)
    P.dma("sp", C.cf[:], din["consts"][:, 0:384])
    C.ident = C.cf[:, 0:128]
    C.triu = C.cf[:, 128:256]
    C.ones = C.cf[:, 256:384]
    C.cb = P.sb("c_bf16", [128, 384], BF16)
    P.copy("dve", C.cb[:], C.cf[:, 0:384])
    C.identb = C.cb[:, 0:128]
    C.triub = C.cb[:, 128:256]
    C.onesb = C.cb[:, 256:384]
    C.voff, nv = vec_layout()
    C.vec = P.sb("c_vec", [128, nv], F32)
    P.dma("sp", C.vec[:], din["vecs"][:])
    C.mean1024 = P.sb("c_mean", [128, 128], BF16)
    P.memset("dve", C.mean1024[:], 1.0 / 1024.0)
    C.cc = P.sb("c_cols", [128, 4], F32)
    P.memset("dve", C.cc[:, 0:1], EPS)
    P.op("dve", lambda e: e.memset(C.cc[:, 1:2].ap, 1.0), wp=[C.cc[:, 1:2]])
    P.op("dve", lambda e: e.memset(C.cc[:, 2:3].ap, -float(np.log(16.0))), wp=[C.cc[:, 2:3]])
    P.op("dve", lambda e: e.memset(C.cc[:, 3:4].ap, 0.0), wp=[C.cc[:, 3:4]])
    C.epsc = C.cc[:, 0:1]
    C.onec = C.cc[:, 1:2]
    C.nl16 = C.cc[:, 2:3]


def vcol(C, name, c=0, n=1):
    o = C.voff[name] + c
    return C.vec[:, o:o + n]


def ln_gen(P, C, y, w, gname, bname, out_f32, tmp, psm, psq, sq, out_bf=None):
    for c in range(8):
        s = sq[c % 2]
        sb_ = View(s[:].ap.bitcast(BF16), s.buf)
        P.tt("pool", sb_[:, 0:w], y[:, c, :w], y[:, c, :w], ALU.mult)
        P.copy("pool", sb_[:, 512:512 + w], y[:, c, :w], part=True)
        P.mm(psm[:, :w], C.mean1024[:], sb_[:, 512:512 + w], c == 0, c == 7)
        P.mm(psq[:, :w], C.mean1024[:], sb_[:, 0:w], c == 0, c == 7)
        yield
    mean = tmp["mean"]
    m2 = tmp["m2"]
    rstd = tmp["rstd"]
    P.copy("act", mean[:, :w], psm[:, :w])
    P.tt("pool", m2[:, :w], mean[:, :w], mean[:, :w], ALU.mult)
    P.tt("dve", m2[:, :w], psq[:, :w], m2[:, :w], ALU.subtract)
    P.act(rstd[:, :w], m2[:, :w], AF.Sqrt, bias=C.epsc[:, 0:1])
    P.op("dve", lambda e, o=rstd[:, :w]: e.reciprocal(o.ap, o.ap), r=[rstd[:, :w]], w=[rstd[:, :w]])
    yield
    for c in range(8):
        t = sq[c % 2]
        P.tt("dve", t[:, :w], y[:, c, :w], mean[:, :w], ALU.subtract)
        P.stt(t[:, :w], t[:, :w], vcol(C, gname, c), rstd[:, :w], ALU.mult, ALU.mult)
        P.act(out_f32[:, c, :w], t[:, :w], AF.Identity, bias=vcol(C, bname, c), part=True)
        if out_bf is not None:
            P.copy("pool", out_bf[:, c, :w], out_f32[:, c, :w], part=True)
        yield


def layer_norm_fm(P, C, y, w, gname, bname, out_f32, tmp, psm, psq, sq, out_bf=None):
    for _ in ln_gen(P, C, y, w, gname, bname, out_f32, tmp, psm, psq, sq, out_bf):
        pass


def phase_embed(P, C, din, H0):
    m = P.mark()
    x = din["x"]
    xin = [P.sb(f"e_xin{i}", [128, 4, 1024], F32) for i in range(2)]
    hT = [P.sb(f"e_hT{i}", [128, 8, 512], F32) for i in range(2)]
    pt = [P.ps(f"e_pt{i}", [128, 512], F32) for i in range(4)]
    H0v = fm(H0)
    xv = x.v(None).re("(k j p) f -> k p j f", j=4, p=128)
    n = 0
    for k in range(9):
        xi = xin[k % 2]
        h = hT[k % 2]
        if k == 8:
            P.dma("sp", xi[0:16, 0, :], din["meta"][:])
            nj, rows, pos, w = 1, 16, 0, 16
        else:
            P.dma("sp", xi[:], xv[k])
            nj, rows, pos, w = 4, 128, 16 + 512 * k, 512
        for c in range(8):
            p_ = pt[n % 4]
            n += 1
            for j in range(nj):
                P.op("pe", lambda e, o=p_[:, j * rows:(j + 1) * rows], i=xi[0:rows, j, c * 128:(c + 1) * 128],
                     idn=C.ident[0:rows, 0:rows]: e.transpose(o.ap, i.ap, idn.ap),
                     r=[xi[:], C.ident], w=[p_[:]] if j == 0 else (), wp=[p_[:]] if j else ())
            if c % 2 == 0:
                P.copy("act", h[:, c, :w], p_[:, :w], part=True)
            else:
                P.copy("dve", h[:, c, :w], p_[:, :w], part=True)
        P.dma("pool", H0v[:, :, pos:pos + w], h[:, :, :w], part=True)
    P.flush()
    P.release(m)


def phase_inproj(P, C, din, H0, XM, SOG, XR, GGR):
    m = P.mark()
    w_in = P.sb("w_in", [128, 8, 4096], BF16)
    load_w(P, w_in, din["a_w_in"].v(None).re("(kc p) n -> p kc n", p=128), 8)
    hf = [P.sb(f"i_hf{i}", [128, 8, 512], F32) for i in range(2)]
    hb = [P.sb(f"i_hb{i}", [128, 8, 512], BF16) for i in range(2)]
    stg = [P.sb(f"i_st{i}", [128, 8, 512], F32) for i in range(2)]
    pss = [P.ps(f"i_ps{i}", [128, 512], F32) for i in range(4)]
    outs = [fm(XM), fm(SOG), fm(XR), fm(GGR)]
    H0v = fm(H0)
    n = 0
    for ti, (pos, w) in enumerate(TILES):
        f = hf[ti % 2]
        b = hb[ti % 2]
        P.dma("sp", f[:, :, :w], H0v[:, :, pos:pos + w])
        P.copy("pool", b[:, :, :w], f[:, :, :w])
        for grp in range(4):
            st = stg[(ti * 4 + grp) % 2]
            for o8 in range(8):
                oc = grp * 8 + o8
                ps = pss[n % 4]
                n += 1
                for kc in range(8):
                    P.mm(ps[:, :w], w_in[:, kc, oc * 128:(oc + 1) * 128], b[:, kc, :w], kc == 0, kc == 7)
                if grp == 1:
                    P.act(st[:, o8, :w], ps[:, :w], AF.Sigmoid, part=True)
                elif grp == 3:
                    P.act(st[:, o8, :w], ps[:, :w], AF.Gelu_apprx_tanh, part=True)
                elif o8 % 2 == 0:
                    P.copy("act", st[:, o8, :w], ps[:, :w], part=True)
                else:
                    P.copy("dve", st[:, o8, :w], ps[:, :w], part=True)
            P.dma("sp", outs[grp][:, :, pos:pos + w], st[:, :, :w], part=True)
    P.flush()
    P.release(m)


def phase_rglru(P, C, din, XR, GGR, HR):
    m = P.mark()
    wa = P.sb("r_wa", [128, 8, 128], BF16)
    wx = P.sb("r_wx", [128, 8, 128], BF16)
    P.dma("pool", wa[:], din["b_w_a"][:])
    P.dma("pool", wx[:], din["b_w_x"][:])
    c8 = P.sb("r_c8", [128, 8], F32)
    P.act(c8[:], vcol(C, "b_lam", 0, 8), AF.Exp, scale=-1.0)
    P.act(c8[:], c8[:], AF.Ln, bias=C.onec[:, 0:1])
    P.ts("dve", c8[:], c8[:], -8.0, ALU.mult)
    carry = P.sb("r_carry", [128, 8], F32)
    P.memset("dve", carry[:], 0.0)
    xe = [P.sb(f"r_xe{i}", [128, 8, 515], F32) for i in range(2)]
    gg = [P.sb(f"r_gg{i}", [128, 8, 512], F32) for i in range(2)]
    ho = [P.sb(f"r_ho{i}", [128, 8, 512], BF16) for i in range(2)]
    xc = P.sb("r_xc", [128, 4, 512], F32)
    xcb = P.sb("r_xcb", [128, 4, 512], BF16)
    rr = P.sb("r_r", [128, 4, 512], F32)
    ii = P.sb("r_i", [128, 4, 512], F32)
    aa = P.sb("r_a", [128, 4, 512], F32)
    a2 = P.sb("r_a2", [128, 4, 512], F32)
    hs = P.sb("r_hs", [128, 4, 512], F32)
    pa = [P.ps(f"r_pa{i}", [128, 512], F32) for i in range(4)]
    px = [P.ps(f"r_px{i}", [128, 512], F32) for i in range(4)]
    XRv, GGv, HRv = fm(XR), fm(GGR), fm(HR)
    for ti, (pos, w) in enumerate(TILES):
        x = xe[ti % 2]
        g = gg[ti % 2]
        h = ho[ti % 2]
        if ti == 0:
            P.memset("pool", x[:, :, 0:3], 0.0)
            P.dma("sp", x[:, :, 3:3 + w], XRv[:, :, pos:pos + w], part=True)
        else:
            P.dma("sp", x[:, :, 0:3 + w], XRv[:, :, pos - 3:pos + w])
        P.dma("sp", g[:, :, :w], GGv[:, :, pos:pos + w])
        for hf_ in range(2):
            cs = [hf_ * 4 + k for k in range(4)]
            for k, c in enumerate(cs):
                P.ts("dve", xc[:, k, :w], x[:, c, 3:3 + w], vcol(C, "b_cw3", c), ALU.mult, vcol(C, "b_cb", c), ALU.add,
                     part=(k > 0))
                for j in range(3):
                    P.stt(xc[:, k, :w], x[:, c, j:j + w], vcol(C, f"b_cw{j}", c), xc[:, k, :w], ALU.mult, ALU.add,
                          part=True)
            for k, c in enumerate(cs):
                P.copy("pool", xcb[:, k, :w], xc[:, k, :w], part=(k > 0))
            for k, c in enumerate(cs):
                P.mm(pa[k][:, :w], wa[:, c, :], xcb[:, k, :w])
                P.mm(px[k][:, :w], wx[:, c, :], xcb[:, k, :w])
            for k, c in enumerate(cs):
                P.act(rr[:, k, :w], pa[k][:, :w], AF.Sigmoid, bias=vcol(C, "b_ba", c), part=(k > 0))
            for k, c in enumerate(cs):
                P.act(ii[:, k, :w], px[k][:, :w], AF.Sigmoid, bias=vcol(C, "b_bx", c), part=(k > 0))
            for k, c in enumerate(cs):
                P.act(aa[:, k, :w], rr[:, k, :w], AF.Exp, scale=c8[:, c:c + 1], part=(k > 0))
            for k, c in enumerate(cs):
                P.tt("pool", a2[:, k, :w], aa[:, k, :w], aa[:, k, :w], ALU.mult, part=(k > 0))
            for k, c in enumerate(cs):
                P.tt("pool", ii[:, k, :w], ii[:, k, :w], xc[:, k, :w], ALU.mult, part=True)
            for k, c in enumerate(cs):
                P.act(a2[:, k, :w], a2[:, k, :w], AF.Sqrt, scale=-1.0, bias=C.onec[:, 0:1], part=True)
            for k, c in enumerate(cs):
                P.tt("dve", ii[:, k, :w], ii[:, k, :w], a2[:, k, :w], ALU.mult, part=True)
                P.scan(hs[:, k, :w], aa[:, k, :w], ii[:, k, :w], carry[:, c:c + 1], ALU.mult, ALU.add, part=(k > 0))
                P.copy("dve", carry[:, c:c + 1], hs[:, k, w - 1:w], part=True)
            for k, c in enumerate(cs):
                P.tt("pool", h[:, c, :w], hs[:, k, :w], g[:, c, :w], ALU.mult, part=True)
        P.dma("sp", HRv[:, :, pos:pos + w], h[:, :, :w], part=True)
    P.flush()
    P.release(m)


def phase_mlstm(P, C, din, XM, SOG, HM):
    m = P.mark()
    XMv, SOGv, HMv = fm(XM), fm(SOG), fm(HM)
    bd = {}
    for nm in ("bdq", "bdk", "bdv"):
        bd[nm] = P.sb("m_" + nm, [128, 8, 128], BF16)
        P.dma("pool", bd[nm][:], din[nm][:])
    pw = [P.ps(f"m_pw{i}", [128, 512], F32) for i in range(2)]
    pbc = P.ps("m_pbc", [128, 512], F32)
    pg = pbc
    pS = P.ps("m_pS", [128, 4, 128], F32)
    pT = P.ps("m_pT", [128, 8, 128], BF16)
    pn = P.ps("m_pn", [128, 512], F32)
    pds = [P.ps(f"m_pd{i}", [128, 512], F32) for i in range(2)]
    m2 = P.mark()
    bdT = P.sb("m_bdT", [128, 3, 8, 128], F32)
    P.dma("sp", bdT[:], din["bdT"][:])
    wif = P.sb("m_wif", [128, 24, 8], F32)
    P.dma("sp", wif[:], din["a_w_if"][:])
    wg = P.sb("m_wg", [128, 2, 8, 8], BF16)
    for c in range(8):
        P.mm(pg[:, c * 8:(c + 1) * 8], bdT[:, 0, c, :], wif[:, c, :], True, False)
        P.mm(pg[:, c * 8:(c + 1) * 8], bdT[:, 1, c, :], wif[:, 8 + c, :], False, True)
        P.mm(pg[:, 64 + c * 8:64 + (c + 1) * 8], bdT[:, 2, c, :], wif[:, 16 + c, :], True, True)
    P.copy("dve", wg[:].re("p a c g -> p (a c g)"), pg[:, 0:128])
    bif = P.sb("m_bif", [8, 1], F32)
    P.dma("sp", bif[:], din["a_b_if"][:])
    rmask = P.sb("m_rmask", [8, 512], F32)
    P.memset("dve", rmask[:], 1.0)
    for j in range(4):
        P.op("dve", lambda e, v=rmask[:, j * 128:j * 128 + 1]: e.memset(v.ap, 0.0), wp=[rmask[:]])
    selt = P.sb("m_sel", [8, 1152], F32)
    P.dma("sp", selt[:], din["consts"][0:8, 384:1536])
    selI = [selt[0:8, 128 * h:128 * (h + 1)] for h in range(4)]
    selF = [selt[0:8, 512 + 128 * h:512 + 128 * (h + 1)] for h in range(4)]
    selI4 = selt[0:8, 1024:1028]
    selF4 = selt[0:8, 1028:1032]
    C32 = P.sb("m_C32", [128, 4, 2, 257], F32)
    Cb = P.sb("m_Cb", [128, 4, 2, 257], BF16)
    P.memset("dve", C32[:], 0.0)
    P.memset("pool", Cb[:], 0.0)
    Vaug = P.sb("m_Vaug", [128, 4, 257], BF16)
    P.memset("pool", Vaug[:], 1.0)
    xe = P.sb("m_xe", [128, 8, 515], F32)
    sog_2 = [P.sb(f"m_sog{i}", [128, 8, 512], F32) for i in range(2)]
    xcb_2 = [P.sb(f"m_xcb{i}", [128, 8, 512], BF16) for i in range(2)]
    xmb_2 = [P.sb(f"m_xmb{i}", [128, 8, 512], BF16) for i in range(2)]
    qh_2 = [P.sb(f"m_qh{i}", [128, 8, 512], BF16) for i in range(2)]
    kh_2 = [P.sb(f"m_kh{i}", [128, 8, 512], BF16) for i in range(2)]
    abc_2 = [P.sb(f"m_abc{i}", [128, 4, 512], F32) for i in range(2)]
    bbc = P.sb("m_bbc", [128, 4, 512], F32)
    hm = [P.sb("m_hm0", [128, 8, 512], BF16)] * 2
    cv = [P.sb(f"m_cv{i}", [128, 512], F32) for i in range(2)]
    Gall_2 = [P.sb(f"m_Gall{i}", [8, 512], F32) for i in range(2)]
    Lg_2 = [P.sb(f"m_Lg{i}", [8, 512], F32) for i in range(2)]
    lcar = P.sb("m_lcar", [8, 1], F32)
    btok = P.sb("m_btok", [128, 4], F32)
    Kt = P.sb("m_Kt", [128, 4, 256], BF16)
    ST = P.sb("m_ST", [128, 4, 128], BF16)
    hn = P.sb("m_hn", [128, 4, 256], BF16)
    st6 = P.sb("m_st6", [128, 6], F32)
    mv = P.sb("m_mv", [128, 2], F32)
    sc = P.sb("m_sc", [128, 4], F32)
    ctmps = [P.sb(f"m_ctmp{i}", [128, 257], F32) for i in range(2)]
    n = 0
    def prologue(ti):
        pos, w = TILES[ti]
        sog, xcb, xmb, qh, kh, abc, Gall, Lg = (sog_2[ti % 2], xcb_2[ti % 2], xmb_2[ti % 2], qh_2[ti % 2], kh_2[ti % 2], abc_2[ti % 2], Gall_2[ti % 2], Lg_2[ti % 2])
        if ti == 0:
            P.memset("pool", xe[:, :, 0:3], 0.0)
            P.dma("sp", xe[:, :, 3:3 + w], XMv[:, :, pos:pos + w], part=True)
        else:
            P.dma("sp", xe[:, :, 0:3 + w], XMv[:, :, pos - 3:pos + w])
        P.dma("sp", sog[:, :, :w], SOGv[:, :, pos:pos + w])
        for c in range(8):
            k = c % 2
            P.ts("dve", cv[k][:, :w], xe[:, c, 3:3 + w], vcol(C, "a_cw3", c), ALU.mult)
            for j in range(3):
                P.stt(cv[k][:, :w], xe[:, c, j:j + w], vcol(C, f"a_cw{j}", c), cv[k][:, :w], ALU.mult, ALU.add)
            P.act(xcb[:, c, :w], cv[k][:, :w], AF.Silu, bias=vcol(C, "a_cb", c), part=(c > 0))
            P.copy("pool", xmb[:, c, :w], xe[:, c, 3:3 + w], part=(c > 0))
            P.act(sog[:, c, :w], sog[:, c, :w], AF.Identity, scale=vcol(C, "a_ng", c), part=True)
            yield
        for c in range(8):
            P.mm(pg[0:8, :w], wg[:, 0, c, :], xcb[:, c, :w], c == 0, False)
        for c in range(8):
            P.mm(pg[0:8, :w], wg[:, 1, c, :], xmb[:, c, :w], False, c == 7)
        P.act(Gall[:, :w], pg[0:8, :w], AF.Identity, bias=bif[:, 0:1])
        P.act(Lg[:, :w], Gall[:, :w], AF.Exp, scale=-1.0)
        P.act(Lg[:, :w], Lg[:, :w], AF.Ln, bias=C.onec[0:8, 0:1])
        P.scan(Lg[:, :w], rmask[:, :w], Lg[:, :w], 0.0, ALU.mult, ALU.add)
        for h in range(4):
            P.mm(pbc[:, :w], selF[h], Lg[:, :w], True, True)
            P.act(abc[:, h, :w], pbc[:, :w], AF.Exp, scale=-1.0, part=(h > 0))
            P.mm(pbc[:, :w], selI[h], Gall[:, :w], True, False)
            P.mm(pbc[:, :w], selF[h], Lg[:, :w], False, True)
            P.act(bbc[:, h, :w], pbc[:, :w], AF.Exp, bias=C.nl16, part=(h > 0))
            yield
        for c in range(8):
            P.mm(pw[0][:, :w], bd["bdq"][:, c, :], xcb[:, c, :w])
            P.tt("dve", qh[:, c, :w], pw[0][:, :w], abc[:, c // 2, :w], ALU.mult, part=(c > 0))
            P.mm(pw[1][:, :w], bd["bdk"][:, c, :], xcb[:, c, :w])
            P.tt("dve", kh[:, c, :w], pw[1][:, :w], bbc[:, c // 2, :w], ALU.mult, part=(c > 0))
            yield
        yield

    def chunks(ti):
        pos, w = TILES[ti]
        sog, xcb, xmb, qh, kh, abc, Gall, Lg = (sog_2[ti % 2], xcb_2[ti % 2], xmb_2[ti % 2], qh_2[ti % 2], kh_2[ti % 2], abc_2[ti % 2], Gall_2[ti % 2], Lg_2[ti % 2])
        hmt = hm[ti % 2]
        for j in range((w + 127) // 128):
            L = min(128, w - j * 128)
            cs = slice(j * 128, j * 128 + L)
            P.mm(pbc[:L, 0:4], Gall[:, cs], selI4, True, False)
            P.mm(pbc[:L, 0:4], Lg[:, cs], selF4, False, True)
            P.act(btok[:L, :], pbc[:L, 0:4], AF.Exp, bias=C.nl16[:L, :])
            for g2 in range(2):
                for c4 in range(4):
                    c = g2 * 4 + c4
                    P.mm(pw[g2][:L, c4 * 128:(c4 + 1) * 128], xmb[:, c, cs], bd["bdv"][:, c, :], True, True)
                P.copy("act", Vaug[:L, 2 * g2:2 * g2 + 2, 0:256], pw[g2][:L, :].re("p (h v) -> p h v", h=2), part=True)
            for g2 in range(2):
                for c4 in range(4):
                    c = g2 * 4 + c4
                    P.mm(pw[g2][:L, c4 * 128:(c4 + 1) * 128], xcb[:, c, cs], bd["bdk"][:, c, :], True, True)
                for h2 in range(2):
                    h = 2 * g2 + h2
                    P.ts("dve", Kt[:L, h, :], pw[g2][:L, h2 * 256:(h2 + 1) * 256], btok[:L, h:h + 1], ALU.mult, part=True)
            for h in range(4):
                for dc in range(2):
                    P.mm(pS[:L, h, :L], kh[:, 2 * h + dc, cs], qh[:, 2 * h + dc, cs], dc == 0, dc == 1)
            P.tt("dve", ST[:L, :, :L], pS[:L, :, :L], bc3(C.triu[:L, :L], 4), ALU.mult)
            for h in range(4):
                P.mm(pn[:L, 0:257], ST[:L, h, :L], Vaug[:L, h, :], True, False)
                for dc in range(2):
                    P.mm(pn[:L, 0:257], qh[:, 2 * h + dc, cs], Cb[:, h, dc, :], False, dc == 1)
                A = abc[:, h, j * 128 + L - 1:j * 128 + L]
                for dc in range(2):
                    pd, ctmp = pds[dc], ctmps[dc]
                    P.mm(pd[:, 0:257], Kt[:L, h, dc * 128:(dc + 1) * 128], Vaug[:L, h, :], True, True)
                    P.tt("dve", ctmp[:], pd[:, 0:257], C32[:, h, dc, :], ALU.add)
                    P.act(Cb[:, h, dc, :], ctmp[:], AF.Identity, scale=A, part=True)
                    P.act(C32[:, h, dc, :], ctmp[:], AF.Identity, scale=A, part=True)
                P.copy("dve", sc[:L, 3:4], pn[:L, 256:257])
                P.act(sc[:L, 0:1], sc[:L, 3:4], AF.Abs)
                P.ts("dve", sc[:L, 0:1], sc[:L, 0:1], 1.0, ALU.max)
                P.op("dve", lambda e, o=sc[:L, 0:1]: e.reciprocal(o.ap, o.ap), r=[sc[:]], w=[sc[:]])
                P.op("dve", lambda e, o=st6[:L, :], i=pn[:L, 0:256]: e.bn_stats(o.ap, i.ap), r=[pn[:]], w=[st6[:]])
                P.op("dve", lambda e, o=mv[:L, :], i=st6[:L, :]: e.bn_aggr(o.ap, i.ap), r=[st6[:]], w=[mv[:]])
                P.tt("dve", sc[:L, 1:2], sc[:L, 0:1], sc[:L, 0:1], ALU.mult, part=True)
                P.ts("dve", sc[:L, 1:2], sc[:L, 1:2], mv[:L, 1:2], ALU.mult, EPS, ALU.add)
                P.act(sc[:L, 1:2], sc[:L, 1:2], AF.Sqrt)
                P.op("dve", lambda e, o=sc[:L, 1:2]: e.reciprocal(o.ap, o.ap), r=[sc[:]], w=[sc[:]])
                P.tt("dve", sc[:L, 2:3], sc[:L, 1:2], sc[:L, 0:1], ALU.mult)
                P.ts("dve", hn[:L, h, :], pn[:L, 0:256], mv[:L, 0:1], ALU.subtract, sc[:L, 2:3], ALU.mult, part=(h > 0))
                yield
            for c in range(8):
                P.op("pe", lambda e, o=pT[:, c, :L], i=hn[:L, c // 2, (c % 2) * 128:(c % 2) * 128 + 128],
                     idn=C.identb[:L, :L]: e.transpose(o.ap, i.ap, idn.ap),
                     r=[hn[:], C.identb], w=[pT[:]] if c == 0 else (), wp=[pT[:]] if c else ())
            P.tt("dve", hmt[:, :, cs], pT[:, :, :L], sog[:, :, cs], ALU.mult, part=(j > 0))
        P.dma("sp", HMv[:, :, pos:pos + w], hmt[:, :, :w], part=True)
        yield

    pro = prologue(0)
    for _ in pro:
        pass
    for ti in range(len(TILES)):
        ck = chunks(ti)
        nxt = prologue(ti + 1) if ti + 1 < len(TILES) else None
        for _ in ck:
            if nxt is not None:
                try:
                    next(nxt)
                    next(nxt)
                except StopIteration:
                    nxt = None
        if nxt is not None:
            for _ in nxt:
                pass
    P.flush()
    P.release(m)


def ln_tmps(P, pfx):
    return {"mean": P.sb(pfx + "_mean", [128, 512], F32), "m2": P.sb(pfx + "_m2", [128, 512], F32),
            "rstd": P.sb(pfx + "_rstd", [128, 512], F32)}


def phase_outproj(P, C, wsrc, KC, ins_bf, Hin, Hout, layer, loader=None):
    m = P.mark()
    wo = P.sb("o_w", [128, KC, 1024], BF16)
    load_w(P, wo, wsrc.v(None).re("(kc p) n -> p kc n", p=128), KC)
    xin = [P.sb(f"o_x{i}", [128, KC, 512], BF16) for i in range(2)]
    hf = [P.sb(f"o_h{i}", [128, 8, 512], F32) for i in range(2)]
    sq = [P.sb(f"o_sq{i}", [128, 512], F32) for i in range(2)]
    tmp = ln_tmps(P, "o")
    pss = [P.ps(f"o_ps{i}", [128, 512], F32) for i in range(4)]
    psm = P.ps("o_psm", [128, 512], F32)
    psq = P.ps("o_psq", [128, 512], F32)
    Hiv, Hov = fm(Hin), fm(Hout)
    n = 0

    def ln_store(ti, h):
        pos, w = TILES[ti]
        for _ in ln_gen(P, C, h, w, f"mix_g{layer}", f"mix_b{layer}", h, tmp, psm, psq, sq):
            yield
        P.dma("pool", Hov[:, :, pos:pos + w], h[:, :, :w], part=True)

    prev = None
    for ti, (pos, w) in enumerate(TILES):
        x = xin[ti % 2]
        h = hf[ti % 2]
        if loader is not None:
            loader(ti, pos, w, x, pss)
        for i, src in enumerate(ins_bf):
            P.dma("sp", x[:, 8 * i:8 * i + 8, :w], fm(src)[:, :, pos:pos + w], part=(i > 0))
        P.dma("sp", h[:, :, :w], Hiv[:, :, pos:pos + w])
        for oc in range(8):
            ps = pss[n % 4]
            n += 1
            for kc in range(KC):
                P.mm(ps[:, :w], wo[:, kc, oc * 128:(oc + 1) * 128], x[:, kc, :w], kc == 0, kc == KC - 1)
            P.stt(h[:, oc, :w], h[:, oc, :w], ALPHA, ps[:, :w], ALU.mult, ALU.add, part=True)
            if prev is not None:
                try:
                    for _ in range(3):
                        next(prev)
                except StopIteration:
                    prev = None
        if prev is not None:
            for _ in prev:
                pass
        prev = ln_store(ti, h)
    for _ in prev:
        pass
    P.flush()
    P.release(m)


def phase_ffn(P, C, din, layer, Hin, Hout, final_out=None):
    m = P.mark()
    wu = P.sb("f_wu", [128, 8, 2 * DFF], BF16)
    wd = P.sb("f_wd", [128, NFC, 1024], BF16)
    load_w(P, wu, din["ffn_w_up"].v(None)[layer].re("(kc p) n -> p kc n", p=128), 8)
    load_w(P, wd, din["ffn_w_down"].v(None)[layer].re("(kc p) n -> p kc n", p=128), NFC)
    hf = P.sb("f_h", [128, 8, 512], F32)
    hb = P.sb("f_hb", [128, 8, 512], BF16)
    hh = P.sb("f_hh", [128, NFC, 512], BF16)
    gx = [P.sb(f"f_gx{i}", [128, 514], F32) for i in range(2)]
    ac = [P.sb(f"f_ac{i}", [128, 512], F32) for i in range(2)]
    halo = P.sb("f_halo", [128, NFC, 2], F32)
    P.memset("dve", halo[:], 0.0)
    sq = [P.sb(f"f_sq{i}", [128, 512], F32) for i in range(2)]
    tmp = ln_tmps(P, "f")
    pgs = [P.ps(f"f_pg{i}", [128, 512], F32) for i in range(2)]
    pus = [P.ps(f"f_pu{i}", [128, 512], F32) for i in range(2)]
    pys = [P.ps(f"f_py{i}", [128, 512], F32) for i in range(2)]
    psm = P.ps("f_psm", [128, 512], F32)
    psq = P.ps("f_psq", [128, 512], F32)
    Hiv = fm(Hin)
    if final_out is None:
        Hov = fm(Hout)
    else:
        ot = P.sb("f_ot0", [128, 1024], F32)
    cw = [f"f_cw{layer}_{j}" for j in range(3)]
    cnt = {"n": 0}

    def load_hb(ti):
        pos, w = TILES[ti]
        for c in range(8):
            P.dma("pool", hb[:, c, :w], Hiv[:, c, pos:pos + w], part=(c > 0), max_dma_last_dim=4096)

    def stage_up(ti):
        pos, w = TILES[ti]
        for fc in range(NFC):
            k = cnt["n"] % 2
            cnt["n"] += 1
            pg, pu, g, a = pgs[k], pus[k], gx[k], ac[k]
            for kc in range(8):
                P.mm(pg[:, :w], wu[:, kc, DFF + fc * 128:DFF + (fc + 1) * 128], hb[:, kc, :w], kc == 0, kc == 7)
            for kc in range(8):
                P.mm(pu[:, :w], wu[:, kc, fc * 128:(fc + 1) * 128], hb[:, kc, :w], kc == 0, kc == 7)
            P.copy("act", g[:, 0:2], halo[:, fc, :])
            P.copy("act", g[:, 2:2 + w], pg[:, :w], part=True)
            P.copy("pool", halo[:, fc, :], g[:, w:w + 2], part=True)
            P.ts("dve", a[:, :w], g[:, 2:2 + w], vcol(C, cw[2], fc), ALU.mult)
            P.stt(a[:, :w], g[:, 1:1 + w], vcol(C, cw[1], fc), a[:, :w], ALU.mult, ALU.add)
            P.stt(a[:, :w], g[:, 0:w], vcol(C, cw[0], fc), a[:, :w], ALU.mult, ALU.add)
            P.act(a[:, :w], a[:, :w], AF.Gelu_apprx_tanh, bias=vcol(C, f"f_cb{layer}", fc))
            P.tt("dve", hh[:, fc, :w], a[:, :w], pu[:, :w], ALU.mult, part=(fc > 0))
            yield

    def stage_down(ti):
        pos, w = TILES[ti]
        P.dma("sp", hf[:, :, :w], Hiv[:, :, pos:pos + w])
        if ti + 1 < len(TILES):
            load_hb(ti + 1)
        for oc in range(8):
            py = pys[oc % 2]
            for fc in range(NFC):
                P.mm(py[:, :w], wd[:, fc, oc * 128:(oc + 1) * 128], hh[:, fc, :w], fc == 0, fc == NFC - 1)
            P.stt(hf[:, oc, :w], hf[:, oc, :w], ALPHA, py[:, :w], ALU.mult, ALU.add, part=True)

    def stage_ln(ti):
        pos, w = TILES[ti]
        for _ in ln_gen(P, C, hf, w, f"ffn_g{layer}", f"ffn_b{layer}", hf, tmp, psm, psq, sq):
            yield
        if final_out is None:
            P.dma("pool", Hov[:, :, pos:pos + w], hf[:, :, :w], part=True)
        else:
            for j in range((w + 127) // 128):
                L = min(128, w - j * 128)
                p0 = pos + j * 128
                o = ot
                for half in range(2):
                    py = pys[half]
                    for c4 in range(4):
                        c = half * 4 + c4
                        P.op("pe", lambda e, o_=py[:L, c4 * 128:(c4 + 1) * 128], i=hf[:, c, j * 128:j * 128 + L],
                             idn=C.ident: e.transpose(o_.ap, i.ap, idn.ap),
                             r=[hf[:], C.ident], w=[py[:]] if c4 == 0 else (), wp=[py[:]] if c4 else ())
                    if half == 0:
                        P.copy("act", o[:L, 0:512], py[:L, :], part=False)
                    else:
                        P.copy("dve", o[:L, 512:1024], py[:L, :], part=True)
                lo = max(p0, NMETA)
                hi = p0 + L
                if hi > lo:
                    P.dma("pool", final_out[lo - NMETA:hi - NMETA, :], o[lo - p0:hi - p0, :], part=True)
                yield

    load_hb(0)
    prev_ln = None
    for ti in range(len(TILES)):
        up = stage_up(ti)
        for _ in up:
            if prev_ln is not None:
                try:
                    next(prev_ln)
                except StopIteration:
                    prev_ln = None
        if prev_ln is not None:
            for _ in prev_ln:
                pass
        stage_down(ti)
        prev_ln = stage_ln(ti)
    for _ in prev_ln:
        pass
    P.flush()
    P.release(m)


IN_SHAPES = {
    "x": [SEQ, D], "meta": [NMETA, D], "consts": [128, 1536], "vecs": None,
    "a_w_in": [D, 4096], "bdq": [128, 8, 128], "bdk": [128, 8, 128], "bdv": [128, 8, 128],
    "bdT": [128, 3, 8, 128], "a_w_if": [128, 24, 8], "a_b_if": [8, 1],
    "b_w_a": [128, 8, 128], "b_w_x": [128, 8, 128], "ab_w_out": [2 * D, D],
    "ffn_w_up": [2, D, 2 * DFF], "ffn_w_down": [2, DFF, D],
    "c_w_in": [D, 712], "c_w_qidx": [384, 512], "c_w_uq": [384, 1024], "c_w_ukT": [128, 8, 256],
    "c_w_uv": [128, 8, 2, 128], "c_w_out": [D, D],
}


def build(stop_after=None, dbg=()):
    nc = bass.Bass("TRN2", target_bir_lowering=False)
    P = Prog(nc)
    C = Ctx()
    _, nv = vec_layout()
    din = {}
    for k, shp in IN_SHAPES.items():
        if k == "vecs":
            shp = [128, nv]
        din[k] = P.dram(k, shp, F32, kind="ExternalInput")
    out = P.dram("out", [SEQ, D], F32, kind="ExternalOutput")

    def scratch(name, dt=F32):
        kind = "ExternalOutput" if name in dbg else "Internal"
        return P.dram(name, [D, T], dt, kind=kind)

    def scratch2(name, shape, dt):
        kind = "ExternalOutput" if name in dbg else "Internal"
        return P.dram(name, shape, dt, kind=kind)

    setup_consts(P, C, din)
    H0 = scratch("H0")
    phase_embed(P, C, din, H0)
    XM, SOG, XR, GGR = scratch("XM"), scratch("SOG"), scratch("XR"), scratch("GGR")
    phase_inproj(P, C, din, H0, XM, SOG, XR, GGR)
    HR = scratch("HR", BF16)
    phase_rglru(P, C, din, XR, GGR, HR)
    HM = scratch("HM", BF16)
    phase_mlstm(P, C, din, XM, SOG, HM)
    H1 = scratch("H1")
    phase_outproj(P, C, din["ab_w_out"], 16, [HM, HR], H0, H1, 0)
    H2 = scratch("H2")
    if stop_after == "L0":
        phase_ffn(P, C, din, 0, H1, H2)
        P.finish()
        return nc
    phase_ffn(P, C, din, 0, H1, H2)
    CKVT = scratch2("CKVT", [256, T], BF16)
    CKV = scratch2("CKV", [T, 256], BF16)
    KIDXT = scratch2("KIDXT", [64, T], BF16)
    WIDX = scratch2("WIDX", [T, 8], F32)
    QIDXT = scratch2("QIDXT", [512, T], BF16)
    QLATT = scratch2("QLATT", [2048, T], BF16)
    OLATT = scratch2("OLATT", [2048, T], BF16)
    phase_dsa_proj(P, C, din, H2, CKVT, CKV, KIDXT, WIDX, QIDXT, QLATT)
    if stop_after == "L1proj":
        P.finish()
        return nc
    phase_dsa_attn(P, C, din, CKVT, CKV, KIDXT, WIDX, QIDXT, QLATT, OLATT)
    if stop_after == "L1attn":
        P.finish()
        return nc
    H3 = scratch("H3")
    muv = P.mark()
    wuv = P.sb("u_wuv", [128, 8, 2, 128], BF16)
    P.dma("pool", wuv[:], din["c_w_uv"][:])
    olb = [P.sb(f"u_ol{i}", [128, 2, 8, 512], BF16) for i in range(2)]
    OLATv = OLATT.v(None).re("(p a) t -> p a t", a=16)

    def uv_loader(ti, pos, w, x, pss):
        ol = olb[ti % 2]
        P.dma("sp", ol[:, :, :, :w].re("p rc h t -> p (rc h) t"), OLATv[:, :, pos:pos + w])
        for h in range(8):
            ps = pss[h % 4]
            for rc in range(2):
                P.mm(ps[:, :w], wuv[:, h, rc, :], ol[:, rc, h, :w], rc == 0, rc == 1)
            if h % 2:
                P.copy("act", x[:, h, :w], ps[:, :w], part=(h > 0))
            else:
                P.copy("dve", x[:, h, :w], ps[:, :w], part=(h > 0))

    phase_outproj(P, C, din["c_w_out"], 8, [], H2, H3, 1, loader=uv_loader)
    P.release(muv)
    if stop_after == "L1mix":
        P.finish()
        return nc
    phase_ffn(P, C, din, 1, H3, None, final_out=out.v(None))
    P.finish()
    return nc


def host_inputs(inp):
    f = lambda a: np.ascontiguousarray(np.asarray(a, np.float32))
    vp = VecPack()
    pack_vecs(vp, inp)
    bds = [block_diag_host(inp[k][0]) for k in ("a_wq", "a_wk", "a_wv")]
    shared = {
        "meta": f(inp["meta_tokens"]), "consts": make_consts(), "vecs": vp.array(),
        "a_w_in": f(inp["a_w_in"][0]),
        "bdq": f(bds[0].transpose(1, 0, 2)), "bdk": f(bds[1].transpose(1, 0, 2)), "bdv": f(bds[2].transpose(1, 0, 2)),
        "bdT": f(np.stack([b.transpose(2, 0, 1) for b in bds], axis=1)),
        "a_w_if": f(np.asarray(inp["a_w_if"][0]).reshape(24, 128, 8).transpose(1, 0, 2)),
        "a_b_if": f(np.asarray(inp["a_b_if"][0]).reshape(8, 1)),
        "b_w_a": f(np.asarray(inp["b_w_a"][0]).transpose(1, 0, 2)),
        "b_w_x": f(np.asarray(inp["b_w_x"][0]).transpose(1, 0, 2)),
        "ab_w_out": f(inp["ab_w_out"][0]),
        "ffn_w_up": f(inp["ffn_w_up"]), "ffn_w_down": f(inp["ffn_w_down"]),
        "c_w_in": f(inp["c_w_in"][0]), "c_w_qidx": f(inp["c_w_qidx"][0]), "c_w_uq": f(inp["c_w_uq"][0]),
        "c_w_ukT": f(np.asarray(inp["c_w_uk"][0]).transpose(2, 0, 1)),
        "c_w_uv": f(np.asarray(inp["c_w_uv"][0]).reshape(8, 2, 128, 128).transpose(2, 0, 1, 3)),
        "c_w_out": f(inp["c_w_out"][0]),
    }
    return shared


_NC_CACHE = {}


def kernel(**inp):
    shared = host_inputs(inp)
    x = np.asarray(inp["x"], np.float32)
    if "full" not in _NC_CACHE:
        _NC_CACHE["full"] = build()
    nc = _NC_CACHE["full"]
    in_maps = []
    for b in range(8):
        d = dict(shared)
        d["x"] = np.ascontiguousarray(x[b])
        in_maps.append(d)
    res = run_bass_kernel_spmd(nc, in_maps, core_ids=list(range(8)))
    return np.stack([res.results[b]["out"] for b in range(8)], axis=0)


CQ_R, CKV_R = 384, 256


def phase_dsa_proj(P, C, din, H2, CKVT, CKV, KIDXT, WIDX, QIDXT, QLATT):
    m = P.mark()
    wci = P.sb("d_wci", [128, 8, 712], BF16)
    load_w(P, wci, din["c_w_in"].v(None).re("(kc p) n -> p kc n", p=128), 8)
    wqi = P.sb("d_wqi", [128, 3, 512], BF16)
    load_w(P, wqi, din["c_w_qidx"].v(None).re("(kc p) n -> p kc n", p=128), 3)
    wuq = P.sb("d_wuq", [128, 3, 1024], BF16)
    load_w(P, wuq, din["c_w_uq"].v(None).re("(kc p) n -> p kc n", p=128), 3)
    wuk = P.sb("d_wuk", [128, 8, 256], BF16)
    P.dma("pool", wuk[:], din["c_w_ukT"][:])
    m384 = P.sb("d_m384", [128, 128], F32)
    P.memset("dve", m384[:], 1.0 / 384.0)
    m256 = P.sb("d_m256", [128, 128], F32)
    P.memset("dve", m256[:], 1.0 / 256.0)
    m64 = P.sb("d_m64", [64, 64], F32)
    P.memset("dve", m64[:], 1.0 / 64.0)
    hf = P.sb("d_hf", [128, 8, 512], F32)
    hb = P.sb("d_hb", [128, 8, 512], BF16)
    raw = P.sb("d_raw", [128, 3, 512], F32)
    sq = [P.sb(f"d_sq{i}", [128, 512], F32) for i in range(2)]
    rs = P.sb("d_rs", [128, 512], F32)
    mn = P.sb("d_mn", [128, 512], F32)
    cqn = P.sb("d_cqn", [128, 3, 512], BF16)
    kvn = [P.sb(f"d_kvn{i}", [128, 2, 512], BF16) for i in range(2)]
    kvt = [P.sb(f"d_kvt{i}", [128, 4, 256], BF16) for i in range(2)]
    kin = [P.sb(f"d_kin{i}", [64, 512], BF16) for i in range(2)]
    wi = [P.sb(f"d_wi{i}", [128, 4, 8], F32) for i in range(2)]
    qix = [P.sb(f"d_qix{i}", [128, 4, 512], BF16) for i in range(2)]
    qf = P.sb("d_qf", [128, 8, 512], BF16)
    qlt = [P.sb(f"d_qlt{i}", [128, 2, 8, 512], BF16) for i in range(2)]
    pss = [P.ps(f"d_ps{i}", [128, 512], F32) for i in range(4)]
    pst = P.ps("d_pst", [128, 512], F32)
    ptr = P.ps("d_ptr", [128, 8, 128], BF16)
    pwi = P.ps("d_pwi", [128, 512], F32)
    H2v = fm(H2)
    CKVTv = fm(CKVT)
    CKVv = CKV.v(None)
    WIDXv = WIDX.v(None)
    QIDXv = fm(QIDXT)
    QLATv = QLATT.v(None).re("(p a) t -> p a t", a=16)
    KIv = KIDXT.v(None)
    n = 0

    def nxt():
        nonlocal n
        n += 1
        return pss[n % 4]

    for ti, (pos, w) in enumerate(TILES):
        k2 = ti % 2
        P.dma("sp", hf[:, :, :w], H2v[:, :, pos:pos + w])
        P.copy("pool", hb[:, :, :w], hf[:, :, :w])

        def proj_chunk(ps, c0, mcols):
            for kc in range(8):
                P.mm(ps[:mcols, :w], wci[:, kc, c0:c0 + mcols], hb[:, kc, :w], kc == 0, kc == 7)

        def rms(nch, c0, mmat, gname, outt):
            for c in range(nch):
                ps = nxt()
                proj_chunk(ps, c0 + c * 128, 128)
                P.copy("act", raw[:, c, :w], ps[:, :w], part=(c > 0))
                s = sq[c % 2]
                P.tt("pool", s[:, :w], raw[:, c, :w], raw[:, c, :w], ALU.mult)
                P.mm(pst[:, :w], mmat[:], s[:, :w], c == 0, c == nch - 1)
            P.act(rs[:, :w], pst[:, :w], AF.Sqrt, bias=C.epsc)
            P.op("dve", lambda e, o=rs[:, :w]: e.reciprocal(o.ap, o.ap), r=[rs[:]], w=[rs[:]])
            for c in range(nch):
                P.stt(outt[:, c, :w], raw[:, c, :w], vcol(C, gname, c), rs[:, :w], ALU.mult, ALU.mult, part=(c > 0))

        rms(3, 0, m384, "c_qg", cqn)
        kv = kvn[k2]
        rms(2, CQ_R, m256, "c_kvg", kv)
        P.dma("pool", CKVTv[:, :, pos:pos + w], kv[:, :, :w], part=True)
        kt = kvt[k2]
        nb = (w + 127) // 128
        for j in range(nb):
            L = min(128, w - j * 128)
            for rc in range(2):
                P.op("pe", lambda e, o=ptr[:L, j * 2 + rc, :], i=kv[:, rc, j * 128:j * 128 + L], idn=C.identb:
                     e.transpose(o.ap, i.ap, idn.ap), r=[kv[:], C.identb],
                     w=[ptr[:]] if (j == 0 and rc == 0) else (), wp=[ptr[:]] if (j or rc) else ())
        if w == 512:
            P.copy("act", kt[:].re("p j r -> p (j r)"), ptr[:].re("p a b -> p (a b)"))
            P.dma("pool", CKVv[pos:pos + 512, :].re("(j p) r -> p j r", p=128), kt[:], part=True)
        else:
            P.copy("act", kt[:w, 0, :], ptr[:w, 0:2, :].re("p a b -> p (a b)"))
            P.dma("pool", CKVv[pos:pos + w, :], kt[:w, 0, :], part=True)
        ps = nxt()
        proj_chunk(ps, CQ_R + CKV_R, 64)
        P.copy("act", raw[0:64, 0, :w], ps[0:64, :w])
        P.tt("pool", sq[0][0:64, :w], raw[0:64, 0, :w], raw[0:64, 0, :w], ALU.mult)
        P.mm(pst[0:64, :w], m64[:], raw[0:64, 0, :w], True, True)
        P.copy("act", mn[0:64, :w], pst[0:64, :w])
        P.mm(pst[0:64, :w], m64[:], sq[0][0:64, :w], True, True)
        P.tt("pool", sq[1][0:64, :w], mn[0:64, :w], mn[0:64, :w], ALU.mult)
        P.tt("dve", rs[0:64, :w], pst[0:64, :w], sq[1][0:64, :w], ALU.subtract)
        P.act(rs[0:64, :w], rs[0:64, :w], AF.Sqrt, bias=C.epsc[0:64, :])
        P.op("dve", lambda e, o=rs[0:64, :w]: e.reciprocal(o.ap, o.ap), r=[rs[:]], w=[rs[:]])
        P.tt("dve", sq[0][0:64, :w], raw[0:64, 0, :w], mn[0:64, :w], ALU.subtract)
        P.stt(sq[0][0:64, :w], sq[0][0:64, :w], vcol(C, "c_kig")[0:64, :], rs[0:64, :w], ALU.mult, ALU.mult)
        ki = kin[k2]
        P.act(ki[:, :w], sq[0][0:64, :w], AF.Identity, bias=vcol(C, "c_kib")[0:64, :])
        P.dma("pool", KIv[:, pos:pos + w], ki[:, :w], part=True)
        wt = wi[k2]
        for j in range(nb):
            L = min(128, w - j * 128)
            for kc in range(8):
                P.mm(pwi[:L, j * 8:(j + 1) * 8], hb[:, kc, j * 128:j * 128 + L], wci[:, kc, 704:712], kc == 0, kc == 7)
        if w == 512:
            P.act(wt[:].re("p j g -> p (j g)"), pwi[:, 0:32], AF.Copy, scale=float(8.0 ** -0.5))
            P.dma("pool", WIDXv[pos:pos + 512, :].re("(j p) g -> p j g", p=128), wt[:], part=True)
        else:
            P.act(wt[:w, 0, :], pwi[:w, 0:8], AF.Copy, scale=float(8.0 ** -0.5))
            P.dma("pool", WIDXv[pos:pos + w, :], wt[:w, 0, :], part=True)
        qx = qix[k2]
        for oc in range(4):
            ps = nxt()
            for kc in range(3):
                P.mm(ps[:, :w], wqi[:, kc, oc * 128:(oc + 1) * 128], cqn[:, kc, :w], kc == 0, kc == 2)
            if oc % 2:
                P.copy("act", qx[:, oc, :w], ps[:, :w], part=(oc > 0))
            else:
                P.copy("dve", qx[:, oc, :w], ps[:, :w], part=(oc > 0))
        P.dma("pool", QIDXv[:, :, pos:pos + w], qx[:, :, :w], part=True)
        for h in range(8):
            ps = nxt()
            for kc in range(3):
                P.mm(ps[:, :w], wuq[:, kc, h * 128:(h + 1) * 128], cqn[:, kc, :w], kc == 0, kc == 2)
            if h % 2:
                P.copy("act", qf[:, h, :w], ps[:, :w], part=(h > 0))
            else:
                P.copy("dve", qf[:, h, :w], ps[:, :w], part=(h > 0))
        ql = qlt[k2]
        for h in range(8):
            for rc in range(2):
                ps = nxt()
                P.mm(ps[:, :w], wuk[:, h, rc * 128:(rc + 1) * 128], qf[:, h, :w])
                first = (h == 0 and rc == 0)
                if rc:
                    P.act(ql[:, rc, h, :w], ps[:, :w], AF.Copy, scale=float(128.0 ** -0.5), part=not first)
                else:
                    P.ts("dve", ql[:, rc, h, :w], ps[:, :w], float(128.0 ** -0.5), ALU.mult, part=not first)
        P.dma("pool", QLATv[:, :, pos:pos + w], ql[:, :, :, :w].re("p rc h t -> p (rc h) t"), part=True)
    P.flush()
    P.release(m)


NKB = 33
NEG = -1.0e30


def phase_dsa_attn(P, C, din, CKVT, CKV, KIDXT, WIDX, QIDXT, QLATT, OLATT):
    m = P.mark()
    kidx2 = P.sb("t_kidx2", [128, T], BF16)
    P.dma("sp", kidx2[0:64, :], KIDXT.v(None)[:, :], part=True)
    P.dma("sp", kidx2[64:128, :], KIDXT.v(None)[:, :], part=True)
    ckvT = P.sb("t_ckvT", [128, 2, T], BF16)
    P.dma("sp", ckvT[:], fm(CKVT)[:, :, :])
    ckv = P.sb("t_ckv", [128, NKB, 256], BF16)
    for i in range(4):
        P.dma("sp", ckv[:, 8 * i:8 * i + 8, :],
              CKV.v(None)[1024 * i:1024 * (i + 1), :].re("(kb p) r -> p kb r", p=128), part=True)
    P.dma("sp", ckv[0:16, 32, :], CKV.v(None)[4096:4112, :], part=True)
    qix = [P.sb(f"t_qix{i}", [128, 4, 128], BF16) for i in range(2)]
    wid = [P.sb(f"t_wid{i}", [128, 8], F32) for i in range(2)]
    qlt = [P.sb(f"t_qlt{i}", [128, 2, 8, 128], BF16) for i in range(3)]
    score = [P.sb(f"t_score{i}", [128, T], F32) for i in range(2)]
    junk = P.sb("t_junk", [128, T], BF16)
    msk = P.sb("t_msk", [128, T + 16], BF16)
    mskTs = [P.sb(f"t_mskT{i}", [128, 40, 128], BF16) for i in range(2)]
    rl = [P.sb(f"t_rl{i}", [128, 512], F32) for i in range(2)]
    ex = [P.sb(f"t_ex{i}", [128, 4, 128], BF16) for i in range(3)]
    pp = [P.sb(f"t_pp{i}", [128, 4, 128], BF16) for i in range(3)]
    bss = [P.sb(f"t_bs{i}", [128, 8], F32) for i in range(2)]
    rden = P.sb("t_rden", [128, 8], F32)
    fr = P.sb("t_fr", [128, NBIS], F32)
    for r in range(1, NBIS + 1):
        P.op("dve", lambda e, o=fr[:, r - 1:r], v=float(2.0 ** -r): e.memset(o.ap, v), wp=[fr[:]])
    stepf = P.sb("t_stepf", [128, NBIS], F32)
    on = P.sb("t_on", [128, 8, 256], BF16)
    olT = [P.sb(f"t_olT{i}", [128, 2, 8, 128], BF16) for i in range(2)]
    zb = P.sb("t_zb", [128, 512], BF16)
    P.memset("pool", zb[:], 0.0)
    po = [P.ps(f"t_po{i}", [128, 2, 256], F32) for i in range(4)]
    pden = P.ps("t_pden", [128, 512], F32)
    pl = [P.ps(f"t_pl{i}", [128, 4, 128], F32) for i in range(2)]
    px = P.ps("t_px", [128, 512], F32)
    pxs = [px, px]
    pxb = View(px.h[:].bitcast(BF16), px.buf)
    QIDXv = fm(QIDXT)
    QLATv = QLATT.v(None).re("(p a) t -> p a t", a=16)
    OLATv = OLATT.v(None).re("(p a) t -> p a t", a=16)
    WIDXv = WIDX.v(None)
    cnt = {"nl": 0}

    def geom(qi):
        if qi == 0:
            return 0, 16, 16
        return 16 + 128 * (qi - 1), 128, 16 + 128 * qi

    def stageA1(qi):
        q0, nq, nk = geom(qi)
        k2 = qi % 2
        qx, wd, ql, sc = qix[k2], wid[k2], qlt[qi % 3], score[k2]
        P.dma("sp", qx[:, :, :nq], QIDXv[:, :, q0:q0 + nq])
        P.dma("sp", wd[:nq, :], WIDXv[q0:q0 + nq, :])
        P.dma("sp", ql[:, :, :, :nq].re("p rc h t -> p (rc h) t"), QLATv[:, :, q0:q0 + nq])
        for c0 in range(0, nk, 512):
            cw = min(512, nk - c0)
            for h in range(8):
                oc, half = h // 2, h % 2
                lo_, hi_ = half * 64, half * 64 + 64
                px_ = pxs[h % 2]
                P.mm(px_[:nq, :cw], qx[lo_:hi_, oc, :nq], kidx2[lo_:hi_, c0:c0 + cw])
                r_ = rl[h % 2]
                P.act(r_[:nq, :cw], px_[:nq, :cw], AF.Relu)
                if h == 0:
                    P.ts("dve", sc[:nq, c0:c0 + cw], r_[:nq, :cw], wd[:nq, 0:1], ALU.mult, part=(c0 > 0))
                else:
                    P.stt(sc[:nq, c0:c0 + cw], r_[:nq, :cw], wd[:nq, h:h + 1], sc[:nq, c0:c0 + cw],
                          ALU.mult, ALU.add, part=True)
                yield 1.0

    def stageA2(qi):
        q0, nq, nk = geom(qi)
        k2 = qi % 2
        sc, bs, mskT = score[k2], bss[k2], mskTs[k2]
        P.op("dve", lambda e, o=bs[:nq, 0:1], i=sc[:nq, :nk]: e.tensor_reduce(o.ap, i.ap, AX.X, ALU.max),
             r=[sc[:]], w=[bs[:]])
        P.op("dve", lambda e, o=bs[:nq, 1:2], i=sc[:nq, :nk]: e.tensor_reduce(o.ap, i.ap, AX.X, ALU.min),
             r=[sc[:]], wp=[bs[:]])
        if qi >= 1:
            P.op("dve", lambda e, o=sc[0:64, nk - 64:nk]: e.memset(o.ap, NEG), r=[bs[:]], wp=[sc[:]])
        P.tt("dve", bs[:nq, 2:3], bs[:nq, 0:1], bs[:nq, 1:2], ALU.subtract, part=True)
        yield 7.0
        if nk > KSEL:
            P.ts("dve", stepf[:nq, :], fr[:nq, :], bs[:nq, 2:3], ALU.mult)
            P.tt("dve", bs[:nq, 3:4], bs[:nq, 1:2], stepf[:nq, 0:1], ALU.add, part=True)
            for r in range(1, NBIS + 1):
                P.ts("dve", junk[:nq, :nk], sc[:nq, :nk], bs[:nq, 3:4], ALU.is_ge, 0.0, ALU.add,
                     accum=bs[:nq, 4:5])
                last = (r == NBIS)
                P.ts("dve", bs[:nq, 5:6], bs[:nq, 4:5], KSEL - 0.5, ALU.is_ge, -1.0 if last else -0.5, ALU.add,
                     part=True)
                P.stt(bs[:nq, 1:2] if last else bs[:nq, 3:4], bs[:nq, 5:6], stepf[:nq, r - 1:r], bs[:nq, 3:4],
                      ALU.mult, ALU.add, part=True)
                yield nk * 1.6 / 960.0 + 0.8
        P.ts("dve", msk[:nq, :nk], sc[:nq, :nk], bs[:nq, 1:2], ALU.is_ge)
        nkb = (nk + 127) // 128
        for b0 in range(0, nkb, 8):
            nb_ = min(8, nkb - b0)
            for j in range(nb_):
                kb = b0 + j
                kl = min(128, nk - kb * 128)
                P.op("pe", lambda e, o=pxb[:kl, j * 128:j * 128 + nq], i=msk[:nq, kb * 128:kb * 128 + kl],
                     idn=C.identb[:nq, :nq]: e.transpose(o.ap, i.ap, idn.ap), r=[msk[:], C.identb],
                     w=[px[:]] if j == 0 else (), wp=[px[:]] if j else ())
            P.copy("act", mskT[:, b0:b0 + nb_, :].re("p a b -> p (a b)"), pxb[:, 0:nb_ * 128], part=(b0 > 0))
            yield 3.0

    def stageB(qi):
        q0, nq, nk = geom(qi)
        k2 = qi % 2
        ql, mskT = qlt[qi % 3], mskTs[k2]
        nkb = (nk + 127) // 128
        for i in range(4):
            P.mm(po[i][:, :, :].re("p a b -> p (a b)"), zb[:, 0:128], zb[:, :], True, False)
        P.mm(pden[:, :], zb[:, 0:128], zb[:, :], True, False)
        steps = [(kb, g) for kb in range(nkb) for g in range(2)]
        base = cnt["nl"]
        cnt["nl"] += len(steps)

        def front(n):
            kb, g = steps[n]
            kl = min(128, nk - kb * 128)
            nl = base + n
            p_l, e_, p_ = pl[nl % 2], ex[nl % 3], pp[nl % 3]
            for rc in range(2):
                P.mm(p_l[:kl, :, :nq], ckvT[:, rc, kb * 128:kb * 128 + kl], ql[:, rc, 4 * g:4 * g + 4, :nq],
                     rc == 0, rc == 1)
            P.act(e_[:kl, :, :nq], p_l[:kl, :, :nq], AF.Exp)
            P.tt("pool", p_[:kl, :, :nq], e_[:kl, :, :nq], bc3(mskT[:kl, kb, :nq], 4), ALU.mult)

        def back(n):
            kb, g = steps[n]
            kl = min(128, nk - kb * 128)
            p_ = pp[(base + n) % 3]
            for h4 in range(4):
                h = 4 * g + h4
                P.mm(po[h // 2][:nq, h % 2, :], p_[:kl, h4, :nq], ckv[:kl, kb, :], False, kb == nkb - 1)
                P.mm(pden[:nq, h:h + 1], p_[:kl, h4, :nq], C.onesb[:kl, 0:1], False, kb == nkb - 1)

        front(0)
        for n in range(len(steps)):
            if n + 1 < len(steps):
                front(n + 1)
            back(n)
            yield 3.0
        P.op("dve", lambda e, o=rden[:nq, :], i=pden[:nq, 0:8]: e.reciprocal(o.ap, i.ap), r=[pden[:]], w=[rden[:]])
        for h in range(8):
            if (h // 2) % 2:
                P.act(on[:nq, h, :], po[h // 2][:nq, h % 2, :], AF.Identity, scale=rden[:nq, h:h + 1], part=(h > 0))
            else:
                P.ts("dve", on[:nq, h, :], po[h // 2][:nq, h % 2, :], rden[:nq, h:h + 1], ALU.mult, part=(h > 0))
        yield 4.0
        ot = olT[k2]
        for rc in range(2):
            for h in range(8):
                P.op("pe", lambda e, o=pxb[:, h * 128:h * 128 + nq], i=on[:nq, h, rc * 128:(rc + 1) * 128],
                     idn=C.identb[:nq, :nq]: e.transpose(o.ap, i.ap, idn.ap), r=[on[:], C.identb],
                     w=[px[:]] if h == 0 else (), wp=[px[:]] if h else ())
            src = pxb[:, :].re("p (h q) -> p h q", h=8)[:, :, :nq]
            if rc:
                P.copy("act", ot[:, rc, :, :nq], src, part=True)
            else:
                P.copy("dve", ot[:, rc, :, :nq], src, part=False)
            yield 3.0
        P.dma("pool", OLATv[:, :, q0:q0 + nq], ot[:, :, :, :nq].re("p rc h t -> p (rc h) t"), part=True)

    def total(qi, which):
        q0, nq, nk = geom(qi)
        nkb = (nk + 127) // 128
        if which == "A1":
            return ((nk + 511) // 512) * 8 * 1.0
        if which == "A2":
            return 7.0 + (NBIS * (nk * 1.6 / 960.0 + 0.8) if nk > KSEL else 0.0) + ((nkb + 7) // 8) * 3.0
        return nkb * 2 * 3.0 + 4.0 + 6.0

    for it in range(35):
        live = []
        if it < 33:
            live.append([stageA1(it), 0.0, total(it, "A1")])
        if 1 <= it < 34:
            live.append([stageA2(it - 1), 0.0, total(it - 1, "A2")])
        if it >= 2:
            live.append([stageB(it - 2), 0.0, total(it - 2, "B")])
        while live:
            g_ = min(live, key=lambda t: t[1] / t[2])
            try:
                g_[1] += next(g_[0])
            except StopIteration:
                live.remove(g_)
    P.flush()
    P.release(m)
```

```python
import numpy as np
import concourse.bass as bass
import concourse.mybir as mybir
from concourse.bass_utils import run_bass_kernel_spmd

F32 = mybir.dt.float32
BF16 = mybir.dt.bfloat16
AF = mybir.ActivationFunctionType
ALU = mybir.AluOpType
AX = mybir.AxisListType

SEM_LIMIT = 30000


class Buf:
    __slots__ = ("name", "w", "r", "wf")

    def __init__(self, name):
        self.name = name
        self.w = {}
        self.r = {}
        self.wf = {}


class View:
    __slots__ = ("ap", "buf")

    def __init__(self, ap, buf):
        self.ap = ap
        self.buf = buf

    def __getitem__(self, idx):
        return View(self.ap[idx], self.buf)

    def re(self, pattern, **kw):
        return View(self.ap.rearrange(pattern, **kw), self.buf)


class Tile:
    def __init__(self, handle, buf):
        self.h = handle
        self.buf = buf

    def __getitem__(self, idx):
        return View(self.h[idx], self.buf)


class DramT:
    def __init__(self, handle, name):
        self.h = handle
        self.name = name
        self.bufs = {}

    def __getitem__(self, idx):
        return self.v(None)[idx]

    def v(self, key):
        if key not in self.bufs:
            self.bufs[key] = Buf(f"{self.name}:{key}")
        return View(self.h.ap(), self.bufs[key])


class _Op:
    __slots__ = ("fn", "deps", "sig", "val", "dma", "sem")

    def __init__(self, fn, deps, dma):
        self.fn = fn
        self.deps = deps
        self.sig = dma is not None
        self.val = None
        self.dma = dma
        self.sem = None


ENGS = ("pe", "act", "dve", "pool", "sp")
NSLOT = 8


class Prog:
    def __init__(self, nc):
        self.nc = nc
        self.ops = {e: [] for e in ENGS}
        self.ctx = []
        self.dma_cnt = {}
        self.dma_rr = {q: 0 for q in ENGS}
        self.n_sb = 0

    def dram(self, name, shape, dt, kind="Internal"):
        h = self.nc.dram_tensor(name, list(shape), dt, kind=kind)
        return DramT(h, name)

    def sb(self, name, shape, dt):
        self.n_sb += 1
        name = f"{name}_{self.n_sb}"
        g = self.nc.sbuf_tensor(name, list(shape), dt)
        h = g.__enter__()
        self.ctx.append(g)
        return Tile(h, Buf(name))

    def ps(self, name, shape, dt=F32):
        self.n_sb += 1
        name = f"{name}_{self.n_sb}"
        g = self.nc.psum_tensor(name, list(shape), dt)
        h = g.__enter__()
        self.ctx.append(g)
        return Tile(h, Buf(name))

    def op(self, eng, fn, r=(), w=(), wp=(), dmaq=False):
        deps = {}

        def add(d):
            for k, v in d.items():
                if deps.get(k, -1) < v:
                    deps[k] = v

        for v in r:
            add(v.buf.w)
        for v in w:
            add(v.buf.w)
            add(v.buf.r)
        for v in wp:
            add(v.buf.r)
            add(v.buf.wf)
        idx = len(self.ops[eng])
        dma = None
        if dmaq:
            slot = self.dma_rr[eng]
            self.dma_rr[eng] = (slot + 1) % NSLOT
            key = ("d", eng, slot)
            n = self.dma_cnt.get(key, 0)
            if n > 0:
                if deps.get(key, -1) < n:
                    deps[key] = n
            self.dma_cnt[key] = n + 1
            dma = (key, n + 1)
            tok = (key, n + 1)
        else:
            tok = (("c", eng), idx)
        if eng == "pe":
            deps.pop(("c", "pe"), None)
        self.ops[eng].append(_Op(fn, deps, dma))
        for v in r:
            if v.buf.r.get(tok[0], -1) < tok[1]:
                v.buf.r[tok[0]] = tok[1]
        for v in w:
            v.buf.w = {tok[0]: tok[1]}
            v.buf.wf = {tok[0]: tok[1]}
            v.buf.r = {}
        for v in wp:
            if v.buf.w.get(tok[0], -1) < tok[1]:
                v.buf.w[tok[0]] = tok[1]

    def _e(self, eng):
        return eng

    def mm(self, out, lhsT, rhs, start=True, stop=True, **kw):
        self.op("pe", lambda e: e.matmul(out.ap, lhsT.ap, rhs.ap, start=start, stop=stop, **kw),
                r=[lhsT, rhs], wp=[out] if not start else (), w=[out] if start else ())

    def transpose(self, out, in_, ident):
        self.op("pe", lambda e: e.transpose(out.ap, in_.ap, ident.ap), r=[in_, ident], w=[out])

    def act(self, out, in_, func, scale=1.0, bias=0.0, accum=None, part=False):
        r = [in_]
        sc = scale
        bi = bias
        if isinstance(scale, View):
            r.append(scale)
            sc = scale.ap
        if isinstance(bias, View):
            r.append(bias)
            bi = bias.ap
        kw = {}
        ws = [out]
        if accum is not None:
            kw["accum_out"] = accum.ap
            ws.append(accum)
        self.op("act", lambda e: e.activation(out.ap, in_.ap, func, scale=sc, bias=bi, **kw),
                r=r, w=() if part else ws, wp=ws if part else ())

    def tt(self, eng, out, a, b, op, part=False):
        self.op(eng, lambda e: e.tensor_tensor(out.ap, a.ap, b.ap, op), r=[a, b],
                w=() if part else [out], wp=[out] if part else ())

    def ts(self, eng, out, a, s1, op0, s2=None, op1=None, accum=None, part=False):
        r = [a]
        x1 = s1
        x2 = s2
        if isinstance(s1, View):
            r.append(s1)
            x1 = s1.ap
        if isinstance(s2, View):
            r.append(s2)
            x2 = s2.ap
        kw = {}
        ws = [out]
        if op1 is not None:
            kw["op1"] = op1
        if accum is not None:
            kw["accum_out"] = accum.ap
            ws.append(accum)
        self.op(eng, lambda e: e.tensor_scalar(out.ap, a.ap, x1, x2, op0, **kw), r=r,
                w=() if part else ws, wp=ws if part else ())

    def stt(self, out, in0, scalar, in1, op0, op1, part=False, eng="dve"):
        r = [in0, in1]
        sc = scalar
        if isinstance(scalar, View):
            r.append(scalar)
            sc = scalar.ap
        self.op(eng, lambda e: e.scalar_tensor_tensor(out.ap, in0.ap, sc, in1.ap, op0, op1), r=r,
                w=() if part else [out], wp=[out] if part else ())

    def copy(self, eng, out, in_, part=False):
        if eng == "act":
            self.op(eng, lambda e: e.copy(out.ap, in_.ap), r=[in_],
                    w=() if part else [out], wp=[out] if part else ())
        else:
            self.op(eng, lambda e: e.tensor_copy(out.ap, in_.ap), r=[in_],
                    w=() if part else [out], wp=[out] if part else ())

    def memset(self, eng, out, val):
        self.op(eng, lambda e: e.memset(out.ap, val), w=[out])

    def scan(self, out, d0, d1, init, op0, op1, part=False):
        r = [d0, d1]
        ini = init
        if isinstance(init, View):
            r.append(init)
            ini = init.ap
        self.op("dve", lambda e: e.tensor_tensor_scan(out.ap, d0.ap, d1.ap, ini, op0, op1), r=r,
                w=() if part else [out], wp=[out] if part else ())

    def dma(self, q, out, in_, part=False, **kw):
        self.op(q, lambda e: e.dma_start(out=out.ap, in_=in_.ap, **kw), r=[in_],
                w=() if part else [out], wp=[out] if part else (), dmaq=True)

    def _sem(self, name):
        if name not in self.sems:
            g = self.nc.semaphore(name)
            self.sems[name] = g.__enter__()
            self.sem_ctx.append(g)
        return self.sems[name]

    def _dma_tok(self, key, n):
        per = SEM_LIMIT // 16
        j = (n - 1) // per
        return f"d_{key[1]}_{key[2]}_{j}", 16 * ((n - 1) % per + 1)

    def _resolve(self, k, v):
        if k[0] == "c":
            lst = self.ops[k[1]]
            while lst[v].val is None:
                v += 1
            o = lst[v]
            return o.sem, o.val
        return self._dma_tok(k, v)

    def mark(self):
        return len(self.ctx)

    def flush(self, final=False):
        nc = self.nc
        ops = self.ops
        if not hasattr(self, "sems"):
            self.sems = {}
            self.sem_ctx = []
            self.emitted = {e: 0 for e in ENGS}
            self.sigcnt = {e: 0 for e in ENGS}
            self.seen = {e: {} for e in ENGS}
        for e in ENGS:
            for o in ops[e][self.emitted[e]:]:
                for k, v in o.deps.items():
                    if k[0] == "c" and v >= self.emitted[k[1]]:
                        ops[k[1]][v].sig = True
        fdeps = {}
        for e in ENGS:
            for j in range(len(ops[e]) - 1, self.emitted[e] - 1, -1):
                if ops[e][j].dma is None and ops[e][j].fn is not None:
                    ops[e][j].sig = True
                    fdeps[("c", e)] = j
                    break
        if final:
            for key, n in self.dma_cnt.items():
                fdeps[key] = n
            fdeps.pop(("c", "sp"), None)
            ops["sp"].append(_Op(None, fdeps, None))
        for e in ENGS:
            for o in ops[e][self.emitted[e]:]:
                if o.dma is None and o.sig and o.fn is not None:
                    c = self.sigcnt[e]
                    o.sem = f"c_{e}_{c // SEM_LIMIT}"
                    o.val = c % SEM_LIMIT + 1
                    self._sem(o.sem)
                    self.sigcnt[e] = c + 1
                elif o.dma is not None:
                    self._sem(self._dma_tok(*o.dma)[0])
        block = nc.Block()
        block.__enter__()
        names = {"pe": "tensor", "act": "scalar", "dve": "vector", "pool": "gpsimd", "sp": "sync"}
        for e in ENGS:
            def body(eng, e=e):
                seen = self.seen[e]
                for o in ops[e][self.emitted[e]:]:
                    for k, v in o.deps.items():
                        s, val = self._resolve(k, v)
                        if seen.get(s, 0) >= val:
                            continue
                        seen[s] = val
                        eng.wait_ge(self.sems[s], val)
                    if o.fn is None:
                        continue
                    ins = o.fn(eng)
                    if o.dma is not None:
                        s, val = self._dma_tok(*o.dma)
                        ins.then_inc(self.sems[s], 16)
                    elif o.sig:
                        ins.then_inc(self.sems[o.sem], 1)
                    o.fn = True
            getattr(block, names[e])(body)
        block.__exit__(None, None, None)
        for e in ENGS:
            self.emitted[e] = len(ops[e])

    def release(self, mark):
        while len(self.ctx) > mark:
            self.ctx.pop().__exit__(None, None, None)

    def finish(self):
        self.flush(final=True)
        self.release(0)
        for g in reversed(self.sem_ctx):
            g.__exit__(None, None, None)


D = 1024
SEQ = 4096
NMETA = 16
T = SEQ + NMETA
DFF = 2816
NFC = DFF // 128
ALPHA = 4.0 ** 0.25
EPS = 1e-5
TILES = [(i * 512, 512) for i in range(8)] + [(4096, 16)]
KSEL = 256
NBIS = 14


def bc(view, shape):
    return View(view.ap.broadcast_to(list(shape)), view.buf)


def bc3(view, n):
    a = view.ap.unsqueeze(1)
    shp = list(a.shape)
    shp[1] = n
    return View(a.broadcast_to(shp), view.buf)


class VecPack:
    def __init__(self):
        self.cols = []
        self.off = {}
        self.n = 0

    def add(self, name, vec):
        vec = np.asarray(vec, np.float32).reshape(-1)
        nchunk = (vec.size + 127) // 128
        pad = np.zeros(nchunk * 128, np.float32)
        pad[:vec.size] = vec
        self.cols.append(pad.reshape(nchunk, 128).T)
        self.off[name] = self.n
        self.n += nchunk

    def array(self):
        return np.ascontiguousarray(np.concatenate(self.cols, axis=1))


def vec_layout():
    vp = VecPack()
    pack_vecs(vp, None)
    return vp.off, vp.n


def pack_vecs(vp, inp):
    def g(name, shape):
        if inp is None:
            return np.zeros(shape, np.float32)
        return np.asarray(inp[name], np.float32)

    cw = g("a_conv_w", (1, 4, D))[0]
    for j in range(4):
        vp.add(f"a_cw{j}", cw[j])
    vp.add("a_cb", g("a_conv_b", (1, D))[0])
    vp.add("a_ng", g("a_norm_g", (1, D))[0])
    cw = g("b_conv_w", (1, 4, D))[0]
    for j in range(4):
        vp.add(f"b_cw{j}", cw[j])
    vp.add("b_cb", g("b_conv_b", (1, D))[0])
    vp.add("b_ba", g("b_b_a", (1, D))[0])
    vp.add("b_bx", g("b_b_x", (1, D))[0])
    vp.add("b_lam", g("b_lambda", (1, D))[0])
    mg = g("mix_ln_g", (2, D))
    mb = g("mix_ln_b", (2, D))
    fg = g("ffn_ln_g", (2, D))
    fb = g("ffn_ln_b", (2, D))
    fcw = g("ffn_conv_w", (2, 3, DFF))
    fcb = g("ffn_conv_b", (2, DFF))
    for l in range(2):
        vp.add(f"mix_g{l}", mg[l])
        vp.add(f"mix_b{l}", mb[l])
        vp.add(f"ffn_g{l}", fg[l])
        vp.add(f"ffn_b{l}", fb[l])
        for j in range(3):
            vp.add(f"f_cw{l}_{j}", fcw[l, j])
        vp.add(f"f_cb{l}", fcb[l])
    vp.add("c_qg", g("c_q_norm_g", (1, 384))[0])
    vp.add("c_kvg", g("c_kv_norm_g", (1, 256))[0])
    kg = g("c_kidx_ln_g", (1, 64))[0]
    kb = g("c_kidx_ln_b", (1, 64))[0]
    vp.add("c_kig", np.concatenate([kg, kg]))
    vp.add("c_kib", np.concatenate([kb, kb]))


def block_diag_host(w):
    w = np.asarray(w, np.float32)
    out = np.zeros((8, 128, 128), np.float32)
    for n in range(256):
        c, b = divmod(n, 32)
        out[c, 4 * b:4 * b + 4, 4 * b:4 * b + 4] = w[n]
    return out


def make_consts():
    c = np.zeros((128, 1536), np.float32)
    c[:, 0:128] = np.eye(128, dtype=np.float32)
    c[:, 128:256] = np.triu(np.ones((128, 128), np.float32))
    c[:, 256:384] = 1.0
    for h in range(4):
        c[h, 384 + 128 * h: 384 + 128 * (h + 1)] = 1.0
        c[4 + h, 896 + 128 * h: 896 + 128 * (h + 1)] = 1.0
        c[h, 1408 + h] = 1.0
        c[4 + h, 1412 + h] = 1.0
    return c


class Ctx:
    pass


def fm(dram, key=None):
    return dram.v(key).re("(c p) t -> p c t", p=128)


def load_w(P, dst, src_view, KC):
    for kc in range(KC):
        P.dma("pool", dst[:, kc, :], src_view[:, kc, :], part=True, max_dma_last_dim=4096)


def setup_consts(P, C, din):
    C.cf = P.sb("c_f32", [128, 384], F32)
    P.dma("sp", C.cf[:], din["consts"][:, 0:384])
    C.ident = C.cf[:, 0:128]
    C.triu = C.cf[:, 128:256]
    C.ones = C.cf[:, 256:384]
    C.cb = P.sb("c_bf16", [128, 384], BF16)
    P.copy("dve", C.cb[:], C.cf[:, 0:384])
    C.identb = C.cb[:, 0:128]
    C.triub = C.cb[:, 128:256]
    C.onesb = C.cb[:, 256:384]
    C.voff, nv = vec_layout()
    C.vec = P.sb("c_vec", [128, nv], F32)
    P.dma("sp", C.vec[:], din["vecs"][:])
    C.mean1024 = P.sb("c_mean", [128, 128], F32)
    P.memset("dve", C.mean1024[:], 1.0 / 1024.0)
    C.cc = P.sb("c_cols", [128, 4], F32)
    P.memset("dve", C.cc[:, 0:1], EPS)
    P.op("dve", lambda e: e.memset(C.cc[:, 1:2].ap, 1.0), wp=[C.cc[:, 1:2]])
    P.op("dve", lambda e: e.memset(C.cc[:, 2:3].ap, -float(np.log(16.0))), wp=[C.cc[:, 2:3]])
    P.op("dve", lambda e: e.memset(C.cc[:, 3:4].ap, 0.0), wp=[C.cc[:, 3:4]])
    C.epsc = C.cc[:, 0:1]
    C.onec = C.cc[:, 1:2]
    C.nl16 = C.cc[:, 2:3]


def vcol(C, name, c=0, n=1):
    o = C.voff[name] + c
    return C.vec[:, o:o + n]


def ln_gen(P, C, y, w, gname, bname, out_f32, tmp, psm, psq, sq, out_bf=None):
    for c in range(8):
        s = sq[c % 2]
        P.tt("pool", s[:, :w], y[:, c, :w], y[:, c, :w], ALU.mult)
        P.mm(psm[:, :w], C.mean1024[:], y[:, c, :w], c == 0, c == 7)
        P.mm(psq[:, :w], C.mean1024[:], s[:, :w], c == 0, c == 7)
        yield
    mean = tmp["mean"]
    m2 = tmp["m2"]
    rstd = tmp["rstd"]
    P.copy("act", mean[:, :w], psm[:, :w])
    P.tt("pool", m2[:, :w], mean[:, :w], mean[:, :w], ALU.mult)
    P.tt("dve", m2[:, :w], psq[:, :w], m2[:, :w], ALU.subtract)
    P.act(rstd[:, :w], m2[:, :w], AF.Sqrt, bias=C.epsc[:, 0:1])
    P.op("dve", lambda e, o=rstd[:, :w]: e.reciprocal(o.ap, o.ap), r=[rstd[:, :w]], w=[rstd[:, :w]])
    yield
    for c in range(8):
        t = sq[c % 2]
        P.tt("dve", t[:, :w], y[:, c, :w], mean[:, :w], ALU.subtract)
        P.stt(t[:, :w], t[:, :w], vcol(C, gname, c), rstd[:, :w], ALU.mult, ALU.mult)
        P.act(out_f32[:, c, :w], t[:, :w], AF.Identity, bias=vcol(C, bname, c), part=True)
        if out_bf is not None:
            P.copy("pool", out_bf[:, c, :w], out_f32[:, c, :w], part=True)
        yield


def layer_norm_fm(P, C, y, w, gname, bname, out_f32, tmp, psm, psq, sq, out_bf=None):
    for _ in ln_gen(P, C, y, w, gname, bname, out_f32, tmp, psm, psq, sq, out_bf):
        pass


def phase_embed(P, C, din, H0):
    m = P.mark()
    x = din["x"]
    xin = [P.sb(f"e_xin{i}", [128, 4, 1024], F32) for i in range(2)]
    hT = [P.sb(f"e_hT{i}", [128, 8, 512], F32) for i in range(2)]
    pt = [P.ps(f"e_pt{i}", [128, 512], F32) for i in range(4)]
    H0v = fm(H0)
    xv = x.v(None).re("(k j p) f -> k p j f", j=4, p=128)
    n = 0
    for k in range(9):
        xi = xin[k % 2]
        h = hT[k % 2]
        if k == 8:
            P.dma("sp", xi[0:16, 0, :], din["meta"][:])
            nj, rows, pos, w = 1, 16, 0, 16
        else:
            P.dma("sp", xi[:], xv[k])
            nj, rows, pos, w = 4, 128, 16 + 512 * k, 512
        for c in range(8):
            p_ = pt[n % 4]
            n += 1
            for j in range(nj):
                P.op("pe", lambda e, o=p_[:, j * rows:(j + 1) * rows], i=xi[0:rows, j, c * 128:(c + 1) * 128],
                     idn=C.ident[0:rows, 0:rows]: e.transpose(o.ap, i.ap, idn.ap),
                     r=[xi[:], C.ident], w=[p_[:]] if j == 0 else (), wp=[p_[:]] if j else ())
            if c % 2 == 0:
                P.copy("act", h[:, c, :w], p_[:, :w], part=True)
            else:
                P.copy("dve", h[:, c, :w], p_[:, :w], part=True)
        P.dma("pool", H0v[:, :, pos:pos + w], h[:, :, :w], part=True)
    P.flush()
    P.release(m)


def phase_inproj(P, C, din, H0, XM, SOG, XR, GGR):
    m = P.mark()
    w_in = P.sb("w_in", [128, 8, 4096], BF16)
    load_w(P, w_in, din["a_w_in"].v(None).re("(kc p) n -> p kc n", p=128), 8)
    hf = [P.sb(f"i_hf{i}", [128, 8, 512], F32) for i in range(2)]
    hb = [P.sb(f"i_hb{i}", [128, 8, 512], BF16) for i in range(2)]
    stg = [P.sb(f"i_st{i}", [128, 8, 512], F32) for i in range(2)]
    pss = [P.ps(f"i_ps{i}", [128, 512], F32) for i in range(4)]
    outs = [fm(XM), fm(SOG), fm(XR), fm(GGR)]
    H0v = fm(H0)
    n = 0
    for ti, (pos, w) in enumerate(TILES):
        f = hf[ti % 2]
        b = hb[ti % 2]
        P.dma("sp", f[:, :, :w], H0v[:, :, pos:pos + w])
        P.copy("pool", b[:, :, :w], f[:, :, :w])
        for grp in range(4):
            st = stg[(ti * 4 + grp) % 2]
            for o8 in range(8):
                oc = grp * 8 + o8
                ps = pss[n % 4]
                n += 1
                for kc in range(8):
                    P.mm(ps[:, :w], w_in[:, kc, oc * 128:(oc + 1) * 128], b[:, kc, :w], kc == 0, kc == 7)
                if grp == 1:
                    P.act(st[:, o8, :w], ps[:, :w], AF.Sigmoid, part=True)
                elif grp == 3:
                    P.act(st[:, o8, :w], ps[:, :w], AF.Gelu_apprx_tanh, part=True)
                elif o8 % 2 == 0:
                    P.copy("act", st[:, o8, :w], ps[:, :w], part=True)
                else:
                    P.copy("dve", st[:, o8, :w], ps[:, :w], part=True)
            P.dma("sp", outs[grp][:, :, pos:pos + w], st[:, :, :w], part=True)
    P.flush()
    P.release(m)


def phase_rglru(P, C, din, XR, GGR, HR):
    m = P.mark()
    wa = P.sb("r_wa", [128, 8, 128], BF16)
    wx = P.sb("r_wx", [128, 8, 128], BF16)
    P.dma("pool", wa[:], din["b_w_a"][:])
    P.dma("pool", wx[:], din["b_w_x"][:])
    c8 = P.sb("r_c8", [128, 8], F32)
    P.act(c8[:], vcol(C, "b_lam", 0, 8), AF.Exp, scale=-1.0)
    P.act(c8[:], c8[:], AF.Ln, bias=C.onec[:, 0:1])
    P.ts("dve", c8[:], c8[:], -8.0, ALU.mult)
    carry = P.sb("r_carry", [128, 8], F32)
    P.memset("dve", carry[:], 0.0)
    xe = [P.sb(f"r_xe{i}", [128, 8, 515], F32) for i in range(2)]
    gg = [P.sb(f"r_gg{i}", [128, 8, 512], F32) for i in range(2)]
    ho = [P.sb(f"r_ho{i}", [128, 8, 512], BF16) for i in range(2)]
    xc = P.sb("r_xc", [128, 4, 512], F32)
    xcb = P.sb("r_xcb", [128, 4, 512], BF16)
    rr = P.sb("r_r", [128, 4, 512], F32)
    ii = P.sb("r_i", [128, 4, 512], F32)
    aa = P.sb("r_a", [128, 4, 512], F32)
    a2 = P.sb("r_a2", [128, 4, 512], F32)
    hs = P.sb("r_hs", [128, 4, 512], F32)
    pa = [P.ps(f"r_pa{i}", [128, 512], F32) for i in range(4)]
    px = [P.ps(f"r_px{i}", [128, 512], F32) for i in range(4)]
    XRv, GGv, HRv = fm(XR), fm(GGR), fm(HR)
    for ti, (pos, w) in enumerate(TILES):
        x = xe[ti % 2]
        g = gg[ti % 2]
        h = ho[ti % 2]
        if ti == 0:
            P.memset("pool", x[:, :, 0:3], 0.0)
            P.dma("sp", x[:, :, 3:3 + w], XRv[:, :, pos:pos + w], part=True)
        else:
            P.dma("sp", x[:, :, 0:3 + w], XRv[:, :, pos - 3:pos + w])
        P.dma("sp", g[:, :, :w], GGv[:, :, pos:pos + w])
        for hf_ in range(2):
            cs = [hf_ * 4 + k for k in range(4)]
            for k, c in enumerate(cs):
                P.ts("dve", xc[:, k, :w], x[:, c, 3:3 + w], vcol(C, "b_cw3", c), ALU.mult, vcol(C, "b_cb", c), ALU.add,
                     part=(k > 0))
                for j in range(3):
                    P.stt(xc[:, k, :w], x[:, c, j:j + w], vcol(C, f"b_cw{j}", c), xc[:, k, :w], ALU.mult, ALU.add,
                          part=True)
            for k, c in enumerate(cs):
                P.copy("pool", xcb[:, k, :w], xc[:, k, :w], part=(k > 0))
            for k, c in enumerate(cs):
                P.mm(pa[k][:, :w], wa[:, c, :], xcb[:, k, :w])
                P.mm(px[k][:, :w], wx[:, c, :], xcb[:, k, :w])
            for k, c in enumerate(cs):
                P.act(rr[:, k, :w], pa[k][:, :w], AF.Sigmoid, bias=vcol(C, "b_ba", c), part=(k > 0))
            for k, c in enumerate(cs):
                P.act(ii[:, k, :w], px[k][:, :w], AF.Sigmoid, bias=vcol(C, "b_bx", c), part=(k > 0))
            for k, c in enumerate(cs):
                P.act(aa[:, k, :w], rr[:, k, :w], AF.Exp, scale=c8[:, c:c + 1], part=(k > 0))
            for k, c in enumerate(cs):
                P.tt("pool", a2[:, k, :w], aa[:, k, :w], aa[:, k, :w], ALU.mult, part=(k > 0))
            for k, c in enumerate(cs):
                P.tt("pool", ii[:, k, :w], ii[:, k, :w], xc[:, k, :w], ALU.mult, part=True)
            for k, c in enumerate(cs):
                P.act(a2[:, k, :w], a2[:, k, :w], AF.Sqrt, scale=-1.0, bias=C.onec[:, 0:1], part=True)
            for k, c in enumerate(cs):
                P.tt("dve", ii[:, k, :w], ii[:, k, :w], a2[:, k, :w], ALU.mult, part=True)
                P.scan(hs[:, k, :w], aa[:, k, :w], ii[:, k, :w], carry[:, c:c + 1], ALU.mult, ALU.add, part=(k > 0))
                P.copy("dve", carry[:, c:c + 1], hs[:, k, w - 1:w], part=True)
            for k, c in enumerate(cs):
                P.tt("pool", h[:, c, :w], hs[:, k, :w], g[:, c, :w], ALU.mult, part=True)
        P.dma("sp", HRv[:, :, pos:pos + w], h[:, :, :w], part=True)
    P.flush()
    P.release(m)


def phase_mlstm(P, C, din, XM, SOG, HM):
    m = P.mark()
    XMv, SOGv, HMv = fm(XM), fm(SOG), fm(HM)
    bd = {}
    for nm in ("bdq", "bdk", "bdv"):
        bd[nm] = P.sb("m_" + nm, [128, 8, 128], BF16)
        P.dma("pool", bd[nm][:], din[nm][:])
    pw = [P.ps(f"m_pw{i}", [128, 512], F32) for i in range(2)]
    pbc = P.ps("m_pbc", [128, 512], F32)
    pg = pbc
    pS = P.ps("m_pS", [128, 4, 128], F32)
    pT = P.ps("m_pT", [128, 8, 128], BF16)
    pn = P.ps("m_pn", [128, 512], F32)
    pds = [P.ps(f"m_pd{i}", [128, 512], F32) for i in range(2)]
    m2 = P.mark()
    bdT = P.sb("m_bdT", [128, 3, 8, 128], F32)
    P.dma("sp", bdT[:], din["bdT"][:])
    wif = P.sb("m_wif", [128, 24, 8], F32)
    P.dma("sp", wif[:], din["a_w_if"][:])
    wg = P.sb("m_wg", [128, 2, 8, 8], BF16)
    for c in range(8):
        P.mm(pg[:, c * 8:(c + 1) * 8], bdT[:, 0, c, :], wif[:, c, :], True, False)
        P.mm(pg[:, c * 8:(c + 1) * 8], bdT[:, 1, c, :], wif[:, 8 + c, :], False, True)
        P.mm(pg[:, 64 + c * 8:64 + (c + 1) * 8], bdT[:, 2, c, :], wif[:, 16 + c, :], True, True)
    P.copy("dve", wg[:].re("p a c g -> p (a c g)"), pg[:, 0:128])
    bif = P.sb("m_bif", [8, 1], F32)
    P.dma("sp", bif[:], din["a_b_if"][:])
    rmask = P.sb("m_rmask", [8, 512], F32)
    P.memset("dve", rmask[:], 1.0)
    for j in range(4):
        P.op("dve", lambda e, v=rmask[:, j * 128:j * 128 + 1]: e.memset(v.ap, 0.0), wp=[rmask[:]])
    selt = P.sb("m_sel", [8, 1152], F32)
    P.dma("sp", selt[:], din["consts"][0:8, 384:1536])
    selI = [selt[0:8, 128 * h:128 * (h + 1)] for h in range(4)]
    selF = [selt[0:8, 512 + 128 * h:512 + 128 * (h + 1)] for h in range(4)]
    selI4 = selt[0:8, 1024:1028]
    selF4 = selt[0:8, 1028:1032]
    C32 = P.sb("m_C32", [128, 4, 2, 257], F32)
    Cb = P.sb("m_Cb", [128, 4, 2, 257], BF16)
    P.memset("dve", C32[:], 0.0)
    P.memset("pool", Cb[:], 0.0)
    Vaug = P.sb("m_Vaug", [128, 4, 257], BF16)
    P.memset("pool", Vaug[:], 1.0)
    xe = P.sb("m_xe", [128, 8, 515], F32)
    sog_2 = [P.sb(f"m_sog{i}", [128, 8, 512], F32) for i in range(2)]
    xcb_2 = [P.sb(f"m_xcb{i}", [128, 8, 512], BF16) for i in range(2)]
    xmb_2 = [P.sb(f"m_xmb{i}", [128, 8, 512], BF16) for i in range(2)]
    qh_2 = [P.sb(f"m_qh{i}", [128, 8, 512], BF16) for i in range(2)]
    kh_2 = [P.sb(f"m_kh{i}", [128, 8, 512], BF16) for i in range(2)]
    abc_2 = [P.sb(f"m_abc{i}", [128, 4, 512], F32) for i in range(2)]
    bbc = P.sb("m_bbc", [128, 4, 512], F32)
    hm = [P.sb("m_hm0", [128, 8, 512], BF16)] * 2
    cv = [P.sb(f"m_cv{i}", [128, 512], F32) for i in range(2)]
    Gall_2 = [P.sb(f"m_Gall{i}", [8, 512], F32) for i in range(2)]
    Lg_2 = [P.sb(f"m_Lg{i}", [8, 512], F32) for i in range(2)]
    lcar = P.sb("m_lcar", [8, 1], F32)
    btok = P.sb("m_btok", [128, 4], F32)
    Kt = P.sb("m_Kt", [128, 4, 256], BF16)
    ST = P.sb("m_ST", [128, 4, 128], BF16)
    hn = P.sb("m_hn", [128, 4, 256], BF16)
    st6 = P.sb("m_st6", [128, 6], F32)
    mv = P.sb("m_mv", [128, 2], F32)
    sc = P.sb("m_sc", [128, 4], F32)
    ctmps = [P.sb(f"m_ctmp{i}", [128, 257], F32) for i in range(2)]
    n = 0
    def prologue(ti):
        pos, w = TILES[ti]
        sog, xcb, xmb, qh, kh, abc, Gall, Lg = (sog_2[ti % 2], xcb_2[ti % 2], xmb_2[ti % 2], qh_2[ti % 2], kh_2[ti % 2], abc_2[ti % 2], Gall_2[ti % 2], Lg_2[ti % 2])
        if ti == 0:
            P.memset("pool", xe[:, :, 0:3], 0.0)
            P.dma("sp", xe[:, :, 3:3 + w], XMv[:, :, pos:pos + w], part=True)
        else:
            P.dma("sp", xe[:, :, 0:3 + w], XMv[:, :, pos - 3:pos + w])
        P.dma("sp", sog[:, :, :w], SOGv[:, :, pos:pos + w])
        for c in range(8):
            k = c % 2
            P.ts("dve", cv[k][:, :w], xe[:, c, 3:3 + w], vcol(C, "a_cw3", c), ALU.mult)
            for j in range(3):
                P.stt(cv[k][:, :w], xe[:, c, j:j + w], vcol(C, f"a_cw{j}", c), cv[k][:, :w], ALU.mult, ALU.add)
            P.act(xcb[:, c, :w], cv[k][:, :w], AF.Silu, bias=vcol(C, "a_cb", c), part=(c > 0))
            P.copy("pool", xmb[:, c, :w], xe[:, c, 3:3 + w], part=(c > 0))
            P.act(sog[:, c, :w], sog[:, c, :w], AF.Identity, scale=vcol(C, "a_ng", c), part=True)
            yield
        for c in range(8):
            P.mm(pg[0:8, :w], wg[:, 0, c, :], xcb[:, c, :w], c == 0, False)
        for c in range(8):
            P.mm(pg[0:8, :w], wg[:, 1, c, :], xmb[:, c, :w], False, c == 7)
        P.act(Gall[:, :w], pg[0:8, :w], AF.Identity, bias=bif[:, 0:1])
        P.act(Lg[:, :w], Gall[:, :w], AF.Exp, scale=-1.0)
        P.act(Lg[:, :w], Lg[:, :w], AF.Ln, bias=C.onec[0:8, 0:1])
        P.scan(Lg[:, :w], rmask[:, :w], Lg[:, :w], 0.0, ALU.mult, ALU.add)
        for h in range(4):
            P.mm(pbc[:, :w], selF[h], Lg[:, :w], True, True)
            P.act(abc[:, h, :w], pbc[:, :w], AF.Exp, scale=-1.0, part=(h > 0))
            P.mm(pbc[:, :w], selI[h], Gall[:, :w], True, False)
            P.mm(pbc[:, :w], selF[h], Lg[:, :w], False, True)
            P.act(bbc[:, h, :w], pbc[:, :w], AF.Exp, bias=C.nl16, part=(h > 0))
            yield
        for c in range(8):
            P.mm(pw[0][:, :w], bd["bdq"][:, c, :], xcb[:, c, :w])
            P.tt("dve", qh[:, c, :w], pw[0][:, :w], abc[:, c // 2, :w], ALU.mult, part=(c > 0))
            P.mm(pw[1][:, :w], bd["bdk"][:, c, :], xcb[:, c, :w])
            P.tt("dve", kh[:, c, :w], pw[1][:, :w], bbc[:, c // 2, :w], ALU.mult, part=(c > 0))
            yield
        yield

    def chunks(ti):
        pos, w = TILES[ti]
        sog, xcb, xmb, qh, kh, abc, Gall, Lg = (sog_2[ti % 2], xcb_2[ti % 2], xmb_2[ti % 2], qh_2[ti % 2], kh_2[ti % 2], abc_2[ti % 2], Gall_2[ti % 2], Lg_2[ti % 2])
        hmt = hm[ti % 2]
        for j in range((w + 127) // 128):
            L = min(128, w - j * 128)
            cs = slice(j * 128, j * 128 + L)
            P.mm(pbc[:L, 0:4], Gall[:, cs], selI4, True, False)
            P.mm(pbc[:L, 0:4], Lg[:, cs], selF4, False, True)
            P.act(btok[:L, :], pbc[:L, 0:4], AF.Exp, bias=C.nl16[:L, :])
            for g2 in range(2):
                for c4 in range(4):
                    c = g2 * 4 + c4
                    P.mm(pw[g2][:L, c4 * 128:(c4 + 1) * 128], xmb[:, c, cs], bd["bdv"][:, c, :], True, True)
                P.copy("act", Vaug[:L, 2 * g2:2 * g2 + 2, 0:256], pw[g2][:L, :].re("p (h v) -> p h v", h=2), part=True)
            for g2 in range(2):
                for c4 in range(4):
                    c = g2 * 4 + c4
                    P.mm(pw[g2][:L, c4 * 128:(c4 + 1) * 128], xcb[:, c, cs], bd["bdk"][:, c, :], True, True)
                for h2 in range(2):
                    h = 2 * g2 + h2
                    P.ts("dve", Kt[:L, h, :], pw[g2][:L, h2 * 256:(h2 + 1) * 256], btok[:L, h:h + 1], ALU.mult, part=True)
            for h in range(4):
                for dc in range(2):
                    P.mm(pS[:L, h, :L], kh[:, 2 * h + dc, cs], qh[:, 2 * h + dc, cs], dc == 0, dc == 1)
            P.tt("dve", ST[:L, :, :L], pS[:L, :, :L], bc3(C.triu[:L, :L], 4), ALU.mult)
            for h in range(4):
                P.mm(pn[:L, 0:257], ST[:L, h, :L], Vaug[:L, h, :], True, False)
                for dc in range(2):
                    P.mm(pn[:L, 0:257], qh[:, 2 * h + dc, cs], Cb[:, h, dc, :], False, dc == 1)
                A = abc[:, h, j * 128 + L - 1:j * 128 + L]
                for dc in range(2):
                    pd, ctmp = pds[dc], ctmps[dc]
                    P.mm(pd[:, 0:257], Kt[:L, h, dc * 128:(dc + 1) * 128], Vaug[:L, h, :], True, True)
                    P.tt("dve", ctmp[:], pd[:, 0:257], C32[:, h, dc, :], ALU.add)
                    P.act(Cb[:, h, dc, :], ctmp[:], AF.Identity, scale=A, part=True)
                    P.act(C32[:, h, dc, :], ctmp[:], AF.Identity, scale=A, part=True)
                P.copy("dve", sc[:L, 3:4], pn[:L, 256:257])
                P.tt("dve", sc[:L, 0:1], sc[:L, 3:4], sc[:L, 3:4], ALU.mult, part=True)
                P.ts("dve", sc[:L, 0:1], sc[:L, 0:1], 1.0, ALU.max, part=True)
                P.op("dve", lambda e, o=sc[:L, 0:1]: e.reciprocal(o.ap, o.ap), r=[sc[:]], wp=[sc[:]])
                P.op("dve", lambda e, o=st6[:L, :], i=pn[:L, 0:256]: e.bn_stats(o.ap, i.ap), r=[pn[:]], w=[st6[:]])
                P.op("dve", lambda e, o=mv[:L, :], i=st6[:L, :]: e.bn_aggr(o.ap, i.ap), r=[st6[:]], w=[mv[:]])
                P.ts("dve", sc[:L, 1:2], sc[:L, 0:1], mv[:L, 1:2], ALU.mult, EPS, ALU.add, part=True)
                P.op("dve", lambda e, o=sc[:L, 1:2]: e.reciprocal(o.ap, o.ap), r=[sc[:]], wp=[sc[:]])
                P.tt("dve", sc[:L, 1:2], sc[:L, 1:2], sc[:L, 0:1], ALU.mult, part=True)
                P.act(sc[:L, 2:3], sc[:L, 1:2], AF.Sqrt, part=True)
                P.ts("dve", hn[:L, h, :], pn[:L, 0:256], mv[:L, 0:1], ALU.subtract, sc[:L, 2:3], ALU.mult, part=(h > 0))
                yield
            for c in range(8):
                P.op("pe", lambda e, o=pT[:, c, :L], i=hn[:L, c // 2, (c % 2) * 128:(c % 2) * 128 + 128],
                     idn=C.identb[:L, :L]: e.transpose(o.ap, i.ap, idn.ap),
                     r=[hn[:], C.identb], w=[pT[:]] if c == 0 else (), wp=[pT[:]] if c else ())
            P.tt("dve", hmt[:, :, cs], pT[:, :, :L], sog[:, :, cs], ALU.mult, part=(j > 0))
        P.dma("sp", HMv[:, :, pos:pos + w], hmt[:, :, :w], part=True)
        yield

    pro = prologue(0)
    for _ in pro:
        pass
    for ti in range(len(TILES)):
        ck = chunks(ti)
        nxt = prologue(ti + 1) if ti + 1 < len(TILES) else None
        for _ in ck:
            if nxt is not None:
                try:
                    next(nxt)
                    next(nxt)
                except StopIteration:
                    nxt = None
        if nxt is not None:
            for _ in nxt:
                pass
    P.flush()
    P.release(m)


def ln_tmps(P, pfx):
    return {"mean": P.sb(pfx + "_mean", [128, 512], F32), "m2": P.sb(pfx + "_m2", [128, 512], F32),
            "rstd": P.sb(pfx + "_rstd", [128, 512], F32)}


def phase_outproj(P, C, wsrc, KC, ins_bf, Hin, Hout, layer, loader=None):
    m = P.mark()
    wo = P.sb("o_w", [128, KC, 1024], BF16)
    load_w(P, wo, wsrc.v(None).re("(kc p) n -> p kc n", p=128), KC)
    xin = [P.sb(f"o_x{i}", [128, KC, 512], BF16) for i in range(2)]
    hf = [P.sb(f"o_h{i}", [128, 8, 512], F32) for i in range(2)]
    sq = [P.sb(f"o_sq{i}", [128, 512], F32) for i in range(2)]
    tmp = ln_tmps(P, "o")
    pss = [P.ps(f"o_ps{i}", [128, 512], F32) for i in range(4)]
    psm = P.ps("o_psm", [128, 512], F32)
    psq = P.ps("o_psq", [128, 512], F32)
    Hiv, Hov = fm(Hin), fm(Hout)
    n = 0
    for ti, (pos, w) in enumerate(TILES):
        x = xin[ti % 2]
        h = hf[ti % 2]
        if loader is not None:
            loader(ti, pos, w, x, pss)
        for i, src in enumerate(ins_bf):
            P.dma("sp", x[:, 8 * i:8 * i + 8, :w], fm(src)[:, :, pos:pos + w], part=(i > 0))
        P.dma("sp", h[:, :, :w], Hiv[:, :, pos:pos + w])
        for oc in range(8):
            ps = pss[n % 4]
            n += 1
            for kc in range(KC):
                P.mm(ps[:, :w], wo[:, kc, oc * 128:(oc + 1) * 128], x[:, kc, :w], kc == 0, kc == KC - 1)
            P.stt(h[:, oc, :w], h[:, oc, :w], ALPHA, ps[:, :w], ALU.mult, ALU.add, part=True)
        layer_norm_fm(P, C, h, w, f"mix_g{layer}", f"mix_b{layer}", h, tmp, psm, psq, sq)
        P.dma("pool", Hov[:, :, pos:pos + w], h[:, :, :w], part=True)
    P.flush()
    P.release(m)


def phase_ffn(P, C, din, layer, Hin, Hout, final_out=None):
    m = P.mark()
    wu = P.sb("f_wu", [128, 8, 2 * DFF], BF16)
    wd = P.sb("f_wd", [128, NFC, 1024], BF16)
    load_w(P, wu, din["ffn_w_up"].v(None)[layer].re("(kc p) n -> p kc n", p=128), 8)
    load_w(P, wd, din["ffn_w_down"].v(None)[layer].re("(kc p) n -> p kc n", p=128), NFC)
    hf = P.sb("f_h", [128, 8, 512], F32)
    hb = P.sb("f_hb", [128, 8, 512], BF16)
    hh = P.sb("f_hh", [128, NFC, 512], BF16)
    gx = [P.sb(f"f_gx{i}", [128, 514], F32) for i in range(2)]
    ac = [P.sb(f"f_ac{i}", [128, 512], F32) for i in range(2)]
    halo = P.sb("f_halo", [128, NFC, 2], F32)
    P.memset("dve", halo[:], 0.0)
    sq = [P.sb(f"f_sq{i}", [128, 512], F32) for i in range(2)]
    tmp = ln_tmps(P, "f")
    pgs = [P.ps(f"f_pg{i}", [128, 512], F32) for i in range(2)]
    pus = [P.ps(f"f_pu{i}", [128, 512], F32) for i in range(2)]
    pys = [P.ps(f"f_py{i}", [128, 512], F32) for i in range(2)]
    psm = P.ps("f_psm", [128, 512], F32)
    psq = P.ps("f_psq", [128, 512], F32)
    Hiv = fm(Hin)
    if final_out is None:
        Hov = fm(Hout)
    else:
        ot = P.sb("f_ot0", [128, 1024], F32)
    cw = [f"f_cw{layer}_{j}" for j in range(3)]
    cnt = {"n": 0}

    def load_hb(ti):
        pos, w = TILES[ti]
        for c in range(8):
            P.dma("pool", hb[:, c, :w], Hiv[:, c, pos:pos + w], part=(c > 0), max_dma_last_dim=4096)

    def stage_up(ti):
        pos, w = TILES[ti]
        for fc in range(NFC):
            k = cnt["n"] % 2
            cnt["n"] += 1
            pg, pu, g, a = pgs[k], pus[k], gx[k], ac[k]
            for kc in range(8):
                P.mm(pg[:, :w], wu[:, kc, DFF + fc * 128:DFF + (fc + 1) * 128], hb[:, kc, :w], kc == 0, kc == 7)
            for kc in range(8):
                P.mm(pu[:, :w], wu[:, kc, fc * 128:(fc + 1) * 128], hb[:, kc, :w], kc == 0, kc == 7)
            P.copy("act", g[:, 0:2], halo[:, fc, :])
            P.copy("act", g[:, 2:2 + w], pg[:, :w], part=True)
            P.copy("pool", halo[:, fc, :], g[:, w:w + 2], part=True)
            P.ts("dve", a[:, :w], g[:, 2:2 + w], vcol(C, cw[2], fc), ALU.mult)
            P.stt(a[:, :w], g[:, 1:1 + w], vcol(C, cw[1], fc), a[:, :w], ALU.mult, ALU.add)
            P.stt(a[:, :w], g[:, 0:w], vcol(C, cw[0], fc), a[:, :w], ALU.mult, ALU.add)
            P.act(a[:, :w], a[:, :w], AF.Gelu_apprx_tanh, bias=vcol(C, f"f_cb{layer}", fc))
            P.tt("dve", hh[:, fc, :w], a[:, :w], pu[:, :w], ALU.mult, part=(fc > 0))
            yield

    def stage_down(ti):
        pos, w = TILES[ti]
        P.dma("sp", hf[:, :, :w], Hiv[:, :, pos:pos + w])
        if ti + 1 < len(TILES):
            load_hb(ti + 1)
        for oc in range(8):
            py = pys[oc % 2]
            for fc in range(NFC):
                P.mm(py[:, :w], wd[:, fc, oc * 128:(oc + 1) * 128], hh[:, fc, :w], fc == 0, fc == NFC - 1)
            P.stt(hf[:, oc, :w], hf[:, oc, :w], ALPHA, py[:, :w], ALU.mult, ALU.add, part=True)

    def stage_ln(ti):
        pos, w = TILES[ti]
        for _ in ln_gen(P, C, hf, w, f"ffn_g{layer}", f"ffn_b{layer}", hf, tmp, psm, psq, sq):
            yield
        if final_out is None:
            P.dma("pool", Hov[:, :, pos:pos + w], hf[:, :, :w], part=True)
        else:
            for j in range((w + 127) // 128):
                L = min(128, w - j * 128)
                p0 = pos + j * 128
                o = ot
                for half in range(2):
                    py = pys[half]
                    for c4 in range(4):
                        c = half * 4 + c4
                        P.op("pe", lambda e, o_=py[:L, c4 * 128:(c4 + 1) * 128], i=hf[:, c, j * 128:j * 128 + L],
                             idn=C.ident: e.transpose(o_.ap, i.ap, idn.ap),
                             r=[hf[:], C.ident], w=[py[:]] if c4 == 0 else (), wp=[py[:]] if c4 else ())
                    if half == 0:
                        P.copy("act", o[:L, 0:512], py[:L, :], part=False)
                    else:
                        P.copy("dve", o[:L, 512:1024], py[:L, :], part=True)
                lo = max(p0, NMETA)
                hi = p0 + L
                if hi > lo:
                    P.dma("pool", final_out[lo - NMETA:hi - NMETA, :], o[lo - p0:hi - p0, :], part=True)
                yield

    load_hb(0)
    prev_ln = None
    for ti in range(len(TILES)):
        up = stage_up(ti)
        for _ in up:
            if prev_ln is not None:
                try:
                    next(prev_ln)
                except StopIteration:
                    prev_ln = None
        if prev_ln is not None:
            for _ in prev_ln:
                pass
        stage_down(ti)
        prev_ln = stage_ln(ti)
    for _ in prev_ln:
        pass
    P.flush()
    P.release(m)


IN_SHAPES = {
    "x": [SEQ, D], "meta": [NMETA, D], "consts": [128, 1536], "vecs": None,
    "a_w_in": [D, 4096], "bdq": [128, 8, 128], "bdk": [128, 8, 128], "bdv": [128, 8, 128],
    "bdT": [128, 3, 8, 128], "a_w_if": [128, 24, 8], "a_b_if": [8, 1],
    "b_w_a": [128, 8, 128], "b_w_x": [128, 8, 128], "ab_w_out": [2 * D, D],
    "ffn_w_up": [2, D, 2 * DFF], "ffn_w_down": [2, DFF, D],
    "c_w_in": [D, 712], "c_w_qidx": [384, 512], "c_w_uq": [384, 1024], "c_w_ukT": [128, 8, 256],
    "c_w_uv": [128, 8, 2, 128], "c_w_out": [D, D],
}


def build(stop_after=None, dbg=()):
    nc = bass.Bass("TRN2", target_bir_lowering=False)
    P = Prog(nc)
    C = Ctx()
    _, nv = vec_layout()
    din = {}
    for k, shp in IN_SHAPES.items():
        if k == "vecs":
            shp = [128, nv]
        din[k] = P.dram(k, shp, F32, kind="ExternalInput")
    out = P.dram("out", [SEQ, D], F32, kind="ExternalOutput")

    def scratch(name, dt=F32):
        kind = "ExternalOutput" if name in dbg else "Internal"
        return P.dram(name, [D, T], dt, kind=kind)

    def scratch2(name, shape, dt):
        kind = "ExternalOutput" if name in dbg else "Internal"
        return P.dram(name, shape, dt, kind=kind)

    setup_consts(P, C, din)
    H0 = scratch("H0")
    phase_embed(P, C, din, H0)
    XM, SOG, XR, GGR = scratch("XM"), scratch("SOG"), scratch("XR"), scratch("GGR")
    phase_inproj(P, C, din, H0, XM, SOG, XR, GGR)
    HR = scratch("HR", BF16)
    phase_rglru(P, C, din, XR, GGR, HR)
    HM = scratch("HM", BF16)
    phase_mlstm(P, C, din, XM, SOG, HM)
    H1 = scratch("H1")
    phase_outproj(P, C, din["ab_w_out"], 16, [HM, HR], H0, H1, 0)
    H2 = scratch("H2")
    if stop_after == "L0":
        phase_ffn(P, C, din, 0, H1, H2)
        P.finish()
        return nc
    phase_ffn(P, C, din, 0, H1, H2)
    CKVT = scratch2("CKVT", [256, T], BF16)
    CKV = scratch2("CKV", [T, 256], BF16)
    KIDXT = scratch2("KIDXT", [64, T], BF16)
    WIDX = scratch2("WIDX", [T, 8], F32)
    QIDXT = scratch2("QIDXT", [512, T], BF16)
    QLATT = scratch2("QLATT", [2048, T], BF16)
    OLATT = scratch2("OLATT", [2048, T], BF16)
    phase_dsa_proj(P, C, din, H2, CKVT, CKV, KIDXT, WIDX, QIDXT, QLATT)
    if stop_after == "L1proj":
        P.finish()
        return nc
    phase_dsa_attn(P, C, din, CKVT, CKV, KIDXT, WIDX, QIDXT, QLATT, OLATT)
    if stop_after == "L1attn":
        P.finish()
        return nc
    H3 = scratch("H3")
    muv = P.mark()
    wuv = P.sb("u_wuv", [128, 8, 2, 128], BF16)
    P.dma("pool", wuv[:], din["c_w_uv"][:])
    olb = [P.sb(f"u_ol{i}", [128, 2, 8, 512], BF16) for i in range(2)]
    OLATv = OLATT.v(None).re("(p a) t -> p a t", a=16)

    def uv_loader(ti, pos, w, x, pss):
        ol = olb[ti % 2]
        P.dma("sp", ol[:, :, :, :w].re("p rc h t -> p (rc h) t"), OLATv[:, :, pos:pos + w])
        for h in range(8):
            ps = pss[h % 4]
            for rc in range(2):
                P.mm(ps[:, :w], wuv[:, h, rc, :], ol[:, rc, h, :w], rc == 0, rc == 1)
            if h % 2:
                P.copy("act", x[:, h, :w], ps[:, :w], part=(h > 0))
            else:
                P.copy("dve", x[:, h, :w], ps[:, :w], part=(h > 0))

    phase_outproj(P, C, din["c_w_out"], 8, [], H2, H3, 1, loader=uv_loader)
    P.release(muv)
    if stop_after == "L1mix":
        P.finish()
        return nc
    phase_ffn(P, C, din, 1, H3, None, final_out=out.v(None))
    P.finish()
    return nc


def host_inputs(inp):
    f = lambda a: np.ascontiguousarray(np.asarray(a, np.float32))
    vp = VecPack()
    pack_vecs(vp, inp)
    bds = [block_diag_host(inp[k][0]) for k in ("a_wq", "a_wk", "a_wv")]
    shared = {
        "meta": f(inp["meta_tokens"]), "consts": make_consts(), "vecs": vp.array(),
        "a_w_in": f(inp["a_w_in"][0]),
        "bdq": f(bds[0].transpose(1, 0, 2)), "bdk": f(bds[1].transpose(1, 0, 2)), "bdv": f(bds[2].transpose(1, 0, 2)),
        "bdT": f(np.stack([b.transpose(2, 0, 1) for b in bds], axis=1)),
        "a_w_if": f(np.asarray(inp["a_w_if"][0]).reshape(24, 128, 8).transpose(1, 0, 2)),
        "a_b_if": f(np.asarray(inp["a_b_if"][0]).reshape(8, 1)),
        "b_w_a": f(np.asarray(inp["b_w_a"][0]).transpose(1, 0, 2)),
        "b_w_x": f(np.asarray(inp["b_w_x"][0]).transpose(1, 0, 2)),
        "ab_w_out": f(inp["ab_w_out"][0]),
        "ffn_w_up": f(inp["ffn_w_up"]), "ffn_w_down": f(inp["ffn_w_down"]),
        "c_w_in": f(inp["c_w_in"][0]), "c_w_qidx": f(inp["c_w_qidx"][0]), "c_w_uq": f(inp["c_w_uq"][0]),
        "c_w_ukT": f(np.asarray(inp["c_w_uk"][0]).transpose(2, 0, 1)),
        "c_w_uv": f(np.asarray(inp["c_w_uv"][0]).reshape(8, 2, 128, 128).transpose(2, 0, 1, 3)),
        "c_w_out": f(inp["c_w_out"][0]),
    }
    return shared


_NC_CACHE = {}


def kernel(**inp):
    shared = host_inputs(inp)
    x = np.asarray(inp["x"], np.float32)
    if "full" not in _NC_CACHE:
        _NC_CACHE["full"] = build()
    nc = _NC_CACHE["full"]
    in_maps = []
    for b in range(8):
        d = dict(shared)
        d["x"] = np.ascontiguousarray(x[b])
        in_maps.append(d)
    res = run_bass_kernel_spmd(nc, in_maps, core_ids=list(range(8)))
    return np.stack([res.results[b]["out"] for b in range(8)], axis=0)


CQ_R, CKV_R = 384, 256


def phase_dsa_proj(P, C, din, H2, CKVT, CKV, KIDXT, WIDX, QIDXT, QLATT):
    m = P.mark()
    wci = P.sb("d_wci", [128, 8, 712], BF16)
    load_w(P, wci, din["c_w_in"].v(None).re("(kc p) n -> p kc n", p=128), 8)
    wqi = P.sb("d_wqi", [128, 3, 512], BF16)
    load_w(P, wqi, din["c_w_qidx"].v(None).re("(kc p) n -> p kc n", p=128), 3)
    wuq = P.sb("d_wuq", [128, 3, 1024], BF16)
    load_w(P, wuq, din["c_w_uq"].v(None).re("(kc p) n -> p kc n", p=128), 3)
    wuk = P.sb("d_wuk", [128, 8, 256], BF16)
    P.dma("pool", wuk[:], din["c_w_ukT"][:])
    m384 = P.sb("d_m384", [128, 128], F32)
    P.memset("dve", m384[:], 1.0 / 384.0)
    m256 = P.sb("d_m256", [128, 128], F32)
    P.memset("dve", m256[:], 1.0 / 256.0)
    m64 = P.sb("d_m64", [64, 64], F32)
    P.memset("dve", m64[:], 1.0 / 64.0)
    hf = P.sb("d_hf", [128, 8, 512], F32)
    hb = P.sb("d_hb", [128, 8, 512], BF16)
    raw = P.sb("d_raw", [128, 3, 512], F32)
    sq = [P.sb(f"d_sq{i}", [128, 512], F32) for i in range(2)]
    rs = P.sb("d_rs", [128, 512], F32)
    mn = P.sb("d_mn", [128, 512], F32)
    cqn = P.sb("d_cqn", [128, 3, 512], BF16)
    kvn = [P.sb(f"d_kvn{i}", [128, 2, 512], BF16) for i in range(2)]
    kvt = [P.sb(f"d_kvt{i}", [128, 4, 256], BF16) for i in range(2)]
    kin = [P.sb(f"d_kin{i}", [64, 512], BF16) for i in range(2)]
    wi = [P.sb(f"d_wi{i}", [128, 4, 8], F32) for i in range(2)]
    qix = [P.sb(f"d_qix{i}", [128, 4, 512], BF16) for i in range(2)]
    qf = P.sb("d_qf", [128, 8, 512], BF16)
    qlt = [P.sb(f"d_qlt{i}", [128, 2, 8, 512], BF16) for i in range(2)]
    pss = [P.ps(f"d_ps{i}", [128, 512], F32) for i in range(4)]
    pst = P.ps("d_pst", [128, 512], F32)
    ptr = P.ps("d_ptr", [128, 8, 128], BF16)
    pwi = P.ps("d_pwi", [128, 512], F32)
    H2v = fm(H2)
    CKVTv = fm(CKVT)
    CKVv = CKV.v(None)
    WIDXv = WIDX.v(None)
    QIDXv = fm(QIDXT)
    QLATv = QLATT.v(None).re("(p a) t -> p a t", a=16)
    KIv = KIDXT.v(None)
    n = 0

    def nxt():
        nonlocal n
        n += 1
        return pss[n % 4]

    for ti, (pos, w) in enumerate(TILES):
        k2 = ti % 2
        P.dma("sp", hf[:, :, :w], H2v[:, :, pos:pos + w])
        P.copy("pool", hb[:, :, :w], hf[:, :, :w])

        def proj_chunk(ps, c0, mcols):
            for kc in range(8):
                P.mm(ps[:mcols, :w], wci[:, kc, c0:c0 + mcols], hb[:, kc, :w], kc == 0, kc == 7)

        def rms(nch, c0, mmat, gname, outt):
            for c in range(nch):
                ps = nxt()
                proj_chunk(ps, c0 + c * 128, 128)
                P.copy("act", raw[:, c, :w], ps[:, :w], part=(c > 0))
                s = sq[c % 2]
                P.tt("pool", s[:, :w], raw[:, c, :w], raw[:, c, :w], ALU.mult)
                P.mm(pst[:, :w], mmat[:], s[:, :w], c == 0, c == nch - 1)
            P.act(rs[:, :w], pst[:, :w], AF.Sqrt, bias=C.epsc)
            P.op("dve", lambda e, o=rs[:, :w]: e.reciprocal(o.ap, o.ap), r=[rs[:]], w=[rs[:]])
            for c in range(nch):
                P.stt(outt[:, c, :w], raw[:, c, :w], vcol(C, gname, c), rs[:, :w], ALU.mult, ALU.mult, part=(c > 0))

        rms(3, 0, m384, "c_qg", cqn)
        kv = kvn[k2]
        rms(2, CQ_R, m256, "c_kvg", kv)
        P.dma("pool", CKVTv[:, :, pos:pos + w], kv[:, :, :w], part=True)
        kt = kvt[k2]
        nb = (w + 127) // 128
        for j in range(nb):
            L = min(128, w - j * 128)
            for rc in range(2):
                P.op("pe", lambda e, o=ptr[:L, j * 2 + rc, :], i=kv[:, rc, j * 128:j * 128 + L], idn=C.identb:
                     e.transpose(o.ap, i.ap, idn.ap), r=[kv[:], C.identb],
                     w=[ptr[:]] if (j == 0 and rc == 0) else (), wp=[ptr[:]] if (j or rc) else ())
        if w == 512:
            P.copy("act", kt[:].re("p j r -> p (j r)"), ptr[:].re("p a b -> p (a b)"))
            P.dma("pool", CKVv[pos:pos + 512, :].re("(j p) r -> p j r", p=128), kt[:], part=True)
        else:
            P.copy("act", kt[:w, 0, :], ptr[:w, 0:2, :].re("p a b -> p (a b)"))
            P.dma("pool", CKVv[pos:pos + w, :], kt[:w, 0, :], part=True)
        ps = nxt()
        proj_chunk(ps, CQ_R + CKV_R, 64)
        P.copy("act", raw[0:64, 0, :w], ps[0:64, :w])
        P.tt("pool", sq[0][0:64, :w], raw[0:64, 0, :w], raw[0:64, 0, :w], ALU.mult)
        P.mm(pst[0:64, :w], m64[:], raw[0:64, 0, :w], True, True)
        P.copy("act", mn[0:64, :w], pst[0:64, :w])
        P.mm(pst[0:64, :w], m64[:], sq[0][0:64, :w], True, True)
        P.tt("pool", sq[1][0:64, :w], mn[0:64, :w], mn[0:64, :w], ALU.mult)
        P.tt("dve", rs[0:64, :w], pst[0:64, :w], sq[1][0:64, :w], ALU.subtract)
        P.act(rs[0:64, :w], rs[0:64, :w], AF.Sqrt, bias=C.epsc[0:64, :])
        P.op("dve", lambda e, o=rs[0:64, :w]: e.reciprocal(o.ap, o.ap), r=[rs[:]], w=[rs[:]])
        P.tt("dve", sq[0][0:64, :w], raw[0:64, 0, :w], mn[0:64, :w], ALU.subtract)
        P.stt(sq[0][0:64, :w], sq[0][0:64, :w], vcol(C, "c_kig")[0:64, :], rs[0:64, :w], ALU.mult, ALU.mult)
        ki = kin[k2]
        P.act(ki[:, :w], sq[0][0:64, :w], AF.Identity, bias=vcol(C, "c_kib")[0:64, :])
        P.dma("pool", KIv[:, pos:pos + w], ki[:, :w], part=True)
        wt = wi[k2]
        for j in range(nb):
            L = min(128, w - j * 128)
            for kc in range(8):
                P.mm(pwi[:L, j * 8:(j + 1) * 8], hb[:, kc, j * 128:j * 128 + L], wci[:, kc, 704:712], kc == 0, kc == 7)
        if w == 512:
            P.act(wt[:].re("p j g -> p (j g)"), pwi[:, 0:32], AF.Copy, scale=float(8.0 ** -0.5))
            P.dma("pool", WIDXv[pos:pos + 512, :].re("(j p) g -> p j g", p=128), wt[:], part=True)
        else:
            P.act(wt[:w, 0, :], pwi[:w, 0:8], AF.Copy, scale=float(8.0 ** -0.5))
            P.dma("pool", WIDXv[pos:pos + w, :], wt[:w, 0, :], part=True)
        qx = qix[k2]
        for oc in range(4):
            ps = nxt()
            for kc in range(3):
                P.mm(ps[:, :w], wqi[:, kc, oc * 128:(oc + 1) * 128], cqn[:, kc, :w], kc == 0, kc == 2)
            if oc % 2:
                P.copy("act", qx[:, oc, :w], ps[:, :w], part=(oc > 0))
            else:
                P.copy("dve", qx[:, oc, :w], ps[:, :w], part=(oc > 0))
        P.dma("pool", QIDXv[:, :, pos:pos + w], qx[:, :, :w], part=True)
        for h in range(8):
            ps = nxt()
            for kc in range(3):
                P.mm(ps[:, :w], wuq[:, kc, h * 128:(h + 1) * 128], cqn[:, kc, :w], kc == 0, kc == 2)
            if h % 2:
                P.copy("act", qf[:, h, :w], ps[:, :w], part=(h > 0))
            else:
                P.copy("dve", qf[:, h, :w], ps[:, :w], part=(h > 0))
        ql = qlt[k2]
        for h in range(8):
            for rc in range(2):
                ps = nxt()
                P.mm(ps[:, :w], wuk[:, h, rc * 128:(rc + 1) * 128], qf[:, h, :w])
                first = (h == 0 and rc == 0)
                if rc:
                    P.act(ql[:, rc, h, :w], ps[:, :w], AF.Copy, scale=float(128.0 ** -0.5), part=not first)
                else:
                    P.ts("dve", ql[:, rc, h, :w], ps[:, :w], float(128.0 ** -0.5), ALU.mult, part=not first)
        P.dma("pool", QLATv[:, :, pos:pos + w], ql[:, :, :, :w].re("p rc h t -> p (rc h) t"), part=True)
    P.flush()
    P.release(m)


NKB = 33
NEG = -1.0e30


def phase_dsa_attn(P, C, din, CKVT, CKV, KIDXT, WIDX, QIDXT, QLATT, OLATT):
    m = P.mark()
    kidx2 = P.sb("t_kidx2", [128, T], BF16)
    P.dma("sp", kidx2[0:64, :], KIDXT.v(None)[:, :], part=True)
    P.dma("sp", kidx2[64:128, :], KIDXT.v(None)[:, :], part=True)
    ckvT = P.sb("t_ckvT", [128, 2, T], BF16)
    P.dma("sp", ckvT[:], fm(CKVT)[:, :, :])
    ckv = P.sb("t_ckv", [128, NKB, 256], BF16)
    for i in range(4):
        P.dma("sp", ckv[:, 8 * i:8 * i + 8, :],
              CKV.v(None)[1024 * i:1024 * (i + 1), :].re("(kb p) r -> p kb r", p=128), part=True)
    P.dma("sp", ckv[0:16, 32, :], CKV.v(None)[4096:4112, :], part=True)
    qix = [P.sb(f"t_qix{i}", [128, 4, 128], BF16) for i in range(2)]
    wid = [P.sb(f"t_wid{i}", [128, 8], F32) for i in range(2)]
    qlt = [P.sb(f"t_qlt{i}", [128, 2, 8, 128], BF16) for i in range(3)]
    score = [P.sb(f"t_score{i}", [128, T], F32) for i in range(2)]
    junk = P.sb("t_junk", [128, T], BF16)
    msk = P.sb("t_msk", [128, T + 16], BF16)
    mskTs = [P.sb(f"t_mskT{i}", [128, 40, 128], BF16) for i in range(2)]
    rl = [P.sb(f"t_rl{i}", [128, 512], F32) for i in range(2)]
    ex = [P.sb(f"t_ex{i}", [128, 4, 128], BF16) for i in range(3)]
    pp = [P.sb(f"t_pp{i}", [128, 4, 128], BF16) for i in range(3)]
    bss = [P.sb(f"t_bs{i}", [128, 8], F32) for i in range(2)]
    rden = P.sb("t_rden", [128, 8], F32)
    fr = P.sb("t_fr", [128, NBIS], F32)
    for r in range(1, NBIS + 1):
        P.op("dve", lambda e, o=fr[:, r - 1:r], v=float(2.0 ** -r): e.memset(o.ap, v), wp=[fr[:]])
    stepf = P.sb("t_stepf", [128, NBIS], F32)
    on = P.sb("t_on", [128, 8, 256], BF16)
    olT = [P.sb(f"t_olT{i}", [128, 2, 8, 128], BF16) for i in range(2)]
    zb = P.sb("t_zb", [128, 512], BF16)
    P.memset("pool", zb[:], 0.0)
    po = [P.ps(f"t_po{i}", [128, 2, 256], F32) for i in range(4)]
    pden = P.ps("t_pden", [128, 512], F32)
    pl = [P.ps(f"t_pl{i}", [128, 4, 128], F32) for i in range(2)]
    px = P.ps("t_px", [128, 512], F32)
    pxs = [px, px]
    pxb = View(px.h[:].bitcast(BF16), px.buf)
    QIDXv = fm(QIDXT)
    QLATv = QLATT.v(None).re("(p a) t -> p a t", a=16)
    OLATv = OLATT.v(None).re("(p a) t -> p a t", a=16)
    WIDXv = WIDX.v(None)
    cnt = {"nl": 0}

    def geom(qi):
        if qi == 0:
            return 0, 16, 16
        return 16 + 128 * (qi - 1), 128, 16 + 128 * qi

    def stageA1(qi):
        q0, nq, nk = geom(qi)
        k2 = qi % 2
        qx, wd, ql, sc = qix[k2], wid[k2], qlt[qi % 3], score[k2]
        P.dma("sp", qx[:, :, :nq], QIDXv[:, :, q0:q0 + nq])
        P.dma("sp", wd[:nq, :], WIDXv[q0:q0 + nq, :])
        P.dma("sp", ql[:, :, :, :nq].re("p rc h t -> p (rc h) t"), QLATv[:, :, q0:q0 + nq])
        for c0 in range(0, nk, 512):
            cw = min(512, nk - c0)
            for h in range(8):
                oc, half = h // 2, h % 2
                lo_, hi_ = half * 64, half * 64 + 64
                px_ = pxs[h % 2]
                P.mm(px_[:nq, :cw], qx[lo_:hi_, oc, :nq], kidx2[lo_:hi_, c0:c0 + cw])
                r_ = rl[h % 2]
                P.act(r_[:nq, :cw], px_[:nq, :cw], AF.Relu)
                if h == 0:
                    P.ts("dve", sc[:nq, c0:c0 + cw], r_[:nq, :cw], wd[:nq, 0:1], ALU.mult, part=(c0 > 0))
                else:
                    P.stt(sc[:nq, c0:c0 + cw], r_[:nq, :cw], wd[:nq, h:h + 1], sc[:nq, c0:c0 + cw],
                          ALU.mult, ALU.add, part=True)
                yield 1.0

    def stageA2(qi):
        q0, nq, nk = geom(qi)
        k2 = qi % 2
        sc, bs, mskT = score[k2], bss[k2], mskTs[k2]
        P.op("dve", lambda e, o=bs[:nq, 0:1], i=sc[:nq, :nk]: e.tensor_reduce(o.ap, i.ap, AX.X, ALU.max),
             r=[sc[:]], w=[bs[:]])
        P.op("dve", lambda e, o=bs[:nq, 1:2], i=sc[:nq, :nk]: e.tensor_reduce(o.ap, i.ap, AX.X, ALU.min),
             r=[sc[:]], wp=[bs[:]])
        if qi >= 1:
            P.op("dve", lambda e, o=sc[0:64, nk - 64:nk]: e.memset(o.ap, NEG), r=[bs[:]], wp=[sc[:]])
        P.tt("dve", bs[:nq, 2:3], bs[:nq, 0:1], bs[:nq, 1:2], ALU.subtract, part=True)
        yield 7.0
        if nk > KSEL:
            P.ts("dve", stepf[:nq, :], fr[:nq, :], bs[:nq, 2:3], ALU.mult)
            P.tt("dve", bs[:nq, 3:4], bs[:nq, 1:2], stepf[:nq, 0:1], ALU.add, part=True)
            for r in range(1, NBIS + 1):
                P.ts("dve", junk[:nq, :nk], sc[:nq, :nk], bs[:nq, 3:4], ALU.is_ge, 0.0, ALU.add,
                     accum=bs[:nq, 4:5])
                last = (r == NBIS)
                P.ts("dve", bs[:nq, 5:6], bs[:nq, 4:5], KSEL - 0.5, ALU.is_ge, -1.0 if last else -0.5, ALU.add,
                     part=True)
                P.stt(bs[:nq, 1:2] if last else bs[:nq, 3:4], bs[:nq, 5:6], stepf[:nq, r - 1:r], bs[:nq, 3:4],
                      ALU.mult, ALU.add, part=True)
                yield nk * 1.6 / 960.0 + 0.8
        P.ts("dve", msk[:nq, :nk], sc[:nq, :nk], bs[:nq, 1:2], ALU.is_ge)
        nkb = (nk + 127) // 128
        for b0 in range(0, nkb, 8):
            nb_ = min(8, nkb - b0)
            for j in range(nb_):
                kb = b0 + j
                kl = min(128, nk - kb * 128)
                P.op("pe", lambda e, o=pxb[:kl, j * 128:j * 128 + nq], i=msk[:nq, kb * 128:kb * 128 + kl],
                     idn=C.identb[:nq, :nq]: e.transpose(o.ap, i.ap, idn.ap), r=[msk[:], C.identb],
                     w=[px[:]] if j == 0 else (), wp=[px[:]] if j else ())
            P.copy("act", mskT[:, b0:b0 + nb_, :].re("p a b -> p (a b)"), pxb[:, 0:nb_ * 128], part=(b0 > 0))
            yield 3.0

    def stageB(qi):
        q0, nq, nk = geom(qi)
        k2 = qi % 2
        ql, mskT = qlt[qi % 3], mskTs[k2]
        nkb = (nk + 127) // 128
        for i in range(4):
            P.mm(po[i][:, :, :].re("p a b -> p (a b)"), zb[:, 0:128], zb[:, :], True, False)
        P.mm(pden[:, :], zb[:, 0:128], zb[:, :], True, False)
        steps = [(kb, g) for kb in range(nkb) for g in range(2)]
        base = cnt["nl"]
        cnt["nl"] += len(steps)

        def front(n):
            kb, g = steps[n]
            kl = min(128, nk - kb * 128)
            nl = base + n
            p_l, e_, p_ = pl[nl % 2], ex[nl % 3], pp[nl % 3]
            for rc in range(2):
                P.mm(p_l[:kl, :, :nq], ckvT[:, rc, kb * 128:kb * 128 + kl], ql[:, rc, 4 * g:4 * g + 4, :nq],
                     rc == 0, rc == 1)
            P.act(e_[:kl, :, :nq], p_l[:kl, :, :nq], AF.Exp)
            P.tt("pool", p_[:kl, :, :nq], e_[:kl, :, :nq], bc3(mskT[:kl, kb, :nq], 4), ALU.mult)

        def back(n):
            kb, g = steps[n]
            kl = min(128, nk - kb * 128)
            p_ = pp[(base + n) % 3]
            for h4 in range(4):
                h = 4 * g + h4
                P.mm(po[h // 2][:nq, h % 2, :], p_[:kl, h4, :nq], ckv[:kl, kb, :], False, kb == nkb - 1)
                P.mm(pden[:nq, h:h + 1], p_[:kl, h4, :nq], C.onesb[:kl, 0:1], False, kb == nkb - 1)

        front(0)
        for n in range(len(steps)):
            if n + 1 < len(steps):
                front(n + 1)
            back(n)
            yield 3.0
        P.op("dve", lambda e, o=rden[:nq, :], i=pden[:nq, 0:8]: e.reciprocal(o.ap, i.ap), r=[pden[:]], w=[rden[:]])
        for h in range(8):
            if (h // 2) % 2:
                P.act(on[:nq, h, :], po[h // 2][:nq, h % 2, :], AF.Identity, scale=rden[:nq, h:h + 1], part=(h > 0))
            else:
                P.ts("dve", on[:nq, h, :], po[h // 2][:nq, h % 2, :], rden[:nq, h:h + 1], ALU.mult, part=(h > 0))
        yield 4.0
        ot = olT[k2]
        for rc in range(2):
            for h in range(8):
                P.op("pe", lambda e, o=pxb[:, h * 128:h * 128 + nq], i=on[:nq, h, rc * 128:(rc + 1) * 128],
                     idn=C.identb[:nq, :nq]: e.transpose(o.ap, i.ap, idn.ap), r=[on[:], C.identb],
                     w=[px[:]] if h == 0 else (), wp=[px[:]] if h else ())
            src = pxb[:, :].re("p (h q) -> p h q", h=8)[:, :, :nq]
            if rc:
                P.copy("act", ot[:, rc, :, :nq], src, part=True)
            else:
                P.copy("dve", ot[:, rc, :, :nq], src, part=False)
            yield 3.0
        P.dma("pool", OLATv[:, :, q0:q0 + nq], ot[:, :, :, :nq].re("p rc h t -> p (rc h) t"), part=True)

    def total(qi, which):
        q0, nq, nk = geom(qi)
        nkb = (nk + 127) // 128
        if which == "A1":
            return ((nk + 511) // 512) * 8 * 1.0
        if which == "A2":
            return 7.0 + (NBIS * (nk * 1.6 / 960.0 + 0.8) if nk > KSEL else 0.0) + ((nkb + 7) // 8) * 3.0
        return nkb * 2 * 3.0 + 4.0 + 6.0

    for it in range(35):
        live = []
        if it < 33:
            live.append([stageA1(it), 0.0, total(it, "A1")])
        if 1 <= it < 34:
            live.append([stageA2(it - 1), 0.0, total(it - 1, "A2")])
        if it >= 2:
            live.append([stageB(it - 2), 0.0, total(it - 2, "B")])
        while live:
            g_ = min(live, key=lambda t: t[1] / t[2])
            try:
                g_[1] += next(g_[0])
            except StopIteration:
                live.remove(g_)
    P.flush()
    P.release(m)
```

```python
import numpy as np
import concourse.bass as bass
import concourse.mybir as mybir
from concourse.bass_utils import run_bass_kernel_spmd

F32 = mybir.dt.float32
BF16 = mybir.dt.bfloat16
AF = mybir.ActivationFunctionType
ALU = mybir.AluOpType
AX = mybir.AxisListType

SEM_LIMIT = 30000


class Buf:
    __slots__ = ("name", "w", "r", "wf")

    def __init__(self, name):
        self.name = name
        self.w = {}
        self.r = {}
        self.wf = {}


class View:
    __slots__ = ("ap", "buf")

    def __init__(self, ap, buf):
        self.ap = ap
        self.buf = buf

    def __getitem__(self, idx):
        return View(self.ap[idx], self.buf)

    def re(self, pattern, **kw):
        return View(self.ap.rearrange(pattern, **kw), self.buf)


class Tile:
    def __init__(self, handle, buf):
        self.h = handle
        self.buf = buf

    def __getitem__(self, idx):
        return View(self.h[idx], self.buf)


class DramT:
    def __init__(self, handle, name):
        self.h = handle
        self.name = name
        self.bufs = {}

    def __getitem__(self, idx):
        return self.v(None)[idx]

    def v(self, key):
        if key not in self.bufs:
            self.bufs[key] = Buf(f"{self.name}:{key}")
        return View(self.h.ap(), self.bufs[key])


class _Op:
    __slots__ = ("fn", "deps", "sig", "val", "dma", "sem")

    def __init__(self, fn, deps, dma):
        self.fn = fn
        self.deps = deps
        self.sig = dma is not None
        self.val = None
        self.dma = dma
        self.sem = None


ENGS = ("pe", "act", "dve", "pool", "sp")
NSLOT = 8


class Prog:
    def __init__(self, nc):
        self.nc = nc
        self.ops = {e: [] for e in ENGS}
        self.ctx = []
        self.dma_cnt = {}
        self.dma_rr = {q: 0 for q in ENGS}
        self.n_sb = 0

    def dram(self, name, shape, dt, kind="Internal"):
        h = self.nc.dram_tensor(name, list(shape), dt, kind=kind)
        return DramT(h, name)

    def sb(self, name, shape, dt):
        self.n_sb += 1
        name = f"{name}_{self.n_sb}"
        g = self.nc.sbuf_tensor(name, list(shape), dt)
        h = g.__enter__()
        self.ctx.append(g)
        return Tile(h, Buf(name))

    def ps(self, name, shape, dt=F32):
        self.n_sb += 1
        name = f"{name}_{self.n_sb}"
        g = self.nc.psum_tensor(name, list(shape), dt)
        h = g.__enter__()
        self.ctx.append(g)
        return Tile(h, Buf(name))

    def op(self, eng, fn, r=(), w=(), wp=(), dmaq=False):
        deps = {}

        def add(d):
            for k, v in d.items():
                if deps.get(k, -1) < v:
                    deps[k] = v

        for v in r:
            add(v.buf.w)
        for v in w:
            add(v.buf.w)
            add(v.buf.r)
        for v in wp:
            add(v.buf.r)
            add(v.buf.wf)
        idx = len(self.ops[eng])
        dma = None
        if dmaq:
            slot = self.dma_rr[eng]
            self.dma_rr[eng] = (slot + 1) % NSLOT
            key = ("d", eng, slot)
            n = self.dma_cnt.get(key, 0)
            if n > 0:
                if deps.get(key, -1) < n:
                    deps[key] = n
            self.dma_cnt[key] = n + 1
            dma = (key, n + 1)
            tok = (key, n + 1)
        else:
            tok = (("c", eng), idx)
        if eng == "pe":
            deps.pop(("c", "pe"), None)
        self.ops[eng].append(_Op(fn, deps, dma))
        for v in r:
            if v.buf.r.get(tok[0], -1) < tok[1]:
                v.buf.r[tok[0]] = tok[1]
        for v in w:
            v.buf.w = {tok[0]: tok[1]}
            v.buf.wf = {tok[0]: tok[1]}
            v.buf.r = {}
        for v in wp:
            if v.buf.w.get(tok[0], -1) < tok[1]:
                v.buf.w[tok[0]] = tok[1]

    def _e(self, eng):
        return eng

    def mm(self, out, lhsT, rhs, start=True, stop=True, **kw):
        self.op("pe", lambda e: e.matmul(out.ap, lhsT.ap, rhs.ap, start=start, stop=stop, **kw),
                r=[lhsT, rhs], wp=[out] if not start else (), w=[out] if start else ())

    def transpose(self, out, in_, ident):
        self.op("pe", lambda e: e.transpose(out.ap, in_.ap, ident.ap), r=[in_, ident], w=[out])

    def act(self, out, in_, func, scale=1.0, bias=0.0, accum=None, part=False):
        r = [in_]
        sc = scale
        bi = bias
        if isinstance(scale, View):
            r.append(scale)
            sc = scale.ap
        if isinstance(bias, View):
            r.append(bias)
            bi = bias.ap
        kw = {}
        ws = [out]
        if accum is not None:
            kw["accum_out"] = accum.ap
            ws.append(accum)
        self.op("act", lambda e: e.activation(out.ap, in_.ap, func, scale=sc, bias=bi, **kw),
                r=r, w=() if part else ws, wp=ws if part else ())

    def tt(self, eng, out, a, b, op, part=False):
        self.op(eng, lambda e: e.tensor_tensor(out.ap, a.ap, b.ap, op), r=[a, b],
                w=() if part else [out], wp=[out] if part else ())

    def ts(self, eng, out, a, s1, op0, s2=None, op1=None, accum=None, part=False):
        r = [a]
        x1 = s1
        x2 = s2
        if isinstance(s1, View):
            r.append(s1)
            x1 = s1.ap
        if isinstance(s2, View):
            r.append(s2)
            x2 = s2.ap
        kw = {}
        ws = [out]
        if op1 is not None:
            kw["op1"] = op1
        if accum is not None:
            kw["accum_out"] = accum.ap
            ws.append(accum)
        self.op(eng, lambda e: e.tensor_scalar(out.ap, a.ap, x1, x2, op0, **kw), r=r,
                w=() if part else ws, wp=ws if part else ())

    def stt(self, out, in0, scalar, in1, op0, op1, part=False, eng="dve"):
        r = [in0, in1]
        sc = scalar
        if isinstance(scalar, View):
            r.append(scalar)
            sc = scalar.ap
        self.op(eng, lambda e: e.scalar_tensor_tensor(out.ap, in0.ap, sc, in1.ap, op0, op1), r=r,
                w=() if part else [out], wp=[out] if part else ())

    def copy(self, eng, out, in_, part=False):
        if eng == "act":
            self.op(eng, lambda e: e.copy(out.ap, in_.ap), r=[in_],
                    w=() if part else [out], wp=[out] if part else ())
        else:
            self.op(eng, lambda e: e.tensor_copy(out.ap, in_.ap), r=[in_],
                    w=() if part else [out], wp=[out] if part else ())

    def memset(self, eng, out, val):
        self.op(eng, lambda e: e.memset(out.ap, val), w=[out])

    def scan(self, out, d0, d1, init, op0, op1, part=False):
        r = [d0, d1]
        ini = init
        if isinstance(init, View):
            r.append(init)
            ini = init.ap
        self.op("dve", lambda e: e.tensor_tensor_scan(out.ap, d0.ap, d1.ap, ini, op0, op1), r=r,
                w=() if part else [out], wp=[out] if part else ())

    def dma(self, q, out, in_, part=False, **kw):
        self.op(q, lambda e: e.dma_start(out=out.ap, in_=in_.ap, **kw), r=[in_],
                w=() if part else [out], wp=[out] if part else (), dmaq=True)

    def _sem(self, name):
        if name not in self.sems:
            g = self.nc.semaphore(name)
            self.sems[name] = g.__enter__()
            self.sem_ctx.append(g)
        return self.sems[name]

    def _dma_tok(self, key, n):
        per = SEM_LIMIT // 16
        j = (n - 1) // per
        return f"d_{key[1]}_{key[2]}_{j}", 16 * ((n - 1) % per + 1)

    def _resolve(self, k, v):
        if k[0] == "c":
            lst = self.ops[k[1]]
            while lst[v].val is None:
                v += 1
            o = lst[v]
            return o.sem, o.val
        return self._dma_tok(k, v)

    def mark(self):
        return len(self.ctx)

    def flush(self, final=False):
        nc = self.nc
        ops = self.ops
        if not hasattr(self, "sems"):
            self.sems = {}
            self.sem_ctx = []
            self.emitted = {e: 0 for e in ENGS}
            self.sigcnt = {e: 0 for e in ENGS}
            self.seen = {e: {} for e in ENGS}
        for e in ENGS:
            for o in ops[e][self.emitted[e]:]:
                for k, v in o.deps.items():
                    if k[0] == "c" and v >= self.emitted[k[1]]:
                        ops[k[1]][v].sig = True
        fdeps = {}
        for e in ENGS:
            for j in range(len(ops[e]) - 1, self.emitted[e] - 1, -1):
                if ops[e][j].dma is None and ops[e][j].fn is not None:
                    ops[e][j].sig = True
                    fdeps[("c", e)] = j
                    break
        if final:
            for key, n in self.dma_cnt.items():
                fdeps[key] = n
            fdeps.pop(("c", "sp"), None)
            ops["sp"].append(_Op(None, fdeps, None))
        for e in ENGS:
            for o in ops[e][self.emitted[e]:]:
                if o.dma is None and o.sig and o.fn is not None:
                    c = self.sigcnt[e]
                    o.sem = f"c_{e}_{c // SEM_LIMIT}"
                    o.val = c % SEM_LIMIT + 1
                    self._sem(o.sem)
                    self.sigcnt[e] = c + 1
                elif o.dma is not None:
                    self._sem(self._dma_tok(*o.dma)[0])
        block = nc.Block()
        block.__enter__()
        names = {"pe": "tensor", "act": "scalar", "dve": "vector", "pool": "gpsimd", "sp": "sync"}
        for e in ENGS:
            def body(eng, e=e):
                seen = self.seen[e]
                for o in ops[e][self.emitted[e]:]:
                    for k, v in o.deps.items():
                        s, val = self._resolve(k, v)
                        if seen.get(s, 0) >= val:
                            continue
                        seen[s] = val
                        eng.wait_ge(self.sems[s], val)
                    if o.fn is None:
                        continue
                    ins = o.fn(eng)
                    if o.dma is not None:
                        s, val = self._dma_tok(*o.dma)
                        ins.then_inc(self.sems[s], 16)
                    elif o.sig:
                        ins.then_inc(self.sems[o.sem], 1)
                    o.fn = True
            getattr(block, names[e])(body)
        block.__exit__(None, None, None)
        for e in ENGS:
            self.emitted[e] = len(ops[e])

    def release(self, mark):
        while len(self.ctx) > mark:
            self.ctx.pop().__exit__(None, None, None)

    def finish(self):
        self.flush(final=True)
        self.release(0)
        for g in reversed(self.sem_ctx):
            g.__exit__(None, None, None)


D = 1024
SEQ = 4096
NMETA = 16
T = SEQ + NMETA
DFF = 2816
NFC = DFF // 128
ALPHA = 4.0 ** 0.25
EPS = 1e-5
TILES = [(i * 512, 512) for i in range(8)] + [(4096, 16)]
KSEL = 256
NBIS = 14


def bc(view, shape):
    return View(view.ap.broadcast_to(list(shape)), view.buf)


def bc3(view, n):
    a = view.ap.unsqueeze(1)
    shp = list(a.shape)
    shp[1] = n
    return View(a.broadcast_to(shp), view.buf)


class VecPack:
    def __init__(self):
        self.cols = []
        self.off = {}
        self.n = 0

    def add(self, name, vec):
        vec = np.asarray(vec, np.float32).reshape(-1)
        nchunk = (vec.size + 127) // 128
        pad = np.zeros(nchunk * 128, np.float32)
        pad[:vec.size] = vec
        self.cols.append(pad.reshape(nchunk, 128).T)
        self.off[name] = self.n
        self.n += nchunk

    def array(self):
        return np.ascontiguousarray(np.concatenate(self.cols, axis=1))


def vec_layout():
    vp = VecPack()
    pack_vecs(vp, None)
    return vp.off, vp.n


def pack_vecs(vp, inp):
    def g(name, shape):
        if inp is None:
            return np.zeros(shape, np.float32)
        return np.asarray(inp[name], np.float32)

    cw = g("a_conv_w", (1, 4, D))[0]
    for j in range(4):
        vp.add(f"a_cw{j}", cw[j])
    vp.add("a_cb", g("a_conv_b", (1, D))[0])
    vp.add("a_ng", g("a_norm_g", (1, D))[0])
    cw = g("b_conv_w", (1, 4, D))[0]
    for j in range(4):
        vp.add(f"b_cw{j}", cw[j])
    vp.add("b_cb", g("b_conv_b", (1, D))[0])
    vp.add("b_ba", g("b_b_a", (1, D))[0])
    vp.add("b_bx", g("b_b_x", (1, D))[0])
    vp.add("b_lam", g("b_lambda", (1, D))[0])
    mg = g("mix_ln_g", (2, D))
    mb = g("mix_ln_b", (2, D))
    fg = g("ffn_ln_g", (2, D))
    fb = g("ffn_ln_b", (2, D))
    fcw = g("ffn_conv_w", (2, 3, DFF))
    fcb = g("ffn_conv_b", (2, DFF))
    for l in range(2):
        vp.add(f"mix_g{l}", mg[l])
        vp.add(f"mix_b{l}", mb[l])
        vp.add(f"ffn_g{l}", fg[l])
        vp.add(f"ffn_b{l}", fb[l])
        for j in range(3):
            vp.add(f"f_cw{l}_{j}", fcw[l, j])
        vp.add(f"f_cb{l}", fcb[l])
    vp.add("c_qg", g("c_q_norm_g", (1, 384))[0])
    vp.add("c_kvg", g("c_kv_norm_g", (1, 256))[0])
    kg = g("c_kidx_ln_g", (1, 64))[0]
    kb = g("c_kidx_ln_b", (1, 64))[0]
    vp.add("c_kig", np.concatenate([kg, kg]))
    vp.add("c_kib", np.concatenate([kb, kb]))


def block_diag_host(w):
    w = np.asarray(w, np.float32)
    out = np.zeros((8, 128, 128), np.float32)
    for n in range(256):
        c, b = divmod(n, 32)
        out[c, 4 * b:4 * b + 4, 4 * b:4 * b + 4] = w[n]
    return out


def make_consts():
    c = np.zeros((128, 1536), np.float32)
    c[:, 0:128] = np.eye(128, dtype=np.float32)
    c[:, 128:256] = np.triu(np.ones((128, 128), np.float32))
    c[:, 256:384] = 1.0
    for h in range(4):
        c[h, 384 + 128 * h: 384 + 128 * (h + 1)] = 1.0
        c[4 + h, 896 + 128 * h: 896 + 128 * (h + 1)] = 1.0
        c[h, 1408 + h] = 1.0
        c[4 + h, 1412 + h] = 1.0
    return c


class Ctx:
    pass


def fm(dram, key=None):
    return dram.v(key).re("(c p) t -> p c t", p=128)


def load_w(P, dst, src_view, KC):
    for kc in range(KC):
        P.dma("pool", dst[:, kc, :], src_view[:, kc, :], part=True, max_dma_last_dim=4096)


def setup_consts(P, C, din):
    C.cf = P.sb("c_f32", [128, 384], F32)
    P.dma("sp", C.cf[:], din["consts"][:, 0:384])
    C.ident = C.cf[:, 0:128]
    C.triu = C.cf[:, 128:256]
    C.ones = C.cf[:, 256:384]
    C.cb = P.sb("c_bf16", [128, 384], BF16)
    P.copy("dve", C.cb[:], C.cf[:, 0:384])
    C.identb = C.cb[:, 0:128]
    C.triub = C.cb[:, 128:256]
    C.onesb = C.cb[:, 256:384]
    C.voff, nv = vec_layout()
    C.vec = P.sb("c_vec", [128, nv], F32)
    P.dma("sp", C.vec[:], din["vecs"][:])
    C.mean1024 = P.sb("c_mean", [128, 128], F32)
    P.memset("dve", C.mean1024[:], 1.0 / 1024.0)
    C.cc = P.sb("c_cols", [128, 4], F32)
    P.memset("dve", C.cc[:, 0:1], EPS)
    P.op("dve", lambda e: e.memset(C.cc[:, 1:2].ap, 1.0), wp=[C.cc[:, 1:2]])
    P.op("dve", lambda e: e.memset(C.cc[:, 2:3].ap, -float(np.log(16.0))), wp=[C.cc[:, 2:3]])
    P.op("dve", lambda e: e.memset(C.cc[:, 3:4].ap, 0.0), wp=[C.cc[:, 3:4]])
    C.epsc = C.cc[:, 0:1]
    C.onec = C.cc[:, 1:2]
    C.nl16 = C.cc[:, 2:3]


def vcol(C, name, c=0, n=1):
    o = C.voff[name] + c
    return C.vec[:, o:o + n]


def ln_gen(P, C, y, w, gname, bname, out_f32, tmp, psm, psq, sq, out_bf=None):
    for c in range(8):
        s = sq[c % 2]
        P.tt("pool", s[:, :w], y[:, c, :w], y[:, c, :w], ALU.mult)
        P.mm(psm[:, :w], C.mean1024[:], y[:, c, :w], c == 0, c == 7)
        P.mm(psq[:, :w], C.mean1024[:], s[:, :w], c == 0, c == 7)
        yield
    mean = tmp["mean"]
    m2 = tmp["m2"]
    rstd = tmp["rstd"]
    P.copy("act", mean[:, :w], psm[:, :w])
    P.tt("pool", m2[:, :w], mean[:, :w], mean[:, :w], ALU.mult)
    P.tt("dve", m2[:, :w], psq[:, :w], m2[:, :w], ALU.subtract)
    P.act(rstd[:, :w], m2[:, :w], AF.Sqrt, bias=C.epsc[:, 0:1])
    P.op("dve", lambda e, o=rstd[:, :w]: e.reciprocal(o.ap, o.ap), r=[rstd[:, :w]], w=[rstd[:, :w]])
    yield
    for c in range(8):
        t = sq[c % 2]
        P.tt("dve", t[:, :w], y[:, c, :w], mean[:, :w], ALU.subtract)
        P.stt(t[:, :w], t[:, :w], vcol(C, gname, c), rstd[:, :w], ALU.mult, ALU.mult)
        P.act(out_f32[:, c, :w], t[:, :w], AF.Identity, bias=vcol(C, bname, c), part=True)
        if out_bf is not None:
            P.copy("pool", out_bf[:, c, :w], out_f32[:, c, :w], part=True)
        yield


def layer_norm_fm(P, C, y, w, gname, bname, out_f32, tmp, psm, psq, sq, out_bf=None):
    for _ in ln_gen(P, C, y, w, gname, bname, out_f32, tmp, psm, psq, sq, out_bf):
        pass


def phase_embed(P, C, din, H0):
    m = P.mark()
    x = din["x"]
    xin = [P.sb(f"e_xin{i}", [128, 4, 1024], F32) for i in range(2)]
    hT = [P.sb(f"e_hT{i}", [128, 8, 512], F32) for i in range(2)]
    pt = [P.ps(f"e_pt{i}", [128, 512], F32) for i in range(4)]
    H0v = fm(H0)
    xv = x.v(None).re("(k j p) f -> k p j f", j=4, p=128)
    n = 0
    for k in range(9):
        xi = xin[k % 2]
        h = hT[k % 2]
        if k == 8:
            P.dma("sp", xi[0:16, 0, :], din["meta"][:])
            nj, rows, pos, w = 1, 16, 0, 16
        else:
            P.dma("sp", xi[:], xv[k])
            nj, rows, pos, w = 4, 128, 16 + 512 * k, 512
        for c in range(8):
            p_ = pt[n % 4]
            n += 1
            for j in range(nj):
                P.op("pe", lambda e, o=p_[:, j * rows:(j + 1) * rows], i=xi[0:rows, j, c * 128:(c + 1) * 128],
                     idn=C.ident[0:rows, 0:rows]: e.transpose(o.ap, i.ap, idn.ap),
                     r=[xi[:], C.ident], w=[p_[:]] if j == 0 else (), wp=[p_[:]] if j else ())
            if c % 2 == 0:
                P.copy("act", h[:, c, :w], p_[:, :w], part=True)
            else:
                P.copy("dve", h[:, c, :w], p_[:, :w], part=True)
        P.dma("pool", H0v[:, :, pos:pos + w], h[:, :, :w], part=True)
    P.flush()
    P.release(m)


def phase_inproj(P, C, din, H0, XM, SOG, XR, GGR):
    m = P.mark()
    w_in = P.sb("w_in", [128, 8, 4096], BF16)
    load_w(P, w_in, din["a_w_in"].v(None).re("(kc p) n -> p kc n", p=128), 8)
    hf = [P.sb(f"i_hf{i}", [128, 8, 512], F32) for i in range(2)]
    hb = [P.sb(f"i_hb{i}", [128, 8, 512], BF16) for i in range(2)]
    stg = [P.sb(f"i_st{i}", [128, 8, 512], F32) for i in range(2)]
    pss = [P.ps(f"i_ps{i}", [128, 512], F32) for i in range(4)]
    outs = [fm(XM), fm(SOG), fm(XR), fm(GGR)]
    H0v = fm(H0)
    n = 0
    for ti, (pos, w) in enumerate(TILES):
        f = hf[ti % 2]
        b = hb[ti % 2]
        P.dma("sp", f[:, :, :w], H0v[:, :, pos:pos + w])
        P.copy("pool", b[:, :, :w], f[:, :, :w])
        for grp in range(4):
            st = stg[(ti * 4 + grp) % 2]
            for o8 in range(8):
                oc = grp * 8 + o8
                ps = pss[n % 4]
                n += 1
                for kc in range(8):
                    P.mm(ps[:, :w], w_in[:, kc, oc * 128:(oc + 1) * 128], b[:, kc, :w], kc == 0, kc == 7)
                if grp == 1:
                    P.act(st[:, o8, :w], ps[:, :w], AF.Sigmoid, part=True)
                elif grp == 3:
                    P.act(st[:, o8, :w], ps[:, :w], AF.Gelu_apprx_tanh, part=True)
                elif o8 % 2 == 0:
                    P.copy("act", st[:, o8, :w], ps[:, :w], part=True)
                else:
                    P.copy("dve", st[:, o8, :w], ps[:, :w], part=True)
            P.dma("sp", outs[grp][:, :, pos:pos + w], st[:, :, :w], part=True)
    P.flush()
    P.release(m)


def phase_rglru(P, C, din, XR, GGR, HR):
    m = P.mark()
    wa = P.sb("r_wa", [128, 8, 128], BF16)
    wx = P.sb("r_wx", [128, 8, 128], BF16)
    P.dma("pool", wa[:], din["b_w_a"][:])
    P.dma("pool", wx[:], din["b_w_x"][:])
    c8 = P.sb("r_c8", [128, 8], F32)
    P.act(c8[:], vcol(C, "b_lam", 0, 8), AF.Exp, scale=-1.0)
    P.act(c8[:], c8[:], AF.Ln, bias=C.onec[:, 0:1])
    P.ts("dve", c8[:], c8[:], -8.0, ALU.mult)
    carry = P.sb("r_carry", [128, 8], F32)
    P.memset("dve", carry[:], 0.0)
    xe = [P.sb(f"r_xe{i}", [128, 8, 515], F32) for i in range(2)]
    gg = [P.sb(f"r_gg{i}", [128, 8, 512], F32) for i in range(2)]
    ho = [P.sb(f"r_ho{i}", [128, 8, 512], BF16) for i in range(2)]
    xc = P.sb("r_xc", [128, 4, 512], F32)
    xcb = P.sb("r_xcb", [128, 4, 512], BF16)
    rr = P.sb("r_r", [128, 4, 512], F32)
    ii = P.sb("r_i", [128, 4, 512], F32)
    aa = P.sb("r_a", [128, 4, 512], F32)
    a2 = P.sb("r_a2", [128, 4, 512], F32)
    hs = P.sb("r_hs", [128, 4, 512], F32)
    pa = [P.ps(f"r_pa{i}", [128, 512], F32) for i in range(4)]
    px = [P.ps(f"r_px{i}", [128, 512], F32) for i in range(4)]
    XRv, GGv, HRv = fm(XR), fm(GGR), fm(HR)
    for ti, (pos, w) in enumerate(TILES):
        x = xe[ti % 2]
        g = gg[ti % 2]
        h = ho[ti % 2]
        if ti == 0:
            P.memset("pool", x[:, :, 0:3], 0.0)
            P.dma("sp", x[:, :, 3:3 + w], XRv[:, :, pos:pos + w], part=True)
        else:
            P.dma("sp", x[:, :, 0:3 + w], XRv[:, :, pos - 3:pos + w])
        P.dma("sp", g[:, :, :w], GGv[:, :, pos:pos + w])
        for hf_ in range(2):
            cs = [hf_ * 4 + k for k in range(4)]
            for k, c in enumerate(cs):
                P.ts("dve", xc[:, k, :w], x[:, c, 3:3 + w], vcol(C, "b_cw3", c), ALU.mult, vcol(C, "b_cb", c), ALU.add,
                     part=(k > 0))
                for j in range(3):
                    P.stt(xc[:, k, :w], x[:, c, j:j + w], vcol(C, f"b_cw{j}", c), xc[:, k, :w], ALU.mult, ALU.add,
                          part=True)
            for k, c in enumerate(cs):
                P.copy("pool", xcb[:, k, :w], xc[:, k, :w], part=(k > 0))
            for k, c in enumerate(cs):
                P.mm(pa[k][:, :w], wa[:, c, :], xcb[:, k, :w])
                P.mm(px[k][:, :w], wx[:, c, :], xcb[:, k, :w])
            for k, c in enumerate(cs):
                P.act(rr[:, k, :w], pa[k][:, :w], AF.Sigmoid, bias=vcol(C, "b_ba", c), part=(k > 0))
            for k, c in enumerate(cs):
                P.act(ii[:, k, :w], px[k][:, :w], AF.Sigmoid, bias=vcol(C, "b_bx", c), part=(k > 0))
            for k, c in enumerate(cs):
                P.act(aa[:, k, :w], rr[:, k, :w], AF.Exp, scale=c8[:, c:c + 1], part=(k > 0))
            for k, c in enumerate(cs):
                P.tt("pool", a2[:, k, :w], aa[:, k, :w], aa[:, k, :w], ALU.mult, part=(k > 0))
            for k, c in enumerate(cs):
                P.tt("pool", ii[:, k, :w], ii[:, k, :w], xc[:, k, :w], ALU.mult, part=True)
            for k, c in enumerate(cs):
                P.act(a2[:, k, :w], a2[:, k, :w], AF.Sqrt, scale=-1.0, bias=C.onec[:, 0:1], part=True)
            for k, c in enumerate(cs):
                P.tt("dve", ii[:, k, :w], ii[:, k, :w], a2[:, k, :w], ALU.mult, part=True)
                P.scan(hs[:, k, :w], aa[:, k, :w], ii[:, k, :w], carry[:, c:c + 1], ALU.mult, ALU.add, part=(k > 0))
                P.copy("dve", carry[:, c:c + 1], hs[:, k, w - 1:w], part=True)
            for k, c in enumerate(cs):
                P.tt("pool", h[:, c, :w], hs[:, k, :w], g[:, c, :w], ALU.mult, part=True)
        P.dma("sp", HRv[:, :, pos:pos + w], h[:, :, :w], part=True)
    P.flush()
    P.release(m)


def phase_mlstm(P, C, din, XM, SOG, HM):
    m = P.mark()
    XMv, SOGv, HMv = fm(XM), fm(SOG), fm(HM)
    bd = {}
    for nm in ("bdq", "bdk", "bdv"):
        bd[nm] = P.sb("m_" + nm, [128, 8, 128], BF16)
        P.dma("pool", bd[nm][:], din[nm][:])
    pw = [P.ps(f"m_pw{i}", [128, 512], F32) for i in range(2)]
    pbc = P.ps("m_pbc", [128, 512], F32)
    pg = pbc
    pS = P.ps("m_pS", [128, 4, 128], F32)
    pT = View(pS.h[:].rearrange("p a b -> p (a b)").bitcast(BF16).rearrange("p (c t) -> p c t", c=8), pS.buf)
    pns = [P.ps(f"m_pn{i}", [128, 512], F32) for i in range(2)]
    pds = [P.ps(f"m_pd{i}", [128, 512], F32) for i in range(2)]
    m2 = P.mark()
    bdT = P.sb("m_bdT", [128, 3, 8, 128], F32)
    P.dma("sp", bdT[:], din["bdT"][:])
    wif = P.sb("m_wif", [128, 24, 8], F32)
    P.dma("sp", wif[:], din["a_w_if"][:])
    wg = P.sb("m_wg", [128, 2, 8, 8], BF16)
    for c in range(8):
        P.mm(pg[:, c * 8:(c + 1) * 8], bdT[:, 0, c, :], wif[:, c, :], True, False)
        P.mm(pg[:, c * 8:(c + 1) * 8], bdT[:, 1, c, :], wif[:, 8 + c, :], False, True)
        P.mm(pg[:, 64 + c * 8:64 + (c + 1) * 8], bdT[:, 2, c, :], wif[:, 16 + c, :], True, True)
    P.copy("dve", wg[:].re("p a c g -> p (a c g)"), pg[:, 0:128])
    bif = P.sb("m_bif", [8, 1], F32)
    P.dma("sp", bif[:], din["a_b_if"][:])
    rmask = P.sb("m_rmask", [8, 512], F32)
    P.memset("dve", rmask[:], 1.0)
    for j in range(4):
        P.op("dve", lambda e, v=rmask[:, j * 128:j * 128 + 1]: e.memset(v.ap, 0.0), wp=[rmask[:]])
    selt = P.sb("m_sel", [8, 1152], F32)
    P.dma("sp", selt[:], din["consts"][0:8, 384:1536])
    selI = [selt[0:8, 128 * h:128 * (h + 1)] for h in range(4)]
    selF = [selt[0:8, 512 + 128 * h:512 + 128 * (h + 1)] for h in range(4)]
    selI4 = selt[0:8, 1024:1028]
    selF4 = selt[0:8, 1028:1032]
    C32 = P.sb("m_C32", [128, 4, 2, 257], F32)
    Cb = P.sb("m_Cb", [128, 4, 2, 257], BF16)
    P.memset("dve", C32[:], 0.0)
    P.memset("pool", Cb[:], 0.0)
    Vaug = P.sb("m_Vaug", [128, 4, 257], BF16)
    P.memset("pool", Vaug[:], 1.0)
    xe = P.sb("m_xe", [128, 8, 515], F32)
    sog_2 = [P.sb(f"m_sog{i}", [128, 8, 512], F32) for i in range(2)]
    xcb_2 = [P.sb(f"m_xcb{i}", [128, 8, 512], BF16) for i in range(2)]
    xmb_2 = [P.sb(f"m_xmb{i}", [128, 8, 512], BF16) for i in range(2)]
    qh_2 = [P.sb(f"m_qh{i}", [128, 8, 512], BF16) for i in range(2)]
    kh_2 = [P.sb(f"m_kh{i}", [128, 8, 512], BF16) for i in range(2)]
    abc_2 = [P.sb(f"m_abc{i}", [128, 4, 512], F32) for i in range(2)]
    bbc = P.sb("m_bbc", [128, 4, 512], F32)
    hm = [P.sb("m_hm0", [128, 8, 512], BF16)] * 2
    cv = [P.sb(f"m_cv{i}", [128, 512], F32) for i in range(2)]
    Gall_2 = [P.sb(f"m_Gall{i}", [8, 512], F32) for i in range(2)]
    Lg_2 = [P.sb(f"m_Lg{i}", [8, 512], F32) for i in range(2)]
    lcar = P.sb("m_lcar", [8, 1], F32)
    btok = P.sb("m_btok", [128, 4], F32)
    Kt = P.sb("m_Kt", [128, 4, 256], BF16)
    ST = P.sb("m_ST", [128, 4, 128], BF16)
    hn = P.sb("m_hn", [128, 4, 256], BF16)
    st6 = P.sb("m_st6", [128, 6], F32)
    mv = P.sb("m_mv", [128, 2], F32)
    sc = P.sb("m_sc", [128, 4], F32)
    ctmps = [P.sb(f"m_ctmp{i}", [128, 257], F32) for i in range(2)]
    n = 0
    def prologue(ti):
        pos, w = TILES[ti]
        sog, xcb, xmb, qh, kh, abc, Gall, Lg = (sog_2[ti % 2], xcb_2[ti % 2], xmb_2[ti % 2], qh_2[ti % 2], kh_2[ti % 2], abc_2[ti % 2], Gall_2[ti % 2], Lg_2[ti % 2])
        if ti == 0:
            P.memset("pool", xe[:, :, 0:3], 0.0)
            P.dma("sp", xe[:, :, 3:3 + w], XMv[:, :, pos:pos + w], part=True)
        else:
            P.dma("sp", xe[:, :, 0:3 + w], XMv[:, :, pos - 3:pos + w])
        P.dma("sp", sog[:, :, :w], SOGv[:, :, pos:pos + w])
        for c in range(8):
            k = c % 2
            P.ts("dve", cv[k][:, :w], xe[:, c, 3:3 + w], vcol(C, "a_cw3", c), ALU.mult)
            for j in range(3):
                P.stt(cv[k][:, :w], xe[:, c, j:j + w], vcol(C, f"a_cw{j}", c), cv[k][:, :w], ALU.mult, ALU.add)
            P.act(xcb[:, c, :w], cv[k][:, :w], AF.Silu, bias=vcol(C, "a_cb", c), part=(c > 0))
            P.copy("pool", xmb[:, c, :w], xe[:, c, 3:3 + w], part=(c > 0))
            P.act(sog[:, c, :w], sog[:, c, :w], AF.Identity, scale=vcol(C, "a_ng", c), part=True)
            yield
        for c in range(8):
            P.mm(pg[0:8, :w], wg[:, 0, c, :], xcb[:, c, :w], c == 0, False)
        for c in range(8):
            P.mm(pg[0:8, :w], wg[:, 1, c, :], xmb[:, c, :w], False, c == 7)
        P.act(Gall[:, :w], pg[0:8, :w], AF.Identity, bias=bif[:, 0:1])
        P.act(Lg[:, :w], Gall[:, :w], AF.Exp, scale=-1.0)
        P.act(Lg[:, :w], Lg[:, :w], AF.Ln, bias=C.onec[0:8, 0:1])
        P.scan(Lg[:, :w], rmask[:, :w], Lg[:, :w], 0.0, ALU.mult, ALU.add)
        for h in range(4):
            P.mm(pbc[:, :w], selF[h], Lg[:, :w], True, True)
            P.act(abc[:, h, :w], pbc[:, :w], AF.Exp, scale=-1.0, part=(h > 0))
            P.mm(pbc[:, :w], selI[h], Gall[:, :w], True, False)
            P.mm(pbc[:, :w], selF[h], Lg[:, :w], False, True)
            P.act(bbc[:, h, :w], pbc[:, :w], AF.Exp, bias=C.nl16, part=(h > 0))
            yield
        for c in range(8):
            P.mm(pw[0][:, :w], bd["bdq"][:, c, :], xcb[:, c, :w])
            P.tt("dve", qh[:, c, :w], pw[0][:, :w], abc[:, c // 2, :w], ALU.mult, part=(c > 0))
            P.mm(pw[1][:, :w], bd["bdk"][:, c, :], xcb[:, c, :w])
            P.tt("dve", kh[:, c, :w], pw[1][:, :w], bbc[:, c // 2, :w], ALU.mult, part=(c > 0))
            yield
        yield

    def chunks(ti):
        pos, w = TILES[ti]
        sog, xcb, xmb, qh, kh, abc, Gall, Lg = (sog_2[ti % 2], xcb_2[ti % 2], xmb_2[ti % 2], qh_2[ti % 2], kh_2[ti % 2], abc_2[ti % 2], Gall_2[ti % 2], Lg_2[ti % 2])
        hmt = hm[ti % 2]
        for j in range((w + 127) // 128):
            L = min(128, w - j * 128)
            cs = slice(j * 128, j * 128 + L)
            P.mm(pbc[:L, 0:4], Gall[:, cs], selI4, True, False)
            P.mm(pbc[:L, 0:4], Lg[:, cs], selF4, False, True)
            P.act(btok[:L, :], pbc[:L, 0:4], AF.Exp, bias=C.nl16[:L, :])
            for g2 in range(2):
                for c4 in range(4):
                    c = g2 * 4 + c4
                    P.mm(pw[g2][:L, c4 * 128:(c4 + 1) * 128], xmb[:, c, cs], bd["bdv"][:, c, :], True, True)
                P.copy("act", Vaug[:L, 2 * g2:2 * g2 + 2, 0:256], pw[g2][:L, :].re("p (h v) -> p h v", h=2), part=True)
            for g2 in range(2):
                for c4 in range(4):
                    c = g2 * 4 + c4
                    P.mm(pw[g2][:L, c4 * 128:(c4 + 1) * 128], xcb[:, c, cs], bd["bdk"][:, c, :], True, True)
                for h2 in range(2):
                    h = 2 * g2 + h2
                    P.ts("dve", Kt[:L, h, :], pw[g2][:L, h2 * 256:(h2 + 1) * 256], btok[:L, h:h + 1], ALU.mult, part=True)
            for h in range(4):
                for dc in range(2):
                    P.mm(pS[:L, h, :L], kh[:, 2 * h + dc, cs], qh[:, 2 * h + dc, cs], dc == 0, dc == 1)
            P.tt("dve", ST[:L, :, :L], pS[:L, :, :L], bc3(C.triu[:L, :L], 4), ALU.mult)
            for h in range(4):
                pn = pns[h % 2]
                P.mm(pn[:L, 0:257], ST[:L, h, :L], Vaug[:L, h, :], True, False)
                for dc in range(2):
                    P.mm(pn[:L, 0:257], qh[:, 2 * h + dc, cs], Cb[:, h, dc, :], False, dc == 1)
                A = abc[:, h, j * 128 + L - 1:j * 128 + L]
                for dc in range(2):
                    pd, ctmp = pds[dc], ctmps[dc]
                    P.mm(pd[:, 0:257], Kt[:L, h, dc * 128:(dc + 1) * 128], Vaug[:L, h, :], True, True)
                    P.tt("dve", ctmp[:], pd[:, 0:257], C32[:, h, dc, :], ALU.add)
                    P.act(Cb[:, h, dc, :], ctmp[:], AF.Identity, scale=A, part=True)
                    P.act(C32[:, h, dc, :], ctmp[:], AF.Identity, scale=A, part=True)
                P.copy("dve", sc[:L, 3:4], pn[:L, 256:257])
                P.tt("dve", sc[:L, 0:1], sc[:L, 3:4], sc[:L, 3:4], ALU.mult, part=True)
                P.ts("dve", sc[:L, 0:1], sc[:L, 0:1], 1.0, ALU.max, part=True)
                P.op("dve", lambda e, o=sc[:L, 0:1]: e.reciprocal(o.ap, o.ap), r=[sc[:]], wp=[sc[:]])
                P.op("dve", lambda e, o=st6[:L, :], i=pn[:L, 0:256]: e.bn_stats(o.ap, i.ap), r=[pn[:]], w=[st6[:]])
                P.op("dve", lambda e, o=mv[:L, :], i=st6[:L, :]: e.bn_aggr(o.ap, i.ap), r=[st6[:]], w=[mv[:]])
                P.ts("dve", sc[:L, 1:2], sc[:L, 0:1], mv[:L, 1:2], ALU.mult, EPS, ALU.add, part=True)
                P.op("dve", lambda e, o=sc[:L, 1:2]: e.reciprocal(o.ap, o.ap), r=[sc[:]], wp=[sc[:]])
                P.tt("dve", sc[:L, 1:2], sc[:L, 1:2], sc[:L, 0:1], ALU.mult, part=True)
                P.act(sc[:L, 2:3], sc[:L, 1:2], AF.Sqrt, part=True)
                P.ts("dve", hn[:L, h, :], pn[:L, 0:256], mv[:L, 0:1], ALU.subtract, sc[:L, 2:3], ALU.mult, part=(h > 0))
                yield
            for c in range(8):
                P.op("pe", lambda e, o=pT[:, c, :L], i=hn[:L, c // 2, (c % 2) * 128:(c % 2) * 128 + 128],
                     idn=C.identb[:L, :L]: e.transpose(o.ap, i.ap, idn.ap),
                     r=[hn[:], C.identb], w=[pT[:]] if c == 0 else (), wp=[pT[:]] if c else ())
            P.tt("dve", hmt[:, :, cs], pT[:, :, :L], sog[:, :, cs], ALU.mult, part=(j > 0))
        P.dma("sp", HMv[:, :, pos:pos + w], hmt[:, :, :w], part=True)
        yield

    pro = prologue(0)
    for _ in pro:
        pass
    for ti in range(len(TILES)):
        ck = chunks(ti)
        nxt = prologue(ti + 1) if ti + 1 < len(TILES) else None
        for _ in ck:
            if nxt is not None:
                try:
                    next(nxt)
                    next(nxt)
                except StopIteration:
                    nxt = None
        if nxt is not None:
            for _ in nxt:
                pass
    P.flush()
    P.release(m)


def ln_tmps(P, pfx):
    return {"mean": P.sb(pfx + "_mean", [128, 512], F32), "m2": P.sb(pfx + "_m2", [128, 512], F32),
            "rstd": P.sb(pfx + "_rstd", [128, 512], F32)}


def phase_outproj(P, C, wsrc, KC, ins_bf, Hin, Hout, layer, loader=None):
    m = P.mark()
    wo = P.sb("o_w", [128, KC, 1024], BF16)
    load_w(P, wo, wsrc.v(None).re("(kc p) n -> p kc n", p=128), KC)
    xin = [P.sb(f"o_x{i}", [128, KC, 512], BF16) for i in range(2)]
    hf = [P.sb(f"o_h{i}", [128, 8, 512], F32) for i in range(2)]
    sq = [P.sb(f"o_sq{i}", [128, 512], F32) for i in range(2)]
    tmp = ln_tmps(P, "o")
    pss = [P.ps(f"o_ps{i}", [128, 512], F32) for i in range(4)]
    psm = P.ps("o_psm", [128, 512], F32)
    psq = P.ps("o_psq", [128, 512], F32)
    Hiv, Hov = fm(Hin), fm(Hout)
    n = 0
    for ti, (pos, w) in enumerate(TILES):
        x = xin[ti % 2]
        h = hf[ti % 2]
        if loader is not None:
            loader(ti, pos, w, x, pss)
        for i, src in enumerate(ins_bf):
            P.dma("sp", x[:, 8 * i:8 * i + 8, :w], fm(src)[:, :, pos:pos + w], part=(i > 0))
        P.dma("sp", h[:, :, :w], Hiv[:, :, pos:pos + w])
        for oc in range(8):
            ps = pss[n % 4]
            n += 1
            for kc in range(KC):
                P.mm(ps[:, :w], wo[:, kc, oc * 128:(oc + 1) * 128], x[:, kc, :w], kc == 0, kc == KC - 1)
            P.stt(h[:, oc, :w], h[:, oc, :w], ALPHA, ps[:, :w], ALU.mult, ALU.add, part=True)
        layer_norm_fm(P, C, h, w, f"mix_g{layer}", f"mix_b{layer}", h, tmp, psm, psq, sq)
        P.dma("pool", Hov[:, :, pos:pos + w], h[:, :, :w], part=True)
    P.flush()
    P.release(m)


def phase_ffn(P, C, din, layer, Hin, Hout, final_out=None):
    m = P.mark()
    wu = P.sb("f_wu", [128, 8, 2 * DFF], BF16)
    wd = P.sb("f_wd", [128, NFC, 1024], BF16)
    load_w(P, wu, din["ffn_w_up"].v(None)[layer].re("(kc p) n -> p kc n", p=128), 8)
    load_w(P, wd, din["ffn_w_down"].v(None)[layer].re("(kc p) n -> p kc n", p=128), NFC)
    hf = P.sb("f_h", [128, 8, 512], F32)
    hb = P.sb("f_hb", [128, 8, 512], BF16)
    hh = P.sb("f_hh", [128, NFC, 512], BF16)
    gx = [P.sb(f"f_gx{i}", [128, 514], F32) for i in range(2)]
    ac = [P.sb(f"f_ac{i}", [128, 512], F32) for i in range(2)]
    halo = P.sb("f_halo", [128, NFC, 2], F32)
    P.memset("dve", halo[:], 0.0)
    sq = [P.sb(f"f_sq{i}", [128, 512], F32) for i in range(2)]
    tmp = ln_tmps(P, "f")
    pgs = [P.ps(f"f_pg{i}", [128, 512], F32) for i in range(2)]
    pus = [P.ps(f"f_pu{i}", [128, 512], F32) for i in range(2)]
    pys = [P.ps(f"f_py{i}", [128, 512], F32) for i in range(2)]
    psm = P.ps("f_psm", [128, 512], F32)
    psq = P.ps("f_psq", [128, 512], F32)
    Hiv = fm(Hin)
    if final_out is None:
        Hov = fm(Hout)
    else:
        ot = P.sb("f_ot0", [128, 1024], F32)
    cw = [f"f_cw{layer}_{j}" for j in range(3)]
    cnt = {"n": 0}

    def load_hb(ti):
        pos, w = TILES[ti]
        for c in range(8):
            P.dma("pool", hb[:, c, :w], Hiv[:, c, pos:pos + w], part=(c > 0), max_dma_last_dim=4096)

    def stage_up(ti):
        pos, w = TILES[ti]
        for fc in range(NFC):
            k = cnt["n"] % 2
            cnt["n"] += 1
            pg, pu, g, a = pgs[k], pus[k], gx[k], ac[k]
            for kc in range(8):
                P.mm(pg[:, :w], wu[:, kc, DFF + fc * 128:DFF + (fc + 1) * 128], hb[:, kc, :w], kc == 0, kc == 7)
            for kc in range(8):
                P.mm(pu[:, :w], wu[:, kc, fc * 128:(fc + 1) * 128], hb[:, kc, :w], kc == 0, kc == 7)
            P.copy("act", g[:, 0:2], halo[:, fc, :])
            P.copy("act", g[:, 2:2 + w], pg[:, :w], part=True)
            P.copy("pool", halo[:, fc, :], g[:, w:w + 2], part=True)
            P.ts("dve", a[:, :w], g[:, 2:2 + w], vcol(C, cw[2], fc), ALU.mult)
            P.stt(a[:, :w], g[:, 1:1 + w], vcol(C, cw[1], fc), a[:, :w], ALU.mult, ALU.add)
            P.stt(a[:, :w], g[:, 0:w], vcol(C, cw[0], fc), a[:, :w], ALU.mult, ALU.add)
            P.act(a[:, :w], a[:, :w], AF.Gelu_apprx_tanh, bias=vcol(C, f"f_cb{layer}", fc))
            P.tt("dve", hh[:, fc, :w], a[:, :w], pu[:, :w], ALU.mult, part=(fc > 0))
            yield

    def stage_down(ti):
        pos, w = TILES[ti]
        P.dma("sp", hf[:, :, :w], Hiv[:, :, pos:pos + w])
        if ti + 1 < len(TILES):
            load_hb(ti + 1)
        for oc in range(8):
            py = pys[oc % 2]
            for fc in range(NFC):
                P.mm(py[:, :w], wd[:, fc, oc * 128:(oc + 1) * 128], hh[:, fc, :w], fc == 0, fc == NFC - 1)
            P.stt(hf[:, oc, :w], hf[:, oc, :w], ALPHA, py[:, :w], ALU.mult, ALU.add, part=True)

    def stage_ln(ti):
        pos, w = TILES[ti]
        for _ in ln_gen(P, C, hf, w, f"ffn_g{layer}", f"ffn_b{layer}", hf, tmp, psm, psq, sq):
            yield
        if final_out is None:
            P.dma("pool", Hov[:, :, pos:pos + w], hf[:, :, :w], part=True)
        else:
            for j in range((w + 127) // 128):
                L = min(128, w - j * 128)
                p0 = pos + j * 128
                o = ot
                for half in range(2):
                    py = pys[half]
                    for c4 in range(4):
                        c = half * 4 + c4
                        P.op("pe", lambda e, o_=py[:L, c4 * 128:(c4 + 1) * 128], i=hf[:, c, j * 128:j * 128 + L],
                             idn=C.ident: e.transpose(o_.ap, i.ap, idn.ap),
                             r=[hf[:], C.ident], w=[py[:]] if c4 == 0 else (), wp=[py[:]] if c4 else ())
                    if half == 0:
                        P.copy("act", o[:L, 0:512], py[:L, :], part=False)
                    else:
                        P.copy("dve", o[:L, 512:1024], py[:L, :], part=True)
                lo = max(p0, NMETA)
                hi = p0 + L
                if hi > lo:
                    P.dma("pool", final_out[lo - NMETA:hi - NMETA, :], o[lo - p0:hi - p0, :], part=True)
                yield

    load_hb(0)
    prev_ln = None
    for ti in range(len(TILES)):
        up = stage_up(ti)
        for _ in up:
            if prev_ln is not None:
                try:
                    next(prev_ln)
                except StopIteration:
                    prev_ln = None
        if prev_ln is not None:
            for _ in prev_ln:
                pass
        stage_down(ti)
        prev_ln = stage_ln(ti)
    for _ in prev_ln:
        pass
    P.flush()
    P.release(m)


IN_SHAPES = {
    "x": [SEQ, D], "meta": [NMETA, D], "consts": [128, 1536], "vecs": None,
    "a_w_in": [D, 4096], "bdq": [128, 8, 128], "bdk": [128, 8, 128], "bdv": [128, 8, 128],
    "bdT": [128, 3, 8, 128], "a_w_if": [128, 24, 8], "a_b_if": [8, 1],
    "b_w_a": [128, 8, 128], "b_w_x": [128, 8, 128], "ab_w_out": [2 * D, D],
    "ffn_w_up": [2, D, 2 * DFF], "ffn_w_down": [2, DFF, D],
    "c_w_in": [D, 712], "c_w_qidx": [384, 512], "c_w_uq": [384, 1024], "c_w_ukT": [128, 8, 256],
    "c_w_uv": [128, 8, 2, 128], "c_w_out": [D, D],
}


def build(stop_after=None, dbg=()):
    nc = bass.Bass("TRN2", target_bir_lowering=False)
    P = Prog(nc)
    C = Ctx()
    _, nv = vec_layout()
    din = {}
    for k, shp in IN_SHAPES.items():
        if k == "vecs":
            shp = [128, nv]
        din[k] = P.dram(k, shp, F32, kind="ExternalInput")
    out = P.dram("out", [SEQ, D], F32, kind="ExternalOutput")

    def scratch(name, dt=F32):
        kind = "ExternalOutput" if name in dbg else "Internal"
        return P.dram(name, [D, T], dt, kind=kind)

    def scratch2(name, shape, dt):
        kind = "ExternalOutput" if name in dbg else "Internal"
        return P.dram(name, shape, dt, kind=kind)

    setup_consts(P, C, din)
    H0 = scratch("H0")
    phase_embed(P, C, din, H0)
    XM, SOG, XR, GGR = scratch("XM"), scratch("SOG"), scratch("XR"), scratch("GGR")
    phase_inproj(P, C, din, H0, XM, SOG, XR, GGR)
    HR = scratch("HR", BF16)
    phase_rglru(P, C, din, XR, GGR, HR)
    HM = scratch("HM", BF16)
    phase_mlstm(P, C, din, XM, SOG, HM)
    H1 = scratch("H1")
    phase_outproj(P, C, din["ab_w_out"], 16, [HM, HR], H0, H1, 0)
    H2 = scratch("H2")
    if stop_after == "L0":
        phase_ffn(P, C, din, 0, H1, H2)
        P.finish()
        return nc
    phase_ffn(P, C, din, 0, H1, H2)
    CKVT = scratch2("CKVT", [256, T], BF16)
    CKV = scratch2("CKV", [T, 256], BF16)
    KIDXT = scratch2("KIDXT", [64, T], BF16)
    WIDX = scratch2("WIDX", [T, 8], F32)
    QIDXT = scratch2("QIDXT", [512, T], BF16)
    QLATT = scratch2("QLATT", [2048, T], BF16)
    OLATT = scratch2("OLATT", [2048, T], BF16)
    phase_dsa_proj(P, C, din, H2, CKVT, CKV, KIDXT, WIDX, QIDXT, QLATT)
    if stop_after == "L1proj":
        P.finish()
        return nc
    phase_dsa_attn(P, C, din, CKVT, CKV, KIDXT, WIDX, QIDXT, QLATT, OLATT)
    if stop_after == "L1attn":
        P.finish()
        return nc
    H3 = scratch("H3")
    muv = P.mark()
    wuv = P.sb("u_wuv", [128, 8, 2, 128], BF16)
    P.dma("pool", wuv[:], din["c_w_uv"][:])
    olb = [P.sb(f"u_ol{i}", [128, 2, 8, 512], BF16) for i in range(2)]
    OLATv = OLATT.v(None).re("(p a) t -> p a t", a=16)

    def uv_loader(ti, pos, w, x, pss):
        ol = olb[ti % 2]
        P.dma("sp", ol[:, :, :, :w].re("p rc h t -> p (rc h) t"), OLATv[:, :, pos:pos + w])
        for h in range(8):
            ps = pss[h % 4]
            for rc in range(2):
                P.mm(ps[:, :w], wuv[:, h, rc, :], ol[:, rc, h, :w], rc == 0, rc == 1)
            if h % 2:
                P.copy("act", x[:, h, :w], ps[:, :w], part=(h > 0))
            else:
                P.copy("dve", x[:, h, :w], ps[:, :w], part=(h > 0))

    phase_outproj(P, C, din["c_w_out"], 8, [], H2, H3, 1, loader=uv_loader)
    P.release(muv)
    if stop_after == "L1mix":
        P.finish()
        return nc
    phase_ffn(P, C, din, 1, H3, None, final_out=out.v(None))
    P.finish()
    return nc


def host_inputs(inp):
    f = lambda a: np.ascontiguousarray(np.asarray(a, np.float32))
    vp = VecPack()
    pack_vecs(vp, inp)
    bds = [block_diag_host(inp[k][0]) for k in ("a_wq", "a_wk", "a_wv")]
    shared = {
        "meta": f(inp["meta_tokens"]), "consts": make_consts(), "vecs": vp.array(),
        "a_w_in": f(inp["a_w_in"][0]),
        "bdq": f(bds[0].transpose(1, 0, 2)), "bdk": f(bds[1].transpose(1, 0, 2)), "bdv": f(bds[2].transpose(1, 0, 2)),
        "bdT": f(np.stack([b.transpose(2, 0, 1) for b in bds], axis=1)),
        "a_w_if": f(np.asarray(inp["a_w_if"][0]).reshape(24, 128, 8).transpose(1, 0, 2)),
        "a_b_if": f(np.asarray(inp["a_b_if"][0]).reshape(8, 1)),
        "b_w_a": f(np.asarray(inp["b_w_a"][0]).transpose(1, 0, 2)),
        "b_w_x": f(np.asarray(inp["b_w_x"][0]).transpose(1, 0, 2)),
        "ab_w_out": f(inp["ab_w_out"][0]),
        "ffn_w_up": f(inp["ffn_w_up"]), "ffn_w_down": f(inp["ffn_w_down"]),
        "c_w_in": f(inp["c_w_in"][0]), "c_w_qidx": f(inp["c_w_qidx"][0]), "c_w_uq": f(inp["c_w_uq"][0]),
        "c_w_ukT": f(np.asarray(inp["c_w_uk"][0]).transpose(2, 0, 1)),
        "c_w_uv": f(np.asarray(inp["c_w_uv"][0]).reshape(8, 2, 128, 128).transpose(2, 0, 1, 3)),
        "c_w_out": f(inp["c_w_out"][0]),
    }
    return shared


_NC_CACHE = {}


def kernel(**inp):
    shared = host_inputs(inp)
    x = np.asarray(inp["x"], np.float32)
    if "full" not in _NC_CACHE:
        _NC_CACHE["full"] = build()
    nc = _NC_CACHE["full"]
    in_maps = []
    for b in range(8):
        d = dict(shared)
        d["x"] = np.ascontiguousarray(x[b])
        in_maps.append(d)
    res = run_bass_kernel_spmd(nc, in_maps, core_ids=list(range(8)))
    return np.stack([res.results[b]["out"] for b in range(8)], axis=0)


CQ_R, CKV_R = 384, 256


def phase_dsa_proj(P, C, din, H2, CKVT, CKV, KIDXT, WIDX, QIDXT, QLATT):
    m = P.mark()
    wci = P.sb("d_wci", [128, 8, 712], BF16)
    load_w(P, wci, din["c_w_in"].v(None).re("(kc p) n -> p kc n", p=128), 8)
    wqi = P.sb("d_wqi", [128, 3, 512], BF16)
    load_w(P, wqi, din["c_w_qidx"].v(None).re("(kc p) n -> p kc n", p=128), 3)
    wuq = P.sb("d_wuq", [128, 3, 1024], BF16)
    load_w(P, wuq, din["c_w_uq"].v(None).re("(kc p) n -> p kc n", p=128), 3)
    wuk = P.sb("d_wuk", [128, 8, 256], BF16)
    P.dma("pool", wuk[:], din["c_w_ukT"][:])
    m384 = P.sb("d_m384", [128, 128], F32)
    P.memset("dve", m384[:], 1.0 / 384.0)
    m256 = P.sb("d_m256", [128, 128], F32)
    P.memset("dve", m256[:], 1.0 / 256.0)
    m64 = P.sb("d_m64", [64, 64], F32)
    P.memset("dve", m64[:], 1.0 / 64.0)
    hf = P.sb("d_hf", [128, 8, 512], F32)
    hb = P.sb("d_hb", [128, 8, 512], BF16)
    raw = P.sb("d_raw", [128, 3, 512], F32)
    sq = [P.sb(f"d_sq{i}", [128, 512], F32) for i in range(2)]
    rs = P.sb("d_rs", [128, 512], F32)
    mn = P.sb("d_mn", [128, 512], F32)
    cqn = P.sb("d_cqn", [128, 3, 512], BF16)
    kvn = [P.sb(f"d_kvn{i}", [128, 2, 512], BF16) for i in range(2)]
    kvt = [P.sb(f"d_kvt{i}", [128, 4, 256], BF16) for i in range(2)]
    kin = [P.sb(f"d_kin{i}", [64, 512], BF16) for i in range(2)]
    wi = [P.sb(f"d_wi{i}", [128, 4, 8], F32) for i in range(2)]
    qix = [P.sb(f"d_qix{i}", [128, 4, 512], BF16) for i in range(2)]
    qf = P.sb("d_qf", [128, 8, 512], BF16)
    qlt = [P.sb(f"d_qlt{i}", [128, 2, 8, 512], BF16) for i in range(2)]
    pss = [P.ps(f"d_ps{i}", [128, 512], F32) for i in range(4)]
    pst = P.ps("d_pst", [128, 512], F32)
    ptr = P.ps("d_ptr", [128, 8, 128], BF16)
    pwi = P.ps("d_pwi", [128, 512], F32)
    H2v = fm(H2)
    CKVTv = fm(CKVT)
    CKVv = CKV.v(None)
    WIDXv = WIDX.v(None)
    QIDXv = fm(QIDXT)
    QLATv = QLATT.v(None).re("(p a) t -> p a t", a=16)
    KIv = KIDXT.v(None)
    n = 0

    def nxt():
        nonlocal n
        n += 1
        return pss[n % 4]

    for ti, (pos, w) in enumerate(TILES):
        k2 = ti % 2
        P.dma("sp", hf[:, :, :w], H2v[:, :, pos:pos + w])
        P.copy("pool", hb[:, :, :w], hf[:, :, :w])

        def proj_chunk(ps, c0, mcols):
            for kc in range(8):
                P.mm(ps[:mcols, :w], wci[:, kc, c0:c0 + mcols], hb[:, kc, :w], kc == 0, kc == 7)

        def rms(nch, c0, mmat, gname, outt):
            for c in range(nch):
                ps = nxt()
                proj_chunk(ps, c0 + c * 128, 128)
                P.copy("act", raw[:, c, :w], ps[:, :w], part=(c > 0))
                s = sq[c % 2]
                P.tt("pool", s[:, :w], raw[:, c, :w], raw[:, c, :w], ALU.mult)
                P.mm(pst[:, :w], mmat[:], s[:, :w], c == 0, c == nch - 1)
            P.act(rs[:, :w], pst[:, :w], AF.Sqrt, bias=C.epsc)
            P.op("dve", lambda e, o=rs[:, :w]: e.reciprocal(o.ap, o.ap), r=[rs[:]], w=[rs[:]])
            for c in range(nch):
                P.stt(outt[:, c, :w], raw[:, c, :w], vcol(C, gname, c), rs[:, :w], ALU.mult, ALU.mult, part=(c > 0))

        rms(3, 0, m384, "c_qg", cqn)
        kv = kvn[k2]
        rms(2, CQ_R, m256, "c_kvg", kv)
        P.dma("pool", CKVTv[:, :, pos:pos + w], kv[:, :, :w], part=True)
        kt = kvt[k2]
        nb = (w + 127) // 128
        for j in range(nb):
            L = min(128, w - j * 128)
            for rc in range(2):
                P.op("pe", lambda e, o=ptr[:L, j * 2 + rc, :], i=kv[:, rc, j * 128:j * 128 + L], idn=C.identb:
                     e.transpose(o.ap, i.ap, idn.ap), r=[kv[:], C.identb],
                     w=[ptr[:]] if (j == 0 and rc == 0) else (), wp=[ptr[:]] if (j or rc) else ())
        if w == 512:
            P.copy("act", kt[:].re("p j r -> p (j r)"), ptr[:].re("p a b -> p (a b)"))
            P.dma("pool", CKVv[pos:pos + 512, :].re("(j p) r -> p j r", p=128), kt[:], part=True)
        else:
            P.copy("act", kt[:w, 0, :], ptr[:w, 0:2, :].re("p a b -> p (a b)"))
            P.dma("pool", CKVv[pos:pos + w, :], kt[:w, 0, :], part=True)
        ps = nxt()
        proj_chunk(ps, CQ_R + CKV_R, 64)
        P.copy("act", raw[0:64, 0, :w], ps[0:64, :w])
        P.tt("pool", sq[0][0:64, :w], raw[0:64, 0, :w], raw[0:64, 0, :w], ALU.mult)
        P.mm(pst[0:64, :w], m64[:], raw[0:64, 0, :w], True, True)
        P.copy("act", mn[0:64, :w], pst[0:64, :w])
        P.mm(pst[0:64, :w], m64[:], sq[0][0:64, :w], True, True)
        P.tt("pool", sq[1][0:64, :w], mn[0:64, :w], mn[0:64, :w], ALU.mult)
        P.tt("dve", rs[0:64, :w], pst[0:64, :w], sq[1][0:64, :w], ALU.subtract)
        P.act(rs[0:64, :w], rs[0:64, :w], AF.Sqrt, bias=C.epsc[0:64, :])
        P.op("dve", lambda e, o=rs[0:64, :w]: e.reciprocal(o.ap, o.ap), r=[rs[:]], w=[rs[:]])
        P.tt("dve", sq[0][0:64, :w], raw[0:64, 0, :w], mn[0:64, :w], ALU.subtract)
        P.stt(sq[0][0:64, :w], sq[0][0:64, :w], vcol(C, "c_kig")[0:64, :], rs[0:64, :w], ALU.mult, ALU.mult)
        ki = kin[k2]
        P.act(ki[:, :w], sq[0][0:64, :w], AF.Identity, bias=vcol(C, "c_kib")[0:64, :])
        P.dma("pool", KIv[:, pos:pos + w], ki[:, :w], part=True)
        wt = wi[k2]
        for j in range(nb):
            L = min(128, w - j * 128)
            for kc in range(8):
                P.mm(pwi[:L, j * 8:(j + 1) * 8], hb[:, kc, j * 128:j * 128 + L], wci[:, kc, 704:712], kc == 0, kc == 7)
        if w == 512:
            P.act(wt[:].re("p j g -> p (j g)"), pwi[:, 0:32], AF.Copy, scale=float(8.0 ** -0.5))
            P.dma("pool", WIDXv[pos:pos + 512, :].re("(j p) g -> p j g", p=128), wt[:], part=True)
        else:
            P.act(wt[:w, 0, :], pwi[:w, 0:8], AF.Copy, scale=float(8.0 ** -0.5))
            P.dma("pool", WIDXv[pos:pos + w, :], wt[:w, 0, :], part=True)
        qx = qix[k2]
        for oc in range(4):
            ps = nxt()
            for kc in range(3):
                P.mm(ps[:, :w], wqi[:, kc, oc * 128:(oc + 1) * 128], cqn[:, kc, :w], kc == 0, kc == 2)
            if oc % 2:
                P.copy("act", qx[:, oc, :w], ps[:, :w], part=(oc > 0))
            else:
                P.copy("dve", qx[:, oc, :w], ps[:, :w], part=(oc > 0))
        P.dma("pool", QIDXv[:, :, pos:pos + w], qx[:, :, :w], part=True)
        for h in range(8):
            ps = nxt()
            for kc in range(3):
                P.mm(ps[:, :w], wuq[:, kc, h * 128:(h + 1) * 128], cqn[:, kc, :w], kc == 0, kc == 2)
            if h % 2:
                P.copy("act", qf[:, h, :w], ps[:, :w], part=(h > 0))
            else:
                P.copy("dve", qf[:, h, :w], ps[:, :w], part=(h > 0))
        ql = qlt[k2]
        for h in range(8):
            for rc in range(2):
                ps = nxt()
                P.mm(ps[:, :w], wuk[:, h, rc * 128:(rc + 1) * 128], qf[:, h, :w])
                first = (h == 0 and rc == 0)
                if rc:
                    P.act(ql[:, rc, h, :w], ps[:, :w], AF.Copy, scale=float(128.0 ** -0.5), part=not first)
                else:
                    P.ts("dve", ql[:, rc, h, :w], ps[:, :w], float(128.0 ** -0.5), ALU.mult, part=not first)
        P.dma("pool", QLATv[:, :, pos:pos + w], ql[:, :, :, :w].re("p rc h t -> p (rc h) t"), part=True)
    P.flush()
    P.release(m)


NKB = 33
NEG = -1.0e30


def phase_dsa_attn(P, C, din, CKVT, CKV, KIDXT, WIDX, QIDXT, QLATT, OLATT):
    m = P.mark()
    kidx2 = P.sb("t_kidx2", [128, T], BF16)
    P.dma("sp", kidx2[0:64, :], KIDXT.v(None)[:, :], part=True)
    P.dma("sp", kidx2[64:128, :], KIDXT.v(None)[:, :], part=True)
    ckvT = P.sb("t_ckvT", [128, 2, T], BF16)
    P.dma("sp", ckvT[:], fm(CKVT)[:, :, :])
    ckv = P.sb("t_ckv", [128, NKB, 256], BF16)
    for i in range(4):
        P.dma("sp", ckv[:, 8 * i:8 * i + 8, :],
              CKV.v(None)[1024 * i:1024 * (i + 1), :].re("(kb p) r -> p kb r", p=128), part=True)
    P.dma("sp", ckv[0:16, 32, :], CKV.v(None)[4096:4112, :], part=True)
    qix = [P.sb(f"t_qix{i}", [128, 4, 128], BF16) for i in range(2)]
    wid = [P.sb(f"t_wid{i}", [128, 8], F32) for i in range(2)]
    qlt = [P.sb(f"t_qlt{i}", [128, 2, 8, 128], BF16) for i in range(3)]
    score = [P.sb(f"t_score{i}", [128, T], F32) for i in range(2)]
    junk = P.sb("t_junk", [128, T], BF16)
    msk = P.sb("t_msk", [128, T + 16], BF16)
    mskTs = [P.sb(f"t_mskT{i}", [128, 40, 128], BF16) for i in range(2)]
    rl = [P.sb(f"t_rl{i}", [128, 512], F32) for i in range(2)]
    ex = [P.sb(f"t_ex{i}", [128, 4, 128], BF16) for i in range(3)]
    pp = [P.sb(f"t_pp{i}", [128, 4, 128], BF16) for i in range(3)]
    bss = [P.sb(f"t_bs{i}", [128, 8], F32) for i in range(2)]
    rden = P.sb("t_rden", [128, 8], F32)
    fr = P.sb("t_fr", [128, NBIS], F32)
    for r in range(1, NBIS + 1):
        P.op("dve", lambda e, o=fr[:, r - 1:r], v=float(2.0 ** -r): e.memset(o.ap, v), wp=[fr[:]])
    stepf = P.sb("t_stepf", [128, NBIS], F32)
    on = P.sb("t_on", [128, 8, 256], BF16)
    olT = [P.sb(f"t_olT{i}", [128, 2, 8, 128], BF16) for i in range(2)]
    zb = P.sb("t_zb", [128, 512], BF16)
    P.memset("pool", zb[:], 0.0)
    po = [P.ps(f"t_po{i}", [128, 2, 256], F32) for i in range(4)]
    pden = P.ps("t_pden", [128, 512], F32)
    pl = [P.ps(f"t_pl{i}", [128, 4, 128], F32) for i in range(2)]
    px = P.ps("t_px", [128, 512], F32)
    pxs = [px, px]
    pxb = View(px.h[:].bitcast(BF16), px.buf)
    QIDXv = fm(QIDXT)
    QLATv = QLATT.v(None).re("(p a) t -> p a t", a=16)
    OLATv = OLATT.v(None).re("(p a) t -> p a t", a=16)
    WIDXv = WIDX.v(None)
    cnt = {"nl": 0}

    def geom(qi):
        if qi == 0:
            return 0, 16, 16
        return 16 + 128 * (qi - 1), 128, 16 + 128 * qi

    def stageA1(qi):
        q0, nq, nk = geom(qi)
        k2 = qi % 2
        qx, wd, ql, sc = qix[k2], wid[k2], qlt[qi % 3], score[k2]
        P.dma("sp", qx[:, :, :nq], QIDXv[:, :, q0:q0 + nq])
        P.dma("sp", wd[:nq, :], WIDXv[q0:q0 + nq, :])
        P.dma("sp", ql[:, :, :, :nq].re("p rc h t -> p (rc h) t"), QLATv[:, :, q0:q0 + nq])
        for c0 in range(0, nk, 512):
            cw = min(512, nk - c0)
            for h in range(8):
                oc, half = h // 2, h % 2
                lo_, hi_ = half * 64, half * 64 + 64
                px_ = pxs[h % 2]
                P.mm(px_[:nq, :cw], qx[lo_:hi_, oc, :nq], kidx2[lo_:hi_, c0:c0 + cw])
                r_ = rl[h % 2]
                P.act(r_[:nq, :cw], px_[:nq, :cw], AF.Relu)
                if h == 0:
                    P.ts("dve", sc[:nq, c0:c0 + cw], r_[:nq, :cw], wd[:nq, 0:1], ALU.mult, part=(c0 > 0))
                else:
                    P.stt(sc[:nq, c0:c0 + cw], r_[:nq, :cw], wd[:nq, h:h + 1], sc[:nq, c0:c0 + cw],
                          ALU.mult, ALU.add, part=True)
                yield 1.0

    def stageA2(qi):
        q0, nq, nk = geom(qi)
        k2 = qi % 2
        sc, bs, mskT = score[k2], bss[k2], mskTs[k2]
        P.op("dve", lambda e, o=bs[:nq, 0:1], i=sc[:nq, :nk]: e.tensor_reduce(o.ap, i.ap, AX.X, ALU.max),
             r=[sc[:]], w=[bs[:]])
        P.op("dve", lambda e, o=bs[:nq, 1:2], i=sc[:nq, :nk]: e.tensor_reduce(o.ap, i.ap, AX.X, ALU.min),
             r=[sc[:]], wp=[bs[:]])
        if qi >= 1:
            P.op("dve", lambda e, o=sc[0:64, nk - 64:nk]: e.memset(o.ap, NEG), r=[bs[:]], wp=[sc[:]])
        P.tt("dve", bs[:nq, 2:3], bs[:nq, 0:1], bs[:nq, 1:2], ALU.subtract, part=True)
        yield 7.0
        if nk > KSEL:
            P.ts("dve", stepf[:nq, :], fr[:nq, :], bs[:nq, 2:3], ALU.mult)
            P.tt("dve", bs[:nq, 3:4], bs[:nq, 1:2], stepf[:nq, 0:1], ALU.add, part=True)
            for r in range(1, NBIS + 1):
                P.ts("dve", junk[:nq, :nk], sc[:nq, :nk], bs[:nq, 3:4], ALU.is_ge, 0.0, ALU.add,
                     accum=bs[:nq, 4:5])
                last = (r == NBIS)
                P.ts("dve", bs[:nq, 5:6], bs[:nq, 4:5], KSEL - 0.5, ALU.is_ge, -1.0 if last else -0.5, ALU.add,
                     part=True)
                P.stt(bs[:nq, 1:2] if last else bs[:nq, 3:4], bs[:nq, 5:6], stepf[:nq, r - 1:r], bs[:nq, 3:4],
                      ALU.mult, ALU.add, part=True)
                yield nk * 1.6 / 960.0 + 0.8
        P.ts("dve", msk[:nq, :nk], sc[:nq, :nk], bs[:nq, 1:2], ALU.is_ge)
        nkb = (nk + 127) // 128
        for b0 in range(0, nkb, 8):
            nb_ = min(8, nkb - b0)
            for j in range(nb_):
                kb = b0 + j
                kl = min(128, nk - kb * 128)
                P.op("pe", lambda e, o=pxb[:kl, j * 128:j * 128 + nq], i=msk[:nq, kb * 128:kb * 128 + kl],
                     idn=C.identb[:nq, :nq]: e.transpose(o.ap, i.ap, idn.ap), r=[msk[:], C.identb],
                     w=[px[:]] if j == 0 else (), wp=[px[:]] if j else ())
            P.copy("act", mskT[:, b0:b0 + nb_, :].re("p a b -> p (a b)"), pxb[:, 0:nb_ * 128], part=(b0 > 0))
            yield 3.0

    def stageB(qi):
        q0, nq, nk = geom(qi)
        k2 = qi % 2
        ql, mskT = qlt[qi % 3], mskTs[k2]
        nkb = (nk + 127) // 128
        for i in range(4):
            P.mm(po[i][:, :, :].re("p a b -> p (a b)"), zb[:, 0:128], zb[:, :], True, False)
        P.mm(pden[:, :], zb[:, 0:128], zb[:, :], True, False)
        steps = [(kb, g) for kb in range(nkb) for g in range(2)]
        base = cnt["nl"]
        cnt["nl"] += len(steps)

        def front(n):
            kb, g = steps[n]
            kl = min(128, nk - kb * 128)
            nl = base + n
            p_l, e_, p_ = pl[nl % 2], ex[nl % 3], pp[nl % 3]
            for rc in range(2):
                P.mm(p_l[:kl, :, :nq], ckvT[:, rc, kb * 128:kb * 128 + kl], ql[:, rc, 4 * g:4 * g + 4, :nq],
                     rc == 0, rc == 1)
            P.act(e_[:kl, :, :nq], p_l[:kl, :, :nq], AF.Exp)
            P.tt("pool", p_[:kl, :, :nq], e_[:kl, :, :nq], bc3(mskT[:kl, kb, :nq], 4), ALU.mult)

        def back(n):
            kb, g = steps[n]
            kl = min(128, nk - kb * 128)
            p_ = pp[(base + n) % 3]
            for h4 in range(4):
                h = 4 * g + h4
                P.mm(po[h // 2][:nq, h % 2, :], p_[:kl, h4, :nq], ckv[:kl, kb, :], False, kb == nkb - 1)
                P.mm(pden[:nq, h:h + 1], p_[:kl, h4, :nq], C.onesb[:kl, 0:1], False, kb == nkb - 1)

        front(0)
        for n in range(len(steps)):
            if n + 1 < len(steps):
                front(n + 1)
            back(n)
            yield 3.0
        P.op("dve", lambda e, o=rden[:nq, :], i=pden[:nq, 0:8]: e.reciprocal(o.ap, i.ap), r=[pden[:]], w=[rden[:]])
        for h in range(8):
            if (h // 2) % 2:
                P.act(on[:nq, h, :], po[h // 2][:nq, h % 2, :], AF.Identity, scale=rden[:nq, h:h + 1], part=(h > 0))
            else:
                P.ts("dve", on[:nq, h, :], po[h // 2][:nq, h % 2, :], rden[:nq, h:h + 1], ALU.mult, part=(h > 0))
        yield 4.0
        ot = olT[k2]
        for rc in range(2):
            for h in range(8):
                P.op("pe", lambda e, o=pxb[:, h * 128:h * 128 + nq], i=on[:nq, h, rc * 128:(rc + 1) * 128],
                     idn=C.identb[:nq, :nq]: e.transpose(o.ap, i.ap, idn.ap), r=[on[:], C.identb],
                     w=[px[:]] if h == 0 else (), wp=[px[:]] if h else ())
            src = pxb[:, :].re("p (h q) -> p h q", h=8)[:, :, :nq]
            if rc:
                P.copy("act", ot[:, rc, :, :nq], src, part=True)
            else:
                P.copy("dve", ot[:, rc, :, :nq], src, part=False)
            yield 3.0
        P.dma("pool", OLATv[:, :, q0:q0 + nq], ot[:, :, :, :nq].re("p rc h t -> p (rc h) t"), part=True)

    def total(qi, which):
        q0, nq, nk = geom(qi)
        nkb = (nk + 127) // 128
        if which == "A1":
            return ((nk + 511) // 512) * 8 * 1.0
        if which == "A2":
            return 7.0 + (NBIS * (nk * 1.6 / 960.0 + 0.8) if nk > KSEL else 0.0) + ((nkb + 7) // 8) * 3.0
        return nkb * 2 * 3.0 + 4.0 + 6.0

    for it in range(35):
        live = []
        if it < 33:
            live.append([stageA1(it), 0.0, total(it, "A1")])
        if 1 <= it < 34:
            live.append([stageA2(it - 1), 0.0, total(it - 1, "A2")])
        if it >= 2:
            live.append([stageB(it - 2), 0.0, total(it - 2, "B")])
        while live:
            g_ = min(live, key=lambda t: t[1] / t[2])
            try:
                g_[1] += next(g_[0])
            except StopIteration:
                live.remove(g_)
    P.flush()
    P.release(m)
```

```python
import numpy as np
import concourse.bass as bass
import concourse.mybir as mybir
from concourse.bass_utils import run_bass_kernel_spmd

F32 = mybir.dt.float32
BF16 = mybir.dt.bfloat16
AF = mybir.ActivationFunctionType
ALU = mybir.AluOpType
AX = mybir.AxisListType

SEM_LIMIT = 30000


class Buf:
    __slots__ = ("name", "w", "r", "wf")

    def __init__(self, name):
        self.name = name
        self.w = {}
        self.r = {}
        self.wf = {}


class View:
    __slots__ = ("ap", "buf")

    def __init__(self, ap, buf):
        self.ap = ap
        self.buf = buf

    def __getitem__(self, idx):
        return View(self.ap[idx], self.buf)

    def re(self, pattern, **kw):
        return View(self.ap.rearrange(pattern, **kw), self.buf)


class Tile:
    def __init__(self, handle, buf):
        self.h = handle
        self.buf = buf

    def __getitem__(self, idx):
        return View(self.h[idx], self.buf)


class DramT:
    def __init__(self, handle, name):
        self.h = handle
        self.name = name
        self.bufs = {}

    def __getitem__(self, idx):
        return self.v(None)[idx]

    def v(self, key):
        if key not in self.bufs:
            self.bufs[key] = Buf(f"{self.name}:{key}")
        return View(self.h.ap(), self.bufs[key])


class _Op:
    __slots__ = ("fn", "deps", "sig", "val", "dma", "sem")

    def __init__(self, fn, deps, dma):
        self.fn = fn
        self.deps = deps
        self.sig = dma is not None
        self.val = None
        self.dma = dma
        self.sem = None


ENGS = ("pe", "act", "dve", "pool", "sp")
NSLOT = 8


class Prog:
    def __init__(self, nc):
        self.nc = nc
        self.ops = {e: [] for e in ENGS}
        self.ctx = []
        self.dma_cnt = {}
        self.dma_rr = {q: 0 for q in ENGS}
        self.n_sb = 0

    def dram(self, name, shape, dt, kind="Internal"):
        h = self.nc.dram_tensor(name, list(shape), dt, kind=kind)
        return DramT(h, name)

    def sb(self, name, shape, dt):
        self.n_sb += 1
        name = f"{name}_{self.n_sb}"
        g = self.nc.sbuf_tensor(name, list(shape), dt)
        h = g.__enter__()
        self.ctx.append(g)
        return Tile(h, Buf(name))

    def ps(self, name, shape, dt=F32):
        self.n_sb += 1
        name = f"{name}_{self.n_sb}"
        g = self.nc.psum_tensor(name, list(shape), dt)
        h = g.__enter__()
        self.ctx.append(g)
        return Tile(h, Buf(name))

    def op(self, eng, fn, r=(), w=(), wp=(), dmaq=False):
        deps = {}

        def add(d):
            for k, v in d.items():
                if deps.get(k, -1) < v:
                    deps[k] = v

        for v in r:
            add(v.buf.w)
        for v in w:
            add(v.buf.w)
            add(v.buf.r)
        for v in wp:
            add(v.buf.r)
            add(v.buf.wf)
        idx = len(self.ops[eng])
        dma = None
        if dmaq:
            slot = self.dma_rr[eng]
            self.dma_rr[eng] = (slot + 1) % NSLOT
            key = ("d", eng, slot)
            n = self.dma_cnt.get(key, 0)
            if n > 0:
                if deps.get(key, -1) < n:
                    deps[key] = n
            self.dma_cnt[key] = n + 1
            dma = (key, n + 1)
            tok = (key, n + 1)
        else:
            tok = (("c", eng), idx)
        if eng == "pe":
            deps.pop(("c", "pe"), None)
        self.ops[eng].append(_Op(fn, deps, dma))
        for v in r:
            if v.buf.r.get(tok[0], -1) < tok[1]:
                v.buf.r[tok[0]] = tok[1]
        for v in w:
            v.buf.w = {tok[0]: tok[1]}
            v.buf.wf = {tok[0]: tok[1]}
            v.buf.r = {}
        for v in wp:
            if v.buf.w.get(tok[0], -1) < tok[1]:
                v.buf.w[tok[0]] = tok[1]

    def _e(self, eng):
        return eng

    def mm(self, out, lhsT, rhs, start=True, stop=True, **kw):
        self.op("pe", lambda e: e.matmul(out.ap, lhsT.ap, rhs.ap, start=start, stop=stop, **kw),
                r=[lhsT, rhs], wp=[out] if not start else (), w=[out] if start else ())

    def transpose(self, out, in_, ident):
        self.op("pe", lambda e: e.transpose(out.ap, in_.ap, ident.ap), r=[in_, ident], w=[out])

    def act(self, out, in_, func, scale=1.0, bias=0.0, accum=None, part=False):
        r = [in_]
        sc = scale
        bi = bias
        if isinstance(scale, View):
            r.append(scale)
            sc = scale.ap
        if isinstance(bias, View):
            r.append(bias)
            bi = bias.ap
        kw = {}
        ws = [out]
        if accum is not None:
            kw["accum_out"] = accum.ap
            ws.append(accum)
        self.op("act", lambda e: e.activation(out.ap, in_.ap, func, scale=sc, bias=bi, **kw),
                r=r, w=() if part else ws, wp=ws if part else ())

    def tt(self, eng, out, a, b, op, part=False):
        self.op(eng, lambda e: e.tensor_tensor(out.ap, a.ap, b.ap, op), r=[a, b],
                w=() if part else [out], wp=[out] if part else ())

    def ts(self, eng, out, a, s1, op0, s2=None, op1=None, accum=None, part=False):
        r = [a]
        x1 = s1
        x2 = s2
        if isinstance(s1, View):
            r.append(s1)
            x1 = s1.ap
        if isinstance(s2, View):
            r.append(s2)
            x2 = s2.ap
        kw = {}
        ws = [out]
        if op1 is not None:
            kw["op1"] = op1
        if accum is not None:
            kw["accum_out"] = accum.ap
            ws.append(accum)
        self.op(eng, lambda e: e.tensor_scalar(out.ap, a.ap, x1, x2, op0, **kw), r=r,
                w=() if part else ws, wp=ws if part else ())

    def stt(self, out, in0, scalar, in1, op0, op1, part=False, eng="dve"):
        r = [in0, in1]
        sc = scalar
        if isinstance(scalar, View):
            r.append(scalar)
            sc = scalar.ap
        self.op(eng, lambda e: e.scalar_tensor_tensor(out.ap, in0.ap, sc, in1.ap, op0, op1), r=r,
                w=() if part else [out], wp=[out] if part else ())

    def copy(self, eng, out, in_, part=False):
        if eng == "act":
            self.op(eng, lambda e: e.copy(out.ap, in_.ap), r=[in_],
                    w=() if part else [out], wp=[out] if part else ())
        else:
            self.op(eng, lambda e: e.tensor_copy(out.ap, in_.ap), r=[in_],
                    w=() if part else [out], wp=[out] if part else ())

    def memset(self, eng, out, val):
        self.op(eng, lambda e: e.memset(out.ap, val), w=[out])

    def scan(self, out, d0, d1, init, op0, op1, part=False):
        r = [d0, d1]
        ini = init
        if isinstance(init, View):
            r.append(init)
            ini = init.ap
        self.op("dve", lambda e: e.tensor_tensor_scan(out.ap, d0.ap, d1.ap, ini, op0, op1), r=r,
                w=() if part else [out], wp=[out] if part else ())

    def dma(self, q, out, in_, part=False, **kw):
        self.op(q, lambda e: e.dma_start(out=out.ap, in_=in_.ap, **kw), r=[in_],
                w=() if part else [out], wp=[out] if part else (), dmaq=True)

    def _sem(self, name):
        if name not in self.sems:
            g = self.nc.semaphore(name)
            self.sems[name] = g.__enter__()
            self.sem_ctx.append(g)
        return self.sems[name]

    def _dma_tok(self, key, n):
        per = SEM_LIMIT // 16
        j = (n - 1) // per
        return f"d_{key[1]}_{key[2]}_{j}", 16 * ((n - 1) % per + 1)

    def _resolve(self, k, v):
        if k[0] == "c":
            lst = self.ops[k[1]]
            while lst[v].val is None:
                v += 1
            o = lst[v]
            return o.sem, o.val
        return self._dma_tok(k, v)

    def mark(self):
        return len(self.ctx)

    def flush(self, final=False):
        nc = self.nc
        ops = self.ops
        if not hasattr(self, "sems"):
            self.sems = {}
            self.sem_ctx = []
            self.emitted = {e: 0 for e in ENGS}
            self.sigcnt = {e: 0 for e in ENGS}
            self.seen = {e: {} for e in ENGS}
        for e in ENGS:
            for o in ops[e][self.emitted[e]:]:
                for k, v in o.deps.items():
                    if k[0] == "c" and v >= self.emitted[k[1]]:
                        ops[k[1]][v].sig = True
        fdeps = {}
        for e in ENGS:
            for j in range(len(ops[e]) - 1, self.emitted[e] - 1, -1):
                if ops[e][j].dma is None and ops[e][j].fn is not None:
                    ops[e][j].sig = True
                    fdeps[("c", e)] = j
                    break
        for key, n in self.dma_cnt.items():
            fdeps[key] = n
        for e in ENGS:
            ops[e].append(_Op(None, dict(fdeps), None))
        for e in ENGS:
            for o in ops[e][self.emitted[e]:]:
                if o.dma is None and o.sig and o.fn is not None:
                    c = self.sigcnt[e]
                    o.sem = f"c_{e}_{c // SEM_LIMIT}"
                    o.val = c % SEM_LIMIT + 1
                    self._sem(o.sem)
                    self.sigcnt[e] = c + 1
                elif o.dma is not None:
                    self._sem(self._dma_tok(*o.dma)[0])
        block = nc.Block()
        block.__enter__()
        names = {"pe": "tensor", "act": "scalar", "dve": "vector", "pool": "gpsimd", "sp": "sync"}
        for e in ENGS:
            def body(eng, e=e):
                seen = self.seen[e]
                for o in ops[e][self.emitted[e]:]:
                    for k, v in o.deps.items():
                        s, val = self._resolve(k, v)
                        if seen.get(s, 0) >= val:
                            continue
                        seen[s] = val
                        eng.wait_ge(self.sems[s], val)
                    if o.fn is None:
                        continue
                    ins = o.fn(eng)
                    if o.dma is not None:
                        s, val = self._dma_tok(*o.dma)
                        ins.then_inc(self.sems[s], 16)
                    elif o.sig:
                        ins.then_inc(self.sems[o.sem], 1)
                    o.fn = True
            getattr(block, names[e])(body)
        block.__exit__(None, None, None)
        for e in ENGS:
            self.emitted[e] = len(ops[e])

    def release(self, mark):
        while len(self.ctx) > mark:
            self.ctx.pop().__exit__(None, None, None)

    def finish(self):
        self.flush(final=True)
        self.release(0)
        for g in reversed(self.sem_ctx):
            g.__exit__(None, None, None)


D = 1024
SEQ = 4096
NMETA = 16
T = SEQ + NMETA
DFF = 2816
NFC = DFF // 128
ALPHA = 4.0 ** 0.25
EPS = 1e-5
TILES = [(i * 512, 512) for i in range(8)] + [(4096, 16)]
KSEL = 256
NBIS = 14


def bc(view, shape):
    return View(view.ap.broadcast_to(list(shape)), view.buf)


def bc3(view, n):
    a = view.ap.unsqueeze(1)
    shp = list(a.shape)
    shp[1] = n
    return View(a.broadcast_to(shp), view.buf)


class VecPack:
    def __init__(self):
        self.cols = []
        self.off = {}
        self.n = 0

    def add(self, name, vec):
        vec = np.asarray(vec, np.float32).reshape(-1)
        nchunk = (vec.size + 127) // 128
        pad = np.zeros(nchunk * 128, np.float32)
        pad[:vec.size] = vec
        self.cols.append(pad.reshape(nchunk, 128).T)
        self.off[name] = self.n
        self.n += nchunk

    def array(self):
        return np.ascontiguousarray(np.concatenate(self.cols, axis=1))


def vec_layout():
    vp = VecPack()
    pack_vecs(vp, None)
    return vp.off, vp.n


def pack_vecs(vp, inp):
    def g(name, shape):
        if inp is None:
            return np.zeros(shape, np.float32)
        return np.asarray(inp[name], np.float32)

    cw = g("a_conv_w", (1, 4, D))[0]
    for j in range(4):
        vp.add(f"a_cw{j}", cw[j])
    vp.add("a_cb", g("a_conv_b", (1, D))[0])
    vp.add("a_ng", g("a_norm_g", (1, D))[0])
    cw = g("b_conv_w", (1, 4, D))[0]
    for j in range(4):
        vp.add(f"b_cw{j}", cw[j])
    vp.add("b_cb", g("b_conv_b", (1, D))[0])
    vp.add("b_ba", g("b_b_a", (1, D))[0])
    vp.add("b_bx", g("b_b_x", (1, D))[0])
    vp.add("b_lam", g("b_lambda", (1, D))[0])
    mg = g("mix_ln_g", (2, D))
    mb = g("mix_ln_b", (2, D))
    fg = g("ffn_ln_g", (2, D))
    fb = g("ffn_ln_b", (2, D))
    fcw = g("ffn_conv_w", (2, 3, DFF))
    fcb = g("ffn_conv_b", (2, DFF))
    for l in range(2):
        vp.add(f"mix_g{l}", mg[l])
        vp.add(f"mix_b{l}", mb[l])
        vp.add(f"ffn_g{l}", fg[l])
        vp.add(f"ffn_b{l}", fb[l])
        for j in range(3):
            vp.add(f"f_cw{l}_{j}", fcw[l, j])
        vp.add(f"f_cb{l}", fcb[l])
    vp.add("c_qg", g("c_q_norm_g", (1, 384))[0])
    vp.add("c_kvg", g("c_kv_norm_g", (1, 256))[0])
    kg = g("c_kidx_ln_g", (1, 64))[0]
    kb = g("c_kidx_ln_b", (1, 64))[0]
    vp.add("c_kig", np.concatenate([kg, kg]))
    vp.add("c_kib", np.concatenate([kb, kb]))


def block_diag_host(w):
    w = np.asarray(w, np.float32)
    out = np.zeros((8, 128, 128), np.float32)
    for n in range(256):
        c, b = divmod(n, 32)
        out[c, 4 * b:4 * b + 4, 4 * b:4 * b + 4] = w[n]
    return out


def make_consts():
    c = np.zeros((128, 1536), np.float32)
    c[:, 0:128] = np.eye(128, dtype=np.float32)
    c[:, 128:256] = np.triu(np.ones((128, 128), np.float32))
    c[:, 256:384] = 1.0
    for h in range(4):
        c[h, 384 + 128 * h: 384 + 128 * (h + 1)] = 1.0
        c[4 + h, 896 + 128 * h: 896 + 128 * (h + 1)] = 1.0
        c[h, 1408 + h] = 1.0
        c[4 + h, 1412 + h] = 1.0
    return c


class Ctx:
    pass


def fm(dram, key=None):
    return dram.v(key).re("(c p) t -> p c t", p=128)


def load_w(P, dst, src_view, KC):
    for kc in range(KC):
        P.dma("pool", dst[:, kc, :], src_view[:, kc, :], part=True, max_dma_last_dim=4096)


def setup_consts(P, C, din):
    C.cf = P.sb("c_f32", [128, 384], F32)
    P.dma("sp", C.cf[:], din["consts"][:, 0:384])
    C.ident = C.cf[:, 0:128]
    C.triu = C.cf[:, 128:256]
    C.ones = C.cf[:, 256:384]
    C.cb = P.sb("c_bf16", [128, 384], BF16)
    P.copy("dve", C.cb[:], C.cf[:, 0:384])
    C.identb = C.cb[:, 0:128]
    C.triub = C.cb[:, 128:256]
    C.onesb = C.cb[:, 256:384]
    C.voff, nv = vec_layout()
    C.vec = P.sb("c_vec", [128, nv], F32)
    P.dma("sp", C.vec[:], din["vecs"][:])
    C.mean1024 = P.sb("c_mean", [128, 128], F32)
    P.memset("dve", C.mean1024[:], 1.0 / 1024.0)
    C.cc = P.sb("c_cols", [128, 4], F32)
    P.memset("dve", C.cc[:, 0:1], EPS)
    P.op("dve", lambda e: e.memset(C.cc[:, 1:2].ap, 1.0), wp=[C.cc[:, 1:2]])
    P.op("dve", lambda e: e.memset(C.cc[:, 2:3].ap, -float(np.log(16.0))), wp=[C.cc[:, 2:3]])
    P.op("dve", lambda e: e.memset(C.cc[:, 3:4].ap, 0.0), wp=[C.cc[:, 3:4]])
    C.epsc = C.cc[:, 0:1]
    C.onec = C.cc[:, 1:2]
    C.nl16 = C.cc[:, 2:3]


def vcol(C, name, c=0, n=1):
    o = C.voff[name] + c
    return C.vec[:, o:o + n]


def ln_gen(P, C, y, w, gname, bname, out_f32, tmp, psm, psq, sq, out_bf=None):
    for c in range(8):
        s = sq[c % 2]
        P.tt("pool", s[:, :w], y[:, c, :w], y[:, c, :w], ALU.mult)
        P.mm(psm[:, :w], C.mean1024[:], y[:, c, :w], c == 0, c == 7)
        P.mm(psq[:, :w], C.mean1024[:], s[:, :w], c == 0, c == 7)
        yield
    mean = tmp["mean"]
    m2 = tmp["m2"]
    rstd = tmp["rstd"]
    P.copy("act", mean[:, :w], psm[:, :w])
    P.tt("pool", m2[:, :w], mean[:, :w], mean[:, :w], ALU.mult)
    P.tt("dve", m2[:, :w], psq[:, :w], m2[:, :w], ALU.subtract)
    P.act(rstd[:, :w], m2[:, :w], AF.Sqrt, bias=C.epsc[:, 0:1])
    P.op("dve", lambda e, o=rstd[:, :w]: e.reciprocal(o.ap, o.ap), r=[rstd[:, :w]], w=[rstd[:, :w]])
    yield
    for c in range(8):
        t = sq[c % 2]
        P.tt("dve", t[:, :w], y[:, c, :w], mean[:, :w], ALU.subtract)
        P.stt(t[:, :w], t[:, :w], vcol(C, gname, c), rstd[:, :w], ALU.mult, ALU.mult)
        P.act(out_f32[:, c, :w], t[:, :w], AF.Identity, bias=vcol(C, bname, c), part=True)
        if out_bf is not None:
            P.copy("pool", out_bf[:, c, :w], out_f32[:, c, :w], part=True)
        yield


def layer_norm_fm(P, C, y, w, gname, bname, out_f32, tmp, psm, psq, sq, out_bf=None):
    for _ in ln_gen(P, C, y, w, gname, bname, out_f32, tmp, psm, psq, sq, out_bf):
        pass


def phase_embed(P, C, din, H0):
    m = P.mark()
    x = din["x"]
    xin = [P.sb(f"e_xin{i}", [128, 4, 1024], F32) for i in range(2)]
    hT = [P.sb(f"e_hT{i}", [128, 8, 512], F32) for i in range(2)]
    pt = [P.ps(f"e_pt{i}", [128, 512], F32) for i in range(4)]
    H0v = fm(H0)
    xv = x.v(None).re("(k j p) f -> k p j f", j=4, p=128)
    n = 0
    for k in range(9):
        xi = xin[k % 2]
        h = hT[k % 2]
        if k == 8:
            P.dma("sp", xi[0:16, 0, :], din["meta"][:])
            nj, rows, pos, w = 1, 16, 0, 16
        else:
            P.dma("sp", xi[:], xv[k])
            nj, rows, pos, w = 4, 128, 16 + 512 * k, 512
        for c in range(8):
            p_ = pt[n % 4]
            n += 1
            for j in range(nj):
                P.op("pe", lambda e, o=p_[:, j * rows:(j + 1) * rows], i=xi[0:rows, j, c * 128:(c + 1) * 128],
                     idn=C.ident[0:rows, 0:rows]: e.transpose(o.ap, i.ap, idn.ap),
                     r=[xi[:], C.ident], w=[p_[:]] if j == 0 else (), wp=[p_[:]] if j else ())
            if c % 2 == 0:
                P.copy("act", h[:, c, :w], p_[:, :w], part=True)
            else:
                P.copy("dve", h[:, c, :w], p_[:, :w], part=True)
        P.dma("pool", H0v[:, :, pos:pos + w], h[:, :, :w], part=True)
    P.flush()
    P.release(m)


def phase_inproj(P, C, din, H0, XM, SOG, XR, GGR):
    m = P.mark()
    w_in = P.sb("w_in", [128, 8, 4096], BF16)
    load_w(P, w_in, din["a_w_in"].v(None).re("(kc p) n -> p kc n", p=128), 8)
    hf = [P.sb(f"i_hf{i}", [128, 8, 512], F32) for i in range(2)]
    hb = [P.sb(f"i_hb{i}", [128, 8, 512], BF16) for i in range(2)]
    stg = [P.sb(f"i_st{i}", [128, 8, 512], F32) for i in range(2)]
    pss = [P.ps(f"i_ps{i}", [128, 512], F32) for i in range(4)]
    outs = [fm(XM), fm(SOG), fm(XR), fm(GGR)]
    H0v = fm(H0)
    n = 0
    for ti, (pos, w) in enumerate(TILES):
        f = hf[ti % 2]
        b = hb[ti % 2]
        P.dma("sp", f[:, :, :w], H0v[:, :, pos:pos + w])
        P.copy("pool", b[:, :, :w], f[:, :, :w])
        for grp in range(4):
            st = stg[(ti * 4 + grp) % 2]
            for o8 in range(8):
                oc = grp * 8 + o8
                ps = pss[n % 4]
                n += 1
                for kc in range(8):
                    P.mm(ps[:, :w], w_in[:, kc, oc * 128:(oc + 1) * 128], b[:, kc, :w], kc == 0, kc == 7)
                if grp == 1:
                    P.act(st[:, o8, :w], ps[:, :w], AF.Sigmoid, part=True)
                elif grp == 3:
                    P.act(st[:, o8, :w], ps[:, :w], AF.Gelu_apprx_tanh, part=True)
                elif o8 % 2 == 0:
                    P.copy("act", st[:, o8, :w], ps[:, :w], part=True)
                else:
                    P.copy("dve", st[:, o8, :w], ps[:, :w], part=True)
            P.dma("sp", outs[grp][:, :, pos:pos + w], st[:, :, :w], part=True)
    P.flush()
    P.release(m)


def phase_rglru(P, C, din, XR, GGR, HR):
    m = P.mark()
    wa = P.sb("r_wa", [128, 8, 128], BF16)
    wx = P.sb("r_wx", [128, 8, 128], BF16)
    P.dma("pool", wa[:], din["b_w_a"][:])
    P.dma("pool", wx[:], din["b_w_x"][:])
    c8 = P.sb("r_c8", [128, 8], F32)
    P.act(c8[:], vcol(C, "b_lam", 0, 8), AF.Exp, scale=-1.0)
    P.act(c8[:], c8[:], AF.Ln, bias=C.onec[:, 0:1])
    P.ts("dve", c8[:], c8[:], -8.0, ALU.mult)
    carry = P.sb("r_carry", [128, 8], F32)
    P.memset("dve", carry[:], 0.0)
    xe = [P.sb(f"r_xe{i}", [128, 8, 515], F32) for i in range(2)]
    gg = [P.sb(f"r_gg{i}", [128, 8, 512], F32) for i in range(2)]
    ho = [P.sb(f"r_ho{i}", [128, 8, 512], BF16) for i in range(2)]
    xc2 = [P.sb(f"r_xc{i}", [128, 4, 512], F32) for i in range(2)]
    xcb2 = [P.sb(f"r_xcb{i}", [128, 4, 512], BF16) for i in range(2)]
    rr2 = [P.sb(f"r_r{i}", [128, 4, 512], F32) for i in range(2)]
    ii2 = [P.sb(f"r_i{i}", [128, 4, 512], F32) for i in range(2)]
    aa2 = [P.sb(f"r_a{i}", [128, 4, 512], F32) for i in range(2)]
    a22 = [P.sb(f"r_a2{i}", [128, 4, 512], F32) for i in range(2)]
    hs = P.sb("r_hs", [128, 4, 512], F32)
    pa = [P.ps(f"r_pa{i}", [128, 512], F32) for i in range(4)]
    px = [P.ps(f"r_px{i}", [128, 512], F32) for i in range(4)]
    XRv, GGv, HRv = fm(XR), fm(GGR), fm(HR)
    NU = 2 * len(TILES)

    def stA(u):
        ti, hf_ = divmod(u, 2)
        pos, w = TILES[ti]
        x, g = xe[ti % 2], gg[ti % 2]
        xc, xcb = xc2[u % 2], xcb2[u % 2]
        if hf_ == 0:
            if ti == 0:
                P.memset("pool", x[:, :, 0:3], 0.0)
                P.dma("sp", x[:, :, 3:3 + w], XRv[:, :, pos:pos + w], part=True)
            else:
                P.dma("sp", x[:, :, 0:3 + w], XRv[:, :, pos - 3:pos + w])
            P.dma("sp", g[:, :, :w], GGv[:, :, pos:pos + w])
        for k in range(4):
            c = hf_ * 4 + k
            P.ts("dve", xc[:, k, :w], x[:, c, 3:3 + w], vcol(C, "b_cw3", c), ALU.mult, vcol(C, "b_cb", c), ALU.add,
                 part=(k > 0))
            for j in range(3):
                P.stt(xc[:, k, :w], x[:, c, j:j + w], vcol(C, f"b_cw{j}", c), xc[:, k, :w], ALU.mult, ALU.add,
                      part=True)
        for k in range(4):
            P.copy("pool", xcb[:, k, :w], xc[:, k, :w], part=(k > 0))

    def stB(u):
        ti, hf_ = divmod(u, 2)
        pos, w = TILES[ti]
        xc, xcb, rr, ii, aa, a2 = xc2[u % 2], xcb2[u % 2], rr2[u % 2], ii2[u % 2], aa2[u % 2], a22[u % 2]
        cs = [hf_ * 4 + k for k in range(4)]
        for k, c in enumerate(cs):
            P.mm(pa[k][:, :w], wa[:, c, :], xcb[:, k, :w])
            P.mm(px[k][:, :w], wx[:, c, :], xcb[:, k, :w])
        for k, c in enumerate(cs):
            P.act(rr[:, k, :w], pa[k][:, :w], AF.Sigmoid, bias=vcol(C, "b_ba", c), part=(k > 0))
        for k, c in enumerate(cs):
            P.act(ii[:, k, :w], px[k][:, :w], AF.Sigmoid, bias=vcol(C, "b_bx", c), part=(k > 0))
        for k, c in enumerate(cs):
            P.act(aa[:, k, :w], rr[:, k, :w], AF.Exp, scale=c8[:, c:c + 1], part=(k > 0))
        for k, c in enumerate(cs):
            P.tt("pool", a2[:, k, :w], aa[:, k, :w], aa[:, k, :w], ALU.mult, part=(k > 0))
        for k, c in enumerate(cs):
            P.tt("pool", ii[:, k, :w], ii[:, k, :w], xc[:, k, :w], ALU.mult, part=True)
        for k, c in enumerate(cs):
            P.act(a2[:, k, :w], a2[:, k, :w], AF.Sqrt, scale=-1.0, bias=C.onec[:, 0:1], part=True)

    def stC(u):
        ti, hf_ = divmod(u, 2)
        pos, w = TILES[ti]
        g, h = gg[ti % 2], ho[ti % 2]
        ii, aa, a2 = ii2[u % 2], aa2[u % 2], a22[u % 2]
        cs = [hf_ * 4 + k for k in range(4)]
        for k, c in enumerate(cs):
            P.tt("dve", ii[:, k, :w], ii[:, k, :w], a2[:, k, :w], ALU.mult, part=True)
            P.scan(hs[:, k, :w], aa[:, k, :w], ii[:, k, :w], carry[:, c:c + 1], ALU.mult, ALU.add, part=(k > 0))
            P.copy("dve", carry[:, c:c + 1], hs[:, k, w - 1:w], part=True)
        for k, c in enumerate(cs):
            P.tt("pool", h[:, c, :w], hs[:, k, :w], g[:, c, :w], ALU.mult, part=True)
        if hf_ == 1:
            P.dma("sp", HRv[:, :, pos:pos + w], h[:, :, :w], part=True)

    for u in range(NU + 2):
        if u < NU:
            stA(u)
        if 1 <= u <= NU:
            stB(u - 1)
        if u >= 2:
            stC(u - 2)
    P.flush()
    P.release(m)


def phase_mlstm(P, C, din, XM, SOG, HM):
    m = P.mark()
    XMv, SOGv, HMv = fm(XM), fm(SOG), fm(HM)
    bd = {}
    for nm in ("bdq", "bdk", "bdv"):
        bd[nm] = P.sb("m_" + nm, [128, 8, 128], BF16)
        P.dma("pool", bd[nm][:], din[nm][:])
    pw = [P.ps(f"m_pw{i}", [128, 512], F32) for i in range(2)]
    pbc = P.ps("m_pbc", [128, 512], F32)
    pg = pbc
    pS = P.ps("m_pS", [128, 4, 128], F32)
    pT = View(pS.h[:].rearrange("p a b -> p (a b)").bitcast(BF16).rearrange("p (c t) -> p c t", c=8), pS.buf)
    pns = [P.ps(f"m_pn{i}", [128, 512], F32) for i in range(2)]
    pds = [P.ps(f"m_pd{i}", [128, 512], F32) for i in range(2)]
    m2 = P.mark()
    bdT = P.sb("m_bdT", [128, 3, 8, 128], F32)
    P.dma("sp", bdT[:], din["bdT"][:])
    wif = P.sb("m_wif", [128, 24, 8], F32)
    P.dma("sp", wif[:], din["a_w_if"][:])
    wg = P.sb("m_wg", [128, 2, 8, 8], BF16)
    for c in range(8):
        P.mm(pg[:, c * 8:(c + 1) * 8], bdT[:, 0, c, :], wif[:, c, :], True, False)
        P.mm(pg[:, c * 8:(c + 1) * 8], bdT[:, 1, c, :], wif[:, 8 + c, :], False, True)
        P.mm(pg[:, 64 + c * 8:64 + (c + 1) * 8], bdT[:, 2, c, :], wif[:, 16 + c, :], True, True)
    P.copy("dve", wg[:].re("p a c g -> p (a c g)"), pg[:, 0:128])
    bif = P.sb("m_bif", [8, 1], F32)
    P.dma("sp", bif[:], din["a_b_if"][:])
    rmask = P.sb("m_rmask", [8, 512], F32)
    P.memset("dve", rmask[:], 1.0)
    for j in range(4):
        P.op("dve", lambda e, v=rmask[:, j * 128:j * 128 + 1]: e.memset(v.ap, 0.0), wp=[rmask[:]])
    selt = P.sb("m_sel", [8, 1152], F32)
    P.dma("sp", selt[:], din["consts"][0:8, 384:1536])
    selI = [selt[0:8, 128 * h:128 * (h + 1)] for h in range(4)]
    selF = [selt[0:8, 512 + 128 * h:512 + 128 * (h + 1)] for h in range(4)]
    selI4 = selt[0:8, 1024:1028]
    selF4 = selt[0:8, 1028:1032]
    C32 = P.sb("m_C32", [128, 4, 2, 257], F32)
    Cb = P.sb("m_Cb", [128, 4, 2, 257], BF16)
    P.memset("dve", C32[:], 0.0)
    P.memset("pool", Cb[:], 0.0)
    Vaug = P.sb("m_Vaug", [128, 4, 257], BF16)
    P.memset("pool", Vaug[:], 1.0)
    xe = P.sb("m_xe", [128, 8, 515], F32)
    sog_2 = [P.sb(f"m_sog{i}", [128, 8, 512], F32) for i in range(2)]
    xcb_2 = [P.sb(f"m_xcb{i}", [128, 8, 512], BF16) for i in range(2)]
    xmb_2 = [P.sb(f"m_xmb{i}", [128, 8, 512], BF16) for i in range(2)]
    qh_2 = [P.sb(f"m_qh{i}", [128, 8, 512], BF16) for i in range(2)]
    kh_2 = [P.sb(f"m_kh{i}", [128, 8, 512], BF16) for i in range(2)]
    abc_2 = [P.sb(f"m_abc{i}", [128, 4, 512], F32) for i in range(2)]
    bbc = P.sb("m_bbc", [128, 4, 512], F32)
    hm = [P.sb("m_hm0", [128, 8, 512], BF16)] * 2
    cv = [P.sb(f"m_cv{i}", [128, 512], F32) for i in range(2)]
    Gall_2 = [P.sb(f"m_Gall{i}", [8, 512], F32) for i in range(2)]
    Lg_2 = [P.sb(f"m_Lg{i}", [8, 512], F32) for i in range(2)]
    lcar = P.sb("m_lcar", [8, 1], F32)
    btok = P.sb("m_btok", [128, 4], F32)
    Kt = P.sb("m_Kt", [128, 4, 256], BF16)
    ST = P.sb("m_ST", [128, 4, 128], BF16)
    hn = P.sb("m_hn", [128, 4, 256], BF16)
    st6 = P.sb("m_st6", [128, 6], F32)
    mv = P.sb("m_mv", [128, 2], F32)
    sc = P.sb("m_sc", [128, 4], F32)
    ctmps = [P.sb(f"m_ctmp{i}", [128, 257], F32) for i in range(2)]
    n = 0
    def prologue(ti):
        pos, w = TILES[ti]
        sog, xcb, xmb, qh, kh, abc, Gall, Lg = (sog_2[ti % 2], xcb_2[ti % 2], xmb_2[ti % 2], qh_2[ti % 2], kh_2[ti % 2], abc_2[ti % 2], Gall_2[ti % 2], Lg_2[ti % 2])
        if ti == 0:
            P.memset("pool", xe[:, :, 0:3], 0.0)
            P.dma("sp", xe[:, :, 3:3 + w], XMv[:, :, pos:pos + w], part=True)
        else:
            P.dma("sp", xe[:, :, 0:3 + w], XMv[:, :, pos - 3:pos + w])
        P.dma("sp", sog[:, :, :w], SOGv[:, :, pos:pos + w])
        for c in range(8):
            k = c % 2
            P.ts("dve", cv[k][:, :w], xe[:, c, 3:3 + w], vcol(C, "a_cw3", c), ALU.mult)
            for j in range(3):
                P.stt(cv[k][:, :w], xe[:, c, j:j + w], vcol(C, f"a_cw{j}", c), cv[k][:, :w], ALU.mult, ALU.add)
            P.act(xcb[:, c, :w], cv[k][:, :w], AF.Silu, bias=vcol(C, "a_cb", c), part=(c > 0))
            P.copy("pool", xmb[:, c, :w], xe[:, c, 3:3 + w], part=(c > 0))
            P.act(sog[:, c, :w], sog[:, c, :w], AF.Identity, scale=vcol(C, "a_ng", c), part=True)
            yield
        for c in range(8):
            P.mm(pg[0:8, :w], wg[:, 0, c, :], xcb[:, c, :w], c == 0, False)
        for c in range(8):
            P.mm(pg[0:8, :w], wg[:, 1, c, :], xmb[:, c, :w], False, c == 7)
        P.act(Gall[:, :w], pg[0:8, :w], AF.Identity, bias=bif[:, 0:1])
        P.act(Lg[:, :w], Gall[:, :w], AF.Exp, scale=-1.0)
        P.act(Lg[:, :w], Lg[:, :w], AF.Ln, bias=C.onec[0:8, 0:1])
        P.scan(Lg[:, :w], rmask[:, :w], Lg[:, :w], 0.0, ALU.mult, ALU.add)
        for h in range(4):
            P.mm(pbc[:, :w], selF[h], Lg[:, :w], True, True)
            P.act(abc[:, h, :w], pbc[:, :w], AF.Exp, scale=-1.0, part=(h > 0))
            P.mm(pbc[:, :w], selI[h], Gall[:, :w], True, False)
            P.mm(pbc[:, :w], selF[h], Lg[:, :w], False, True)
            P.act(bbc[:, h, :w], pbc[:, :w], AF.Exp, bias=C.nl16, part=(h > 0))
            yield
        for c in range(8):
            P.mm(pw[0][:, :w], bd["bdq"][:, c, :], xcb[:, c, :w])
            P.tt("dve", qh[:, c, :w], pw[0][:, :w], abc[:, c // 2, :w], ALU.mult, part=(c > 0))
            P.mm(pw[1][:, :w], bd["bdk"][:, c, :], xcb[:, c, :w])
            P.tt("dve", kh[:, c, :w], pw[1][:, :w], bbc[:, c // 2, :w], ALU.mult, part=(c > 0))
            yield
        yield

    def chunks(ti):
        pos, w = TILES[ti]
        sog, xcb, xmb, qh, kh, abc, Gall, Lg = (sog_2[ti % 2], xcb_2[ti % 2], xmb_2[ti % 2], qh_2[ti % 2], kh_2[ti % 2], abc_2[ti % 2], Gall_2[ti % 2], Lg_2[ti % 2])
        hmt = hm[ti % 2]
        for j in range((w + 127) // 128):
            L = min(128, w - j * 128)
            cs = slice(j * 128, j * 128 + L)
            P.mm(pbc[:L, 0:4], Gall[:, cs], selI4, True, False)
            P.mm(pbc[:L, 0:4], Lg[:, cs], selF4, False, True)
            P.act(btok[:L, :], pbc[:L, 0:4], AF.Exp, bias=C.nl16[:L, :])
            for g2 in range(2):
                for c4 in range(4):
                    c = g2 * 4 + c4
                    P.mm(pw[g2][:L, c4 * 128:(c4 + 1) * 128], xmb[:, c, cs], bd["bdv"][:, c, :], True, True)
                P.copy("act", Vaug[:L, 2 * g2:2 * g2 + 2, 0:256], pw[g2][:L, :].re("p (h v) -> p h v", h=2), part=True)
            for g2 in range(2):
                for c4 in range(4):
                    c = g2 * 4 + c4
                    P.mm(pw[g2][:L, c4 * 128:(c4 + 1) * 128], xcb[:, c, cs], bd["bdk"][:, c, :], True, True)
                for h2 in range(2):
                    h = 2 * g2 + h2
                    P.ts("dve", Kt[:L, h, :], pw[g2][:L, h2 * 256:(h2 + 1) * 256], btok[:L, h:h + 1], ALU.mult, part=True)
            for h in range(4):
                for dc in range(2):
                    P.mm(pS[:L, h, :L], kh[:, 2 * h + dc, cs], qh[:, 2 * h + dc, cs], dc == 0, dc == 1)
            P.tt("dve", ST[:L, :, :L], pS[:L, :, :L], bc3(C.triu[:L, :L], 4), ALU.mult)
            for h in range(4):
                pn = pns[h % 2]
                P.mm(pn[:L, 0:257], ST[:L, h, :L], Vaug[:L, h, :], True, False)
                for dc in range(2):
                    P.mm(pn[:L, 0:257], qh[:, 2 * h + dc, cs], Cb[:, h, dc, :], False, dc == 1)
                A = abc[:, h, j * 128 + L - 1:j * 128 + L]
                for dc in range(2):
                    pd, ctmp = pds[dc], ctmps[dc]
                    P.mm(pd[:, 0:257], Kt[:L, h, dc * 128:(dc + 1) * 128], Vaug[:L, h, :], True, True)
                    P.tt("dve", ctmp[:], pd[:, 0:257], C32[:, h, dc, :], ALU.add)
                    P.act(Cb[:, h, dc, :], ctmp[:], AF.Identity, scale=A, part=True)
                    P.act(C32[:, h, dc, :], ctmp[:], AF.Identity, scale=A, part=True)
                P.copy("dve", sc[:L, 3:4], pn[:L, 256:257])
                P.tt("dve", sc[:L, 0:1], sc[:L, 3:4], sc[:L, 3:4], ALU.mult, part=True)
                P.ts("dve", sc[:L, 0:1], sc[:L, 0:1], 1.0, ALU.max, part=True)
                P.op("dve", lambda e, o=sc[:L, 0:1]: e.reciprocal(o.ap, o.ap), r=[sc[:]], wp=[sc[:]])
                P.op("dve", lambda e, o=st6[:L, :], i=pn[:L, 0:256]: e.bn_stats(o.ap, i.ap), r=[pn[:]], w=[st6[:]])
                P.op("dve", lambda e, o=mv[:L, :], i=st6[:L, :]: e.bn_aggr(o.ap, i.ap), r=[st6[:]], w=[mv[:]])
                P.ts("dve", sc[:L, 1:2], sc[:L, 0:1], mv[:L, 1:2], ALU.mult, EPS, ALU.add, part=True)
                P.op("dve", lambda e, o=sc[:L, 1:2]: e.reciprocal(o.ap, o.ap), r=[sc[:]], wp=[sc[:]])
                P.tt("dve", sc[:L, 1:2], sc[:L, 1:2], sc[:L, 0:1], ALU.mult, part=True)
                P.act(sc[:L, 2:3], sc[:L, 1:2], AF.Sqrt, part=True)
                P.ts("dve", hn[:L, h, :], pn[:L, 0:256], mv[:L, 0:1], ALU.subtract, sc[:L, 2:3], ALU.mult, part=(h > 0))
                yield
            for c in range(8):
                P.op("pe", lambda e, o=pT[:, c, :L], i=hn[:L, c // 2, (c % 2) * 128:(c % 2) * 128 + 128],
                     idn=C.identb[:L, :L]: e.transpose(o.ap, i.ap, idn.ap),
                     r=[hn[:], C.identb], w=[pT[:]] if c == 0 else (), wp=[pT[:]] if c else ())
            P.tt("dve", hmt[:, :, cs], pT[:, :, :L], sog[:, :, cs], ALU.mult, part=(j > 0))
        P.dma("sp", HMv[:, :, pos:pos + w], hmt[:, :, :w], part=True)
        yield

    pro = prologue(0)
    for _ in pro:
        pass
    for ti in range(len(TILES)):
        ck = chunks(ti)
        nxt = prologue(ti + 1) if ti + 1 < len(TILES) else None
        for _ in ck:
            if nxt is not None:
                try:
                    next(nxt)
                    next(nxt)
                except StopIteration:
                    nxt = None
        if nxt is not None:
            for _ in nxt:
                pass
    P.flush()
    P.release(m)


def ln_tmps(P, pfx):
    return {"mean": P.sb(pfx + "_mean", [128, 512], F32), "m2": P.sb(pfx + "_m2", [128, 512], F32),
            "rstd": P.sb(pfx + "_rstd", [128, 512], F32)}


def phase_outproj(P, C, wsrc, KC, ins_bf, Hin, Hout, layer, loader=None):
    m = P.mark()
    wo = P.sb("o_w", [128, KC, 1024], BF16)
    load_w(P, wo, wsrc.v(None).re("(kc p) n -> p kc n", p=128), KC)
    xin = [P.sb(f"o_x{i}", [128, KC, 512], BF16) for i in range(2)]
    hf = [P.sb(f"o_h{i}", [128, 8, 512], F32) for i in range(2)]
    sq = [P.sb(f"o_sq{i}", [128, 512], F32) for i in range(2)]
    tmp = ln_tmps(P, "o")
    pss = [P.ps(f"o_ps{i}", [128, 512], F32) for i in range(4)]
    psm = P.ps("o_psm", [128, 512], F32)
    psq = P.ps("o_psq", [128, 512], F32)
    Hiv, Hov = fm(Hin), fm(Hout)
    n = 0
    for ti, (pos, w) in enumerate(TILES):
        x = xin[ti % 2]
        h = hf[ti % 2]
        if loader is not None:
            loader(ti, pos, w, x, pss)
        for i, src in enumerate(ins_bf):
            P.dma("sp", x[:, 8 * i:8 * i + 8, :w], fm(src)[:, :, pos:pos + w], part=(i > 0))
        P.dma("sp", h[:, :, :w], Hiv[:, :, pos:pos + w])
        for oc in range(8):
            ps = pss[n % 4]
            n += 1
            for kc in range(KC):
                P.mm(ps[:, :w], wo[:, kc, oc * 128:(oc + 1) * 128], x[:, kc, :w], kc == 0, kc == KC - 1)
            P.stt(h[:, oc, :w], h[:, oc, :w], ALPHA, ps[:, :w], ALU.mult, ALU.add, part=True)
        layer_norm_fm(P, C, h, w, f"mix_g{layer}", f"mix_b{layer}", h, tmp, psm, psq, sq)
        P.dma("pool", Hov[:, :, pos:pos + w], h[:, :, :w], part=True)
    P.flush()
    P.release(m)


def phase_ffn(P, C, din, layer, Hin, Hout, final_out=None):
    m = P.mark()
    wu = P.sb("f_wu", [128, 8, 2 * DFF], BF16)
    wd = P.sb("f_wd", [128, NFC, 1024], BF16)
    load_w(P, wu, din["ffn_w_up"].v(None)[layer].re("(kc p) n -> p kc n", p=128), 8)
    load_w(P, wd, din["ffn_w_down"].v(None)[layer].re("(kc p) n -> p kc n", p=128), NFC)
    hf = P.sb("f_h", [128, 8, 512], F32)
    hb = P.sb("f_hb", [128, 8, 512], BF16)
    hh = P.sb("f_hh", [128, NFC, 512], BF16)
    gx = [P.sb(f"f_gx{i}", [128, 514], F32) for i in range(2)]
    ac = [P.sb(f"f_ac{i}", [128, 512], F32) for i in range(2)]
    halo = P.sb("f_halo", [128, NFC, 2], F32)
    P.memset("dve", halo[:], 0.0)
    sq = [P.sb(f"f_sq{i}", [128, 512], F32) for i in range(2)]
    tmp = ln_tmps(P, "f")
    pgs = [P.ps(f"f_pg{i}", [128, 512], F32) for i in range(2)]
    pus = [P.ps(f"f_pu{i}", [128, 512], F32) for i in range(2)]
    pys = [P.ps(f"f_py{i}", [128, 512], F32) for i in range(2)]
    psm = P.ps("f_psm", [128, 512], F32)
    psq = P.ps("f_psq", [128, 512], F32)
    Hiv = fm(Hin)
    if final_out is None:
        Hov = fm(Hout)
    else:
        ot = P.sb("f_ot0", [128, 1024], F32)
    cw = [f"f_cw{layer}_{j}" for j in range(3)]
    cnt = {"n": 0}

    def load_hb(ti):
        pos, w = TILES[ti]
        for c in range(8):
            P.dma("pool", hb[:, c, :w], Hiv[:, c, pos:pos + w], part=(c > 0), max_dma_last_dim=4096)

    def stage_up(ti):
        pos, w = TILES[ti]
        for fc in range(NFC):
            k = cnt["n"] % 2
            cnt["n"] += 1
            pg, pu, g, a = pgs[k], pus[k], gx[k], ac[k]
            for kc in range(8):
                P.mm(pg[:, :w], wu[:, kc, DFF + fc * 128:DFF + (fc + 1) * 128], hb[:, kc, :w], kc == 0, kc == 7)
            for kc in range(8):
                P.mm(pu[:, :w], wu[:, kc, fc * 128:(fc + 1) * 128], hb[:, kc, :w], kc == 0, kc == 7)
            P.copy("act", g[:, 0:2], halo[:, fc, :])
            P.copy("act", g[:, 2:2 + w], pg[:, :w], part=True)
            P.copy("pool", halo[:, fc, :], g[:, w:w + 2], part=True)
            P.ts("dve", a[:, :w], g[:, 2:2 + w], vcol(C, cw[2], fc), ALU.mult)
            P.stt(a[:, :w], g[:, 1:1 + w], vcol(C, cw[1], fc), a[:, :w], ALU.mult, ALU.add)
            P.stt(a[:, :w], g[:, 0:w], vcol(C, cw[0], fc), a[:, :w], ALU.mult, ALU.add)
            P.act(a[:, :w], a[:, :w], AF.Gelu_apprx_tanh, bias=vcol(C, f"f_cb{layer}", fc))
            P.tt("dve", hh[:, fc, :w], a[:, :w], pu[:, :w], ALU.mult, part=(fc > 0))
            yield

    def stage_down(ti):
        pos, w = TILES[ti]
        P.dma("sp", hf[:, :, :w], Hiv[:, :, pos:pos + w])
        if ti + 1 < len(TILES):
            load_hb(ti + 1)
        for oc in range(8):
            py = pys[oc % 2]
            for fc in range(NFC):
                P.mm(py[:, :w], wd[:, fc, oc * 128:(oc + 1) * 128], hh[:, fc, :w], fc == 0, fc == NFC - 1)
            P.stt(hf[:, oc, :w], hf[:, oc, :w], ALPHA, py[:, :w], ALU.mult, ALU.add, part=True)

    def stage_ln(ti):
        pos, w = TILES[ti]
        for _ in ln_gen(P, C, hf, w, f"ffn_g{layer}", f"ffn_b{layer}", hf, tmp, psm, psq, sq):
            yield
        if final_out is None:
            P.dma("pool", Hov[:, :, pos:pos + w], hf[:, :, :w], part=True)
        else:
            for j in range((w + 127) // 128):
                L = min(128, w - j * 128)
                p0 = pos + j * 128
                o = ot
                for half in range(2):
                    py = pys[half]
                    for c4 in range(4):
                        c = half * 4 + c4
                        P.op("pe", lambda e, o_=py[:L, c4 * 128:(c4 + 1) * 128], i=hf[:, c, j * 128:j * 128 + L],
                             idn=C.ident: e.transpose(o_.ap, i.ap, idn.ap),
                             r=[hf[:], C.ident], w=[py[:]] if c4 == 0 else (), wp=[py[:]] if c4 else ())
                    if half == 0:
                        P.copy("act", o[:L, 0:512], py[:L, :], part=False)
                    else:
                        P.copy("dve", o[:L, 512:1024], py[:L, :], part=True)
                lo = max(p0, NMETA)
                hi = p0 + L
                if hi > lo:
                    P.dma("pool", final_out[lo - NMETA:hi - NMETA, :], o[lo - p0:hi - p0, :], part=True)
                yield

    load_hb(0)
    prev_ln = None
    for ti in range(len(TILES)):
        up = stage_up(ti)
        for _ in up:
            if prev_ln is not None:
                try:
                    next(prev_ln)
                except StopIteration:
                    prev_ln = None
        if prev_ln is not None:
            for _ in prev_ln:
                pass
        stage_down(ti)
        prev_ln = stage_ln(ti)
    for _ in prev_ln:
        pass
    P.flush()
    P.release(m)


IN_SHAPES = {
    "x": [SEQ, D], "meta": [NMETA, D], "consts": [128, 1536], "vecs": None,
    "a_w_in": [D, 4096], "bdq": [128, 8, 128], "bdk": [128, 8, 128], "bdv": [128, 8, 128],
    "bdT": [128, 3, 8, 128], "a_w_if": [128, 24, 8], "a_b_if": [8, 1],
    "b_w_a": [128, 8, 128], "b_w_x": [128, 8, 128], "ab_w_out": [2 * D, D],
    "ffn_w_up": [2, D, 2 * DFF], "ffn_w_down": [2, DFF, D],
    "c_w_in": [D, 712], "c_w_qidx": [384, 512], "c_w_uq": [384, 1024], "c_w_ukT": [128, 8, 256],
    "c_w_uv": [128, 8, 2, 128], "c_w_out": [D, D],
}


def build(stop_after=None, dbg=()):
    nc = bass.Bass("TRN2", target_bir_lowering=False)
    P = Prog(nc)
    C = Ctx()
    _, nv = vec_layout()
    din = {}
    for k, shp in IN_SHAPES.items():
        if k == "vecs":
            shp = [128, nv]
        din[k] = P.dram(k, shp, F32, kind="ExternalInput")
    out = P.dram("out", [SEQ, D], F32, kind="ExternalOutput")

    def scratch(name, dt=F32):
        kind = "ExternalOutput" if name in dbg else "Internal"
        return P.dram(name, [D, T], dt, kind=kind)

    def scratch2(name, shape, dt):
        kind = "ExternalOutput" if name in dbg else "Internal"
        return P.dram(name, shape, dt, kind=kind)

    setup_consts(P, C, din)
    H0 = scratch("H0")
    phase_embed(P, C, din, H0)
    XM, SOG, XR, GGR = scratch("XM"), scratch("SOG"), scratch("XR"), scratch("GGR")
    phase_inproj(P, C, din, H0, XM, SOG, XR, GGR)
    HR = scratch("HR", BF16)
    phase_rglru(P, C, din, XR, GGR, HR)
    HM = scratch("HM", BF16)
    phase_mlstm(P, C, din, XM, SOG, HM)
    H1 = scratch("H1")
    phase_outproj(P, C, din["ab_w_out"], 16, [HM, HR], H0, H1, 0)
    H2 = scratch("H2")
    if stop_after == "L0":
        phase_ffn(P, C, din, 0, H1, H2)
        P.finish()
        return nc
    phase_ffn(P, C, din, 0, H1, H2)
    CKVT = scratch2("CKVT", [256, T], BF16)
    CKV = scratch2("CKV", [T, 256], BF16)
    KIDXT = scratch2("KIDXT", [64, T], BF16)
    WIDX = scratch2("WIDX", [T, 8], F32)
    QIDXT = scratch2("QIDXT", [512, T], BF16)
    QLATT = scratch2("QLATT", [2048, T], BF16)
    OLATT = scratch2("OLATT", [2048, T], BF16)
    phase_dsa_proj(P, C, din, H2, CKVT, CKV, KIDXT, WIDX, QIDXT, QLATT)
    if stop_after == "L1proj":
        P.finish()
        return nc
    phase_dsa_attn(P, C, din, CKVT, CKV, KIDXT, WIDX, QIDXT, QLATT, OLATT)
    if stop_after == "L1attn":
        P.finish()
        return nc
    H3 = scratch("H3")
    muv = P.mark()
    wuv = P.sb("u_wuv", [128, 8, 2, 128], BF16)
    P.dma("pool", wuv[:], din["c_w_uv"][:])
    olb = [P.sb(f"u_ol{i}", [128, 2, 8, 512], BF16) for i in range(2)]
    OLATv = OLATT.v(None).re("(p a) t -> p a t", a=16)

    def uv_loader(ti, pos, w, x, pss):
        ol = olb[ti % 2]
        P.dma("sp", ol[:, :, :, :w].re("p rc h t -> p (rc h) t"), OLATv[:, :, pos:pos + w])
        for h in range(8):
            ps = pss[h % 4]
            for rc in range(2):
                P.mm(ps[:, :w], wuv[:, h, rc, :], ol[:, rc, h, :w], rc == 0, rc == 1)
            if h % 2:
                P.copy("act", x[:, h, :w], ps[:, :w], part=(h > 0))
            else:
                P.copy("dve", x[:, h, :w], ps[:, :w], part=(h > 0))

    phase_outproj(P, C, din["c_w_out"], 8, [], H2, H3, 1, loader=uv_loader)
    P.release(muv)
    if stop_after == "L1mix":
        P.finish()
        return nc
    phase_ffn(P, C, din, 1, H3, None, final_out=out.v(None))
    P.finish()
    return nc


def host_inputs(inp):
    f = lambda a: np.ascontiguousarray(np.asarray(a, np.float32))
    vp = VecPack()
    pack_vecs(vp, inp)
    bds = [block_diag_host(inp[k][0]) for k in ("a_wq", "a_wk", "a_wv")]
    shared = {
        "meta": f(inp["meta_tokens"]), "consts": make_consts(), "vecs": vp.array(),
        "a_w_in": f(inp["a_w_in"][0]),
        "bdq": f(bds[0].transpose(1, 0, 2)), "bdk": f(bds[1].transpose(1, 0, 2)), "bdv": f(bds[2].transpose(1, 0, 2)),
        "bdT": f(np.stack([b.transpose(2, 0, 1) for b in bds], axis=1)),
        "a_w_if": f(np.asarray(inp["a_w_if"][0]).reshape(24, 128, 8).transpose(1, 0, 2)),
        "a_b_if": f(np.asarray(inp["a_b_if"][0]).reshape(8, 1)),
        "b_w_a": f(np.asarray(inp["b_w_a"][0]).transpose(1, 0, 2)),
        "b_w_x": f(np.asarray(inp["b_w_x"][0]).transpose(1, 0, 2)),
        "ab_w_out": f(inp["ab_w_out"][0]),
        "ffn_w_up": f(inp["ffn_w_up"]), "ffn_w_down": f(inp["ffn_w_down"]),
        "c_w_in": f(inp["c_w_in"][0]), "c_w_qidx": f(inp["c_w_qidx"][0]), "c_w_uq": f(inp["c_w_uq"][0]),
        "c_w_ukT": f(np.asarray(inp["c_w_uk"][0]).transpose(2, 0, 1)),
        "c_w_uv": f(np.asarray(inp["c_w_uv"][0]).reshape(8, 2, 128, 128).transpose(2, 0, 1, 3)),
        "c_w_out": f(inp["c_w_out"][0]),
    }
    return shared


_NC_CACHE = {}


def kernel(**inp):
    shared = host_inputs(inp)
    x = np.asarray(inp["x"], np.float32)
    if "full" not in _NC_CACHE:
        _NC_CACHE["full"] = build()
    nc = _NC_CACHE["full"]
    in_maps = []
    for b in range(8):
        d = dict(shared)
        d["x"] = np.ascontiguousarray(x[b])
        in_maps.append(d)
    res = run_bass_kernel_spmd(nc, in_maps, core_ids=list(range(8)))
    return np.stack([res.results[b]["out"] for b in range(8)], axis=0)


CQ_R, CKV_R = 384, 256


def phase_dsa_proj(P, C, din, H2, CKVT, CKV, KIDXT, WIDX, QIDXT, QLATT):
    m = P.mark()
    wci = P.sb("d_wci", [128, 8, 712], BF16)
    load_w(P, wci, din["c_w_in"].v(None).re("(kc p) n -> p kc n", p=128), 8)
    wqi = P.sb("d_wqi", [128, 3, 512], BF16)
    load_w(P, wqi, din["c_w_qidx"].v(None).re("(kc p) n -> p kc n", p=128), 3)
    wuq = P.sb("d_wuq", [128, 3, 1024], BF16)
    load_w(P, wuq, din["c_w_uq"].v(None).re("(kc p) n -> p kc n", p=128), 3)
    wuk = P.sb("d_wuk", [128, 8, 256], BF16)
    P.dma("pool", wuk[:], din["c_w_ukT"][:])
    m384 = P.sb("d_m384", [128, 128], F32)
    P.memset("dve", m384[:], 1.0 / 384.0)
    m256 = P.sb("d_m256", [128, 128], F32)
    P.memset("dve", m256[:], 1.0 / 256.0)
    m64 = P.sb("d_m64", [64, 64], F32)
    P.memset("dve", m64[:], 1.0 / 64.0)
    hf = P.sb("d_hf", [128, 8, 512], F32)
    hb = P.sb("d_hb", [128, 8, 512], BF16)
    raw = P.sb("d_raw", [128, 3, 512], F32)
    sq = [P.sb(f"d_sq{i}", [128, 512], F32) for i in range(2)]
    rs = P.sb("d_rs", [128, 512], F32)
    mn = P.sb("d_mn", [128, 512], F32)
    cqn = P.sb("d_cqn", [128, 3, 512], BF16)
    kvn = [P.sb(f"d_kvn{i}", [128, 2, 512], BF16) for i in range(2)]
    kvt = [P.sb(f"d_kvt{i}", [128, 4, 256], BF16) for i in range(2)]
    kin = [P.sb(f"d_kin{i}", [64, 512], BF16) for i in range(2)]
    wi = [P.sb(f"d_wi{i}", [128, 4, 8], F32) for i in range(2)]
    qix = [P.sb(f"d_qix{i}", [128, 4, 512], BF16) for i in range(2)]
    qf = P.sb("d_qf", [128, 8, 512], BF16)
    qlt = [P.sb(f"d_qlt{i}", [128, 2, 8, 512], BF16) for i in range(2)]
    pss = [P.ps(f"d_ps{i}", [128, 512], F32) for i in range(4)]
    pst = P.ps("d_pst", [128, 512], F32)
    ptr = P.ps("d_ptr", [128, 8, 128], BF16)
    pwi = P.ps("d_pwi", [128, 512], F32)
    H2v = fm(H2)
    CKVTv = fm(CKVT)
    CKVv = CKV.v(None)
    WIDXv = WIDX.v(None)
    QIDXv = fm(QIDXT)
    QLATv = QLATT.v(None).re("(p a) t -> p a t", a=16)
    KIv = KIDXT.v(None)
    n = 0

    def nxt():
        nonlocal n
        n += 1
        return pss[n % 4]

    for ti, (pos, w) in enumerate(TILES):
        k2 = ti % 2
        P.dma("sp", hf[:, :, :w], H2v[:, :, pos:pos + w])
        P.copy("pool", hb[:, :, :w], hf[:, :, :w])

        def proj_chunk(ps, c0, mcols):
            for kc in range(8):
                P.mm(ps[:mcols, :w], wci[:, kc, c0:c0 + mcols], hb[:, kc, :w], kc == 0, kc == 7)

        def rms(nch, c0, mmat, gname, outt):
            for c in range(nch):
                ps = nxt()
                proj_chunk(ps, c0 + c * 128, 128)
                P.copy("act", raw[:, c, :w], ps[:, :w], part=(c > 0))
                s = sq[c % 2]
                P.tt("pool", s[:, :w], raw[:, c, :w], raw[:, c, :w], ALU.mult)
                P.mm(pst[:, :w], mmat[:], s[:, :w], c == 0, c == nch - 1)
            P.act(rs[:, :w], pst[:, :w], AF.Sqrt, bias=C.epsc)
            P.op("dve", lambda e, o=rs[:, :w]: e.reciprocal(o.ap, o.ap), r=[rs[:]], w=[rs[:]])
            for c in range(nch):
                P.stt(outt[:, c, :w], raw[:, c, :w], vcol(C, gname, c), rs[:, :w], ALU.mult, ALU.mult, part=(c > 0))

        rms(3, 0, m384, "c_qg", cqn)
        kv = kvn[k2]
        rms(2, CQ_R, m256, "c_kvg", kv)
        P.dma("pool", CKVTv[:, :, pos:pos + w], kv[:, :, :w], part=True)
        kt = kvt[k2]
        nb = (w + 127) // 128
        for j in range(nb):
            L = min(128, w - j * 128)
            for rc in range(2):
                P.op("pe", lambda e, o=ptr[:L, j * 2 + rc, :], i=kv[:, rc, j * 128:j * 128 + L], idn=C.identb:
                     e.transpose(o.ap, i.ap, idn.ap), r=[kv[:], C.identb],
                     w=[ptr[:]] if (j == 0 and rc == 0) else (), wp=[ptr[:]] if (j or rc) else ())
        if w == 512:
            P.copy("act", kt[:].re("p j r -> p (j r)"), ptr[:].re("p a b -> p (a b)"))
            P.dma("pool", CKVv[pos:pos + 512, :].re("(j p) r -> p j r", p=128), kt[:], part=True)
        else:
            P.copy("act", kt[:w, 0, :], ptr[:w, 0:2, :].re("p a b -> p (a b)"))
            P.dma("pool", CKVv[pos:pos + w, :], kt[:w, 0, :], part=True)
        ps = nxt()
        proj_chunk(ps, CQ_R + CKV_R, 64)
        P.copy("act", raw[0:64, 0, :w], ps[0:64, :w])
        P.tt("pool", sq[0][0:64, :w], raw[0:64, 0, :w], raw[0:64, 0, :w], ALU.mult)
        P.mm(pst[0:64, :w], m64[:], raw[0:64, 0, :w], True, True)
        P.copy("act", mn[0:64, :w], pst[0:64, :w])
        P.mm(pst[0:64, :w], m64[:], sq[0][0:64, :w], True, True)
        P.tt("pool", sq[1][0:64, :w], mn[0:64, :w], mn[0:64, :w], ALU.mult)
        P.tt("dve", rs[0:64, :w], pst[0:64, :w], sq[1][0:64, :w], ALU.subtract)
        P.act(rs[0:64, :w], rs[0:64, :w], AF.Sqrt, bias=C.epsc[0:64, :])
        P.op("dve", lambda e, o=rs[0:64, :w]: e.reciprocal(o.ap, o.ap), r=[rs[:]], w=[rs[:]])
        P.tt("dve", sq[0][0:64, :w], raw[0:64, 0, :w], mn[0:64, :w], ALU.subtract)
        P.stt(sq[0][0:64, :w], sq[0][0:64, :w], vcol(C, "c_kig")[0:64, :], rs[0:64, :w], ALU.mult, ALU.mult)
        ki = kin[k2]
        P.act(ki[:, :w], sq[0][0:64, :w], AF.Identity, bias=vcol(C, "c_kib")[0:64, :])
        P.dma("pool", KIv[:, pos:pos + w], ki[:, :w], part=True)
        wt = wi[k2]
        for j in range(nb):
            L = min(128, w - j * 128)
            for kc in range(8):
                P.mm(pwi[:L, j * 8:(j + 1) * 8], hb[:, kc, j * 128:j * 128 + L], wci[:, kc, 704:712], kc == 0, kc == 7)
        if w == 512:
            P.act(wt[:].re("p j g -> p (j g)"), pwi[:, 0:32], AF.Copy, scale=float(8.0 ** -0.5))
            P.dma("pool", WIDXv[pos:pos + 512, :].re("(j p) g -> p j g", p=128), wt[:], part=True)
        else:
            P.act(wt[:w, 0, :], pwi[:w, 0:8], AF.Copy, scale=float(8.0 ** -0.5))
            P.dma("pool", WIDXv[pos:pos + w, :], wt[:w, 0, :], part=True)
        qx = qix[k2]
        for oc in range(4):
            ps = nxt()
            for kc in range(3):
                P.mm(ps[:, :w], wqi[:, kc, oc * 128:(oc + 1) * 128], cqn[:, kc, :w], kc == 0, kc == 2)
            if oc % 2:
                P.copy("act", qx[:, oc, :w], ps[:, :w], part=(oc > 0))
            else:
                P.copy("dve", qx[:, oc, :w], ps[:, :w], part=(oc > 0))
        P.dma("pool", QIDXv[:, :, pos:pos + w], qx[:, :, :w], part=True)
        for h in range(8):
            ps = nxt()
            for kc in range(3):
                P.mm(ps[:, :w], wuq[:, kc, h * 128:(h + 1) * 128], cqn[:, kc, :w], kc == 0, kc == 2)
            if h % 2:
                P.copy("act", qf[:, h, :w], ps[:, :w], part=(h > 0))
            else:
                P.copy("dve", qf[:, h, :w], ps[:, :w], part=(h > 0))
        ql = qlt[k2]
        for h in range(8):
            for rc in range(2):
                ps = nxt()
                P.mm(ps[:, :w], wuk[:, h, rc * 128:(rc + 1) * 128], qf[:, h, :w])
                first = (h == 0 and rc == 0)
                if rc:
                    P.act(ql[:, rc, h, :w], ps[:, :w], AF.Copy, scale=float(128.0 ** -0.5), part=not first)
                else:
                    P.ts("dve", ql[:, rc, h, :w], ps[:, :w], float(128.0 ** -0.5), ALU.mult, part=not first)
        P.dma("pool", QLATv[:, :, pos:pos + w], ql[:, :, :, :w].re("p rc h t -> p (rc h) t"), part=True)
    P.flush()
    P.release(m)


NKB = 33
NEG = -1.0e30


def phase_dsa_attn(P, C, din, CKVT, CKV, KIDXT, WIDX, QIDXT, QLATT, OLATT):
    m = P.mark()
    kidx2 = P.sb("t_kidx2", [128, T], BF16)
    P.dma("sp", kidx2[0:64, :], KIDXT.v(None)[:, :], part=True)
    P.dma("sp", kidx2[64:128, :], KIDXT.v(None)[:, :], part=True)
    ckvT = P.sb("t_ckvT", [128, 2, T], BF16)
    P.dma("sp", ckvT[:], fm(CKVT)[:, :, :])
    ckv = P.sb("t_ckv", [128, NKB, 256], BF16)
    for i in range(4):
        P.dma("sp", ckv[:, 8 * i:8 * i + 8, :],
              CKV.v(None)[1024 * i:1024 * (i + 1), :].re("(kb p) r -> p kb r", p=128), part=True)
    P.dma("sp", ckv[0:16, 32, :], CKV.v(None)[4096:4112, :], part=True)
    qix = [P.sb(f"t_qix{i}", [128, 4, 128], BF16) for i in range(2)]
    wid = [P.sb(f"t_wid{i}", [128, 8], F32) for i in range(2)]
    qlt = [P.sb(f"t_qlt{i}", [128, 2, 8, 128], BF16) for i in range(3)]
    score = [P.sb(f"t_score{i}", [128, T], F32) for i in range(2)]
    junk = P.sb("t_junk", [128, T], BF16)
    msk = P.sb("t_msk", [128, T + 16], BF16)
    mskTs = [P.sb(f"t_mskT{i}", [128, 40, 128], BF16) for i in range(2)]
    rl = [P.sb(f"t_rl{i}", [128, 512], F32) for i in range(2)]
    ex = [P.sb(f"t_ex{i}", [128, 4, 128], BF16) for i in range(3)]
    pp = [P.sb(f"t_pp{i}", [128, 4, 128], BF16) for i in range(3)]
    bss = [P.sb(f"t_bs{i}", [128, 8], F32) for i in range(2)]
    rden = P.sb("t_rden", [128, 8], F32)
    fr = P.sb("t_fr", [128, NBIS], F32)
    for r in range(1, NBIS + 1):
        P.op("dve", lambda e, o=fr[:, r - 1:r], v=float(2.0 ** -r): e.memset(o.ap, v), wp=[fr[:]])
    stepf = P.sb("t_stepf", [128, NBIS], F32)
    on = P.sb("t_on", [128, 8, 256], BF16)
    olT = [P.sb(f"t_olT{i}", [128, 2, 8, 128], BF16) for i in range(2)]
    zb = P.sb("t_zb", [128, 512], BF16)
    P.memset("pool", zb[:], 0.0)
    po = [P.ps(f"t_po{i}", [128, 2, 256], F32) for i in range(4)]
    pden = P.ps("t_pden", [128, 512], F32)
    pl = [P.ps(f"t_pl{i}", [128, 4, 128], F32) for i in range(2)]
    px = P.ps("t_px", [128, 512], F32)
    pxs = [px, px]
    pxb = View(px.h[:].bitcast(BF16), px.buf)
    QIDXv = fm(QIDXT)
    QLATv = QLATT.v(None).re("(p a) t -> p a t", a=16)
    OLATv = OLATT.v(None).re("(p a) t -> p a t", a=16)
    WIDXv = WIDX.v(None)
    cnt = {"nl": 0}

    def geom(qi):
        if qi == 0:
            return 0, 16, 16
        return 16 + 128 * (qi - 1), 128, 16 + 128 * qi

    def stageA1(qi):
        q0, nq, nk = geom(qi)
        k2 = qi % 2
        qx, wd, ql, sc = qix[k2], wid[k2], qlt[qi % 3], score[k2]
        P.dma("sp", qx[:, :, :nq], QIDXv[:, :, q0:q0 + nq])
        P.dma("sp", wd[:nq, :], WIDXv[q0:q0 + nq, :])
        P.dma("sp", ql[:, :, :, :nq].re("p rc h t -> p (rc h) t"), QLATv[:, :, q0:q0 + nq])
        for c0 in range(0, nk, 512):
            cw = min(512, nk - c0)
            for h in range(8):
                oc, half = h // 2, h % 2
                lo_, hi_ = half * 64, half * 64 + 64
                px_ = pxs[h % 2]
                P.mm(px_[:nq, :cw], qx[lo_:hi_, oc, :nq], kidx2[lo_:hi_, c0:c0 + cw])
                r_ = rl[h % 2]
                P.act(r_[:nq, :cw], px_[:nq, :cw], AF.Relu)
                if h == 0:
                    P.ts("dve", sc[:nq, c0:c0 + cw], r_[:nq, :cw], wd[:nq, 0:1], ALU.mult, part=(c0 > 0))
                else:
                    P.stt(sc[:nq, c0:c0 + cw], r_[:nq, :cw], wd[:nq, h:h + 1], sc[:nq, c0:c0 + cw],
                          ALU.mult, ALU.add, part=True)
                yield 1.0

    def stageA2(qi):
        q0, nq, nk = geom(qi)
        k2 = qi % 2
        sc, bs, mskT = score[k2], bss[k2], mskTs[k2]
        P.op("dve", lambda e, o=bs[:nq, 0:1], i=sc[:nq, :nk]: e.tensor_reduce(o.ap, i.ap, AX.X, ALU.max),
             r=[sc[:]], w=[bs[:]])
        P.op("dve", lambda e, o=bs[:nq, 1:2], i=sc[:nq, :nk]: e.tensor_reduce(o.ap, i.ap, AX.X, ALU.min),
             r=[sc[:]], wp=[bs[:]])
        if qi >= 1:
            P.op("dve", lambda e, o=sc[0:64, nk - 64:nk]: e.memset(o.ap, NEG), r=[bs[:]], wp=[sc[:]])
        P.tt("dve", bs[:nq, 2:3], bs[:nq, 0:1], bs[:nq, 1:2], ALU.subtract, part=True)
        yield 7.0
        if nk > KSEL:
            P.ts("dve", stepf[:nq, :], fr[:nq, :], bs[:nq, 2:3], ALU.mult)
            P.tt("dve", bs[:nq, 3:4], bs[:nq, 1:2], stepf[:nq, 0:1], ALU.add, part=True)
            for r in range(1, NBIS + 1):
                P.ts("dve", junk[:nq, :nk], sc[:nq, :nk], bs[:nq, 3:4], ALU.is_ge, 0.0, ALU.add,
                     accum=bs[:nq, 4:5])
                last = (r == NBIS)
                P.ts("dve", bs[:nq, 5:6], bs[:nq, 4:5], KSEL - 0.5, ALU.is_ge, -1.0 if last else -0.5, ALU.add,
                     part=True)
                P.stt(bs[:nq, 1:2] if last else bs[:nq, 3:4], bs[:nq, 5:6], stepf[:nq, r - 1:r], bs[:nq, 3:4],
                      ALU.mult, ALU.add, part=True)
                yield nk * 1.6 / 960.0 + 0.8
        P.ts("dve", msk[:nq, :nk], sc[:nq, :nk], bs[:nq, 1:2], ALU.is_ge)
        nkb = (nk + 127) // 128
        for b0 in range(0, nkb, 8):
            nb_ = min(8, nkb - b0)
            for j in range(nb_):
                kb = b0 + j
                kl = min(128, nk - kb * 128)
                P.op("pe", lambda e, o=pxb[:kl, j * 128:j * 128 + nq], i=msk[:nq, kb * 128:kb * 128 + kl],
                     idn=C.identb[:nq, :nq]: e.transpose(o.ap, i.ap, idn.ap), r=[msk[:], C.identb],
                     w=[px[:]] if j == 0 else (), wp=[px[:]] if j else ())
            P.copy("act", mskT[:, b0:b0 + nb_, :].re("p a b -> p (a b)"), pxb[:, 0:nb_ * 128], part=(b0 > 0))
            yield 3.0

    def stageB(qi):
        q0, nq, nk = geom(qi)
        k2 = qi % 2
        ql, mskT = qlt[qi % 3], mskTs[k2]
        nkb = (nk + 127) // 128
        for i in range(4):
            P.mm(po[i][:, :, :].re("p a b -> p (a b)"), zb[:, 0:128], zb[:, :], True, False)
        P.mm(pden[:, :], zb[:, 0:128], zb[:, :], True, False)
        steps = [(kb, g) for kb in range(nkb) for g in range(2)]
        base = cnt["nl"]
        cnt["nl"] += len(steps)

        def front(n):
            kb, g = steps[n]
            kl = min(128, nk - kb * 128)
            nl = base + n
            p_l, e_, p_ = pl[nl % 2], ex[nl % 3], pp[nl % 3]
            for rc in range(2):
                P.mm(p_l[:kl, :, :nq], ckvT[:, rc, kb * 128:kb * 128 + kl], ql[:, rc, 4 * g:4 * g + 4, :nq],
                     rc == 0, rc == 1)
            P.act(e_[:kl, :, :nq], p_l[:kl, :, :nq], AF.Exp)
            P.tt("pool", p_[:kl, :, :nq], e_[:kl, :, :nq], bc3(mskT[:kl, kb, :nq], 4), ALU.mult)

        def back(n):
            kb, g = steps[n]
            kl = min(128, nk - kb * 128)
            p_ = pp[(base + n) % 3]
            for h4 in range(4):
                h = 4 * g + h4
                P.mm(po[h // 2][:nq, h % 2, :], p_[:kl, h4, :nq], ckv[:kl, kb, :], False, kb == nkb - 1)
                P.mm(pden[:nq, h:h + 1], p_[:kl, h4, :nq], C.onesb[:kl, 0:1], False, kb == nkb - 1)

        front(0)
        for n in range(len(steps)):
            if n + 1 < len(steps):
                front(n + 1)
            back(n)
            yield 3.0
        P.op("dve", lambda e, o=rden[:nq, :], i=pden[:nq, 0:8]: e.reciprocal(o.ap, i.ap), r=[pden[:]], w=[rden[:]])
        for h in range(8):
            if (h // 2) % 2:
                P.act(on[:nq, h, :], po[h // 2][:nq, h % 2, :], AF.Identity, scale=rden[:nq, h:h + 1], part=(h > 0))
            else:
                P.ts("dve", on[:nq, h, :], po[h // 2][:nq, h % 2, :], rden[:nq, h:h + 1], ALU.mult, part=(h > 0))
        yield 4.0
        ot = olT[k2]
        for rc in range(2):
            for h in range(8):
                P.op("pe", lambda e, o=pxb[:, h * 128:h * 128 + nq], i=on[:nq, h, rc * 128:(rc + 1) * 128],
                     idn=C.identb[:nq, :nq]: e.transpose(o.ap, i.ap, idn.ap), r=[on[:], C.identb],
                     w=[px[:]] if h == 0 else (), wp=[px[:]] if h else ())
            src = pxb[:, :].re("p (h q) -> p h q", h=8)[:, :, :nq]
            if rc:
                P.copy("act", ot[:, rc, :, :nq], src, part=True)
            else:
                P.copy("dve", ot[:, rc, :, :nq], src, part=False)
            yield 3.0
        P.dma("pool", OLATv[:, :, q0:q0 + nq], ot[:, :, :, :nq].re("p rc h t -> p (rc h) t"), part=True)

    def total(qi, which):
        q0, nq, nk = geom(qi)
        nkb = (nk + 127) // 128
        if which == "A1":
            return ((nk + 511) // 512) * 8 * 1.0
        if which == "A2":
            return 7.0 + (NBIS * (nk * 1.6 / 960.0 + 0.8) if nk > KSEL else 0.0) + ((nkb + 7) // 8) * 3.0
        return nkb * 2 * 3.0 + 4.0 + 6.0

    for it in range(35):
        live = []
        if it < 33:
            live.append([stageA1(it), 0.0, total(it, "A1")])
        if 1 <= it < 34:
            live.append([stageA2(it - 1), 0.0, total(it - 1, "A2")])
        if it >= 2:
            live.append([stageB(it - 2), 0.0, total(it - 2, "B")])
        while live:
            g_ = min(live, key=lambda t: t[1] / t[2])
            try:
                g_[1] += next(g_[0])
            except StopIteration:
                live.remove(g_)
    P.flush()
    P.release(m)
```

```python
import numpy as np
import concourse.bass as bass
import concourse.mybir as mybir
from concourse.bass_utils import run_bass_kernel_spmd

F32 = mybir.dt.float32
BF16 = mybir.dt.bfloat16
AF = mybir.ActivationFunctionType
ALU = mybir.AluOpType
AX = mybir.AxisListType

SEM_LIMIT = 30000


class Buf:
    __slots__ = ("name", "w", "r", "wf")

    def __init__(self, name):
        self.name = name
        self.w = {}
        self.r = {}
        self.wf = {}


class View:
    __slots__ = ("ap", "buf")

    def __init__(self, ap, buf):
        self.ap = ap
        self.buf = buf

    def __getitem__(self, idx):
        return View(self.ap[idx], self.buf)

    def re(self, pattern, **kw):
        return View(self.ap.rearrange(pattern, **kw), self.buf)


class Tile:
    def __init__(self, handle, buf):
        self.h = handle
        self.buf = buf

    def __getitem__(self, idx):
        return View(self.h[idx], self.buf)


class DramT:
    def __init__(self, handle, name):
        self.h = handle
        self.name = name
        self.bufs = {}

    def __getitem__(self, idx):
        return self.v(None)[idx]

    def v(self, key):
        if key not in self.bufs:
            self.bufs[key] = Buf(f"{self.name}:{key}")
        return View(self.h.ap(), self.bufs[key])


class _Op:
    __slots__ = ("fn", "deps", "sig", "val", "dma", "sem")

    def __init__(self, fn, deps, dma):
        self.fn = fn
        self.deps = deps
        self.sig = dma is not None
        self.val = None
        self.dma = dma
        self.sem = None


ENGS = ("pe", "act", "dve", "pool", "sp")
NSLOT = 8


class Prog:
    def __init__(self, nc):
        self.nc = nc
        self.ops = {e: [] for e in ENGS}
        self.ctx = []
        self.dma_cnt = {}
        self.dma_rr = {q: 0 for q in ENGS}
        self.n_sb = 0

    def dram(self, name, shape, dt, kind="Internal"):
        h = self.nc.dram_tensor(name, list(shape), dt, kind=kind)
        return DramT(h, name)

    def sb(self, name, shape, dt):
        self.n_sb += 1
        name = f"{name}_{self.n_sb}"
        g = self.nc.sbuf_tensor(name, list(shape), dt)
        h = g.__enter__()
        self.ctx.append(g)
        return Tile(h, Buf(name))

    def ps(self, name, shape, dt=F32):
        self.n_sb += 1
        name = f"{name}_{self.n_sb}"
        g = self.nc.psum_tensor(name, list(shape), dt)
        h = g.__enter__()
        self.ctx.append(g)
        return Tile(h, Buf(name))

    def op(self, eng, fn, r=(), w=(), wp=(), dmaq=False):
        deps = {}

        def add(d):
            for k, v in d.items():
                if deps.get(k, -1) < v:
                    deps[k] = v

        for v in r:
            add(v.buf.w)
        for v in w:
            add(v.buf.w)
            add(v.buf.r)
        for v in wp:
            add(v.buf.r)
            add(v.buf.wf)
        idx = len(self.ops[eng])
        dma = None
        if dmaq:
            slot = self.dma_rr[eng]
            self.dma_rr[eng] = (slot + 1) % NSLOT
            key = ("d", eng, slot)
            n = self.dma_cnt.get(key, 0)
            if n > 0:
                if deps.get(key, -1) < n:
                    deps[key] = n
            self.dma_cnt[key] = n + 1
            dma = (key, n + 1)
            tok = (key, n + 1)
        else:
            tok = (("c", eng), idx)
        if eng == "pe":
            deps.pop(("c", "pe"), None)
        self.ops[eng].append(_Op(fn, deps, dma))
        for v in r:
            if v.buf.r.get(tok[0], -1) < tok[1]:
                v.buf.r[tok[0]] = tok[1]
        for v in w:
            v.buf.w = {tok[0]: tok[1]}
            v.buf.wf = {tok[0]: tok[1]}
            v.buf.r = {}
        for v in wp:
            if v.buf.w.get(tok[0], -1) < tok[1]:
                v.buf.w[tok[0]] = tok[1]

    def _e(self, eng):
        return eng

    def mm(self, out, lhsT, rhs, start=True, stop=True, **kw):
        self.op("pe", lambda e: e.matmul(out.ap, lhsT.ap, rhs.ap, start=start, stop=stop, **kw),
                r=[lhsT, rhs], wp=[out] if not start else (), w=[out] if start else ())

    def transpose(self, out, in_, ident):
        self.op("pe", lambda e: e.transpose(out.ap, in_.ap, ident.ap), r=[in_, ident], w=[out])

    def act(self, out, in_, func, scale=1.0, bias=0.0, accum=None, part=False):
        r = [in_]
        sc = scale
        bi = bias
        if isinstance(scale, View):
            r.append(scale)
            sc = scale.ap
        if isinstance(bias, View):
            r.append(bias)
            bi = bias.ap
        kw = {}
        ws = [out]
        if accum is not None:
            kw["accum_out"] = accum.ap
            ws.append(accum)
        self.op("act", lambda e: e.activation(out.ap, in_.ap, func, scale=sc, bias=bi, **kw),
                r=r, w=() if part else ws, wp=ws if part else ())

    def tt(self, eng, out, a, b, op, part=False):
        self.op(eng, lambda e: e.tensor_tensor(out.ap, a.ap, b.ap, op), r=[a, b],
                w=() if part else [out], wp=[out] if part else ())

    def ts(self, eng, out, a, s1, op0, s2=None, op1=None, accum=None, part=False):
        r = [a]
        x1 = s1
        x2 = s2
        if isinstance(s1, View):
            r.append(s1)
            x1 = s1.ap
        if isinstance(s2, View):
            r.append(s2)
            x2 = s2.ap
        kw = {}
        ws = [out]
        if op1 is not None:
            kw["op1"] = op1
        if accum is not None:
            kw["accum_out"] = accum.ap
            ws.append(accum)
        self.op(eng, lambda e: e.tensor_scalar(out.ap, a.ap, x1, x2, op0, **kw), r=r,
                w=() if part else ws, wp=ws if part else ())

    def stt(self, out, in0, scalar, in1, op0, op1, part=False, eng="dve"):
        r = [in0, in1]
        sc = scalar
        if isinstance(scalar, View):
            r.append(scalar)
            sc = scalar.ap
        self.op(eng, lambda e: e.scalar_tensor_tensor(out.ap, in0.ap, sc, in1.ap, op0, op1), r=r,
                w=() if part else [out], wp=[out] if part else ())

    def copy(self, eng, out, in_, part=False):
        if eng == "act":
            self.op(eng, lambda e: e.copy(out.ap, in_.ap), r=[in_],
                    w=() if part else [out], wp=[out] if part else ())
        else:
            self.op(eng, lambda e: e.tensor_copy(out.ap, in_.ap), r=[in_],
                    w=() if part else [out], wp=[out] if part else ())

    def memset(self, eng, out, val):
        self.op(eng, lambda e: e.memset(out.ap, val), w=[out])

    def scan(self, out, d0, d1, init, op0, op1, part=False):
        r = [d0, d1]
        ini = init
        if isinstance(init, View):
            r.append(init)
            ini = init.ap
        self.op("dve", lambda e: e.tensor_tensor_scan(out.ap, d0.ap, d1.ap, ini, op0, op1), r=r,
                w=() if part else [out], wp=[out] if part else ())

    def dma(self, q, out, in_, part=False, **kw):
        self.op(q, lambda e: e.dma_start(out=out.ap, in_=in_.ap, **kw), r=[in_],
                w=() if part else [out], wp=[out] if part else (), dmaq=True)

    def _sem(self, name):
        if name not in self.sems:
            g = self.nc.semaphore(name)
            self.sems[name] = g.__enter__()
            self.sem_ctx.append(g)
        return self.sems[name]

    def _dma_tok(self, key, n):
        per = SEM_LIMIT // 16
        j = (n - 1) // per
        return f"d_{key[1]}_{key[2]}_{j}", 16 * ((n - 1) % per + 1)

    def _resolve(self, k, v):
        if k[0] == "c":
            lst = self.ops[k[1]]
            while lst[v].val is None:
                v += 1
            o = lst[v]
            return o.sem, o.val
        return self._dma_tok(k, v)

    def mark(self):
        return len(self.ctx)

    def flush(self, final=False):
        nc = self.nc
        ops = self.ops
        if not hasattr(self, "sems"):
            self.sems = {}
            self.sem_ctx = []
            self.emitted = {e: 0 for e in ENGS}
            self.sigcnt = {e: 0 for e in ENGS}
            self.seen = {e: {} for e in ENGS}
        for e in ENGS:
            for o in ops[e][self.emitted[e]:]:
                for k, v in o.deps.items():
                    if k[0] == "c" and v >= self.emitted[k[1]]:
                        ops[k[1]][v].sig = True
        fdeps = {}
        for e in ENGS:
            for j in range(len(ops[e]) - 1, self.emitted[e] - 1, -1):
                if ops[e][j].dma is None and ops[e][j].fn is not None:
                    ops[e][j].sig = True
                    fdeps[("c", e)] = j
                    break
        for key, n in self.dma_cnt.items():
            fdeps[key] = n
        for e in ENGS:
            ops[e].append(_Op(None, dict(fdeps), None))
        for e in ENGS:
            for o in ops[e][self.emitted[e]:]:
                if o.dma is None and o.sig and o.fn is not None:
                    c = self.sigcnt[e]
                    o.sem = f"c_{e}_{c // SEM_LIMIT}"
                    o.val = c % SEM_LIMIT + 1
                    self._sem(o.sem)
                    self.sigcnt[e] = c + 1
                elif o.dma is not None:
                    self._sem(self._dma_tok(*o.dma)[0])
        block = nc.Block()
        block.__enter__()
        names = {"pe": "tensor", "act": "scalar", "dve": "vector", "pool": "gpsimd", "sp": "sync"}
        for e in ENGS:
            def body(eng, e=e):
                seen = self.seen[e]
                for o in ops[e][self.emitted[e]:]:
                    for k, v in o.deps.items():
                        s, val = self._resolve(k, v)
                        if seen.get(s, 0) >= val:
                            continue
                        seen[s] = val
                        eng.wait_ge(self.sems[s], val)
                    if o.fn is None:
                        continue
                    ins = o.fn(eng)
                    if o.dma is not None:
                        s, val = self._dma_tok(*o.dma)
                        ins.then_inc(self.sems[s], 16)
                    elif o.sig:
                        ins.then_inc(self.sems[o.sem], 1)
                    o.fn = True
            getattr(block, names[e])(body)
        block.__exit__(None, None, None)
        for e in ENGS:
            self.emitted[e] = len(ops[e])

    def release(self, mark):
        while len(self.ctx) > mark:
            self.ctx.pop().__exit__(None, None, None)

    def finish(self):
        self.flush(final=True)
        self.release(0)
        for g in reversed(self.sem_ctx):
            g.__exit__(None, None, None)


D = 1024
SEQ = 4096
NMETA = 16
T = SEQ + NMETA
DFF = 2816
NFC = DFF // 128
ALPHA = 4.0 ** 0.25
EPS = 1e-5
TILES = [(i * 512, 512) for i in range(8)] + [(4096, 16)]
KSEL = 256
NBIS = 14


def bc(view, shape):
    return View(view.ap.broadcast_to(list(shape)), view.buf)


def bc3(view, n):
    a = view.ap.unsqueeze(1)
    shp = list(a.shape)
    shp[1] = n
    return View(a.broadcast_to(shp), view.buf)


class VecPack:
    def __init__(self):
        self.cols = []
        self.off = {}
        self.n = 0

    def add(self, name, vec):
        vec = np.asarray(vec, np.float32).reshape(-1)
        nchunk = (vec.size + 127) // 128
        pad = np.zeros(nchunk * 128, np.float32)
        pad[:vec.size] = vec
        self.cols.append(pad.reshape(nchunk, 128).T)
        self.off[name] = self.n
        self.n += nchunk

    def array(self):
        return np.ascontiguousarray(np.concatenate(self.cols, axis=1))


def vec_layout():
    vp = VecPack()
    pack_vecs(vp, None)
    return vp.off, vp.n


def pack_vecs(vp, inp):
    def g(name, shape):
        if inp is None:
            return np.zeros(shape, np.float32)
        return np.asarray(inp[name], np.float32)

    cw = g("a_conv_w", (1, 4, D))[0]
    for j in range(4):
        vp.add(f"a_cw{j}", cw[j])
    vp.add("a_cb", g("a_conv_b", (1, D))[0])
    vp.add("a_ng", g("a_norm_g", (1, D))[0])
    cw = g("b_conv_w", (1, 4, D))[0]
    for j in range(4):
        vp.add(f"b_cw{j}", cw[j])
    vp.add("b_cb", g("b_conv_b", (1, D))[0])
    vp.add("b_ba", g("b_b_a", (1, D))[0])
    vp.add("b_bx", g("b_b_x", (1, D))[0])
    vp.add("b_lam", g("b_lambda", (1, D))[0])
    mg = g("mix_ln_g", (2, D))
    mb = g("mix_ln_b", (2, D))
    fg = g("ffn_ln_g", (2, D))
    fb = g("ffn_ln_b", (2, D))
    fcw = g("ffn_conv_w", (2, 3, DFF))
    fcb = g("ffn_conv_b", (2, DFF))
    for l in range(2):
        vp.add(f"mix_g{l}", mg[l])
        vp.add(f"mix_b{l}", mb[l])
        vp.add(f"ffn_g{l}", fg[l])
        vp.add(f"ffn_b{l}", fb[l])
        for j in range(3):
            vp.add(f"f_cw{l}_{j}", fcw[l, j])
        vp.add(f"f_cb{l}", fcb[l])
    vp.add("c_qg", g("c_q_norm_g", (1, 384))[0])
    vp.add("c_kvg", g("c_kv_norm_g", (1, 256))[0])
    kg = g("c_kidx_ln_g", (1, 64))[0]
    kb = g("c_kidx_ln_b", (1, 64))[0]
    vp.add("c_kig", np.concatenate([kg, kg]))
    vp.add("c_kib", np.concatenate([kb, kb]))


def block_diag_host(w):
    w = np.asarray(w, np.float32)
    out = np.zeros((8, 128, 128), np.float32)
    for n in range(256):
        c, b = divmod(n, 32)
        out[c, 4 * b:4 * b + 4, 4 * b:4 * b + 4] = w[n]
    return out


def make_consts():
    c = np.zeros((128, 1536), np.float32)
    c[:, 0:128] = np.eye(128, dtype=np.float32)
    c[:, 128:256] = np.triu(np.ones((128, 128), np.float32))
    c[:, 256:384] = 1.0
    for h in range(4):
        c[h, 384 + 128 * h: 384 + 128 * (h + 1)] = 1.0
        c[4 + h, 896 + 128 * h: 896 + 128 * (h + 1)] = 1.0
        c[h, 1408 + h] = 1.0
        c[4 + h, 1412 + h] = 1.0
    return c


class Ctx:
    pass


def fm(dram, key=None):
    return dram.v(key).re("(c p) t -> p c t", p=128)


def load_w(P, dst, src_view, KC):
    for kc in range(KC):
        P.dma("pool", dst[:, kc, :], src_view[:, kc, :], part=True, max_dma_last_dim=4096)


def setup_consts(P, C, din):
    C.cf = P.sb("c_f32", [128, 384], F32)
    P.dma("sp", C.cf[:], din["consts"][:, 0:384])
    C.ident = C.cf[:, 0:128]
    C.triu = C.cf[:, 128:256]
    C.ones = C.cf[:, 256:384]
    C.cb = P.sb("c_bf16", [128, 384], BF16)
    P.copy("dve", C.cb[:], C.cf[:, 0:384])
    C.identb = C.cb[:, 0:128]
    C.triub = C.cb[:, 128:256]
    C.onesb = C.cb[:, 256:384]
    C.voff, nv = vec_layout()
    C.vec = P.sb("c_vec", [128, nv], F32)
    P.dma("sp", C.vec[:], din["vecs"][:])
    C.mean1024 = P.sb("c_mean", [128, 128], F32)
    P.memset("dve", C.mean1024[:], 1.0 / 1024.0)
    C.cc = P.sb("c_cols", [128, 4], F32)
    P.memset("dve", C.cc[:, 0:1], EPS)
    P.op("dve", lambda e: e.memset(C.cc[:, 1:2].ap, 1.0), wp=[C.cc[:, 1:2]])
    P.op("dve", lambda e: e.memset(C.cc[:, 2:3].ap, -float(np.log(16.0))), wp=[C.cc[:, 2:3]])
    P.op("dve", lambda e: e.memset(C.cc[:, 3:4].ap, 0.0), wp=[C.cc[:, 3:4]])
    C.epsc = C.cc[:, 0:1]
    C.onec = C.cc[:, 1:2]
    C.nl16 = C.cc[:, 2:3]


def vcol(C, name, c=0, n=1):
    o = C.voff[name] + c
    return C.vec[:, o:o + n]


def ln_gen(P, C, y, w, gname, bname, out_f32, tmp, psm, psq, sq, out_bf=None):
    for c in range(8):
        s = sq[c % 2]
        P.tt("pool", s[:, :w], y[:, c, :w], y[:, c, :w], ALU.mult)
        P.mm(psm[:, :w], C.mean1024[:], y[:, c, :w], c == 0, c == 7)
        P.mm(psq[:, :w], C.mean1024[:], s[:, :w], c == 0, c == 7)
        yield
    mean = tmp["mean"]
    m2 = tmp["m2"]
    rstd = tmp["rstd"]
    P.copy("act", mean[:, :w], psm[:, :w])
    P.tt("pool", m2[:, :w], mean[:, :w], mean[:, :w], ALU.mult)
    P.tt("dve", m2[:, :w], psq[:, :w], m2[:, :w], ALU.subtract)
    P.act(rstd[:, :w], m2[:, :w], AF.Sqrt, bias=C.epsc[:, 0:1])
    P.op("dve", lambda e, o=rstd[:, :w]: e.reciprocal(o.ap, o.ap), r=[rstd[:, :w]], w=[rstd[:, :w]])
    yield
    for c in range(8):
        t = sq[c % 2]
        P.tt("dve", t[:, :w], y[:, c, :w], mean[:, :w], ALU.subtract)
        P.stt(t[:, :w], t[:, :w], vcol(C, gname, c), rstd[:, :w], ALU.mult, ALU.mult)
        P.act(out_f32[:, c, :w], t[:, :w], AF.Identity, bias=vcol(C, bname, c), part=True)
        if out_bf is not None:
            P.copy("pool", out_bf[:, c, :w], out_f32[:, c, :w], part=True)
        yield


def layer_norm_fm(P, C, y, w, gname, bname, out_f32, tmp, psm, psq, sq, out_bf=None):
    for _ in ln_gen(P, C, y, w, gname, bname, out_f32, tmp, psm, psq, sq, out_bf):
        pass


def phase_embed(P, C, din, H0):
    m = P.mark()
    x = din["x"]
    xin = [P.sb(f"e_xin{i}", [128, 4, 1024], F32) for i in range(2)]
    hT = [P.sb(f"e_hT{i}", [128, 8, 512], F32) for i in range(2)]
    pt = [P.ps(f"e_pt{i}", [128, 512], F32) for i in range(4)]
    H0v = fm(H0)
    xv = x.v(None).re("(k j p) f -> k p j f", j=4, p=128)
    n = 0
    for k in range(9):
        xi = xin[k % 2]
        h = hT[k % 2]
        if k == 8:
            P.dma("sp", xi[0:16, 0, :], din["meta"][:])
            nj, rows, pos, w = 1, 16, 0, 16
        else:
            P.dma("sp", xi[:], xv[k])
            nj, rows, pos, w = 4, 128, 16 + 512 * k, 512
        for c in range(8):
            p_ = pt[n % 4]
            n += 1
            for j in range(nj):
                P.op("pe", lambda e, o=p_[:, j * rows:(j + 1) * rows], i=xi[0:rows, j, c * 128:(c + 1) * 128],
                     idn=C.ident[0:rows, 0:rows]: e.transpose(o.ap, i.ap, idn.ap),
                     r=[xi[:], C.ident], w=[p_[:]] if j == 0 else (), wp=[p_[:]] if j else ())
            if c % 2 == 0:
                P.copy("act", h[:, c, :w], p_[:, :w], part=True)
            else:
                P.copy("dve", h[:, c, :w], p_[:, :w], part=True)
        P.dma("pool", H0v[:, :, pos:pos + w], h[:, :, :w], part=True)
    P.flush()
    P.release(m)


def phase_inproj(P, C, din, H0, XM, SOG, XR, GGR):
    m = P.mark()
    w_in = P.sb("w_in", [128, 8, 4096], BF16)
    load_w(P, w_in, din["a_w_in"].v(None).re("(kc p) n -> p kc n", p=128), 8)
    hf = [P.sb(f"i_hf{i}", [128, 8, 512], F32) for i in range(2)]
    hb = [P.sb(f"i_hb{i}", [128, 8, 512], BF16) for i in range(2)]
    stg = [P.sb(f"i_st{i}", [128, 8, 512], F32) for i in range(2)]
    pss = [P.ps(f"i_ps{i}", [128, 512], F32) for i in range(4)]
    outs = [fm(XM), fm(SOG), fm(XR), fm(GGR)]
    H0v = fm(H0)
    n = 0
    for ti, (pos, w) in enumerate(TILES):
        f = hf[ti % 2]
        b = hb[ti % 2]
        P.dma("sp", f[:, :, :w], H0v[:, :, pos:pos + w])
        P.copy("pool", b[:, :, :w], f[:, :, :w])
        for grp in range(4):
            st = stg[(ti * 4 + grp) % 2]
            for o8 in range(8):
                oc = grp * 8 + o8
                ps = pss[n % 4]
                n += 1
                for kc in range(8):
                    P.mm(ps[:, :w], w_in[:, kc, oc * 128:(oc + 1) * 128], b[:, kc, :w], kc == 0, kc == 7)
                if grp == 1:
                    P.act(st[:, o8, :w], ps[:, :w], AF.Sigmoid, part=True)
                elif grp == 3:
                    P.act(st[:, o8, :w], ps[:, :w], AF.Gelu_apprx_tanh, part=True)
                elif o8 % 2 == 0:
                    P.copy("act", st[:, o8, :w], ps[:, :w], part=True)
                else:
                    P.copy("dve", st[:, o8, :w], ps[:, :w], part=True)
            P.dma("sp", outs[grp][:, :, pos:pos + w], st[:, :, :w], part=True)
    P.flush()
    P.release(m)


def phase_rglru(P, C, din, XR, GGR, HR):
    m = P.mark()
    wa = P.sb("r_wa", [128, 8, 128], BF16)
    wx = P.sb("r_wx", [128, 8, 128], BF16)
    P.dma("pool", wa[:], din["b_w_a"][:])
    P.dma("pool", wx[:], din["b_w_x"][:])
    c8 = P.sb("r_c8", [128, 8], F32)
    P.act(c8[:], vcol(C, "b_lam", 0, 8), AF.Exp, scale=-1.0)
    P.act(c8[:], c8[:], AF.Ln, bias=C.onec[:, 0:1])
    P.ts("dve", c8[:], c8[:], -8.0, ALU.mult)
    carry = P.sb("r_carry", [128, 8], F32)
    P.memset("dve", carry[:], 0.0)
    xe = [P.sb(f"r_xe{i}", [128, 8, 515], F32) for i in range(2)]
    gg = [P.sb(f"r_gg{i}", [128, 8, 512], F32) for i in range(2)]
    ho = [P.sb(f"r_ho{i}", [128, 8, 512], BF16) for i in range(2)]
    xc2 = [P.sb(f"r_xc{i}", [128, 4, 512], F32) for i in range(2)]
    xcb2 = [P.sb(f"r_xcb{i}", [128, 4, 512], BF16) for i in range(2)]
    rr2 = [P.sb(f"r_r{i}", [128, 4, 512], F32) for i in range(2)]
    ii2 = [P.sb(f"r_i{i}", [128, 4, 512], F32) for i in range(2)]
    aa2 = [P.sb(f"r_a{i}", [128, 4, 512], F32) for i in range(2)]
    a22 = [P.sb(f"r_a2{i}", [128, 4, 512], F32) for i in range(2)]
    hs = P.sb("r_hs", [128, 4, 512], F32)
    pa = [P.ps(f"r_pa{i}", [128, 512], F32) for i in range(4)]
    px = [P.ps(f"r_px{i}", [128, 512], F32) for i in range(4)]
    XRv, GGv, HRv = fm(XR), fm(GGR), fm(HR)
    NU = 2 * len(TILES)

    def stA(u):
        ti, hf_ = divmod(u, 2)
        pos, w = TILES[ti]
        x, g = xe[ti % 2], gg[ti % 2]
        xc, xcb = xc2[u % 2], xcb2[u % 2]
        if hf_ == 0:
            if ti == 0:
                P.memset("pool", x[:, :, 0:3], 0.0)
                P.dma("sp", x[:, :, 3:3 + w], XRv[:, :, pos:pos + w], part=True)
            else:
                P.dma("sp", x[:, :, 0:3 + w], XRv[:, :, pos - 3:pos + w])
            P.dma("sp", g[:, :, :w], GGv[:, :, pos:pos + w])
        for k in range(4):
            c = hf_ * 4 + k
            P.ts("dve", xc[:, k, :w], x[:, c, 3:3 + w], vcol(C, "b_cw3", c), ALU.mult, vcol(C, "b_cb", c), ALU.add,
                 part=(k > 0))
            for j in range(3):
                P.stt(xc[:, k, :w], x[:, c, j:j + w], vcol(C, f"b_cw{j}", c), xc[:, k, :w], ALU.mult, ALU.add,
                      part=True)
        for k in range(4):
            P.copy("pool", xcb[:, k, :w], xc[:, k, :w], part=(k > 0))

    def stB(u):
        ti, hf_ = divmod(u, 2)
        pos, w = TILES[ti]
        xc, xcb, rr, ii, aa, a2 = xc2[u % 2], xcb2[u % 2], rr2[u % 2], ii2[u % 2], aa2[u % 2], a22[u % 2]
        cs = [hf_ * 4 + k for k in range(4)]
        for k, c in enumerate(cs):
            P.mm(pa[k][:, :w], wa[:, c, :], xcb[:, k, :w])
            P.mm(px[k][:, :w], wx[:, c, :], xcb[:, k, :w])
        for k, c in enumerate(cs):
            P.act(rr[:, k, :w], pa[k][:, :w], AF.Sigmoid, bias=vcol(C, "b_ba", c), part=(k > 0))
        for k, c in enumerate(cs):
            P.act(ii[:, k, :w], px[k][:, :w], AF.Sigmoid, bias=vcol(C, "b_bx", c), part=(k > 0))
        for k, c in enumerate(cs):
            P.act(aa[:, k, :w], rr[:, k, :w], AF.Exp, scale=c8[:, c:c + 1], part=(k > 0))
        for k, c in enumerate(cs):
            P.tt("pool", a2[:, k, :w], aa[:, k, :w], aa[:, k, :w], ALU.mult, part=(k > 0))
        for k, c in enumerate(cs):
            P.tt("pool", ii[:, k, :w], ii[:, k, :w], xc[:, k, :w], ALU.mult, part=True)
        for k, c in enumerate(cs):
            P.act(a2[:, k, :w], a2[:, k, :w], AF.Sqrt, scale=-1.0, bias=C.onec[:, 0:1], part=True)

    def stC(u):
        ti, hf_ = divmod(u, 2)
        pos, w = TILES[ti]
        g, h = gg[ti % 2], ho[ti % 2]
        ii, aa, a2 = ii2[u % 2], aa2[u % 2], a22[u % 2]
        cs = [hf_ * 4 + k for k in range(4)]
        for k, c in enumerate(cs):
            P.tt("dve", ii[:, k, :w], ii[:, k, :w], a2[:, k, :w], ALU.mult, part=True)
            P.scan(hs[:, k, :w], aa[:, k, :w], ii[:, k, :w], carry[:, c:c + 1], ALU.mult, ALU.add, part=(k > 0))
            P.copy("dve", carry[:, c:c + 1], hs[:, k, w - 1:w], part=True)
        for k, c in enumerate(cs):
            P.tt("pool", h[:, c, :w], hs[:, k, :w], g[:, c, :w], ALU.mult, part=True)
        if hf_ == 1:
            P.dma("sp", HRv[:, :, pos:pos + w], h[:, :, :w], part=True)

    for u in range(NU + 2):
        if u < NU:
            stA(u)
        if 1 <= u <= NU:
            stB(u - 1)
        if u >= 2:
            stC(u - 2)
    P.flush()
    P.release(m)


def phase_mlstm(P, C, din, XM, SOG, HM):
    m = P.mark()
    XMv, SOGv, HMv = fm(XM), fm(SOG), fm(HM)
    bd = {}
    for nm in ("bdq", "bdk", "bdv"):
        bd[nm] = P.sb("m_" + nm, [128, 8, 128], BF16)
        P.dma("pool", bd[nm][:], din[nm][:])
    pw = [P.ps(f"m_pw{i}", [128, 512], F32) for i in range(2)]
    pbc = P.ps("m_pbc", [128, 512], F32)
    pg = pbc
    pS = P.ps("m_pS", [128, 4, 128], F32)
    pT = View(pS.h[:].rearrange("p a b -> p (a b)").bitcast(BF16).rearrange("p (c t) -> p c t", c=8), pS.buf)
    pns = [P.ps(f"m_pn{i}", [128, 512], F32) for i in range(2)]
    pds = [P.ps(f"m_pd{i}", [128, 512], F32) for i in range(2)]
    bif = P.sb("m_bif", [8, 1], F32)
    P.dma("sp", bif[:], din["a_b_if"][:])
    rmask = P.sb("m_rmask", [8, 512], F32)
    P.memset("dve", rmask[:], 1.0)
    for j in range(4):
        P.op("dve", lambda e, v=rmask[:, j * 128:j * 128 + 1]: e.memset(v.ap, 0.0), wp=[rmask[:]])
    selt = P.sb("m_sel", [8, 1152], F32)
    P.dma("sp", selt[:], din["consts"][0:8, 384:1536])
    selI = [selt[0:8, 128 * h:128 * (h + 1)] for h in range(4)]
    selF = [selt[0:8, 512 + 128 * h:512 + 128 * (h + 1)] for h in range(4)]
    selI4 = selt[0:8, 1024:1028]
    selF4 = selt[0:8, 1028:1032]
    C32 = P.sb("m_C32", [128, 4, 2, 257], F32)
    Cb = P.sb("m_Cb", [128, 4, 2, 257], BF16)
    P.memset("dve", C32[:], 0.0)
    P.memset("pool", Cb[:], 0.0)
    Vaug_2 = [P.sb(f"m_Vaug{i}", [128, 4, 257], BF16) for i in range(2)]
    for v_ in Vaug_2:
        P.memset("pool", v_[:], 1.0)
    xe = P.sb("m_xe", [128, 8, 515], F32)
    m2 = P.mark()
    bdT_src = din["bdT"].v(None)
    for t_ in range(3):
        for hb_ in range(2):
            P.dma("sp", xe[:, t_ * 2 + hb_, 0:512], bdT_src[:, t_, 4 * hb_:4 * hb_ + 4, :].re("p c d -> p (c d)"),
                  part=(t_ + hb_ > 0))

    def bdTv(t_, c):
        return xe[:, t_ * 2 + c // 4, (c % 4) * 128:(c % 4 + 1) * 128]

    wif = P.sb("m_wif", [128, 24, 8], F32)
    P.dma("sp", wif[:], din["a_w_if"][:])
    wg = P.sb("m_wg", [128, 2, 8, 8], BF16)
    for c in range(8):
        P.mm(pg[:, c * 8:(c + 1) * 8], bdTv(0, c), wif[:, c, :], True, False)
        P.mm(pg[:, c * 8:(c + 1) * 8], bdTv(1, c), wif[:, 8 + c, :], False, True)
        P.mm(pg[:, 64 + c * 8:64 + (c + 1) * 8], bdTv(2, c), wif[:, 16 + c, :], True, True)
    P.copy("dve", wg[:].re("p a c g -> p (a c g)"), pg[:, 0:128])
    sog_2 = [P.sb(f"m_sog{i}", [128, 8, 512], F32) for i in range(2)]
    xcb_2 = [P.sb(f"m_xcb{i}", [128, 8, 512], BF16) for i in range(2)]
    xmb_2 = [P.sb(f"m_xmb{i}", [128, 8, 512], BF16) for i in range(2)]
    qh_2 = [P.sb(f"m_qh{i}", [128, 8, 512], BF16) for i in range(2)]
    kh_2 = [P.sb(f"m_kh{i}", [128, 8, 512], BF16) for i in range(2)]
    abc_2 = [P.sb(f"m_abc{i}", [128, 4, 512], F32) for i in range(2)]
    bbc = P.sb("m_bbc", [128, 4, 512], F32)
    hm = [P.sb("m_hm0", [128, 8, 512], BF16)] * 2
    cv = [P.sb(f"m_cv{i}", [128, 512], F32) for i in range(2)]
    Gall_2 = [P.sb(f"m_Gall{i}", [8, 512], F32) for i in range(2)]
    Lg_2 = [P.sb(f"m_Lg{i}", [8, 512], F32) for i in range(2)]
    lcar = P.sb("m_lcar", [8, 1], F32)
    btok_2 = [P.sb(f"m_btok{i}", [128, 4], F32) for i in range(2)]
    Kt_2 = [P.sb(f"m_Kt{i}", [128, 4, 256], BF16) for i in range(2)]
    ST_2 = [P.sb(f"m_ST{i}", [128, 4, 128], BF16) for i in range(2)]
    hn = P.sb("m_hn", [128, 4, 256], BF16)
    st6 = P.sb("m_st6", [128, 6], F32)
    mv = P.sb("m_mv", [128, 2], F32)
    sc = P.sb("m_sc", [128, 4], F32)
    ctmps = [P.sb(f"m_ctmp{i}", [128, 257], F32) for i in range(2)]
    n = 0
    def prologue(ti):
        pos, w = TILES[ti]
        sog, xcb, xmb, qh, kh, abc, Gall, Lg = (sog_2[ti % 2], xcb_2[ti % 2], xmb_2[ti % 2], qh_2[ti % 2], kh_2[ti % 2], abc_2[ti % 2], Gall_2[ti % 2], Lg_2[ti % 2])
        if ti == 0:
            P.memset("pool", xe[:, :, 0:3], 0.0)
            P.dma("sp", xe[:, :, 3:3 + w], XMv[:, :, pos:pos + w], part=True)
        else:
            P.dma("sp", xe[:, :, 0:3 + w], XMv[:, :, pos - 3:pos + w])
        P.dma("sp", sog[:, :, :w], SOGv[:, :, pos:pos + w])
        for c in range(8):
            k = c % 2
            P.ts("dve", cv[k][:, :w], xe[:, c, 3:3 + w], vcol(C, "a_cw3", c), ALU.mult)
            for j in range(3):
                P.stt(cv[k][:, :w], xe[:, c, j:j + w], vcol(C, f"a_cw{j}", c), cv[k][:, :w], ALU.mult, ALU.add)
            P.act(xcb[:, c, :w], cv[k][:, :w], AF.Silu, bias=vcol(C, "a_cb", c), part=(c > 0))
            P.copy("pool", xmb[:, c, :w], xe[:, c, 3:3 + w], part=(c > 0))
            P.act(sog[:, c, :w], sog[:, c, :w], AF.Identity, scale=vcol(C, "a_ng", c), part=True)
            yield
        for c in range(8):
            P.mm(pg[0:8, :w], wg[:, 0, c, :], xcb[:, c, :w], c == 0, False)
        for c in range(8):
            P.mm(pg[0:8, :w], wg[:, 1, c, :], xmb[:, c, :w], False, c == 7)
        P.act(Gall[:, :w], pg[0:8, :w], AF.Identity, bias=bif[:, 0:1])
        P.act(Lg[:, :w], Gall[:, :w], AF.Exp, scale=-1.0)
        P.act(Lg[:, :w], Lg[:, :w], AF.Ln, bias=C.onec[0:8, 0:1])
        P.scan(Lg[:, :w], rmask[:, :w], Lg[:, :w], 0.0, ALU.mult, ALU.add)
        for h in range(4):
            P.mm(pbc[:, :w], selF[h], Lg[:, :w], True, True)
            P.act(abc[:, h, :w], pbc[:, :w], AF.Exp, scale=-1.0, part=(h > 0))
            P.mm(pbc[:, :w], selI[h], Gall[:, :w], True, False)
            P.mm(pbc[:, :w], selF[h], Lg[:, :w], False, True)
            P.act(bbc[:, h, :w], pbc[:, :w], AF.Exp, bias=C.nl16, part=(h > 0))
            yield
        for c in range(8):
            P.mm(pw[0][:, :w], bd["bdq"][:, c, :], xcb[:, c, :w])
            P.tt("dve", qh[:, c, :w], pw[0][:, :w], abc[:, c // 2, :w], ALU.mult, part=(c > 0))
            P.mm(pw[1][:, :w], bd["bdk"][:, c, :], xcb[:, c, :w])
            P.tt("dve", kh[:, c, :w], pw[1][:, :w], bbc[:, c // 2, :w], ALU.mult, part=(c > 0))
            yield
        yield

    def chunks(ti):
        pos, w = TILES[ti]
        sog, xcb, xmb, qh, kh, abc, Gall, Lg = (sog_2[ti % 2], xcb_2[ti % 2], xmb_2[ti % 2], qh_2[ti % 2], kh_2[ti % 2], abc_2[ti % 2], Gall_2[ti % 2], Lg_2[ti % 2])
        hmt = hm[ti % 2]
        nch = (w + 127) // 128

        def stX(j):
            L = min(128, w - j * 128)
            cs = slice(j * 128, j * 128 + L)
            btok, Vaug, Kt, ST = btok_2[j % 2], Vaug_2[j % 2], Kt_2[j % 2], ST_2[j % 2]
            P.mm(pbc[:L, 0:4], Gall[:, cs], selI4, True, False)
            P.mm(pbc[:L, 0:4], Lg[:, cs], selF4, False, True)
            P.act(btok[:L, :], pbc[:L, 0:4], AF.Exp, bias=C.nl16[:L, :])
            for g2 in range(2):
                for c4 in range(4):
                    c = g2 * 4 + c4
                    P.mm(pw[g2][:L, c4 * 128:(c4 + 1) * 128], xmb[:, c, cs], bd["bdv"][:, c, :], True, True)
                P.copy("act", Vaug[:L, 2 * g2:2 * g2 + 2, 0:256], pw[g2][:L, :].re("p (h v) -> p h v", h=2), part=True)
            for g2 in range(2):
                for c4 in range(4):
                    c = g2 * 4 + c4
                    P.mm(pw[g2][:L, c4 * 128:(c4 + 1) * 128], xcb[:, c, cs], bd["bdk"][:, c, :], True, True)
                for h2 in range(2):
                    h = 2 * g2 + h2
                    P.ts("dve", Kt[:L, h, :], pw[g2][:L, h2 * 256:(h2 + 1) * 256], btok[:L, h:h + 1], ALU.mult, part=True)
            for h in range(4):
                for dc in range(2):
                    P.mm(pS[:L, h, :L], kh[:, 2 * h + dc, cs], qh[:, 2 * h + dc, cs], dc == 0, dc == 1)
            P.tt("dve", ST[:L, :, :L], pS[:L, :, :L], bc3(C.triu[:L, :L], 4), ALU.mult)

        def stY(j):
            L = min(128, w - j * 128)
            cs = slice(j * 128, j * 128 + L)
            btok, Vaug, Kt, ST = btok_2[j % 2], Vaug_2[j % 2], Kt_2[j % 2], ST_2[j % 2]
            for h in range(4):
                pn = pns[h % 2]
                P.mm(pn[:L, 0:257], ST[:L, h, :L], Vaug[:L, h, :], True, False)
                for dc in range(2):
                    P.mm(pn[:L, 0:257], qh[:, 2 * h + dc, cs], Cb[:, h, dc, :], False, dc == 1)
                A = abc[:, h, j * 128 + L - 1:j * 128 + L]
                for dc in range(2):
                    pd, ctmp = pds[dc], ctmps[dc]
                    P.mm(pd[:, 0:257], Kt[:L, h, dc * 128:(dc + 1) * 128], Vaug[:L, h, :], True, True)
                    P.tt("dve", ctmp[:], pd[:, 0:257], C32[:, h, dc, :], ALU.add)
                    P.act(Cb[:, h, dc, :], ctmp[:], AF.Identity, scale=A, part=True)
                    P.act(C32[:, h, dc, :], ctmp[:], AF.Identity, scale=A, part=True)
                P.copy("dve", sc[:L, 3:4], pn[:L, 256:257])
                P.tt("dve", sc[:L, 0:1], sc[:L, 3:4], sc[:L, 3:4], ALU.mult, part=True)
                P.ts("dve", sc[:L, 0:1], sc[:L, 0:1], 1.0, ALU.max, part=True)
                P.op("dve", lambda e, o=sc[:L, 0:1]: e.reciprocal(o.ap, o.ap), r=[sc[:]], wp=[sc[:]])
                P.op("dve", lambda e, o=st6[:L, :], i=pn[:L, 0:256]: e.bn_stats(o.ap, i.ap), r=[pn[:]], w=[st6[:]])
                P.op("dve", lambda e, o=mv[:L, :], i=st6[:L, :]: e.bn_aggr(o.ap, i.ap), r=[st6[:]], w=[mv[:]])
                P.ts("dve", sc[:L, 1:2], sc[:L, 0:1], mv[:L, 1:2], ALU.mult, EPS, ALU.add, part=True)
                P.op("dve", lambda e, o=sc[:L, 1:2]: e.reciprocal(o.ap, o.ap), r=[sc[:]], wp=[sc[:]])
                P.tt("dve", sc[:L, 1:2], sc[:L, 1:2], sc[:L, 0:1], ALU.mult, part=True)
                P.act(sc[:L, 2:3], sc[:L, 1:2], AF.Sqrt, part=True)
                P.ts("dve", hn[:L, h, :], pn[:L, 0:256], mv[:L, 0:1], ALU.subtract, sc[:L, 2:3], ALU.mult, part=(h > 0))
                yield
            for c in range(8):
                P.op("pe", lambda e, o=pT[:, c, :L], i=hn[:L, c // 2, (c % 2) * 128:(c % 2) * 128 + 128],
                     idn=C.identb[:L, :L]: e.transpose(o.ap, i.ap, idn.ap),
                     r=[hn[:], C.identb], w=[pT[:]] if c == 0 else (), wp=[pT[:]] if c else ())
            P.tt("dve", hmt[:, :, cs], pT[:, :, :L], sog[:, :, cs], ALU.mult, part=(j > 0))

        stX(0)
        for j in range(nch):
            if j + 1 < nch:
                stX(j + 1)
            for _ in stY(j):
                yield
        P.dma("sp", HMv[:, :, pos:pos + w], hmt[:, :, :w], part=True)
        yield

    pro = prologue(0)
    for _ in pro:
        pass
    for ti in range(len(TILES)):
        ck = chunks(ti)
        nxt = prologue(ti + 1) if ti + 1 < len(TILES) else None
        for _ in ck:
            if nxt is not None:
                try:
                    next(nxt)
                    next(nxt)
                except StopIteration:
                    nxt = None
        if nxt is not None:
            for _ in nxt:
                pass
    P.flush()
    P.release(m)


def ln_tmps(P, pfx):
    return {"mean": P.sb(pfx + "_mean", [128, 512], F32), "m2": P.sb(pfx + "_m2", [128, 512], F32),
            "rstd": P.sb(pfx + "_rstd", [128, 512], F32)}


def phase_outproj(P, C, wsrc, KC, ins_bf, Hin, Hout, layer, loader=None):
    m = P.mark()
    wo = P.sb("o_w", [128, KC, 1024], BF16)
    load_w(P, wo, wsrc.v(None).re("(kc p) n -> p kc n", p=128), KC)
    xin = [P.sb(f"o_x{i}", [128, KC, 512], BF16) for i in range(2)]
    hf = [P.sb(f"o_h{i}", [128, 8, 512], F32) for i in range(2)]
    sq = [P.sb(f"o_sq{i}", [128, 512], F32) for i in range(2)]
    tmp = ln_tmps(P, "o")
    pss = [P.ps(f"o_ps{i}", [128, 512], F32) for i in range(4)]
    psm = P.ps("o_psm", [128, 512], F32)
    psq = P.ps("o_psq", [128, 512], F32)
    Hiv, Hov = fm(Hin), fm(Hout)
    n = 0
    for ti, (pos, w) in enumerate(TILES):
        x = xin[ti % 2]
        h = hf[ti % 2]
        if loader is not None:
            loader(ti, pos, w, x, pss)
        for i, src in enumerate(ins_bf):
            P.dma("sp", x[:, 8 * i:8 * i + 8, :w], fm(src)[:, :, pos:pos + w], part=(i > 0))
        P.dma("sp", h[:, :, :w], Hiv[:, :, pos:pos + w])
        for oc in range(8):
            ps = pss[n % 4]
            n += 1
            for kc in range(KC):
                P.mm(ps[:, :w], wo[:, kc, oc * 128:(oc + 1) * 128], x[:, kc, :w], kc == 0, kc == KC - 1)
            P.stt(h[:, oc, :w], h[:, oc, :w], ALPHA, ps[:, :w], ALU.mult, ALU.add, part=True)
        layer_norm_fm(P, C, h, w, f"mix_g{layer}", f"mix_b{layer}", h, tmp, psm, psq, sq)
        P.dma("pool", Hov[:, :, pos:pos + w], h[:, :, :w], part=True)
    P.flush()
    P.release(m)


def phase_ffn(P, C, din, layer, Hin, Hout, final_out=None):
    m = P.mark()
    wu = P.sb("f_wu", [128, 8, 2 * DFF], BF16)
    wd = P.sb("f_wd", [128, NFC, 1024], BF16)
    load_w(P, wu, din["ffn_w_up"].v(None)[layer].re("(kc p) n -> p kc n", p=128), 8)
    load_w(P, wd, din["ffn_w_down"].v(None)[layer].re("(kc p) n -> p kc n", p=128), NFC)
    hf = P.sb("f_h", [128, 8, 512], F32)
    hb = P.sb("f_hb", [128, 8, 512], BF16)
    hh = P.sb("f_hh", [128, NFC, 512], BF16)
    gx = [P.sb(f"f_gx{i}", [128, 514], F32) for i in range(2)]
    ac = [P.sb(f"f_ac{i}", [128, 512], F32) for i in range(2)]
    halo = P.sb("f_halo", [128, NFC, 2], F32)
    P.memset("dve", halo[:], 0.0)
    sq = [P.sb(f"f_sq{i}", [128, 512], F32) for i in range(2)]
    tmp = ln_tmps(P, "f")
    pgs = [P.ps(f"f_pg{i}", [128, 512], F32) for i in range(2)]
    pus = [P.ps(f"f_pu{i}", [128, 512], F32) for i in range(2)]
    pys = [P.ps(f"f_py{i}", [128, 512], F32) for i in range(2)]
    psm = P.ps("f_psm", [128, 512], F32)
    psq = P.ps("f_psq", [128, 512], F32)
    Hiv = fm(Hin)
    if final_out is None:
        Hov = fm(Hout)
    else:
        ot = P.sb("f_ot0", [128, 1024], F32)
    cw = [f"f_cw{layer}_{j}" for j in range(3)]
    cnt = {"n": 0}

    def load_hb(ti):
        pos, w = TILES[ti]
        for c in range(8):
            P.dma("pool", hb[:, c, :w], Hiv[:, c, pos:pos + w], part=(c > 0), max_dma_last_dim=4096)

    def stage_up(ti):
        pos, w = TILES[ti]
        for fc in range(NFC):
            k = cnt["n"] % 2
            cnt["n"] += 1
            pg, pu, g, a = pgs[k], pus[k], gx[k], ac[k]
            for kc in range(8):
                P.mm(pg[:, :w], wu[:, kc, DFF + fc * 128:DFF + (fc + 1) * 128], hb[:, kc, :w], kc == 0, kc == 7)
            for kc in range(8):
                P.mm(pu[:, :w], wu[:, kc, fc * 128:(fc + 1) * 128], hb[:, kc, :w], kc == 0, kc == 7)
            P.copy("act", g[:, 0:2], halo[:, fc, :])
            P.copy("act", g[:, 2:2 + w], pg[:, :w], part=True)
            P.copy("pool", halo[:, fc, :], g[:, w:w + 2], part=True)
            P.ts("dve", a[:, :w], g[:, 2:2 + w], vcol(C, cw[2], fc), ALU.mult)
            P.stt(a[:, :w], g[:, 1:1 + w], vcol(C, cw[1], fc), a[:, :w], ALU.mult, ALU.add)
            P.stt(a[:, :w], g[:, 0:w], vcol(C, cw[0], fc), a[:, :w], ALU.mult, ALU.add)
            P.act(a[:, :w], a[:, :w], AF.Gelu_apprx_tanh, bias=vcol(C, f"f_cb{layer}", fc))
            P.tt("dve", hh[:, fc, :w], a[:, :w], pu[:, :w], ALU.mult, part=(fc > 0))
            yield

    def stage_down(ti):
        pos, w = TILES[ti]
        P.dma("sp", hf[:, :, :w], Hiv[:, :, pos:pos + w])
        if ti + 1 < len(TILES):
            load_hb(ti + 1)
        for oc in range(8):
            py = pys[oc % 2]
            for fc in range(NFC):
                P.mm(py[:, :w], wd[:, fc, oc * 128:(oc + 1) * 128], hh[:, fc, :w], fc == 0, fc == NFC - 1)
            P.stt(hf[:, oc, :w], hf[:, oc, :w], ALPHA, py[:, :w], ALU.mult, ALU.add, part=True)

    def stage_ln(ti):
        pos, w = TILES[ti]
        for _ in ln_gen(P, C, hf, w, f"ffn_g{layer}", f"ffn_b{layer}", hf, tmp, psm, psq, sq):
            yield
        if final_out is None:
            P.dma("pool", Hov[:, :, pos:pos + w], hf[:, :, :w], part=True)
        else:
            for j in range((w + 127) // 128):
                L = min(128, w - j * 128)
                p0 = pos + j * 128
                o = ot
                for half in range(2):
                    py = pys[half]
                    for c4 in range(4):
                        c = half * 4 + c4
                        P.op("pe", lambda e, o_=py[:L, c4 * 128:(c4 + 1) * 128], i=hf[:, c, j * 128:j * 128 + L],
                             idn=C.ident: e.transpose(o_.ap, i.ap, idn.ap),
                             r=[hf[:], C.ident], w=[py[:]] if c4 == 0 else (), wp=[py[:]] if c4 else ())
                    if half == 0:
                        P.copy("act", o[:L, 0:512], py[:L, :], part=False)
                    else:
                        P.copy("dve", o[:L, 512:1024], py[:L, :], part=True)
                lo = max(p0, NMETA)
                hi = p0 + L
                if hi > lo:
                    P.dma("pool", final_out[lo - NMETA:hi - NMETA, :], o[lo - p0:hi - p0, :], part=True)
                yield

    load_hb(0)
    prev_ln = None
    for ti in range(len(TILES)):
        up = stage_up(ti)
        for _ in up:
            if prev_ln is not None:
                try:
                    next(prev_ln)
                except StopIteration:
                    prev_ln = None
        if prev_ln is not None:
            for _ in prev_ln:
                pass
        stage_down(ti)
        prev_ln = stage_ln(ti)
    for _ in prev_ln:
        pass
    P.flush()
    P.release(m)


IN_SHAPES = {
    "x": [SEQ, D], "meta": [NMETA, D], "consts": [128, 1536], "vecs": None,
    "a_w_in": [D, 4096], "bdq": [128, 8, 128], "bdk": [128, 8, 128], "bdv": [128, 8, 128],
    "bdT": [128, 3, 8, 128], "a_w_if": [128, 24, 8], "a_b_if": [8, 1],
    "b_w_a": [128, 8, 128], "b_w_x": [128, 8, 128], "ab_w_out": [2 * D, D],
    "ffn_w_up": [2, D, 2 * DFF], "ffn_w_down": [2, DFF, D],
    "c_w_in": [D, 712], "c_w_qidx": [384, 512], "c_w_uq": [384, 1024], "c_w_ukT": [128, 8, 256],
    "c_w_uv": [128, 8, 2, 128], "c_w_out": [D, D],
}


def build(stop_after=None, dbg=()):
    nc = bass.Bass("TRN2", target_bir_lowering=False)
    P = Prog(nc)
    C = Ctx()
    _, nv = vec_layout()
    din = {}
    for k, shp in IN_SHAPES.items():
        if k == "vecs":
            shp = [128, nv]
        din[k] = P.dram(k, shp, F32, kind="ExternalInput")
    out = P.dram("out", [SEQ, D], F32, kind="ExternalOutput")

    def scratch(name, dt=F32):
        kind = "ExternalOutput" if name in dbg else "Internal"
        return P.dram(name, [D, T], dt, kind=kind)

    def scratch2(name, shape, dt):
        kind = "ExternalOutput" if name in dbg else "Internal"
        return P.dram(name, shape, dt, kind=kind)

    setup_consts(P, C, din)
    H0 = scratch("H0")
    phase_embed(P, C, din, H0)
    XM, SOG, XR, GGR = scratch("XM"), scratch("SOG"), scratch("XR"), scratch("GGR")
    phase_inproj(P, C, din, H0, XM, SOG, XR, GGR)
    HR = scratch("HR", BF16)
    phase_rglru(P, C, din, XR, GGR, HR)
    HM = scratch("HM", BF16)
    phase_mlstm(P, C, din, XM, SOG, HM)
    H1 = scratch("H1")
    phase_outproj(P, C, din["ab_w_out"], 16, [HM, HR], H0, H1, 0)
    H2 = scratch("H2")
    if stop_after == "L0":
        phase_ffn(P, C, din, 0, H1, H2)
        P.finish()
        return nc
    phase_ffn(P, C, din, 0, H1, H2)
    CKVT = scratch2("CKVT", [256, T], BF16)
    CKV = scratch2("CKV", [T, 256], BF16)
    KIDXT = scratch2("KIDXT", [64, T], BF16)
    WIDX = scratch2("WIDX", [T, 8], F32)
    QIDXT = scratch2("QIDXT", [512, T], BF16)
    QLATT = scratch2("QLATT", [2048, T], BF16)
    OLATT = scratch2("OLATT", [2048, T], BF16)
    phase_dsa_proj(P, C, din, H2, CKVT, CKV, KIDXT, WIDX, QIDXT, QLATT)
    if stop_after == "L1proj":
        P.finish()
        return nc
    phase_dsa_attn(P, C, din, CKVT, CKV, KIDXT, WIDX, QIDXT, QLATT, OLATT)
    if stop_after == "L1attn":
        P.finish()
        return nc
    H3 = scratch("H3")
    muv = P.mark()
    wuv = P.sb("u_wuv", [128, 8, 2, 128], BF16)
    P.dma("pool", wuv[:], din["c_w_uv"][:])
    olb = [P.sb(f"u_ol{i}", [128, 2, 8, 512], BF16) for i in range(2)]
    OLATv = OLATT.v(None).re("(p a) t -> p a t", a=16)

    def uv_loader(ti, pos, w, x, pss):
        ol = olb[ti % 2]
        P.dma("sp", ol[:, :, :, :w].re("p rc h t -> p (rc h) t"), OLATv[:, :, pos:pos + w])
        for h in range(8):
            ps = pss[h % 4]
            for rc in range(2):
                P.mm(ps[:, :w], wuv[:, h, rc, :], ol[:, rc, h, :w], rc == 0, rc == 1)
            if h % 2:
                P.copy("act", x[:, h, :w], ps[:, :w], part=(h > 0))
            else:
                P.copy("dve", x[:, h, :w], ps[:, :w], part=(h > 0))

    phase_outproj(P, C, din["c_w_out"], 8, [], H2, H3, 1, loader=uv_loader)
    P.release(muv)
    if stop_after == "L1mix":
        P.finish()
        return nc
    phase_ffn(P, C, din, 1, H3, None, final_out=out.v(None))
    P.finish()
    return nc


def host_inputs(inp):
    f = lambda a: np.ascontiguousarray(np.asarray(a, np.float32))
    vp = VecPack()
    pack_vecs(vp, inp)
    bds = [block_diag_host(inp[k][0]) for k in ("a_wq", "a_wk", "a_wv")]
    shared = {
        "meta": f(inp["meta_tokens"]), "consts": make_consts(), "vecs": vp.array(),
        "a_w_in": f(inp["a_w_in"][0]),
        "bdq": f(bds[0].transpose(1, 0, 2)), "bdk": f(bds[1].transpose(1, 0, 2)), "bdv": f(bds[2].transpose(1, 0, 2)),
        "bdT": f(np.stack([b.transpose(2, 0, 1) for b in bds], axis=1)),
        "a_w_if": f(np.asarray(inp["a_w_if"][0]).reshape(24, 128, 8).transpose(1, 0, 2)),
        "a_b_if": f(np.asarray(inp["a_b_if"][0]).reshape(8, 1)),
        "b_w_a": f(np.asarray(inp["b_w_a"][0]).transpose(1, 0, 2)),
        "b_w_x": f(np.asarray(inp["b_w_x"][0]).transpose(1, 0, 2)),
        "ab_w_out": f(inp["ab_w_out"][0]),
        "ffn_w_up": f(inp["ffn_w_up"]), "ffn_w_down": f(inp["ffn_w_down"]),
        "c_w_in": f(inp["c_w_in"][0]), "c_w_qidx": f(inp["c_w_qidx"][0]), "c_w_uq": f(inp["c_w_uq"][0]),
        "c_w_ukT": f(np.asarray(inp["c_w_uk"][0]).transpose(2, 0, 1)),
        "c_w_uv": f(np.asarray(inp["c_w_uv"][0]).reshape(8, 2, 128, 128).transpose(2, 0, 1, 3)),
        "c_w_out": f(inp["c_w_out"][0]),
    }
    return shared


_NC_CACHE = {}


def kernel(**inp):
    shared = host_inputs(inp)
    x = np.asarray(inp["x"], np.float32)
    if "full" not in _NC_CACHE:
        _NC_CACHE["full"] = build()
    nc = _NC_CACHE["full"]
    in_maps = []
    for b in range(8):
        d = dict(shared)
        d["x"] = np.ascontiguousarray(x[b])
        in_maps.append(d)
    res = run_bass_kernel_spmd(nc, in_maps, core_ids=list(range(8)))
    return np.stack([res.results[b]["out"] for b in range(8)], axis=0)


CQ_R, CKV_R = 384, 256


def phase_dsa_proj(P, C, din, H2, CKVT, CKV, KIDXT, WIDX, QIDXT, QLATT):
    m = P.mark()
    wci = P.sb("d_wci", [128, 8, 712], BF16)
    load_w(P, wci, din["c_w_in"].v(None).re("(kc p) n -> p kc n", p=128), 8)
    wqi = P.sb("d_wqi", [128, 3, 512], BF16)
    load_w(P, wqi, din["c_w_qidx"].v(None).re("(kc p) n -> p kc n", p=128), 3)
    wuq = P.sb("d_wuq", [128, 3, 1024], BF16)
    load_w(P, wuq, din["c_w_uq"].v(None).re("(kc p) n -> p kc n", p=128), 3)
    wuk = P.sb("d_wuk", [128, 8, 256], BF16)
    P.dma("pool", wuk[:], din["c_w_ukT"][:])
    m384 = P.sb("d_m384", [128, 128], F32)
    P.memset("dve", m384[:], 1.0 / 384.0)
    m256 = P.sb("d_m256", [128, 128], F32)
    P.memset("dve", m256[:], 1.0 / 256.0)
    m64 = P.sb("d_m64", [64, 64], F32)
    P.memset("dve", m64[:], 1.0 / 64.0)
    hf = P.sb("d_hf", [128, 8, 512], F32)
    hb = P.sb("d_hb", [128, 8, 512], BF16)
    raw = P.sb("d_raw", [128, 3, 512], F32)
    sq = [P.sb(f"d_sq{i}", [128, 512], F32) for i in range(2)]
    rs = P.sb("d_rs", [128, 512], F32)
    mn = P.sb("d_mn", [128, 512], F32)
    cqn = P.sb("d_cqn", [128, 3, 512], BF16)
    kvn = [P.sb(f"d_kvn{i}", [128, 2, 512], BF16) for i in range(2)]
    kvt = [P.sb(f"d_kvt{i}", [128, 4, 256], BF16) for i in range(2)]
    kin = [P.sb(f"d_kin{i}", [64, 512], BF16) for i in range(2)]
    wi = [P.sb(f"d_wi{i}", [128, 4, 8], F32) for i in range(2)]
    qix = [P.sb(f"d_qix{i}", [128, 4, 512], BF16) for i in range(2)]
    qf = P.sb("d_qf", [128, 8, 512], BF16)
    qlt = [P.sb(f"d_qlt{i}", [128, 2, 8, 512], BF16) for i in range(2)]
    pss = [P.ps(f"d_ps{i}", [128, 512], F32) for i in range(4)]
    pst = P.ps("d_pst", [128, 512], F32)
    ptr = P.ps("d_ptr", [128, 8, 128], BF16)
    pwi = P.ps("d_pwi", [128, 512], F32)
    H2v = fm(H2)
    CKVTv = fm(CKVT)
    CKVv = CKV.v(None)
    WIDXv = WIDX.v(None)
    QIDXv = fm(QIDXT)
    QLATv = QLATT.v(None).re("(p a) t -> p a t", a=16)
    KIv = KIDXT.v(None)
    n = 0

    def nxt():
        nonlocal n
        n += 1
        return pss[n % 4]

    for ti, (pos, w) in enumerate(TILES):
        k2 = ti % 2
        P.dma("sp", hf[:, :, :w], H2v[:, :, pos:pos + w])
        P.copy("pool", hb[:, :, :w], hf[:, :, :w])

        def proj_chunk(ps, c0, mcols):
            for kc in range(8):
                P.mm(ps[:mcols, :w], wci[:, kc, c0:c0 + mcols], hb[:, kc, :w], kc == 0, kc == 7)

        def rms(nch, c0, mmat, gname, outt):
            for c in range(nch):
                ps = nxt()
                proj_chunk(ps, c0 + c * 128, 128)
                P.copy("act", raw[:, c, :w], ps[:, :w], part=(c > 0))
                s = sq[c % 2]
                P.tt("pool", s[:, :w], raw[:, c, :w], raw[:, c, :w], ALU.mult)
                P.mm(pst[:, :w], mmat[:], s[:, :w], c == 0, c == nch - 1)
            P.act(rs[:, :w], pst[:, :w], AF.Sqrt, bias=C.epsc)
            P.op("dve", lambda e, o=rs[:, :w]: e.reciprocal(o.ap, o.ap), r=[rs[:]], w=[rs[:]])
            for c in range(nch):
                P.stt(outt[:, c, :w], raw[:, c, :w], vcol(C, gname, c), rs[:, :w], ALU.mult, ALU.mult, part=(c > 0))

        rms(3, 0, m384, "c_qg", cqn)
        kv = kvn[k2]
        rms(2, CQ_R, m256, "c_kvg", kv)
        P.dma("pool", CKVTv[:, :, pos:pos + w], kv[:, :, :w], part=True)
        kt = kvt[k2]
        nb = (w + 127) // 128
        for j in range(nb):
            L = min(128, w - j * 128)
            for rc in range(2):
                P.op("pe", lambda e, o=ptr[:L, j * 2 + rc, :], i=kv[:, rc, j * 128:j * 128 + L], idn=C.identb:
                     e.transpose(o.ap, i.ap, idn.ap), r=[kv[:], C.identb],
                     w=[ptr[:]] if (j == 0 and rc == 0) else (), wp=[ptr[:]] if (j or rc) else ())
        if w == 512:
            P.copy("act", kt[:].re("p j r -> p (j r)"), ptr[:].re("p a b -> p (a b)"))
            P.dma("pool", CKVv[pos:pos + 512, :].re("(j p) r -> p j r", p=128), kt[:], part=True)
        else:
            P.copy("act", kt[:w, 0, :], ptr[:w, 0:2, :].re("p a b -> p (a b)"))
            P.dma("pool", CKVv[pos:pos + w, :], kt[:w, 0, :], part=True)
        ps = nxt()
        proj_chunk(ps, CQ_R + CKV_R, 64)
        P.copy("act", raw[0:64, 0, :w], ps[0:64, :w])
        P.tt("pool", sq[0][0:64, :w], raw[0:64, 0, :w], raw[0:64, 0, :w], ALU.mult)
        P.mm(pst[0:64, :w], m64[:], raw[0:64, 0, :w], True, True)
        P.copy("act", mn[0:64, :w], pst[0:64, :w])
        P.mm(pst[0:64, :w], m64[:], sq[0][0:64, :w], True, True)
        P.tt("pool", sq[1][0:64, :w], mn[0:64, :w], mn[0:64, :w], ALU.mult)
        P.tt("dve", rs[0:64, :w], pst[0:64, :w], sq[1][0:64, :w], ALU.subtract)
        P.act(rs[0:64, :w], rs[0:64, :w], AF.Sqrt, bias=C.epsc[0:64, :])
        P.op("dve", lambda e, o=rs[0:64, :w]: e.reciprocal(o.ap, o.ap), r=[rs[:]], w=[rs[:]])
        P.tt("dve", sq[0][0:64, :w], raw[0:64, 0, :w], mn[0:64, :w], ALU.subtract)
        P.stt(sq[0][0:64, :w], sq[0][0:64, :w], vcol(C, "c_kig")[0:64, :], rs[0:64, :w], ALU.mult, ALU.mult)
        ki = kin[k2]
        P.act(ki[:, :w], sq[0][0:64, :w], AF.Identity, bias=vcol(C, "c_kib")[0:64, :])
        P.dma("pool", KIv[:, pos:pos + w], ki[:, :w], part=True)
        wt = wi[k2]
        for j in range(nb):
            L = min(128, w - j * 128)
            for kc in range(8):
                P.mm(pwi[:L, j * 8:(j + 1) * 8], hb[:, kc, j * 128:j * 128 + L], wci[:, kc, 704:712], kc == 0, kc == 7)
        if w == 512:
            P.act(wt[:].re("p j g -> p (j g)"), pwi[:, 0:32], AF.Copy, scale=float(8.0 ** -0.5))
            P.dma("pool", WIDXv[pos:pos + 512, :].re("(j p) g -> p j g", p=128), wt[:], part=True)
        else:
            P.act(wt[:w, 0, :], pwi[:w, 0:8], AF.Copy, scale=float(8.0 ** -0.5))
            P.dma("pool", WIDXv[pos:pos + w, :], wt[:w, 0, :], part=True)
        qx = qix[k2]
        for oc in range(4):
            ps = nxt()
            for kc in range(3):
                P.mm(ps[:, :w], wqi[:, kc, oc * 128:(oc + 1) * 128], cqn[:, kc, :w], kc == 0, kc == 2)
            if oc % 2:
                P.copy("act", qx[:, oc, :w], ps[:, :w], part=(oc > 0))
            else:
                P.copy("dve", qx[:, oc, :w], ps[:, :w], part=(oc > 0))
        P.dma("pool", QIDXv[:, :, pos:pos + w], qx[:, :, :w], part=True)
        for h in range(8):
            ps = nxt()
            for kc in range(3):
                P.mm(ps[:, :w], wuq[:, kc, h * 128:(h + 1) * 128], cqn[:, kc, :w], kc == 0, kc == 2)
            if h % 2:
                P.copy("act", qf[:, h, :w], ps[:, :w], part=(h > 0))
            else:
                P.copy("dve", qf[:, h, :w], ps[:, :w], part=(h > 0))
        ql = qlt[k2]
        for h in range(8):
            for rc in range(2):
                ps = nxt()
                P.mm(ps[:, :w], wuk[:, h, rc * 128:(rc + 1) * 128], qf[:, h, :w])
                first = (h == 0 and rc == 0)
                if rc:
                    P.act(ql[:, rc, h, :w], ps[:, :w], AF.Copy, scale=float(128.0 ** -0.5), part=not first)
                else:
                    P.ts("dve", ql[:, rc, h, :w], ps[:, :w], float(128.0 ** -0.5), ALU.mult, part=not first)
        P.dma("pool", QLATv[:, :, pos:pos + w], ql[:, :, :, :w].re("p rc h t -> p (rc h) t"), part=True)
    P.flush()
    P.release(m)


NKB = 33
NEG = -1.0e30


def phase_dsa_attn(P, C, din, CKVT, CKV, KIDXT, WIDX, QIDXT, QLATT, OLATT):
    m = P.mark()
    kidx2 = P.sb("t_kidx2", [128, T], BF16)
    P.dma("sp", kidx2[0:64, :], KIDXT.v(None)[:, :], part=True)
    P.dma("sp", kidx2[64:128, :], KIDXT.v(None)[:, :], part=True)
    ckvT = P.sb("t_ckvT", [128, 2, T], BF16)
    P.dma("sp", ckvT[:], fm(CKVT)[:, :, :])
    ckv = P.sb("t_ckv", [128, NKB, 256], BF16)
    for i in range(4):
        P.dma("sp", ckv[:, 8 * i:8 * i + 8, :],
              CKV.v(None)[1024 * i:1024 * (i + 1), :].re("(kb p) r -> p kb r", p=128), part=True)
    P.dma("sp", ckv[0:16, 32, :], CKV.v(None)[4096:4112, :], part=True)
    qix = [P.sb(f"t_qix{i}", [128, 4, 128], BF16) for i in range(2)]
    wid = [P.sb(f"t_wid{i}", [128, 8], F32) for i in range(2)]
    qlt = [P.sb(f"t_qlt{i}", [128, 2, 8, 128], BF16) for i in range(3)]
    score = [P.sb(f"t_score{i}", [128, T], F32) for i in range(2)]
    junk = P.sb("t_junk", [128, T], BF16)
    msk = P.sb("t_msk", [128, T + 16], BF16)
    mskTs = [P.sb(f"t_mskT{i}", [128, 40, 128], BF16) for i in range(2)]
    rl = [P.sb(f"t_rl{i}", [128, 512], F32) for i in range(2)]
    ex = [P.sb(f"t_ex{i}", [128, 4, 128], BF16) for i in range(3)]
    pp = [P.sb(f"t_pp{i}", [128, 4, 128], BF16) for i in range(3)]
    bss = [P.sb(f"t_bs{i}", [128, 8], F32) for i in range(2)]
    rden = P.sb("t_rden", [128, 8], F32)
    fr = P.sb("t_fr", [128, NBIS], F32)
    for r in range(1, NBIS + 1):
        P.op("dve", lambda e, o=fr[:, r - 1:r], v=float(2.0 ** -r): e.memset(o.ap, v), wp=[fr[:]])
    stepf = P.sb("t_stepf", [128, NBIS], F32)
    on = P.sb("t_on", [128, 8, 256], BF16)
    olT = [P.sb(f"t_olT{i}", [128, 2, 8, 128], BF16) for i in range(2)]
    zb = P.sb("t_zb", [128, 512], BF16)
    P.memset("pool", zb[:], 0.0)
    po = [P.ps(f"t_po{i}", [128, 2, 256], F32) for i in range(4)]
    pden = P.ps("t_pden", [128, 512], F32)
    pl = [P.ps(f"t_pl{i}", [128, 4, 128], F32) for i in range(2)]
    px = P.ps("t_px", [128, 512], F32)
    pxs = [px, px]
    pxb = View(px.h[:].bitcast(BF16), px.buf)
    QIDXv = fm(QIDXT)
    QLATv = QLATT.v(None).re("(p a) t -> p a t", a=16)
    OLATv = OLATT.v(None).re("(p a) t -> p a t", a=16)
    WIDXv = WIDX.v(None)
    cnt = {"nl": 0}

    def geom(qi):
        if qi == 0:
            return 0, 16, 16
        return 16 + 128 * (qi - 1), 128, 16 + 128 * qi

    def stageA1(qi):
        q0, nq, nk = geom(qi)
        k2 = qi % 2
        qx, wd, ql, sc = qix[k2], wid[k2], qlt[qi % 3], score[k2]
        P.dma("sp", qx[:, :, :nq], QIDXv[:, :, q0:q0 + nq])
        P.dma("sp", wd[:nq, :], WIDXv[q0:q0 + nq, :])
        P.dma("sp", ql[:, :, :, :nq].re("p rc h t -> p (rc h) t"), QLATv[:, :, q0:q0 + nq])
        for c0 in range(0, nk, 512):
            cw = min(512, nk - c0)
            for h in range(8):
                oc, half = h // 2, h % 2
                lo_, hi_ = half * 64, half * 64 + 64
                px_ = pxs[h % 2]
                P.mm(px_[:nq, :cw], qx[lo_:hi_, oc, :nq], kidx2[lo_:hi_, c0:c0 + cw])
                r_ = rl[h % 2]
                P.act(r_[:nq, :cw], px_[:nq, :cw], AF.Relu)
                if h == 0:
                    P.ts("dve", sc[:nq, c0:c0 + cw], r_[:nq, :cw], wd[:nq, 0:1], ALU.mult, part=(c0 > 0))
                else:
                    P.stt(sc[:nq, c0:c0 + cw], r_[:nq, :cw], wd[:nq, h:h + 1], sc[:nq, c0:c0 + cw],
                          ALU.mult, ALU.add, part=True)
                yield 1.0

    def stageA2(qi):
        q0, nq, nk = geom(qi)
        k2 = qi % 2
        sc, bs, mskT = score[k2], bss[k2], mskTs[k2]
        P.op("dve", lambda e, o=bs[:nq, 0:1], i=sc[:nq, :nk]: e.tensor_reduce(o.ap, i.ap, AX.X, ALU.max),
             r=[sc[:]], w=[bs[:]])
        P.op("dve", lambda e, o=bs[:nq, 1:2], i=sc[:nq, :nk]: e.tensor_reduce(o.ap, i.ap, AX.X, ALU.min),
             r=[sc[:]], wp=[bs[:]])
        if qi >= 1:
            P.op("dve", lambda e, o=sc[0:64, nk - 64:nk]: e.memset(o.ap, NEG), r=[bs[:]], wp=[sc[:]])
        P.tt("dve", bs[:nq, 2:3], bs[:nq, 0:1], bs[:nq, 1:2], ALU.subtract, part=True)
        yield 7.0
        if nk > KSEL:
            P.ts("dve", stepf[:nq, :], fr[:nq, :], bs[:nq, 2:3], ALU.mult)
            P.tt("dve", bs[:nq, 3:4], bs[:nq, 1:2], stepf[:nq, 0:1], ALU.add, part=True)
            for r in range(1, NBIS + 1):
                P.ts("dve", junk[:nq, :nk], sc[:nq, :nk], bs[:nq, 3:4], ALU.is_ge, 0.0, ALU.add,
                     accum=bs[:nq, 4:5])
                last = (r == NBIS)
                P.ts("dve", bs[:nq, 5:6], bs[:nq, 4:5], KSEL - 0.5, ALU.is_ge, -1.0 if last else -0.5, ALU.add,
                     part=True)
                P.stt(bs[:nq, 1:2] if last else bs[:nq, 3:4], bs[:nq, 5:6], stepf[:nq, r - 1:r], bs[:nq, 3:4],
                      ALU.mult, ALU.add, part=True)
                yield nk * 1.6 / 960.0 + 0.8
        P.ts("dve", msk[:nq, :nk], sc[:nq, :nk], bs[:nq, 1:2], ALU.is_ge)
        nkb = (nk + 127) // 128
        for b0 in range(0, nkb, 8):
            nb_ = min(8, nkb - b0)
            for j in range(nb_):
                kb = b0 + j
                kl = min(128, nk - kb * 128)
                P.op("pe", lambda e, o=pxb[:kl, j * 128:j * 128 + nq], i=msk[:nq, kb * 128:kb * 128 + kl],
                     idn=C.identb[:nq, :nq]: e.transpose(o.ap, i.ap, idn.ap), r=[msk[:], C.identb],
                     w=[px[:]] if j == 0 else (), wp=[px[:]] if j else ())
            P.copy("act", mskT[:, b0:b0 + nb_, :].re("p a b -> p (a b)"), pxb[:, 0:nb_ * 128], part=(b0 > 0))
            yield 3.0

    def stageB(qi):
        q0, nq, nk = geom(qi)
        k2 = qi % 2
        ql, mskT = qlt[qi % 3], mskTs[k2]
        nkb = (nk + 127) // 128
        for i in range(4):
            P.mm(po[i][:, :, :].re("p a b -> p (a b)"), zb[:, 0:128], zb[:, :], True, False)
        P.mm(pden[:, :], zb[:, 0:128], zb[:, :], True, False)
        steps = [(kb, g) for kb in range(nkb) for g in range(2)]
        base = cnt["nl"]
        cnt["nl"] += len(steps)

        def front(n):
            kb, g = steps[n]
            kl = min(128, nk - kb * 128)
            nl = base + n
            p_l, e_, p_ = pl[nl % 2], ex[nl % 3], pp[nl % 3]
            for rc in range(2):
                P.mm(p_l[:kl, :, :nq], ckvT[:, rc, kb * 128:kb * 128 + kl], ql[:, rc, 4 * g:4 * g + 4, :nq],
                     rc == 0, rc == 1)
            P.act(e_[:kl, :, :nq], p_l[:kl, :, :nq], AF.Exp)
            P.tt("pool", p_[:kl, :, :nq], e_[:kl, :, :nq], bc3(mskT[:kl, kb, :nq], 4), ALU.mult)

        def back(n):
            kb, g = steps[n]
            kl = min(128, nk - kb * 128)
            p_ = pp[(base + n) % 3]
            for h4 in range(4):
                h = 4 * g + h4
                P.mm(po[h // 2][:nq, h % 2, :], p_[:kl, h4, :nq], ckv[:kl, kb, :], False, kb == nkb - 1)
                P.mm(pden[:nq, h:h + 1], p_[:kl, h4, :nq], C.onesb[:kl, 0:1], False, kb == nkb - 1)

        front(0)
        for n in range(len(steps)):
            if n + 1 < len(steps):
                front(n + 1)
            back(n)
            yield 3.0
        P.op("dve", lambda e, o=rden[:nq, :], i=pden[:nq, 0:8]: e.reciprocal(o.ap, i.ap), r=[pden[:]], w=[rden[:]])
        for h in range(8):
            if (h // 2) % 2:
                P.act(on[:nq, h, :], po[h // 2][:nq, h % 2, :], AF.Identity, scale=rden[:nq, h:h + 1], part=(h > 0))
            else:
                P.ts("dve", on[:nq, h, :], po[h // 2][:nq, h % 2, :], rden[:nq, h:h + 1], ALU.mult, part=(h > 0))
        yield 4.0
        ot = olT[k2]
        for rc in range(2):
            for h in range(8):
                P.op("pe", lambda e, o=pxb[:, h * 128:h * 128 + nq], i=on[:nq, h, rc * 128:(rc + 1) * 128],
                     idn=C.identb[:nq, :nq]: e.transpose(o.ap, i.ap, idn.ap), r=[on[:], C.identb],
                     w=[px[:]] if h == 0 else (), wp=[px[:]] if h else ())
            src = pxb[:, :].re("p (h q) -> p h q", h=8)[:, :, :nq]
            if rc:
                P.copy("act", ot[:, rc, :, :nq], src, part=True)
            else:
                P.copy("dve", ot[:, rc, :, :nq], src, part=False)
            yield 3.0
        P.dma("pool", OLATv[:, :, q0:q0 + nq], ot[:, :, :, :nq].re("p rc h t -> p (rc h) t"), part=True)

    def total(qi, which):
        q0, nq, nk = geom(qi)
        nkb = (nk + 127) // 128
        if which == "A1":
            return ((nk + 511) // 512) * 8 * 1.0
        if which == "A2":
            return 7.0 + (NBIS * (nk * 1.6 / 960.0 + 0.8) if nk > KSEL else 0.0) + ((nkb + 7) // 8) * 3.0
        return nkb * 2 * 3.0 + 4.0 + 6.0

    for it in range(35):
        live = []
        if it < 33:
            live.append([stageA1(it), 0.0, total(it, "A1")])
        if 1 <= it < 34:
            live.append([stageA2(it - 1), 0.0, total(it - 1, "A2")])
        if it >= 2:
            live.append([stageB(it - 2), 0.0, total(it - 2, "B")])
        while live:
            g_ = min(live, key=lambda t: t[1] / t[2])
            try:
                g_[1] += next(g_[0])
            except StopIteration:
                live.remove(g_)
    P.flush()
    P.release(m)
```

```python
import numpy as np
import concourse.bass as bass
import concourse.mybir as mybir
from concourse.bass_utils import run_bass_kernel_spmd

F32 = mybir.dt.float32
BF16 = mybir.dt.bfloat16
AF = mybir.ActivationFunctionType
ALU = mybir.AluOpType
AX = mybir.AxisListType

SEM_LIMIT = 30000


class Buf:
    __slots__ = ("name", "w", "r", "wf")

    def __init__(self, name):
        self.name = name
        self.w = {}
        self.r = {}
        self.wf = {}


class View:
    __slots__ = ("ap", "buf")

    def __init__(self, ap, buf):
        self.ap = ap
        self.buf = buf

    def __getitem__(self, idx):
        return View(self.ap[idx], self.buf)

    def re(self, pattern, **kw):
        return View(self.ap.rearrange(pattern, **kw), self.buf)


class Tile:
    def __init__(self, handle, buf):
        self.h = handle
        self.buf = buf

    def __getitem__(self, idx):
        return View(self.h[idx], self.buf)


class DramT:
    def __init__(self, handle, name):
        self.h = handle
        self.name = name
        self.bufs = {}

    def __getitem__(self, idx):
        return self.v(None)[idx]

    def v(self, key):
        if key not in self.bufs:
            self.bufs[key] = Buf(f"{self.name}:{key}")
        return View(self.h.ap(), self.bufs[key])


class _Op:
    __slots__ = ("fn", "deps", "sig", "val", "dma", "sem")

    def __init__(self, fn, deps, dma):
        self.fn = fn
        self.deps = deps
        self.sig = dma is not None
        self.val = None
        self.dma = dma
        self.sem = None


ENGS = ("pe", "act", "dve", "pool", "sp")
NSLOT = 8


class Prog:
    def __init__(self, nc):
        self.nc = nc
        self.ops = {e: [] for e in ENGS}
        self.ctx = []
        self.dma_cnt = {}
        self.dma_rr = {q: 0 for q in ENGS}
        self.n_sb = 0

    def dram(self, name, shape, dt, kind="Internal"):
        h = self.nc.dram_tensor(name, list(shape), dt, kind=kind)
        return DramT(h, name)

    def sb(self, name, shape, dt):
        self.n_sb += 1
        name = f"{name}_{self.n_sb}"
        g = self.nc.sbuf_tensor(name, list(shape), dt)
        h = g.__enter__()
        self.ctx.append(g)
        return Tile(h, Buf(name))

    def ps(self, name, shape, dt=F32):
        self.n_sb += 1
        name = f"{name}_{self.n_sb}"
        g = self.nc.psum_tensor(name, list(shape), dt)
        h = g.__enter__()
        self.ctx.append(g)
        return Tile(h, Buf(name))

    def op(self, eng, fn, r=(), w=(), wp=(), dmaq=False):
        deps = {}

        def add(d):
            for k, v in d.items():
                if deps.get(k, -1) < v:
                    deps[k] = v

        for v in r:
            add(v.buf.w)
        for v in w:
            add(v.buf.w)
            add(v.buf.r)
        for v in wp:
            add(v.buf.r)
            add(v.buf.wf)
        idx = len(self.ops[eng])
        dma = None
        if dmaq:
            slot = self.dma_rr[eng]
            self.dma_rr[eng] = (slot + 1) % NSLOT
            key = ("d", eng, slot)
            n = self.dma_cnt.get(key, 0)
            if n > 0:
                if deps.get(key, -1) < n:
                    deps[key] = n
            self.dma_cnt[key] = n + 1
            dma = (key, n + 1)
            tok = (key, n + 1)
        else:
            tok = (("c", eng), idx)
        if eng == "pe":
            deps.pop(("c", "pe"), None)
        self.ops[eng].append(_Op(fn, deps, dma))
        for v in r:
            if v.buf.r.get(tok[0], -1) < tok[1]:
                v.buf.r[tok[0]] = tok[1]
        for v in w:
            v.buf.w = {tok[0]: tok[1]}
            v.buf.wf = {tok[0]: tok[1]}
            v.buf.r = {}
        for v in wp:
            if v.buf.w.get(tok[0], -1) < tok[1]:
                v.buf.w[tok[0]] = tok[1]

    def _e(self, eng):
        return eng

    def mm(self, out, lhsT, rhs, start=True, stop=True, **kw):
        self.op("pe", lambda e: e.matmul(out.ap, lhsT.ap, rhs.ap, start=start, stop=stop, **kw),
                r=[lhsT, rhs], wp=[out] if not start else (), w=[out] if start else ())

    def transpose(self, out, in_, ident):
        self.op("pe", lambda e: e.transpose(out.ap, in_.ap, ident.ap), r=[in_, ident], w=[out])

    def act(self, out, in_, func, scale=1.0, bias=0.0, accum=None, part=False):
        r = [in_]
        sc = scale
        bi = bias
        if isinstance(scale, View):
            r.append(scale)
            sc = scale.ap
        if isinstance(bias, View):
            r.append(bias)
            bi = bias.ap
        kw = {}
        ws = [out]
        if accum is not None:
            kw["accum_out"] = accum.ap
            ws.append(accum)
        self.op("act", lambda e: e.activation(out.ap, in_.ap, func, scale=sc, bias=bi, **kw),
                r=r, w=() if part else ws, wp=ws if part else ())

    def tt(self, eng, out, a, b, op, part=False):
        self.op(eng, lambda e: e.tensor_tensor(out.ap, a.ap, b.ap, op), r=[a, b],
                w=() if part else [out], wp=[out] if part else ())

    def ts(self, eng, out, a, s1, op0, s2=None, op1=None, accum=None, part=False):
        r = [a]
        x1 = s1
        x2 = s2
        if isinstance(s1, View):
            r.append(s1)
            x1 = s1.ap
        if isinstance(s2, View):
            r.append(s2)
            x2 = s2.ap
        kw = {}
        ws = [out]
        if op1 is not None:
            kw["op1"] = op1
        if accum is not None:
            kw["accum_out"] = accum.ap
            ws.append(accum)
        self.op(eng, lambda e: e.tensor_scalar(out.ap, a.ap, x1, x2, op0, **kw), r=r,
                w=() if part else ws, wp=ws if part else ())

    def stt(self, out, in0, scalar, in1, op0, op1, part=False, eng="dve"):
        r = [in0, in1]
        sc = scalar
        if isinstance(scalar, View):
            r.append(scalar)
            sc = scalar.ap
        self.op(eng, lambda e: e.scalar_tensor_tensor(out.ap, in0.ap, sc, in1.ap, op0, op1), r=r,
                w=() if part else [out], wp=[out] if part else ())

    def copy(self, eng, out, in_, part=False):
        if eng == "act":
            self.op(eng, lambda e: e.copy(out.ap, in_.ap), r=[in_],
                    w=() if part else [out], wp=[out] if part else ())
        else:
            self.op(eng, lambda e: e.tensor_copy(out.ap, in_.ap), r=[in_],
                    w=() if part else [out], wp=[out] if part else ())

    def memset(self, eng, out, val):
        self.op(eng, lambda e: e.memset(out.ap, val), w=[out])

    def scan(self, out, d0, d1, init, op0, op1, part=False):
        r = [d0, d1]
        ini = init
        if isinstance(init, View):
            r.append(init)
            ini = init.ap
        self.op("dve", lambda e: e.tensor_tensor_scan(out.ap, d0.ap, d1.ap, ini, op0, op1), r=r,
                w=() if part else [out], wp=[out] if part else ())

    def dma(self, q, out, in_, part=False, **kw):
        self.op(q, lambda e: e.dma_start(out=out.ap, in_=in_.ap, **kw), r=[in_],
                w=() if part else [out], wp=[out] if part else (), dmaq=True)

    def _sem(self, name):
        if name not in self.sems:
            g = self.nc.semaphore(name)
            self.sems[name] = g.__enter__()
            self.sem_ctx.append(g)
        return self.sems[name]

    def _dma_tok(self, key, n):
        per = SEM_LIMIT // 16
        j = (n - 1) // per
        return f"d_{key[1]}_{key[2]}_{j}", 16 * ((n - 1) % per + 1)

    def _resolve(self, k, v):
        if k[0] == "c":
            lst = self.ops[k[1]]
            while lst[v].val is None:
                v += 1
            o = lst[v]
            return o.sem, o.val
        return self._dma_tok(k, v)

    def mark(self):
        return len(self.ctx)

    def flush(self, final=False):
        nc = self.nc
        ops = self.ops
        if not hasattr(self, "sems"):
            self.sems = {}
            self.sem_ctx = []
            self.emitted = {e: 0 for e in ENGS}
            self.sigcnt = {e: 0 for e in ENGS}
            self.seen = {e: {} for e in ENGS}
        for e in ENGS:
            for o in ops[e][self.emitted[e]:]:
                for k, v in o.deps.items():
                    if k[0] == "c" and v >= self.emitted[k[1]]:
                        ops[k[1]][v].sig = True
        fdeps = {}
        for e in ENGS:
            for j in range(len(ops[e]) - 1, self.emitted[e] - 1, -1):
                if ops[e][j].dma is None and ops[e][j].fn is not None:
                    ops[e][j].sig = True
                    fdeps[("c", e)] = j
                    break
        for key, n in self.dma_cnt.items():
            fdeps[key] = n
        for e in ENGS:
            ops[e].append(_Op(None, dict(fdeps), None))
        for e in ENGS:
            for o in ops[e][self.emitted[e]:]:
                if o.dma is None and o.sig and o.fn is not None:
                    c = self.sigcnt[e]
                    o.sem = f"c_{e}_{c // SEM_LIMIT}"
                    o.val = c % SEM_LIMIT + 1
                    self._sem(o.sem)
                    self.sigcnt[e] = c + 1
                elif o.dma is not None:
                    self._sem(self._dma_tok(*o.dma)[0])
        block = nc.Block()
        block.__enter__()
        names = {"pe": "tensor", "act": "scalar", "dve": "vector", "pool": "gpsimd", "sp": "sync"}
        for e in ENGS:
            def body(eng, e=e):
                seen = self.seen[e]
                for o in ops[e][self.emitted[e]:]:
                    for k, v in o.deps.items():
                        s, val = self._resolve(k, v)
                        if seen.get(s, 0) >= val:
                            continue
                        seen[s] = val
                        eng.wait_ge(self.sems[s], val)
                    if o.fn is None:
                        continue
                    ins = o.fn(eng)
                    if o.dma is not None:
                        s, val = self._dma_tok(*o.dma)
                        ins.then_inc(self.sems[s], 16)
                    elif o.sig:
                        ins.then_inc(self.sems[o.sem], 1)
                    o.fn = True
            getattr(block, names[e])(body)
        block.__exit__(None, None, None)
        for e in ENGS:
            self.emitted[e] = len(ops[e])

    def release(self, mark):
        while len(self.ctx) > mark:
            self.ctx.pop().__exit__(None, None, None)

    def finish(self):
        self.flush(final=True)
        self.release(0)
        for g in reversed(self.sem_ctx):
            g.__exit__(None, None, None)


D = 1024
SEQ = 4096
NMETA = 16
T = SEQ + NMETA
DFF = 2816
NFC = DFF // 128
ALPHA = 4.0 ** 0.25
EPS = 1e-5
TILES = [(i * 512, 512) for i in range(8)] + [(4096, 16)]
KSEL = 256
NBIS = 14


def bc(view, shape):
    return View(view.ap.broadcast_to(list(shape)), view.buf)


def bc3(view, n):
    a = view.ap.unsqueeze(1)
    shp = list(a.shape)
    shp[1] = n
    return View(a.broadcast_to(shp), view.buf)


class VecPack:
    def __init__(self):
        self.cols = []
        self.off = {}
        self.n = 0

    def add(self, name, vec):
        vec = np.asarray(vec, np.float32).reshape(-1)
        nchunk = (vec.size + 127) // 128
        pad = np.zeros(nchunk * 128, np.float32)
        pad[:vec.size] = vec
        self.cols.append(pad.reshape(nchunk, 128).T)
        self.off[name] = self.n
        self.n += nchunk

    def array(self):
        return np.ascontiguousarray(np.concatenate(self.cols, axis=1))


def vec_layout():
    vp = VecPack()
    pack_vecs(vp, None)
    return vp.off, vp.n


def pack_vecs(vp, inp):
    def g(name, shape):
        if inp is None:
            return np.zeros(shape, np.float32)
        return np.asarray(inp[name], np.float32)

    cw = g("a_conv_w", (1, 4, D))[0]
    for j in range(4):
        vp.add(f"a_cw{j}", cw[j])
    vp.add("a_cb", g("a_conv_b", (1, D))[0])
    vp.add("a_ng", g("a_norm_g", (1, D))[0])
    cw = g("b_conv_w", (1, 4, D))[0]
    for j in range(4):
        vp.add(f"b_cw{j}", cw[j])
    vp.add("b_cb", g("b_conv_b", (1, D))[0])
    vp.add("b_ba", g("b_b_a", (1, D))[0])
    vp.add("b_bx", g("b_b_x", (1, D))[0])
    vp.add("b_lam", g("b_lambda", (1, D))[0])
    mg = g("mix_ln_g", (2, D))
    mb = g("mix_ln_b", (2, D))
    fg = g("ffn_ln_g", (2, D))
    fb = g("ffn_ln_b", (2, D))
    fcw = g("ffn_conv_w", (2, 3, DFF))
    fcb = g("ffn_conv_b", (2, DFF))
    for l in range(2):
        vp.add(f"mix_g{l}", mg[l])
        vp.add(f"mix_b{l}", mb[l])
        vp.add(f"ffn_g{l}", fg[l])
        vp.add(f"ffn_b{l}", fb[l])
        for j in range(3):
            vp.add(f"f_cw{l}_{j}", fcw[l, j])
        vp.add(f"f_cb{l}", fcb[l])
    vp.add("c_qg", g("c_q_norm_g", (1, 384))[0])
    vp.add("c_kvg", g("c_kv_norm_g", (1, 256))[0])
    kg = g("c_kidx_ln_g", (1, 64))[0]
    kb = g("c_kidx_ln_b", (1, 64))[0]
    vp.add("c_kig", np.concatenate([kg, kg]))
    vp.add("c_kib", np.concatenate([kb, kb]))


def block_diag_host(w):
    w = np.asarray(w, np.float32)
    out = np.zeros((8, 128, 128), np.float32)
    for n in range(256):
        c, b = divmod(n, 32)
        out[c, 4 * b:4 * b + 4, 4 * b:4 * b + 4] = w[n]
    return out


def make_consts():
    c = np.zeros((128, 1536), np.float32)
    c[:, 0:128] = np.eye(128, dtype=np.float32)
    c[:, 128:256] = np.triu(np.ones((128, 128), np.float32))
    c[:, 256:384] = 1.0
    for h in range(4):
        c[h, 384 + 128 * h: 384 + 128 * (h + 1)] = 1.0
        c[4 + h, 896 + 128 * h: 896 + 128 * (h + 1)] = 1.0
        c[h, 1408 + h] = 1.0
        c[4 + h, 1412 + h] = 1.0
    return c


class Ctx:
    pass


def fm(dram, key=None):
    return dram.v(key).re("(c p) t -> p c t", p=128)


def load_w(P, dst, src_view, KC):
    for kc in range(KC):
        P.dma("pool", dst[:, kc, :], src_view[:, kc, :], part=True, max_dma_last_dim=4096)


def setup_consts(P, C, din):
    C.cf = P.sb("c_f32", [128, 384], F32)
    P.dma("sp", C.cf[:], din["consts"][:, 0:384])
    C.ident = C.cf[:, 0:128]
    C.triu = C.cf[:, 128:256]
    C.ones = C.cf[:, 256:384]
    C.cb = P.sb("c_bf16", [128, 384], BF16)
    P.copy("dve", C.cb[:], C.cf[:, 0:384])
    C.identb = C.cb[:, 0:128]
    C.triub = C.cb[:, 128:256]
    C.onesb = C.cb[:, 256:384]
    C.voff, nv = vec_layout()
    C.vec = P.sb("c_vec", [128, nv], F32)
    P.dma("sp", C.vec[:], din["vecs"][:])
    C.mean1024 = P.sb("c_mean", [128, 128], F32)
    P.memset("dve", C.mean1024[:], 1.0 / 1024.0)
    C.cc = P.sb("c_cols", [128, 4], F32)
    P.memset("dve", C.cc[:, 0:1], EPS)
    P.op("dve", lambda e: e.memset(C.cc[:, 1:2].ap, 1.0), wp=[C.cc[:, 1:2]])
    P.op("dve", lambda e: e.memset(C.cc[:, 2:3].ap, -float(np.log(16.0))), wp=[C.cc[:, 2:3]])
    P.op("dve", lambda e: e.memset(C.cc[:, 3:4].ap, 0.0), wp=[C.cc[:, 3:4]])
    C.epsc = C.cc[:, 0:1]
    C.onec = C.cc[:, 1:2]
    C.nl16 = C.cc[:, 2:3]


def vcol(C, name, c=0, n=1):
    o = C.voff[name] + c
    return C.vec[:, o:o + n]


def ln_gen(P, C, y, w, gname, bname, out_f32, tmp, psm, psq, sq, out_bf=None):
    for c in range(8):
        s = sq[c % 2]
        P.tt("pool", s[:, :w], y[:, c, :w], y[:, c, :w], ALU.mult)
        P.mm(psm[:, :w], C.mean1024[:], y[:, c, :w], c == 0, c == 7)
        P.mm(psq[:, :w], C.mean1024[:], s[:, :w], c == 0, c == 7)
        yield
    mean = tmp["mean"]
    m2 = tmp["m2"]
    rstd = tmp["rstd"]
    P.copy("act", mean[:, :w], psm[:, :w])
    P.tt("pool", m2[:, :w], mean[:, :w], mean[:, :w], ALU.mult)
    P.tt("dve", m2[:, :w], psq[:, :w], m2[:, :w], ALU.subtract)
    P.act(rstd[:, :w], m2[:, :w], AF.Sqrt, bias=C.epsc[:, 0:1])
    P.op("dve", lambda e, o=rstd[:, :w]: e.reciprocal(o.ap, o.ap), r=[rstd[:, :w]], w=[rstd[:, :w]])
    yield
    for c in range(8):
        t = sq[c % 2]
        P.tt("dve", t[:, :w], y[:, c, :w], mean[:, :w], ALU.subtract)
        P.stt(t[:, :w], t[:, :w], vcol(C, gname, c), rstd[:, :w], ALU.mult, ALU.mult)
        P.act(out_f32[:, c, :w], t[:, :w], AF.Identity, bias=vcol(C, bname, c), part=True)
        if out_bf is not None:
            P.copy("pool", out_bf[:, c, :w], out_f32[:, c, :w], part=True)
        yield


def layer_norm_fm(P, C, y, w, gname, bname, out_f32, tmp, psm, psq, sq, out_bf=None):
    for _ in ln_gen(P, C, y, w, gname, bname, out_f32, tmp, psm, psq, sq, out_bf):
        pass


def phase_embed(P, C, din, H0):
    m = P.mark()
    x = din["x"]
    xin = [P.sb(f"e_xin{i}", [128, 4, 1024], F32) for i in range(2)]
    hT = [P.sb(f"e_hT{i}", [128, 8, 512], F32) for i in range(2)]
    pt = [P.ps(f"e_pt{i}", [128, 512], F32) for i in range(4)]
    H0v = fm(H0)
    xv = x.v(None).re("(k j p) f -> k p j f", j=4, p=128)
    n = 0
    for k in range(9):
        xi = xin[k % 2]
        h = hT[k % 2]
        if k == 8:
            P.dma("sp", xi[0:16, 0, :], din["meta"][:])
            nj, rows, pos, w = 1, 16, 0, 16
        else:
            P.dma("sp", xi[:], xv[k])
            nj, rows, pos, w = 4, 128, 16 + 512 * k, 512
        for c in range(8):
            p_ = pt[n % 4]
            n += 1
            for j in range(nj):
                P.op("pe", lambda e, o=p_[:, j * rows:(j + 1) * rows], i=xi[0:rows, j, c * 128:(c + 1) * 128],
                     idn=C.ident[0:rows, 0:rows]: e.transpose(o.ap, i.ap, idn.ap),
                     r=[xi[:], C.ident], w=[p_[:]] if j == 0 else (), wp=[p_[:]] if j else ())
            if c % 2 == 0:
                P.copy("act", h[:, c, :w], p_[:, :w], part=True)
            else:
                P.copy("dve", h[:, c, :w], p_[:, :w], part=True)
        P.dma("pool", H0v[:, :, pos:pos + w], h[:, :, :w], part=True)
    P.flush()
    P.release(m)


def phase_inproj(P, C, din, H0, XM, SOG, XR, GGR):
    m = P.mark()
    w_in = P.sb("w_in", [128, 8, 4096], BF16)
    load_w(P, w_in, din["a_w_in"].v(None).re("(kc p) n -> p kc n", p=128), 8)
    hf = [P.sb(f"i_hf{i}", [128, 8, 512], F32) for i in range(2)]
    hb = [P.sb(f"i_hb{i}", [128, 8, 512], BF16) for i in range(2)]
    stg = [P.sb(f"i_st{i}", [128, 8, 512], F32) for i in range(2)]
    pss = [P.ps(f"i_ps{i}", [128, 512], F32) for i in range(4)]
    outs = [fm(XM), fm(SOG), fm(XR), fm(GGR)]
    H0v = fm(H0)
    n = 0
    for ti, (pos, w) in enumerate(TILES):
        f = hf[ti % 2]
        b = hb[ti % 2]
        P.dma("sp", f[:, :, :w], H0v[:, :, pos:pos + w])
        P.copy("pool", b[:, :, :w], f[:, :, :w])
        for grp in range(4):
            st = stg[(ti * 4 + grp) % 2]
            for o8 in range(8):
                oc = grp * 8 + o8
                ps = pss[n % 4]
                n += 1
                for kc in range(8):
                    P.mm(ps[:, :w], w_in[:, kc, oc * 128:(oc + 1) * 128], b[:, kc, :w], kc == 0, kc == 7)
                if grp == 1:
                    P.act(st[:, o8, :w], ps[:, :w], AF.Sigmoid, part=True)
                elif grp == 3:
                    P.act(st[:, o8, :w], ps[:, :w], AF.Gelu_apprx_tanh, part=True)
                elif o8 % 2 == 0:
                    P.copy("act", st[:, o8, :w], ps[:, :w], part=True)
                else:
                    P.copy("dve", st[:, o8, :w], ps[:, :w], part=True)
            P.dma("sp", outs[grp][:, :, pos:pos + w], st[:, :, :w], part=True)
    P.flush()
    P.release(m)


def phase_rglru(P, C, din, XR, GGR, HR):
    m = P.mark()
    wa = P.sb("r_wa", [128, 8, 128], BF16)
    wx = P.sb("r_wx", [128, 8, 128], BF16)
    P.dma("pool", wa[:], din["b_w_a"][:])
    P.dma("pool", wx[:], din["b_w_x"][:])
    c8 = P.sb("r_c8", [128, 8], F32)
    P.act(c8[:], vcol(C, "b_lam", 0, 8), AF.Exp, scale=-1.0)
    P.act(c8[:], c8[:], AF.Ln, bias=C.onec[:, 0:1])
    P.ts("dve", c8[:], c8[:], -8.0, ALU.mult)
    carry = P.sb("r_carry", [128, 8], F32)
    P.memset("dve", carry[:], 0.0)
    xe = [P.sb(f"r_xe{i}", [128, 8, 515], F32) for i in range(2)]
    gg = [P.sb(f"r_gg{i}", [128, 8, 512], F32) for i in range(2)]
    ho = [P.sb(f"r_ho{i}", [128, 8, 512], BF16) for i in range(2)]
    xc2 = [P.sb(f"r_xc{i}", [128, 4, 512], F32) for i in range(2)]
    xcb2 = [P.sb(f"r_xcb{i}", [128, 4, 512], BF16) for i in range(2)]
    rr2 = [P.sb(f"r_r{i}", [128, 4, 512], F32) for i in range(2)]
    ii2 = [P.sb(f"r_i{i}", [128, 4, 512], F32) for i in range(2)]
    aa2 = [P.sb(f"r_a{i}", [128, 4, 512], F32) for i in range(2)]
    a22 = [P.sb(f"r_a2{i}", [128, 4, 512], F32) for i in range(2)]
    hs = P.sb("r_hs", [128, 4, 512], F32)
    pa = [P.ps(f"r_pa{i}", [128, 512], F32) for i in range(4)]
    px = [P.ps(f"r_px{i}", [128, 512], F32) for i in range(4)]
    XRv, GGv, HRv = fm(XR), fm(GGR), fm(HR)
    NU = 2 * len(TILES)

    def stA(u):
        ti, hf_ = divmod(u, 2)
        pos, w = TILES[ti]
        x, g = xe[ti % 2], gg[ti % 2]
        xc, xcb = xc2[u % 2], xcb2[u % 2]
        if hf_ == 0:
            if ti == 0:
                P.memset("pool", x[:, :, 0:3], 0.0)
                P.dma("sp", x[:, :, 3:3 + w], XRv[:, :, pos:pos + w], part=True)
            else:
                P.dma("sp", x[:, :, 0:3 + w], XRv[:, :, pos - 3:pos + w])
            P.dma("sp", g[:, :, :w], GGv[:, :, pos:pos + w])
        for k in range(4):
            c = hf_ * 4 + k
            P.ts("dve", xc[:, k, :w], x[:, c, 3:3 + w], vcol(C, "b_cw3", c), ALU.mult, vcol(C, "b_cb", c), ALU.add,
                 part=(k > 0))
            for j in range(3):
                P.stt(xc[:, k, :w], x[:, c, j:j + w], vcol(C, f"b_cw{j}", c), xc[:, k, :w], ALU.mult, ALU.add,
                      part=True)
        for k in range(4):
            P.copy("pool", xcb[:, k, :w], xc[:, k, :w], part=(k > 0))

    def stB(u):
        ti, hf_ = divmod(u, 2)
        pos, w = TILES[ti]
        xc, xcb, rr, ii, aa, a2 = xc2[u % 2], xcb2[u % 2], rr2[u % 2], ii2[u % 2], aa2[u % 2], a22[u % 2]
        cs = [hf_ * 4 + k for k in range(4)]
        for k, c in enumerate(cs):
            P.mm(pa[k][:, :w], wa[:, c, :], xcb[:, k, :w])
            P.mm(px[k][:, :w], wx[:, c, :], xcb[:, k, :w])
        for k, c in enumerate(cs):
            P.act(rr[:, k, :w], pa[k][:, :w], AF.Sigmoid, bias=vcol(C, "b_ba", c), part=(k > 0))
        for k, c in enumerate(cs):
            P.act(ii[:, k, :w], px[k][:, :w], AF.Sigmoid, bias=vcol(C, "b_bx", c), part=(k > 0))
        for k, c in enumerate(cs):
            P.act(aa[:, k, :w], rr[:, k, :w], AF.Exp, scale=c8[:, c:c + 1], part=(k > 0))
        for k, c in enumerate(cs):
            P.tt("pool", a2[:, k, :w], aa[:, k, :w], aa[:, k, :w], ALU.mult, part=(k > 0))
        for k, c in enumerate(cs):
            P.tt("pool", ii[:, k, :w], ii[:, k, :w], xc[:, k, :w], ALU.mult, part=True)
        for k, c in enumerate(cs):
            P.act(a2[:, k, :w], a2[:, k, :w], AF.Sqrt, scale=-1.0, bias=C.onec[:, 0:1], part=True)

    def stC(u):
        ti, hf_ = divmod(u, 2)
        pos, w = TILES[ti]
        g, h = gg[ti % 2], ho[ti % 2]
        ii, aa, a2 = ii2[u % 2], aa2[u % 2], a22[u % 2]
        cs = [hf_ * 4 + k for k in range(4)]
        for k, c in enumerate(cs):
            P.tt("dve", ii[:, k, :w], ii[:, k, :w], a2[:, k, :w], ALU.mult, part=True)
            P.scan(hs[:, k, :w], aa[:, k, :w], ii[:, k, :w], carry[:, c:c + 1], ALU.mult, ALU.add, part=(k > 0))
            P.copy("dve", carry[:, c:c + 1], hs[:, k, w - 1:w], part=True)
        for k, c in enumerate(cs):
            P.tt("pool", h[:, c, :w], hs[:, k, :w], g[:, c, :w], ALU.mult, part=True)
        if hf_ == 1:
            P.dma("sp", HRv[:, :, pos:pos + w], h[:, :, :w], part=True)

    for u in range(NU + 2):
        if u < NU:
            stA(u)
        if 1 <= u <= NU:
            stB(u - 1)
        if u >= 2:
            stC(u - 2)
    P.flush()
    P.release(m)


def phase_mlstm(P, C, din, XM, SOG, HM):
    m = P.mark()
    XMv, SOGv, HMv = fm(XM), fm(SOG), fm(HM)
    bd = {}
    for nm in ("bdq", "bdk", "bdv"):
        bd[nm] = P.sb("m_" + nm, [128, 8, 128], BF16)
        P.dma("pool", bd[nm][:], din[nm][:])
    pw = [P.ps(f"m_pw{i}", [128, 512], F32) for i in range(2)]
    pbc = P.ps("m_pbc", [128, 512], F32)
    pg = pbc
    pS = P.ps("m_pS", [128, 4, 128], F32)
    pT = View(pS.h[:].rearrange("p a b -> p (a b)").bitcast(BF16).rearrange("p (c t) -> p c t", c=8), pS.buf)
    pns = [P.ps(f"m_pn{i}", [128, 512], F32) for i in range(2)]
    pds = [P.ps(f"m_pd{i}", [128, 512], F32) for i in range(2)]
    bif = P.sb("m_bif", [8, 1], F32)
    P.dma("sp", bif[:], din["a_b_if"][:])
    rmask = P.sb("m_rmask", [8, 512], F32)
    P.memset("dve", rmask[:], 1.0)
    for j in range(4):
        P.op("dve", lambda e, v=rmask[:, j * 128:j * 128 + 1]: e.memset(v.ap, 0.0), wp=[rmask[:]])
    selt = P.sb("m_sel", [8, 1152], F32)
    P.dma("sp", selt[:], din["consts"][0:8, 384:1536])
    selI = [selt[0:8, 128 * h:128 * (h + 1)] for h in range(4)]
    selF = [selt[0:8, 512 + 128 * h:512 + 128 * (h + 1)] for h in range(4)]
    selI4 = selt[0:8, 1024:1028]
    selF4 = selt[0:8, 1028:1032]
    C32 = P.sb("m_C32", [128, 4, 2, 257], F32)
    Cb = P.sb("m_Cb", [128, 4, 2, 257], BF16)
    P.memset("dve", C32[:], 0.0)
    P.memset("pool", Cb[:], 0.0)
    Vaug_2 = [P.sb(f"m_Vaug{i}", [128, 4, 257], BF16) for i in range(2)]
    for v_ in Vaug_2:
        P.memset("pool", v_[:], 1.0)
    xe = P.sb("m_xe", [128, 8, 515], F32)
    m2 = P.mark()
    bdT_src = din["bdT"].v(None)
    for t_ in range(3):
        for hb_ in range(2):
            P.dma("sp", xe[:, t_ * 2 + hb_, 0:512], bdT_src[:, t_, 4 * hb_:4 * hb_ + 4, :].re("p c d -> p (c d)"),
                  part=(t_ + hb_ > 0))

    def bdTv(t_, c):
        return xe[:, t_ * 2 + c // 4, (c % 4) * 128:(c % 4 + 1) * 128]

    wif = P.sb("m_wif", [128, 24, 8], F32)
    P.dma("sp", wif[:], din["a_w_if"][:])
    wg = P.sb("m_wg", [128, 2, 8, 8], BF16)
    for c in range(8):
        P.mm(pg[:, c * 8:(c + 1) * 8], bdTv(0, c), wif[:, c, :], True, False)
        P.mm(pg[:, c * 8:(c + 1) * 8], bdTv(1, c), wif[:, 8 + c, :], False, True)
        P.mm(pg[:, 64 + c * 8:64 + (c + 1) * 8], bdTv(2, c), wif[:, 16 + c, :], True, True)
    P.copy("dve", wg[:].re("p a c g -> p (a c g)"), pg[:, 0:128])
    sog_2 = [P.sb(f"m_sog{i}", [128, 8, 512], F32) for i in range(2)]
    xcb_2 = [P.sb(f"m_xcb{i}", [128, 8, 512], BF16) for i in range(2)]
    xmb_2 = [P.sb(f"m_xmb{i}", [128, 8, 512], BF16) for i in range(2)]
    qh_2 = [P.sb(f"m_qh{i}", [128, 8, 512], BF16) for i in range(2)]
    kh_2 = [P.sb(f"m_kh{i}", [128, 8, 512], BF16) for i in range(2)]
    abc_2 = [P.sb(f"m_abc{i}", [128, 4, 512], F32) for i in range(2)]
    bbc = P.sb("m_bbc", [128, 4, 512], F32)
    hm = [P.sb("m_hm0", [128, 8, 512], BF16)] * 2
    cv = [P.sb(f"m_cv{i}", [128, 512], F32) for i in range(2)]
    Gall_2 = [P.sb(f"m_Gall{i}", [8, 512], F32) for i in range(2)]
    Lg_2 = [P.sb(f"m_Lg{i}", [8, 512], F32) for i in range(2)]
    lcar = P.sb("m_lcar", [8, 1], F32)
    btok_2 = [P.sb(f"m_btok{i}", [128, 4], F32) for i in range(2)]
    Kt_2 = [P.sb(f"m_Kt{i}", [128, 4, 256], BF16) for i in range(2)]
    ST_2 = [P.sb(f"m_ST{i}", [128, 4, 128], BF16) for i in range(2)]
    hn = P.sb("m_hn", [128, 4, 256], BF16)
    st6 = P.sb("m_st6", [128, 6], F32)
    mv = P.sb("m_mv", [128, 2], F32)
    sc = P.sb("m_sc", [128, 4], F32)
    ctmps = [P.sb(f"m_ctmp{i}", [128, 257], F32) for i in range(2)]
    n = 0
    def prologue(ti):
        pos, w = TILES[ti]
        sog, xcb, xmb, qh, kh, abc, Gall, Lg = (sog_2[ti % 2], xcb_2[ti % 2], xmb_2[ti % 2], qh_2[ti % 2], kh_2[ti % 2], abc_2[ti % 2], Gall_2[ti % 2], Lg_2[ti % 2])
        if ti == 0:
            P.memset("pool", xe[:, :, 0:3], 0.0)
            P.dma("sp", xe[:, :, 3:3 + w], XMv[:, :, pos:pos + w], part=True)
        else:
            P.dma("sp", xe[:, :, 0:3 + w], XMv[:, :, pos - 3:pos + w])
        P.dma("sp", sog[:, :, :w], SOGv[:, :, pos:pos + w])
        for c in range(8):
            k = c % 2
            P.ts("dve", cv[k][:, :w], xe[:, c, 3:3 + w], vcol(C, "a_cw3", c), ALU.mult)
            for j in range(3):
                P.stt(cv[k][:, :w], xe[:, c, j:j + w], vcol(C, f"a_cw{j}", c), cv[k][:, :w], ALU.mult, ALU.add)
            P.act(xcb[:, c, :w], cv[k][:, :w], AF.Silu, bias=vcol(C, "a_cb", c), part=(c > 0))
            P.copy("pool", xmb[:, c, :w], xe[:, c, 3:3 + w], part=(c > 0))
            P.act(sog[:, c, :w], sog[:, c, :w], AF.Identity, scale=vcol(C, "a_ng", c), part=True)
            yield
        for c in range(8):
            P.mm(pg[0:8, :w], wg[:, 0, c, :], xcb[:, c, :w], c == 0, False)
        for c in range(8):
            P.mm(pg[0:8, :w], wg[:, 1, c, :], xmb[:, c, :w], False, c == 7)
        P.act(Gall[:, :w], pg[0:8, :w], AF.Identity, bias=bif[:, 0:1])
        P.act(Lg[:, :w], Gall[:, :w], AF.Exp, scale=-1.0)
        P.act(Lg[:, :w], Lg[:, :w], AF.Ln, bias=C.onec[0:8, 0:1])
        P.scan(Lg[:, :w], rmask[:, :w], Lg[:, :w], 0.0, ALU.mult, ALU.add)
        for h in range(4):
            P.mm(pbc[:, :w], selF[h], Lg[:, :w], True, True)
            P.act(abc[:, h, :w], pbc[:, :w], AF.Exp, scale=-1.0, part=(h > 0))
            P.mm(pbc[:, :w], selI[h], Gall[:, :w], True, False)
            P.mm(pbc[:, :w], selF[h], Lg[:, :w], False, True)
            P.act(bbc[:, h, :w], pbc[:, :w], AF.Exp, bias=C.nl16, part=(h > 0))
            yield
        for c in range(8):
            P.mm(pw[0][:, :w], bd["bdq"][:, c, :], xcb[:, c, :w])
            P.tt("dve", qh[:, c, :w], pw[0][:, :w], abc[:, c // 2, :w], ALU.mult, part=(c > 0))
            P.mm(pw[1][:, :w], bd["bdk"][:, c, :], xcb[:, c, :w])
            P.tt("dve", kh[:, c, :w], pw[1][:, :w], bbc[:, c // 2, :w], ALU.mult, part=(c > 0))
            yield
        yield

    def chunks(ti):
        pos, w = TILES[ti]
        sog, xcb, xmb, qh, kh, abc, Gall, Lg = (sog_2[ti % 2], xcb_2[ti % 2], xmb_2[ti % 2], qh_2[ti % 2], kh_2[ti % 2], abc_2[ti % 2], Gall_2[ti % 2], Lg_2[ti % 2])
        hmt = hm[ti % 2]
        nch = (w + 127) // 128

        def stX(j):
            L = min(128, w - j * 128)
            cs = slice(j * 128, j * 128 + L)
            btok, Vaug, Kt, ST = btok_2[j % 2], Vaug_2[j % 2], Kt_2[j % 2], ST_2[j % 2]
            P.mm(pbc[:L, 0:4], Gall[:, cs], selI4, True, False)
            P.mm(pbc[:L, 0:4], Lg[:, cs], selF4, False, True)
            P.act(btok[:L, :], pbc[:L, 0:4], AF.Exp, bias=C.nl16[:L, :])
            for g2 in range(2):
                for c4 in range(4):
                    c = g2 * 4 + c4
                    P.mm(pw[g2][:L, c4 * 128:(c4 + 1) * 128], xmb[:, c, cs], bd["bdv"][:, c, :], True, True)
                P.copy("act", Vaug[:L, 2 * g2:2 * g2 + 2, 0:256], pw[g2][:L, :].re("p (h v) -> p h v", h=2), part=True)
            for g2 in range(2):
                for c4 in range(4):
                    c = g2 * 4 + c4
                    P.mm(pw[g2][:L, c4 * 128:(c4 + 1) * 128], xcb[:, c, cs], bd["bdk"][:, c, :], True, True)
                for h2 in range(2):
                    h = 2 * g2 + h2
                    P.ts("dve", Kt[:L, h, :], pw[g2][:L, h2 * 256:(h2 + 1) * 256], btok[:L, h:h + 1], ALU.mult, part=True)
            for h in range(4):
                for dc in range(2):
                    P.mm(pS[:L, h, :L], kh[:, 2 * h + dc, cs], qh[:, 2 * h + dc, cs], dc == 0, dc == 1)
            P.tt("dve", ST[:L, :, :L], pS[:L, :, :L], bc3(C.triu[:L, :L], 4), ALU.mult)

        def stY(j):
            L = min(128, w - j * 128)
            cs = slice(j * 128, j * 128 + L)
            btok, Vaug, Kt, ST = btok_2[j % 2], Vaug_2[j % 2], Kt_2[j % 2], ST_2[j % 2]
            for h in range(4):
                pn = pns[h % 2]
                P.mm(pn[:L, 0:257], ST[:L, h, :L], Vaug[:L, h, :], True, False)
                for dc in range(2):
                    P.mm(pn[:L, 0:257], qh[:, 2 * h + dc, cs], Cb[:, h, dc, :], False, dc == 1)
                A = abc[:, h, j * 128 + L - 1:j * 128 + L]
                for dc in range(2):
                    pd, ctmp = pds[dc], ctmps[dc]
                    P.mm(pd[:, 0:257], Kt[:L, h, dc * 128:(dc + 1) * 128], Vaug[:L, h, :], True, True)
                    P.tt("dve", ctmp[:], pd[:, 0:257], C32[:, h, dc, :], ALU.add)
                    P.act(Cb[:, h, dc, :], ctmp[:], AF.Identity, scale=A, part=True)
                    P.act(C32[:, h, dc, :], ctmp[:], AF.Identity, scale=A, part=True)
                P.copy("dve", sc[:L, 3:4], pn[:L, 256:257])
                P.tt("dve", sc[:L, 0:1], sc[:L, 3:4], sc[:L, 3:4], ALU.mult, part=True)
                P.ts("dve", sc[:L, 0:1], sc[:L, 0:1], 1.0, ALU.max, part=True)
                P.op("dve", lambda e, o=sc[:L, 0:1]: e.reciprocal(o.ap, o.ap), r=[sc[:]], wp=[sc[:]])
                P.op("dve", lambda e, o=st6[:L, :], i=pn[:L, 0:256]: e.bn_stats(o.ap, i.ap), r=[pn[:]], w=[st6[:]])
                P.op("dve", lambda e, o=mv[:L, :], i=st6[:L, :]: e.bn_aggr(o.ap, i.ap), r=[st6[:]], w=[mv[:]])
                P.ts("dve", sc[:L, 1:2], sc[:L, 0:1], mv[:L, 1:2], ALU.mult, EPS, ALU.add, part=True)
                P.op("dve", lambda e, o=sc[:L, 1:2]: e.reciprocal(o.ap, o.ap), r=[sc[:]], wp=[sc[:]])
                P.tt("dve", sc[:L, 1:2], sc[:L, 1:2], sc[:L, 0:1], ALU.mult, part=True)
                P.act(sc[:L, 2:3], sc[:L, 1:2], AF.Sqrt, part=True)
                P.ts("dve", hn[:L, h, :], pn[:L, 0:256], mv[:L, 0:1], ALU.subtract, sc[:L, 2:3], ALU.mult, part=(h > 0))
                yield
            for c in range(8):
                P.op("pe", lambda e, o=pT[:, c, :L], i=hn[:L, c // 2, (c % 2) * 128:(c % 2) * 128 + 128],
                     idn=C.identb[:L, :L]: e.transpose(o.ap, i.ap, idn.ap),
                     r=[hn[:], C.identb], w=[pT[:]] if c == 0 else (), wp=[pT[:]] if c else ())
            P.tt("dve", hmt[:, :, cs], pT[:, :, :L], sog[:, :, cs], ALU.mult, part=(j > 0))

        stX(0)
        for j in range(nch):
            if j + 1 < nch:
                stX(j + 1)
            for _ in stY(j):
                yield
        P.dma("sp", HMv[:, :, pos:pos + w], hmt[:, :, :w], part=True)
        yield

    pro = prologue(0)
    for _ in pro:
        pass
    for ti in range(len(TILES)):
        ck = chunks(ti)
        nxt = prologue(ti + 1) if ti + 1 < len(TILES) else None
        for _ in ck:
            if nxt is not None:
                try:
                    next(nxt)
                    next(nxt)
                except StopIteration:
                    nxt = None
        if nxt is not None:
            for _ in nxt:
                pass
    P.flush()
    P.release(m)


def ln_tmps(P, pfx):
    return {"mean": P.sb(pfx + "_mean", [128, 512], F32), "m2": P.sb(pfx + "_m2", [128, 512], F32),
            "rstd": P.sb(pfx + "_rstd", [128, 512], F32)}


def phase_outproj(P, C, wsrc, KC, ins_bf, Hin, Hout, layer, loader=None):
    m = P.mark()
    wo = P.sb("o_w", [128, KC, 1024], BF16)
    load_w(P, wo, wsrc.v(None).re("(kc p) n -> p kc n", p=128), KC)
    xin = [P.sb(f"o_x{i}", [128, KC, 512], BF16) for i in range(2)]
    hf = [P.sb(f"o_h{i}", [128, 8, 512], F32) for i in range(2)]
    sq = [P.sb(f"o_sq{i}", [128, 512], F32) for i in range(2)]
    tmp = ln_tmps(P, "o")
    pss = [P.ps(f"o_ps{i}", [128, 512], F32) for i in range(4)]
    psm = P.ps("o_psm", [128, 512], F32)
    psq = P.ps("o_psq", [128, 512], F32)
    Hiv, Hov = fm(Hin), fm(Hout)
    n = 0
    for ti, (pos, w) in enumerate(TILES):
        x = xin[ti % 2]
        h = hf[ti % 2]
        if loader is not None:
            loader(ti, pos, w, x, pss)
        for i, src in enumerate(ins_bf):
            P.dma("sp", x[:, 8 * i:8 * i + 8, :w], fm(src)[:, :, pos:pos + w], part=(i > 0))
        P.dma("sp", h[:, :, :w], Hiv[:, :, pos:pos + w])
        for oc in range(8):
            ps = pss[n % 4]
            n += 1
            for kc in range(KC):
                P.mm(ps[:, :w], wo[:, kc, oc * 128:(oc + 1) * 128], x[:, kc, :w], kc == 0, kc == KC - 1)
            P.stt(h[:, oc, :w], h[:, oc, :w], ALPHA, ps[:, :w], ALU.mult, ALU.add, part=True)
        layer_norm_fm(P, C, h, w, f"mix_g{layer}", f"mix_b{layer}", h, tmp, psm, psq, sq)
        P.dma("pool", Hov[:, :, pos:pos + w], h[:, :, :w], part=True)
    P.flush()
    P.release(m)


def phase_ffn(P, C, din, layer, Hin, Hout, final_out=None):
    m = P.mark()
    wu = P.sb("f_wu", [128, 8, 2 * DFF], BF16)
    wd = P.sb("f_wd", [128, NFC, 1024], BF16)
    load_w(P, wu, din["ffn_w_up"].v(None)[layer].re("(kc p) n -> p kc n", p=128), 8)
    load_w(P, wd, din["ffn_w_down"].v(None)[layer].re("(kc p) n -> p kc n", p=128), NFC)
    hf = P.sb("f_h", [128, 8, 512], F32)
    hb = P.sb("f_hb", [128, 8, 512], BF16)
    hh = P.sb("f_hh", [128, NFC, 512], BF16)
    gx = [P.sb(f"f_gx{i}", [128, 514], F32) for i in range(2)]
    ac = [P.sb(f"f_ac{i}", [128, 512], F32) for i in range(2)]
    halo = P.sb("f_halo", [128, NFC, 2], F32)
    P.memset("dve", halo[:], 0.0)
    sq = [P.sb(f"f_sq{i}", [128, 512], F32) for i in range(2)]
    tmp = ln_tmps(P, "f")
    pgs = [P.ps(f"f_pg{i}", [128, 512], F32) for i in range(2)]
    pus = [P.ps(f"f_pu{i}", [128, 512], F32) for i in range(2)]
    pys = [P.ps(f"f_py{i}", [128, 512], F32) for i in range(2)]
    psm = P.ps("f_psm", [128, 512], F32)
    psq = P.ps("f_psq", [128, 512], F32)
    Hiv = fm(Hin)
    if final_out is None:
        Hov = fm(Hout)
    else:
        ot = P.sb("f_ot0", [128, 1024], F32)
    cw = [f"f_cw{layer}_{j}" for j in range(3)]
    cnt = {"n": 0}

    def load_hb(ti):
        pos, w = TILES[ti]
        for c in range(8):
            P.dma("pool", hb[:, c, :w], Hiv[:, c, pos:pos + w], part=(c > 0), max_dma_last_dim=4096)

    def stage_up(ti):
        pos, w = TILES[ti]
        for fc in range(NFC):
            k = cnt["n"] % 2
            cnt["n"] += 1
            pg, pu, g, a = pgs[k], pus[k], gx[k], ac[k]
            for kc in range(8):
                P.mm(pg[:, :w], wu[:, kc, DFF + fc * 128:DFF + (fc + 1) * 128], hb[:, kc, :w], kc == 0, kc == 7)
            for kc in range(8):
                P.mm(pu[:, :w], wu[:, kc, fc * 128:(fc + 1) * 128], hb[:, kc, :w], kc == 0, kc == 7)
            P.copy("act", g[:, 0:2], halo[:, fc, :])
            P.copy("act", g[:, 2:2 + w], pg[:, :w], part=True)
            P.copy("pool", halo[:, fc, :], g[:, w:w + 2], part=True)
            P.ts("dve", a[:, :w], g[:, 2:2 + w], vcol(C, cw[2], fc), ALU.mult)
            P.stt(a[:, :w], g[:, 1:1 + w], vcol(C, cw[1], fc), a[:, :w], ALU.mult, ALU.add)
            P.stt(a[:, :w], g[:, 0:w], vcol(C, cw[0], fc), a[:, :w], ALU.mult, ALU.add)
            P.act(a[:, :w], a[:, :w], AF.Gelu_apprx_tanh, bias=vcol(C, f"f_cb{layer}", fc))
            P.tt("dve", hh[:, fc, :w], a[:, :w], pu[:, :w], ALU.mult, part=(fc > 0))
            yield

    def stage_down(ti):
        pos, w = TILES[ti]
        P.dma("sp", hf[:, :, :w], Hiv[:, :, pos:pos + w])
        if ti + 1 < len(TILES):
            load_hb(ti + 1)
        for oc in range(8):
            py = pys[oc % 2]
            for fc in range(NFC):
                P.mm(py[:, :w], wd[:, fc, oc * 128:(oc + 1) * 128], hh[:, fc, :w], fc == 0, fc == NFC - 1)
            P.stt(hf[:, oc, :w], hf[:, oc, :w], ALPHA, py[:, :w], ALU.mult, ALU.add, part=True)

    def stage_ln(ti):
        pos, w = TILES[ti]
        for _ in ln_gen(P, C, hf, w, f"ffn_g{layer}", f"ffn_b{layer}", hf, tmp, psm, psq, sq):
            yield
        if final_out is None:
            P.dma("pool", Hov[:, :, pos:pos + w], hf[:, :, :w], part=True)
        else:
            for j in range((w + 127) // 128):
                L = min(128, w - j * 128)
                p0 = pos + j * 128
                o = ot
                for half in range(2):
                    py = pys[half]
                    for c4 in range(4):
                        c = half * 4 + c4
                        P.op("pe", lambda e, o_=py[:L, c4 * 128:(c4 + 1) * 128], i=hf[:, c, j * 128:j * 128 + L],
                             idn=C.ident: e.transpose(o_.ap, i.ap, idn.ap),
                             r=[hf[:], C.ident], w=[py[:]] if c4 == 0 else (), wp=[py[:]] if c4 else ())
                    if half == 0:
                        P.copy("act", o[:L, 0:512], py[:L, :], part=False)
                    else:
                        P.copy("dve", o[:L, 512:1024], py[:L, :], part=True)
                lo = max(p0, NMETA)
                hi = p0 + L
                if hi > lo:
                    P.dma("pool", final_out[lo - NMETA:hi - NMETA, :], o[lo - p0:hi - p0, :], part=True)
                yield

    load_hb(0)
    prev_ln = None
    for ti in range(len(TILES)):
        up = stage_up(ti)
        for _ in up:
            if prev_ln is not None:
                try:
                    next(prev_ln)
                except StopIteration:
                    prev_ln = None
        if prev_ln is not None:
            for _ in prev_ln:
                pass
        stage_down(ti)
        prev_ln = stage_ln(ti)
    for _ in prev_ln:
        pass
    P.flush()
    P.release(m)


IN_SHAPES = {
    "x": [SEQ, D], "meta": [NMETA, D], "consts": [128, 1536], "vecs": None,
    "a_w_in": [D, 4096], "bdq": [128, 8, 128], "bdk": [128, 8, 128], "bdv": [128, 8, 128],
    "bdT": [128, 3, 8, 128], "a_w_if": [128, 24, 8], "a_b_if": [8, 1],
    "b_w_a": [128, 8, 128], "b_w_x": [128, 8, 128], "ab_w_out": [2 * D, D],
    "ffn_w_up": [2, D, 2 * DFF], "ffn_w_down": [2, DFF, D],
    "c_w_in": [D, 712], "c_w_qidx": [384, 512], "c_w_uq": [384, 1024], "c_w_ukT": [128, 8, 256],
    "c_w_uv": [128, 8, 2, 128], "c_w_out": [D, D],
}


def build(stop_after=None, dbg=()):
    nc = bass.Bass("TRN2", target_bir_lowering=False)
    P = Prog(nc)
    C = Ctx()
    _, nv = vec_layout()
    din = {}
    for k, shp in IN_SHAPES.items():
        if k == "vecs":
            shp = [128, nv]
        din[k] = P.dram(k, shp, F32, kind="ExternalInput")
    out = P.dram("out", [SEQ, D], F32, kind="ExternalOutput")

    def scratch(name, dt=F32):
        kind = "ExternalOutput" if name in dbg else "Internal"
        return P.dram(name, [D, T], dt, kind=kind)

    def scratch2(name, shape, dt):
        kind = "ExternalOutput" if name in dbg else "Internal"
        return P.dram(name, shape, dt, kind=kind)

    setup_consts(P, C, din)
    H0 = scratch("H0")
    phase_embed(P, C, din, H0)
    XM, SOG, XR, GGR = scratch("XM"), scratch("SOG"), scratch("XR"), scratch("GGR")
    phase_inproj(P, C, din, H0, XM, SOG, XR, GGR)
    HR = scratch("HR", BF16)
    phase_rglru(P, C, din, XR, GGR, HR)
    HM = scratch("HM", BF16)
    phase_mlstm(P, C, din, XM, SOG, HM)
    H1 = scratch("H1")
    phase_outproj(P, C, din["ab_w_out"], 16, [HM, HR], H0, H1, 0)
    H2 = scratch("H2")
    if stop_after == "L0":
        phase_ffn(P, C, din, 0, H1, H2)
        P.finish()
        return nc
    phase_ffn(P, C, din, 0, H1, H2)
    CKVT = scratch2("CKVT", [256, T], BF16)
    CKV = scratch2("CKV", [T, 256], BF16)
    KIDXT = scratch2("KIDXT", [64, T], BF16)
    WIDX = scratch2("WIDX", [T, 8], F32)
    QIDXT = scratch2("QIDXT", [512, T], BF16)
    QLATT = scratch2("QLATT", [2048, T], BF16)
    OLATT = scratch2("OLATT", [2048, T], BF16)
    phase_dsa_proj(P, C, din, H2, CKVT, CKV, KIDXT, WIDX, QIDXT, QLATT)
    if stop_after == "L1proj":
        P.finish()
        return nc
    phase_dsa_attn(P, C, din, CKVT, CKV, KIDXT, WIDX, QIDXT, QLATT, OLATT)
    if stop_after == "L1attn":
        P.finish()
        return nc
    H3 = scratch("H3")
    muv = P.mark()
    wuv = P.sb("u_wuv", [128, 8, 2, 128], BF16)
    P.dma("pool", wuv[:], din["c_w_uv"][:])
    olb = [P.sb(f"u_ol{i}", [128, 2, 8, 512], BF16) for i in range(2)]
    OLATv = OLATT.v(None).re("(p a) t -> p a t", a=16)

    def uv_loader(ti, pos, w, x, pss):
        ol = olb[ti % 2]
        P.dma("sp", ol[:, :, :, :w].re("p rc h t -> p (rc h) t"), OLATv[:, :, pos:pos + w])
        for h in range(8):
            ps = pss[h % 4]
            for rc in range(2):
                P.mm(ps[:, :w], wuv[:, h, rc, :], ol[:, rc, h, :w], rc == 0, rc == 1)
            if h % 2:
                P.copy("act", x[:, h, :w], ps[:, :w], part=(h > 0))
            else:
                P.copy("dve", x[:, h, :w], ps[:, :w], part=(h > 0))

    phase_outproj(P, C, din["c_w_out"], 8, [], H2, H3, 1, loader=uv_loader)
    P.release(muv)
    if stop_after == "L1mix":
        P.finish()
        return nc
    phase_ffn(P, C, din, 1, H3, None, final_out=out.v(None))
    P.finish()
    return nc


def host_inputs(inp):
    f = lambda a: np.ascontiguousarray(np.asarray(a, np.float32))
    vp = VecPack()
    pack_vecs(vp, inp)
    bds = [block_diag_host(inp[k][0]) for k in ("a_wq", "a_wk", "a_wv")]
    shared = {
        "meta": f(inp["meta_tokens"]), "consts": make_consts(), "vecs": vp.array(),
        "a_w_in": f(inp["a_w_in"][0]),
        "bdq": f(bds[0].transpose(1, 0, 2)), "bdk": f(bds[1].transpose(1, 0, 2)), "bdv": f(bds[2].transpose(1, 0, 2)),
        "bdT": f(np.stack([b.transpose(2, 0, 1) for b in bds], axis=1)),
        "a_w_if": f(np.asarray(inp["a_w_if"][0]).reshape(24, 128, 8).transpose(1, 0, 2)),
        "a_b_if": f(np.asarray(inp["a_b_if"][0]).reshape(8, 1)),
        "b_w_a": f(np.asarray(inp["b_w_a"][0]).transpose(1, 0, 2)),
        "b_w_x": f(np.asarray(inp["b_w_x"][0]).transpose(1, 0, 2)),
        "ab_w_out": f(inp["ab_w_out"][0]),
        "ffn_w_up": f(inp["ffn_w_up"]), "ffn_w_down": f(inp["ffn_w_down"]),
        "c_w_in": f(inp["c_w_in"][0]), "c_w_qidx": f(inp["c_w_qidx"][0]), "c_w_uq": f(inp["c_w_uq"][0]),
        "c_w_ukT": f(np.asarray(inp["c_w_uk"][0]).transpose(2, 0, 1)),
        "c_w_uv": f(np.asarray(inp["c_w_uv"][0]).reshape(8, 2, 128, 128).transpose(2, 0, 1, 3)),
        "c_w_out": f(inp["c_w_out"][0]),
    }
    return shared


_NC_CACHE = {}


def kernel(**inp):
    shared = host_inputs(inp)
    x = np.asarray(inp["x"], np.float32)
    if "full" not in _NC_CACHE:
        _NC_CACHE["full"] = build()
    nc = _NC_CACHE["full"]
    in_maps = []
    for b in range(8):
        d = dict(shared)
        d["x"] = np.ascontiguousarray(x[b])
        in_maps.append(d)
    res = run_bass_kernel_spmd(nc, in_maps, core_ids=list(range(8)))
    return np.stack([res.results[b]["out"] for b in range(8)], axis=0)


CQ_R, CKV_R = 384, 256


def phase_dsa_proj(P, C, din, H2, CKVT, CKV, KIDXT, WIDX, QIDXT, QLATT):
    m = P.mark()
    wci = P.sb("d_wci", [128, 8, 712], BF16)
    load_w(P, wci, din["c_w_in"].v(None).re("(kc p) n -> p kc n", p=128), 8)
    wqi = P.sb("d_wqi", [128, 3, 512], BF16)
    load_w(P, wqi, din["c_w_qidx"].v(None).re("(kc p) n -> p kc n", p=128), 3)
    wuq = P.sb("d_wuq", [128, 3, 1024], BF16)
    load_w(P, wuq, din["c_w_uq"].v(None).re("(kc p) n -> p kc n", p=128), 3)
    wuk = P.sb("d_wuk", [128, 8, 256], BF16)
    P.dma("pool", wuk[:], din["c_w_ukT"][:])
    m384 = P.sb("d_m384", [128, 128], F32)
    P.memset("dve", m384[:], 1.0 / 384.0)
    m256 = P.sb("d_m256", [128, 128], F32)
    P.memset("dve", m256[:], 1.0 / 256.0)
    m64 = P.sb("d_m64", [64, 64], F32)
    P.memset("dve", m64[:], 1.0 / 64.0)
    hf = P.sb("d_hf", [128, 8, 512], F32)
    hb = P.sb("d_hb", [128, 8, 512], BF16)
    raw = P.sb("d_raw", [128, 3, 512], F32)
    sq = [P.sb(f"d_sq{i}", [128, 512], F32) for i in range(2)]
    rs = P.sb("d_rs", [128, 512], F32)
    mn = P.sb("d_mn", [128, 512], F32)
    cqn = P.sb("d_cqn", [128, 3, 512], BF16)
    kvn = [P.sb(f"d_kvn{i}", [128, 2, 512], BF16) for i in range(2)]
    kvt = [P.sb(f"d_kvt{i}", [128, 4, 256], BF16) for i in range(2)]
    kin = [P.sb(f"d_kin{i}", [64, 512], BF16) for i in range(2)]
    wi = [P.sb(f"d_wi{i}", [128, 4, 8], F32) for i in range(2)]
    qix = [P.sb(f"d_qix{i}", [128, 4, 512], BF16) for i in range(2)]
    qf = P.sb("d_qf", [128, 8, 512], BF16)
    qlt = [P.sb(f"d_qlt{i}", [128, 2, 8, 512], BF16) for i in range(2)]
    pss = [P.ps(f"d_ps{i}", [128, 512], F32) for i in range(4)]
    pst = P.ps("d_pst", [128, 512], F32)
    ptr = P.ps("d_ptr", [128, 8, 128], BF16)
    pwi = P.ps("d_pwi", [128, 512], F32)
    H2v = fm(H2)
    CKVTv = fm(CKVT)
    CKVv = CKV.v(None)
    WIDXv = WIDX.v(None)
    QIDXv = fm(QIDXT)
    QLATv = QLATT.v(None).re("(p a) t -> p a t", a=16)
    KIv = KIDXT.v(None)
    n = 0

    def nxt():
        nonlocal n
        n += 1
        return pss[n % 4]

    for ti, (pos, w) in enumerate(TILES):
        k2 = ti % 2
        P.dma("sp", hf[:, :, :w], H2v[:, :, pos:pos + w])
        P.copy("pool", hb[:, :, :w], hf[:, :, :w])

        def proj_chunk(ps, c0, mcols):
            for kc in range(8):
                P.mm(ps[:mcols, :w], wci[:, kc, c0:c0 + mcols], hb[:, kc, :w], kc == 0, kc == 7)

        def rms(nch, c0, mmat, gname, outt):
            for c in range(nch):
                ps = nxt()
                proj_chunk(ps, c0 + c * 128, 128)
                P.copy("act", raw[:, c, :w], ps[:, :w], part=(c > 0))
                s = sq[c % 2]
                P.tt("pool", s[:, :w], raw[:, c, :w], raw[:, c, :w], ALU.mult)
                P.mm(pst[:, :w], mmat[:], s[:, :w], c == 0, c == nch - 1)
            P.act(rs[:, :w], pst[:, :w], AF.Sqrt, bias=C.epsc)
            P.op("dve", lambda e, o=rs[:, :w]: e.reciprocal(o.ap, o.ap), r=[rs[:]], w=[rs[:]])
            for c in range(nch):
                P.stt(outt[:, c, :w], raw[:, c, :w], vcol(C, gname, c), rs[:, :w], ALU.mult, ALU.mult, part=(c > 0))

        rms(3, 0, m384, "c_qg", cqn)
        kv = kvn[k2]
        rms(2, CQ_R, m256, "c_kvg", kv)
        P.dma("pool", CKVTv[:, :, pos:pos + w], kv[:, :, :w], part=True)
        kt = kvt[k2]
        nb = (w + 127) // 128
        for j in range(nb):
            L = min(128, w - j * 128)
            for rc in range(2):
                P.op("pe", lambda e, o=ptr[:L, j * 2 + rc, :], i=kv[:, rc, j * 128:j * 128 + L], idn=C.identb:
                     e.transpose(o.ap, i.ap, idn.ap), r=[kv[:], C.identb],
                     w=[ptr[:]] if (j == 0 and rc == 0) else (), wp=[ptr[:]] if (j or rc) else ())
        if w == 512:
            P.copy("act", kt[:].re("p j r -> p (j r)"), ptr[:].re("p a b -> p (a b)"))
            P.dma("pool", CKVv[pos:pos + 512, :].re("(j p) r -> p j r", p=128), kt[:], part=True)
        else:
            P.copy("act", kt[:w, 0, :], ptr[:w, 0:2, :].re("p a b -> p (a b)"))
            P.dma("pool", CKVv[pos:pos + w, :], kt[:w, 0, :], part=True)
        ps = nxt()
        proj_chunk(ps, CQ_R + CKV_R, 64)
        P.copy("act", raw[0:64, 0, :w], ps[0:64, :w])
        P.tt("pool", sq[0][0:64, :w], raw[0:64, 0, :w], raw[0:64, 0, :w], ALU.mult)
        P.mm(pst[0:64, :w], m64[:], raw[0:64, 0, :w], True, True)
        P.copy("act", mn[0:64, :w], pst[0:64, :w])
        P.mm(pst[0:64, :w], m64[:], sq[0][0:64, :w], True, True)
        P.tt("pool", sq[1][0:64, :w], mn[0:64, :w], mn[0:64, :w], ALU.mult)
        P.tt("dve", rs[0:64, :w], pst[0:64, :w], sq[1][0:64, :w], ALU.subtract)
        P.act(rs[0:64, :w], rs[0:64, :w], AF.Sqrt, bias=C.epsc[0:64, :])
        P.op("dve", lambda e, o=rs[0:64, :w]: e.reciprocal(o.ap, o.ap), r=[rs[:]], w=[rs[:]])
        P.tt("dve", sq[0][0:64, :w], raw[0:64, 0, :w], mn[0:64, :w], ALU.subtract)
        P.stt(sq[0][0:64, :w], sq[0][0:64, :w], vcol(C, "c_kig")[0:64, :], rs[0:64, :w], ALU.mult, ALU.mult)
        ki = kin[k2]
        P.act(ki[:, :w], sq[0][0:64, :w], AF.Identity, bias=vcol(C, "c_kib")[0:64, :])
        P.dma("pool", KIv[:, pos:pos + w], ki[:, :w], part=True)
        wt = wi[k2]
        for j in range(nb):
            L = min(128, w - j * 128)
            for kc in range(8):
                P.mm(pwi[:L, j * 8:(j + 1) * 8], hb[:, kc, j * 128:j * 128 + L], wci[:, kc, 704:712], kc == 0, kc == 7)
        if w == 512:
            P.act(wt[:].re("p j g -> p (j g)"), pwi[:, 0:32], AF.Copy, scale=float(8.0 ** -0.5))
            P.dma("pool", WIDXv[pos:pos + 512, :].re("(j p) g -> p j g", p=128), wt[:], part=True)
        else:
            P.act(wt[:w, 0, :], pwi[:w, 0:8], AF.Copy, scale=float(8.0 ** -0.5))
            P.dma("pool", WIDXv[pos:pos + w, :], wt[:w, 0, :], part=True)
        qx = qix[k2]
        for oc in range(4):
            ps = nxt()
            for kc in range(3):
                P.mm(ps[:, :w], wqi[:, kc, oc * 128:(oc + 1) * 128], cqn[:, kc, :w], kc == 0, kc == 2)
            if oc % 2:
                P.copy("act", qx[:, oc, :w], ps[:, :w], part=(oc > 0))
            else:
                P.copy("dve", qx[:, oc, :w], ps[:, :w], part=(oc > 0))
        P.dma("pool", QIDXv[:, :, pos:pos + w], qx[:, :, :w], part=True)
        for h in range(8):
            ps = nxt()
            for kc in range(3):
                P.mm(ps[:, :w], wuq[:, kc, h * 128:(h + 1) * 128], cqn[:, kc, :w], kc == 0, kc == 2)
            if h % 2:
                P.copy("act", qf[:, h, :w], ps[:, :w], part=(h > 0))
            else:
                P.copy("dve", qf[:, h, :w], ps[:, :w], part=(h > 0))
        ql = qlt[k2]
        for h in range(8):
            for rc in range(2):
                ps = nxt()
                P.mm(ps[:, :w], wuk[:, h, rc * 128:(rc + 1) * 128], qf[:, h, :w])
                first = (h == 0 and rc == 0)
                if rc:
                    P.act(ql[:, rc, h, :w], ps[:, :w], AF.Copy, scale=float(128.0 ** -0.5), part=not first)
                else:
                    P.ts("dve", ql[:, rc, h, :w], ps[:, :w], float(128.0 ** -0.5), ALU.mult, part=not first)
        P.dma("pool", QLATv[:, :, pos:pos + w], ql[:, :, :, :w].re("p rc h t -> p (rc h) t"), part=True)
    P.flush()
    P.release(m)


NKB = 33
NEG = -1.0e30


def phase_dsa_attn(P, C, din, CKVT, CKV, KIDXT, WIDX, QIDXT, QLATT, OLATT):
    m = P.mark()
    kidx2 = P.sb("t_kidx2", [128, T], BF16)
    P.dma("sp", kidx2[0:64, :], KIDXT.v(None)[:, :], part=True)
    P.dma("sp", kidx2[64:128, :], KIDXT.v(None)[:, :], part=True)
    ckvT = P.sb("t_ckvT", [128, 2, T], BF16)
    P.dma("sp", ckvT[:], fm(CKVT)[:, :, :])
    ckv = P.sb("t_ckv", [128, NKB, 256], BF16)
    for i in range(4):
        P.dma("sp", ckv[:, 8 * i:8 * i + 8, :],
              CKV.v(None)[1024 * i:1024 * (i + 1), :].re("(kb p) r -> p kb r", p=128), part=True)
    P.dma("sp", ckv[0:16, 32, :], CKV.v(None)[4096:4112, :], part=True)
    qix = [P.sb(f"t_qix{i}", [128, 4, 128], BF16) for i in range(2)]
    wid = [P.sb(f"t_wid{i}", [128, 8], F32) for i in range(2)]
    qlt = [P.sb(f"t_qlt{i}", [128, 2, 8, 128], BF16) for i in range(3)]
    score = [P.sb(f"t_score{i}", [128, T], F32) for i in range(2)]
    junk = P.sb("t_junk", [128, T], BF16)
    msk = P.sb("t_msk", [128, T + 16], BF16)
    mskTs = [P.sb(f"t_mskT{i}", [128, 40, 128], BF16) for i in range(2)]
    rl = [P.sb(f"t_rl{i}", [128, 512], F32) for i in range(2)]
    ex = [P.sb(f"t_ex{i}", [128, 4, 128], BF16) for i in range(3)]
    pp = [P.sb(f"t_pp{i}", [128, 4, 128], BF16) for i in range(3)]
    bss = [P.sb(f"t_bs{i}", [128, 8], F32) for i in range(2)]
    rden = P.sb("t_rden", [128, 8], F32)
    fr = P.sb("t_fr", [128, NBIS], F32)
    for r in range(1, NBIS + 1):
        P.op("dve", lambda e, o=fr[:, r - 1:r], v=float(2.0 ** -r): e.memset(o.ap, v), wp=[fr[:]])
    stepf = P.sb("t_stepf", [128, NBIS], F32)
    on = P.sb("t_on", [128, 8, 256], BF16)
    olT = [P.sb(f"t_olT{i}", [128, 2, 8, 128], BF16) for i in range(2)]
    zb = P.sb("t_zb", [128, 512], BF16)
    P.memset("pool", zb[:], 0.0)
    po = [P.ps(f"t_po{i}", [128, 2, 256], F32) for i in range(4)]
    pden = P.ps("t_pden", [128, 512], F32)
    pl = [P.ps(f"t_pl{i}", [128, 4, 128], F32) for i in range(2)]
    px = P.ps("t_px", [128, 512], F32)
    pxs = [px, px]
    pxb = View(px.h[:].bitcast(BF16), px.buf)
    QIDXv = fm(QIDXT)
    QLATv = QLATT.v(None).re("(p a) t -> p a t", a=16)
    OLATv = OLATT.v(None).re("(p a) t -> p a t", a=16)
    WIDXv = WIDX.v(None)
    cnt = {"nl": 0}

    def geom(qi):
        if qi == 0:
            return 0, 16, 16
        return 16 + 128 * (qi - 1), 128, 16 + 128 * qi

    def stageA1(qi):
        q0, nq, nk = geom(qi)
        k2 = qi % 2
        qx, wd, ql, sc = qix[k2], wid[k2], qlt[qi % 3], score[k2]
        P.dma("sp", qx[:, :, :nq], QIDXv[:, :, q0:q0 + nq])
        P.dma("sp", wd[:nq, :], WIDXv[q0:q0 + nq, :])
        P.dma("sp", ql[:, :, :, :nq].re("p rc h t -> p (rc h) t"), QLATv[:, :, q0:q0 + nq])
        for c0 in range(0, nk, 512):
            cw = min(512, nk - c0)
            for h in range(8):
                oc, half = h // 2, h % 2
                lo_, hi_ = half * 64, half * 64 + 64
                px_ = pxs[h % 2]
                P.mm(px_[:nq, :cw], qx[lo_:hi_, oc, :nq], kidx2[lo_:hi_, c0:c0 + cw])
                r_ = rl[h % 2]
                P.act(r_[:nq, :cw], px_[:nq, :cw], AF.Relu)
                if h == 0:
                    P.ts("dve", sc[:nq, c0:c0 + cw], r_[:nq, :cw], wd[:nq, 0:1], ALU.mult, part=(c0 > 0))
                else:
                    P.stt(sc[:nq, c0:c0 + cw], r_[:nq, :cw], wd[:nq, h:h + 1], sc[:nq, c0:c0 + cw],
                          ALU.mult, ALU.add, part=True)
                yield 1.0

    def stageA2(qi):
        q0, nq, nk = geom(qi)
        k2 = qi % 2
        sc, bs, mskT = score[k2], bss[k2], mskTs[k2]
        P.op("dve", lambda e, o=bs[:nq, 0:1], i=sc[:nq, :nk]: e.tensor_reduce(o.ap, i.ap, AX.X, ALU.max),
             r=[sc[:]], w=[bs[:]])
        P.op("dve", lambda e, o=bs[:nq, 1:2], i=sc[:nq, :nk]: e.tensor_reduce(o.ap, i.ap, AX.X, ALU.min),
             r=[sc[:]], wp=[bs[:]])
        if qi >= 1:
            P.op("dve", lambda e, o=sc[0:64, nk - 64:nk]: e.memset(o.ap, NEG), r=[bs[:]], wp=[sc[:]])
        P.tt("dve", bs[:nq, 2:3], bs[:nq, 0:1], bs[:nq, 1:2], ALU.subtract, part=True)
        yield 7.0
        if nk > KSEL:
            P.ts("dve", stepf[:nq, :], fr[:nq, :], bs[:nq, 2:3], ALU.mult)
            P.tt("dve", bs[:nq, 3:4], bs[:nq, 1:2], stepf[:nq, 0:1], ALU.add, part=True)
            for r in range(1, NBIS + 1):
                P.ts("dve", junk[:nq, :nk], sc[:nq, :nk], bs[:nq, 3:4], ALU.is_ge, 0.0, ALU.add,
                     accum=bs[:nq, 4:5])
                last = (r == NBIS)
                P.ts("dve", bs[:nq, 5:6], bs[:nq, 4:5], KSEL - 0.5, ALU.is_ge, -1.0 if last else -0.5, ALU.add,
                     part=True)
                P.stt(bs[:nq, 1:2] if last else bs[:nq, 3:4], bs[:nq, 5:6], stepf[:nq, r - 1:r], bs[:nq, 3:4],
                      ALU.mult, ALU.add, part=True)
                yield nk * 1.6 / 960.0 + 0.8
        P.ts("dve", msk[:nq, :nk], sc[:nq, :nk], bs[:nq, 1:2], ALU.is_ge)
        nkb = (nk + 127) // 128
        for b0 in range(0, nkb, 8):
            nb_ = min(8, nkb - b0)
            for j in range(nb_):
                kb = b0 + j
                kl = min(128, nk - kb * 128)
                P.op("pe", lambda e, o=pxb[:kl, j * 128:j * 128 + nq], i=msk[:nq, kb * 128:kb * 128 + kl],
                     idn=C.identb[:nq, :nq]: e.transpose(o.ap, i.ap, idn.ap), r=[msk[:], C.identb],
                     w=[px[:]] if j == 0 else (), wp=[px[:]] if j else ())
            nfull = sum(1 for j in range(nb_) if nk - (b0 + j) * 128 >= 128)
            if nfull:
                P.copy("act", mskT[:, b0:b0 + nfull, :nq],
                       pxb[:, 0:nfull * 128].re("p (a b) -> p a b", b=128)[:, :, :nq], part=(b0 > 0))
            if nfull < nb_:
                kb = b0 + nfull
                kl = nk - kb * 128
                P.copy("act", mskT[:kl, kb, :nq], pxb[:kl, nfull * 128:nfull * 128 + nq], part=(b0 > 0 or nfull > 0))
            yield 3.0

    def stageB(qi):
        q0, nq, nk = geom(qi)
        k2 = qi % 2
        ql, mskT = qlt[qi % 3], mskTs[k2]
        nkb = (nk + 127) // 128
        for i in range(4):
            P.mm(po[i][:, :, :].re("p a b -> p (a b)"), zb[:, 0:128], zb[:, :], True, False)
        P.mm(pden[:, :], zb[:, 0:128], zb[:, :], True, False)
        steps = [(kb, g) for kb in range(nkb) for g in range(2)]
        base = cnt["nl"]
        cnt["nl"] += len(steps)

        def front(n):
            kb, g = steps[n]
            kl = min(128, nk - kb * 128)
            nl = base + n
            p_l, e_, p_ = pl[nl % 2], ex[nl % 3], pp[nl % 3]
            for rc in range(2):
                P.mm(p_l[:kl, :, :nq], ckvT[:, rc, kb * 128:kb * 128 + kl], ql[:, rc, 4 * g:4 * g + 4, :nq],
                     rc == 0, rc == 1)
            P.act(e_[:kl, :, :nq], p_l[:kl, :, :nq], AF.Exp)
            P.tt("pool", p_[:kl, :, :nq], e_[:kl, :, :nq], bc3(mskT[:kl, kb, :nq], 4), ALU.mult)

        def back(n):
            kb, g = steps[n]
            kl = min(128, nk - kb * 128)
            p_ = pp[(base + n) % 3]
            for h4 in range(4):
                h = 4 * g + h4
                P.mm(po[h // 2][:nq, h % 2, :], p_[:kl, h4, :nq], ckv[:kl, kb, :], False, kb == nkb - 1)
                P.mm(pden[:nq, h:h + 1], p_[:kl, h4, :nq], C.onesb[:kl, 0:1], False, kb == nkb - 1)

        front(0)
        for n in range(len(steps)):
            if n + 1 < len(steps):
                front(n + 1)
            back(n)
            yield 3.0
        P.op("dve", lambda e, o=rden[:nq, :], i=pden[:nq, 0:8]: e.reciprocal(o.ap, i.ap), r=[pden[:]], w=[rden[:]])
        for h in range(8):
            if (h // 2) % 2:
                P.act(on[:nq, h, :], po[h // 2][:nq, h % 2, :], AF.Identity, scale=rden[:nq, h:h + 1], part=(h > 0))
            else:
                P.ts("dve", on[:nq, h, :], po[h // 2][:nq, h % 2, :], rden[:nq, h:h + 1], ALU.mult, part=(h > 0))
        yield 4.0
        ot = olT[k2]
        for rc in range(2):
            for h in range(8):
                P.op("pe", lambda e, o=pxb[:, h * 128:h * 128 + nq], i=on[:nq, h, rc * 128:(rc + 1) * 128],
                     idn=C.identb[:nq, :nq]: e.transpose(o.ap, i.ap, idn.ap), r=[on[:], C.identb],
                     w=[px[:]] if h == 0 else (), wp=[px[:]] if h else ())
            src = pxb[:, :].re("p (h q) -> p h q", h=8)[:, :, :nq]
            if rc:
                P.copy("act", ot[:, rc, :, :nq], src, part=True)
            else:
                P.copy("dve", ot[:, rc, :, :nq], src, part=False)
            yield 3.0
        P.dma("pool", OLATv[:, :, q0:q0 + nq], ot[:, :, :, :nq].re("p rc h t -> p (rc h) t"), part=True)

    def total(qi, which):
        q0, nq, nk = geom(qi)
        nkb = (nk + 127) // 128
        if which == "A1":
            return ((nk + 511) // 512) * 8 * 1.0
        if which == "A2":
            return 7.0 + (NBIS * (nk * 1.6 / 960.0 + 0.8) if nk > KSEL else 0.0) + ((nkb + 7) // 8) * 3.0
        return nkb * 2 * 3.0 + 4.0 + 6.0

    for it in range(35):
        live = []
        if it < 33:
            live.append([stageA1(it), 0.0, total(it, "A1")])
        if 1 <= it < 34:
            live.append([stageA2(it - 1), 0.0, total(it - 1, "A2")])
        if it >= 2:
            live.append([stageB(it - 2), 0.0, total(it - 2, "B")])
        while live:
            g_ = min(live, key=lambda t: t[1] / t[2])
            try:
                g_[1] += next(g_[0])
            except StopIteration:
                live.remove(g_)
    P.flush()
    P.release(m)
```
